# Optimizing a Trainium2 kernel written in Bass

```python
import math
import jax, jax.numpy as jnp
from jax import lax
import numpy as np

D_MODEL = 1024
BATCH = 2
SEQ = 8192
DEPTH = 2

CHUNK = 64
Q_BLOCK = 128
ROPE_THETA = 500000.0
EPS = 1e-6

GLA_HEADS = 4
GLA_KEY_DIM = D_MODEL // 2
GLA_VAL_DIM = D_MODEL
GLA_DK = GLA_KEY_DIM // GLA_HEADS
GLA_DV = GLA_VAL_DIM // GLA_HEADS
GLA_LOW_RANK = 16
GLA_GATE_NORMALIZER = 16.0

SSM_D_INNER = 2 * D_MODEL
SSM_HEAD_DIM = 64
SSM_HEADS = SSM_D_INNER // SSM_HEAD_DIM
SSM_GROUPS = 4
SSM_HEADS_PER_GROUP = SSM_HEADS // SSM_GROUPS
SSM_STATE = 128
SSM_CONV_W = 4
SSM_BC_WIDTH = SSM_GROUPS * SSM_STATE
SSM_CONV_DIM = SSM_D_INNER + 2 * SSM_BC_WIDTH

DIFF_HEADS = 8
DIFF_HEAD_DIM = 64
DIFF_V_DIM = 2 * DIFF_HEAD_DIM
DIFF_QK_WIDTH = DIFF_HEADS * 2 * DIFF_HEAD_DIM
DIFF_V_WIDTH = DIFF_HEADS * DIFF_V_DIM
ROT_DIM = DIFF_HEAD_DIM // 4
DIFF_SUBLN_EPS = 1e-5

N_BRANCHES = 3
D_FF = 4 * D_MODEL

IN_SPLITS = (GLA_KEY_DIM, GLA_KEY_DIM, GLA_VAL_DIM, GLA_LOW_RANK, GLA_VAL_DIM,
             SSM_D_INNER, SSM_CONV_DIM, SSM_HEADS,
             DIFF_QK_WIDTH, DIFF_QK_WIDTH, DIFF_V_WIDTH,
             N_BRANCHES * D_MODEL)
IN_COLS = (GLA_KEY_DIM + GLA_KEY_DIM + GLA_VAL_DIM + GLA_LOW_RANK + GLA_VAL_DIM
           + SSM_D_INNER + SSM_CONV_DIM + SSM_HEADS
           + DIFF_QK_WIDTH + DIFF_QK_WIDTH + DIFF_V_WIDTH + N_BRANCHES * D_MODEL)

kernel_name = 'hybrid_gated_gla_ssd_diffattn_trunk'

F32 = jnp.float32


def _rms(x, eps=EPS):
    xf = x.astype(F32)
    return xf * lax.rsqrt(jnp.mean(xf * xf, axis=-1, keepdims=True) + eps)


def rmsnorm(x, g):
    return (_rms(x) * g.astype(F32)).astype(x.dtype)


def _split_points():
    pts = []
    acc = 0
    for s in IN_SPLITS[:-1]:
        acc += s
        pts.append(acc)
    return pts


def segsum(t):
    T = t.shape[-1]
    tr = jnp.broadcast_to(t[..., :, None], t.shape + (T,))
    strict = jnp.tril(jnp.ones((T, T), bool), -1)
    cs = jnp.cumsum(jnp.where(strict, tr, 0.0), axis=-2)
    return jnp.where(jnp.tril(jnp.ones((T, T), bool)), cs, -jnp.inf)


def gla_mixer(q, k, v, gk_low, g_out, w_gk2, b_gk, norm_g):
    bsz, s_len, _ = q.shape
    nc = s_len // CHUNK
    gk = jax.nn.log_sigmoid(gk_low.astype(F32) @ w_gk2.astype(F32) + b_gk.astype(F32)) / GLA_GATE_NORMALIZER

    def to_chunks(t, d):
        return t.astype(F32).reshape(bsz, nc, CHUNK, GLA_HEADS, d).transpose(1, 0, 3, 2, 4)

    qc = to_chunks(q, GLA_DK) * (GLA_DK ** -0.5)
    kc = to_chunks(k, GLA_DK)
    vc = to_chunks(v, GLA_DV)
    bc = jnp.cumsum(to_chunks(gk, GLA_DK), axis=-2)
    causal = jnp.tril(jnp.ones((CHUNK, CHUNK), bool))

    def step(state, inp):
        qi, ki, vi, bi = inp
        o_inter = jnp.einsum('bhcd,bhde->bhce', qi * jnp.exp(bi), state)
        rel = bi[:, :, :, None, :] - bi[:, :, None, :, :]
        decay = jnp.exp(jnp.where(causal[:, :, None], rel, -jnp.inf))
        scores = jnp.einsum('bhid,bhjd,bhijd->bhij', qi, ki, decay)
        o = o_inter + jnp.einsum('bhij,bhje->bhie', scores, vi)
        b_last = bi[:, :, -1:, :]
        state = state * jnp.exp(b_last)[:, :, 0, :, None] + jnp.einsum(
            'bhcd,bhce->bhde', ki * jnp.exp(b_last - bi), vi)
        return state, o

    s0 = jnp.zeros((bsz, GLA_HEADS, GLA_DK, GLA_DV), F32)
    _, o = lax.scan(step, s0, (qc, kc, vc, bc))
    o = o.transpose(1, 0, 3, 2, 4).reshape(bsz, s_len, GLA_HEADS, GLA_DV)
    g = g_out.astype(F32).reshape(bsz, s_len, GLA_HEADS, GLA_DV)
    o = _rms(o) * norm_g.astype(F32) * jax.nn.silu(g)
    return o.reshape(bsz, s_len, GLA_VAL_DIM).astype(q.dtype)


def ssd_mixer(z, xbc, dt_raw, conv_w, conv_b, dt_bias, a_log, d_skip, norm_g):
    bsz, s_len, _ = z.shape
    nc = s_len // CHUNK
    G, R, P, N = SSM_GROUPS, SSM_HEADS_PER_GROUP, SSM_HEAD_DIM, SSM_STATE
    xp = jnp.pad(xbc.astype(F32), ((0, 0), (SSM_CONV_W - 1, 0), (0, 0)))
    conv = conv_b.astype(F32)
    for i in range(SSM_CONV_W):
        conv = conv + xp[:, i:i + s_len, :] * conv_w[i].astype(F32)
    xbc = jax.nn.silu(conv)
    xs = xbc[..., :SSM_D_INNER]
    bm = xbc[..., SSM_D_INNER:SSM_D_INNER + SSM_BC_WIDTH]
    cm = xbc[..., SSM_D_INNER + SSM_BC_WIDTH:]
    dt = jax.nn.softplus(dt_raw.astype(F32) + dt_bias.astype(F32))
    a = -jnp.exp(a_log.astype(F32))

    x_c = xs.reshape(bsz, nc, CHUNK, G, R, P)
    dtc = dt.reshape(bsz, nc, CHUNK, G, R)
    xdt = x_c * dtc[..., None]
    bc = bm.reshape(bsz, nc, CHUNK, G, N)
    cc = cm.reshape(bsz, nc, CHUNK, G, N)
    da = (dtc * a.reshape(G, R)).transpose(0, 3, 4, 1, 2)
    a_cum = jnp.cumsum(da, axis=-1)

    lmat = jnp.exp(segsum(da))
    y_diag = jnp.einsum('bclgn,bcsgn,bgrcls,bcsgrp->bclgrp', cc, bc, lmat, xdt)
    decay_states = jnp.exp(a_cum[..., -1:] - a_cum)
    states = jnp.einsum('bclgn,bgrcl,bclgrp->bcgrpn', bc, decay_states, xdt)
    states = jnp.concatenate([jnp.zeros_like(states[:, :1]), states], axis=1)
    chunk_decay = jnp.exp(segsum(jnp.pad(a_cum[..., -1], ((0, 0), (0, 0), (0, 0), (1, 0)))))
    states = jnp.einsum('bgrzc,bcgrpn->bzgrpn', chunk_decay, states)[:, :-1]
    y_off = jnp.einsum('bclgn,bcgrpn,bgrcl->bclgrp', cc, states, jnp.exp(a_cum))
    y = y_diag + y_off + x_c * d_skip.astype(F32).reshape(G, R)[:, :, None]
    y = y.reshape(bsz, s_len, SSM_D_INNER) * jax.nn.silu(z.astype(F32))
    y = _rms(y.reshape(bsz, s_len, G, SSM_D_INNER // G)).reshape(bsz, s_len, SSM_D_INNER)
    return (y * norm_g.astype(F32)).astype(z.dtype)


def partial_rope(t, cos, sin):
    half = ROT_DIM // 2
    t1 = t[..., :half]
    t2 = t[..., half:ROT_DIM]
    return jnp.concatenate([t1 * cos - t2 * sin, t2 * cos + t1 * sin, t[..., ROT_DIM:]], axis=-1)


def diff_mixer(q, k, v, positions, lq1, lk1, lq2, lk2, norm_g, lambda_init):
    bsz, s_len, _ = q.shape
    nb = s_len // Q_BLOCK
    q = q.astype(F32).reshape(bsz, s_len, DIFF_HEADS, 2, DIFF_HEAD_DIM)
    k = k.astype(F32).reshape(bsz, s_len, DIFF_HEADS, 2, DIFF_HEAD_DIM)
    v = v.astype(F32).reshape(bsz, s_len, DIFF_HEADS, DIFF_V_DIM)
    inv_freq = ROPE_THETA ** (-jnp.arange(0, ROT_DIM, 2, dtype=F32) / ROT_DIM)
    ang = positions.astype(F32)[..., None] * inv_freq
    cos = jnp.cos(ang)[:, :, None, None, :]
    sin = jnp.sin(ang)[:, :, None, None, :]
    q = partial_rope(q, cos, sin)
    k = partial_rope(k, cos, sin)
    lam = (jnp.exp(jnp.sum(lq1.astype(F32) * lk1.astype(F32)))
           - jnp.exp(jnp.sum(lq2.astype(F32) * lk2.astype(F32))) + lambda_init)

    qb = q.reshape(bsz, nb, Q_BLOCK, DIFF_HEADS, 2, DIFF_HEAD_DIM).transpose(1, 0, 3, 4, 2, 5)
    kt = k.transpose(0, 2, 3, 1, 4)
    vt = v.transpose(0, 2, 1, 3)
    key_chunk = jnp.arange(s_len) // CHUNK
    scale = DIFF_HEAD_DIM ** -0.5

    def attend(args):
        q_blk, blk = args
        q_chunk = (blk * Q_BLOCK + jnp.arange(Q_BLOCK)) // CHUNK
        allowed = key_chunk[None, :] <= q_chunk[:, None]
        s = jnp.einsum('bhtqd,bhtkd->bhtqk', q_blk, kt) * scale
        p = jax.nn.softmax(jnp.where(allowed, s, -jnp.inf), axis=-1)
        w = p[:, :, 0] - lam * p[:, :, 1]
        return jnp.einsum('bhqk,bhkd->bhqd', w, vt)

    o = lax.map(attend, (qb, jnp.arange(nb)))
    o = o.transpose(1, 0, 3, 2, 4).reshape(bsz, s_len, DIFF_HEADS, DIFF_V_DIM)
    o = _rms(o, DIFF_SUBLN_EPS) * norm_g.astype(F32) * (1.0 - lambda_init)
    return o.reshape(bsz, s_len, DIFF_V_WIDTH).astype(positions.dtype if False else jnp.result_type(norm_g))


def setup_inputs(seed: int = 0) -> dict:
    key = jax.random.key(seed)
    ks = jax.random.split(key, 32)

    def nrm(k, shape, scale):
        return jax.random.normal(k, shape, F32) * scale

    x = jax.random.normal(ks[0], (BATCH, SEQ, D_MODEL), F32)
    offset = jax.random.randint(ks[1], (BATCH,), 0, 4096, dtype=jnp.int32)
    positions = offset[:, None] + jnp.arange(SEQ, dtype=jnp.int32)[None, :]
    dt = jnp.exp(jax.random.uniform(ks[10], (DEPTH, SSM_HEADS), F32, math.log(1e-3), math.log(1e-1)))
    return {
        'x': x,
        'positions': positions,
        'norm_mix_g': 1.0 + nrm(ks[2], (DEPTH, D_MODEL), 0.01),
        'w_in': nrm(ks[3], (DEPTH, D_MODEL, IN_COLS), D_MODEL ** -0.5),
        'b_gate': nrm(ks[4], (DEPTH, N_BRANCHES * D_MODEL), 0.01),
        'gla_w_gk2': nrm(ks[5], (DEPTH, GLA_LOW_RANK, GLA_KEY_DIM), GLA_LOW_RANK ** -0.5),
        'gla_b_gk': nrm(ks[6], (DEPTH, GLA_KEY_DIM), 0.01),
        'gla_norm_g': 1.0 + nrm(ks[7], (DEPTH, GLA_DV), 0.01),
        'ssm_conv_w': nrm(ks[8], (DEPTH, SSM_CONV_W, SSM_CONV_DIM), SSM_CONV_W ** -0.5),
        'ssm_conv_b': nrm(ks[9], (DEPTH, SSM_CONV_DIM), 0.01),
        'ssm_dt_bias': dt + jnp.log(-jnp.expm1(-dt)),
        'ssm_a_log': jnp.log(jax.random.uniform(ks[11], (DEPTH, SSM_HEADS), F32, 1.0, 16.0)),
        'ssm_d': 1.0 + nrm(ks[12], (DEPTH, SSM_HEADS), 0.01),
        'ssm_norm_g': 1.0 + nrm(ks[13], (DEPTH, SSM_D_INNER), 0.01),
        'diff_lq1': nrm(ks[14], (DEPTH, DIFF_HEAD_DIM), 0.1),
        'diff_lk1': nrm(ks[15], (DEPTH, DIFF_HEAD_DIM), 0.1),
        'diff_lq2': nrm(ks[16], (DEPTH, DIFF_HEAD_DIM), 0.1),
        'diff_lk2': nrm(ks[17], (DEPTH, DIFF_HEAD_DIM), 0.1),
        'diff_norm_g': 1.0 + nrm(ks[18], (DEPTH, DIFF_V_DIM), 0.01),
        'w_br_gla': nrm(ks[19], (DEPTH, GLA_VAL_DIM, D_MODEL), GLA_VAL_DIM ** -0.5),
        'w_br_ssm': nrm(ks[20], (DEPTH, SSM_D_INNER, D_MODEL), SSM_D_INNER ** -0.5),
        'w_br_diff': nrm(ks[21], (DEPTH, DIFF_V_WIDTH, D_MODEL), DIFF_V_WIDTH ** -0.5),
        'w_out': nrm(ks[22], (DEPTH, D_MODEL, D_MODEL), D_MODEL ** -0.5),
        'norm_mlp_g': 1.0 + nrm(ks[23], (DEPTH, D_MODEL), 0.01),
        'w_mlp_up': nrm(ks[24], (DEPTH, D_MODEL, D_FF), D_MODEL ** -0.5),
        'w_mlp_down': nrm(ks[25], (DEPTH, D_FF, D_MODEL), D_FF ** -0.5),
        'norm_final_g': 1.0 + nrm(ks[26], (D_MODEL,), 0.01),
    }


def reference(x, positions, norm_mix_g, w_in, b_gate, gla_w_gk2, gla_b_gk, gla_norm_g,
              ssm_conv_w, ssm_conv_b, ssm_dt_bias, ssm_a_log, ssm_d, ssm_norm_g,
              diff_lq1, diff_lk1, diff_lq2, diff_lk2, diff_norm_g,
              w_br_gla, w_br_ssm, w_br_diff, w_out, norm_mlp_g, w_mlp_up, w_mlp_down,
              norm_final_g):
    bsz, s_len, _ = x.shape
    pts = _split_points()
    for l in range(DEPTH):
        h = rmsnorm(x, norm_mix_g[l])
        proj = h @ w_in[l]
        (a_q, a_k, a_v, a_gk, a_g, b_z, b_xbc, b_dt,
         c_q, c_k, c_v, gate_logits) = jnp.split(proj, pts, axis=-1)
        y_gla = gla_mixer(a_q, a_k, a_v, a_gk, a_g, gla_w_gk2[l], gla_b_gk[l], gla_norm_g[l])
        y_ssm = ssd_mixer(b_z, b_xbc, b_dt, ssm_conv_w[l], ssm_conv_b[l], ssm_dt_bias[l],
                          ssm_a_log[l], ssm_d[l], ssm_norm_g[l])
        lambda_init = 0.8 - 0.6 * math.exp(-0.3 * l)
        y_diff = diff_mixer(c_q, c_k, c_v, positions, diff_lq1[l], diff_lk1[l], diff_lq2[l],
                            diff_lk2[l], diff_norm_g[l], lambda_init)
        gates = jax.nn.sigmoid((gate_logits + b_gate[l]).astype(F32))
        gates = gates.reshape(bsz, s_len, N_BRANCHES, D_MODEL).astype(x.dtype)
        mixed = (gates[:, :, 0] * (y_gla @ w_br_gla[l])
                 + gates[:, :, 1] * (y_ssm @ w_br_ssm[l])
                 + gates[:, :, 2] * (y_diff.astype(x.dtype) @ w_br_diff[l]))
        x = x + mixed @ w_out[l]
        h = rmsnorm(x, norm_mlp_g[l])
        x = x + jnp.square(jax.nn.relu(h @ w_mlp_up[l])) @ w_mlp_down[l]
    return rmsnorm(x, norm_final_g)
```

```python
import math
from contextlib import ExitStack
import numpy as np
import ml_dtypes
import concourse.bass as bass
import concourse.mybir as mybir
from concourse.bass_utils import run_bass_kernel_spmd

F32 = mybir.dt.float32
BF16 = mybir.dt.bfloat16
I32 = mybir.dt.int32
AF = mybir.ActivationFunctionType
ALU = mybir.AluOpType
AX = mybir.AxisListType

NCORES = 8
B, SEQ, DM, DEPTH = 2, 8192, 1024, 2
NT = 2048
TT = 512
EPS = 1e-6
ENGS = ("pe", "act", "dve", "pool", "sp")


class Buf:
    __slots__ = ("name", "t", "w", "r", "dsem", "dcnt")

    def __init__(self, name, t=None):
        self.name = name
        self.t = t
        self.w = None
        self.r = {}
        self.dsem = None
        self.dcnt = 0

    def __getitem__(self, idx):
        return self.t[idx]


class VBuf:
    def __init__(self, name, parent, ap):
        self.name = name
        self.parent = parent
        self.t = ap

    def __getitem__(self, idx):
        return self.t[idx]

    w = property(lambda s: s.parent.w, lambda s, v: setattr(s.parent, "w", v))
    r = property(lambda s: s.parent.r, lambda s, v: setattr(s.parent, "r", v))
    dsem = property(lambda s: s.parent.dsem, lambda s, v: setattr(s.parent, "dsem", v))
    dcnt = property(lambda s: s.parent.dcnt, lambda s, v: setattr(s.parent, "dcnt", v))


class Sched:
    def __init__(self, nc, stack):
        self.nc = nc
        self.stack = stack
        self.q = {e: [] for e in ENGS}
        self.sem = {}
        for e in ENGS:
            self.sem[e] = stack.enter_context(nc.semaphore("s_" + e))
        self.cnt = {e: 0 for e in ENGS}
        self.waited = {e: {} for e in ENGS}
        self.ndsem = 0
        self.final_waits = {}
        self.alloc_stack = stack
        self.nname = 0
        self.free_dsems = {}
        self.dsem_cls = {}
        self.phase_bufs = []
        self.global_bufs = []

    def sb(self, name, shape, dt):
        self.nname += 1
        t = self.alloc_stack.enter_context(self.nc.sbuf_tensor("sb%d_%s" % (self.nname, name), list(shape), dt))
        b = Buf(name, t)
        self.phase_bufs.append(b)
        return b

    def ps(self, name, shape, dt=F32):
        self.nname += 1
        t = self.alloc_stack.enter_context(self.nc.psum_tensor("ps%d_%s" % (self.nname, name), list(shape), dt))
        return Buf(name, t)

    def view(self, name, ap):
        return Buf(name, ap)

    def _dsem(self, b, eng="sp"):
        cls = {"pool": "sw", "cc": "cc"}.get(eng, "hw")
        if b.dsem is not None:
            assert self.dsem_cls[b.dsem] == cls, (b.name, cls)
        if b.dsem is None:
            pool = self.free_dsems.setdefault(cls, [])
            if pool:
                b.dsem, b.dcnt = pool.pop()
            else:
                b.dsem = "d%d" % self.ndsem
                self.ndsem += 1
                self.sem[b.dsem] = self.stack.enter_context(self.nc.semaphore(b.dsem))
                self.dsem_cls[b.dsem] = cls
        return b.dsem

    def _deps(self, eng, reads, writes, same_ok):
        deps = {}

        def add(k, v):
            if deps.get(k, 0) < v:
                deps[k] = v

        for b in reads:
            if b.w is not None:
                add(*b.w)
        for b in writes:
            if b.w is not None:
                add(*b.w)
            for k, v in b.r.items():
                add(k, v)
        waits = []
        wd = self.waited[eng]
        for k, v in deps.items():
            if k == eng and same_ok:
                continue
            if wd.get(k, 0) >= v:
                continue
            wd[k] = v
            waits.append((k, v))
        return waits

    def _commit(self, tk, reads, writes):
        k, v = tk
        for b in writes:
            b.w = tk
            b.r = {}
        for b in reads:
            if b.r.get(k, 0) < v:
                b.r[k] = v

    def op(self, eng, fn, reads=(), writes=()):
        waits = self._deps(eng, reads, writes, same_ok=(eng == "pe"))
        self.cnt[eng] += 1
        tk = (eng, self.cnt[eng])
        sem = self.sem
        me = sem[eng]

        def emit(E):
            for k, v in waits:
                E.wait_ge(sem[k], v)
            fn(E).then_inc(me, 1)

        self.q[eng].append(emit)
        self._commit(tk, reads, writes)
        return tk

    def dma(self, eng, out_ap, in_ap, reads=(), writes=(), sembuf=None, group=False):
        sb = sembuf if sembuf is not None else (writes[0] if writes else reads[0])
        dk = self._dsem(sb, eng)
        saved = None
        if group and writes and writes[0].w is not None and writes[0].w[0] == dk:
            saved = writes[0].w
            writes[0].w = None
        waits = self._deps(eng, reads, writes, same_ok=False)
        if saved is not None:
            writes[0].w = saved
        sb.dcnt += 16
        tk = (dk, sb.dcnt)
        sem = self.sem
        ds = sem[dk]

        def emit(E):
            for k, v in waits:
                E.wait_ge(sem[k], v)
            E.dma_start(out=out_ap, in_=(in_ap(E) if callable(in_ap) else in_ap)).then_inc(ds, 16)

        self.q[eng].append(emit)
        self._commit(tk, reads, writes)
        self.final_waits[dk] = sb.dcnt
        return tk

    def collective(self, kind, sb_, s_ap, db_, d_ap, groups):
        dk = self._dsem(db_, "cc")
        waits = self._deps("pool", [sb_], [db_], same_ok=False)
        db_.dcnt += 1
        tk = (dk, db_.dcnt)
        sem = self.sem
        ds = sem[dk]

        def emit(E):
            for k, v in waits:
                E.wait_ge(sem[k], v)
            E.collective_compute(kind, ALU.bypass, replica_groups=groups, ins=[s_ap], outs=[d_ap]).then_inc(ds, 1)

        self.q["pool"].append(emit)
        self._commit(tk, [sb_], [db_])
        self.final_waits[dk] = db_.dcnt
        return tk

    def mm(self, ob, o, lb, l, rb, r, start=True, stop=True):
        return self.op("pe", lambda E: E.matmul(o, lhsT=l, rhs=r, start=start, stop=stop), reads=[lb, rb], writes=[ob])

    def act(self, ob, o, ib, i, func, bias=None, scale=None, accum=None, rd=(), wr=()):
        kw = {}
        if bias is not None:
            kw["bias"] = bias
        if scale is not None:
            kw["scale"] = scale
        if accum is not None:
            kw["accum_out"] = accum
        return self.op("act", lambda E: E.activation(out=o, in_=i, func=func, **kw), reads=[ib] + list(rd), writes=[ob] + list(wr))

    def tt(self, eng, ob, o, ab, a, bb, b, op):
        return self.op(eng, lambda E: E.tensor_tensor(out=o, in0=a, in1=b, op=op), reads=[ab, bb], writes=[ob])

    def ts(self, eng, ob, o, ab, a, s1, s2, op0, op1=None, rd=()):
        if op1 is None:
            return self.op(eng, lambda E: E.tensor_scalar(out=o, in0=a, scalar1=s1, scalar2=None, op0=op0), reads=[ab] + list(rd), writes=[ob])
        return self.op(eng, lambda E: E.tensor_scalar(out=o, in0=a, scalar1=s1, scalar2=s2, op0=op0, op1=op1), reads=[ab] + list(rd), writes=[ob])

    def stt(self, ob, o, ab, a, sc, bb, b, op0, op1, rd=()):
        return self.op("dve", lambda E: E.scalar_tensor_tensor(out=o, in0=a, scalar=sc, in1=b, op0=op0, op1=op1), reads=[ab, bb] + list(rd), writes=[ob])

    def copy(self, eng, ob, o, ib, i):
        if eng == "act":
            return self.op("act", lambda E: E.copy(out=o, in_=i), reads=[ib], writes=[ob])
        return self.op(eng, lambda E: E.tensor_copy(out=o, in_=i), reads=[ib], writes=[ob])

    def memset(self, eng, ob, o, val):
        return self.op(eng, lambda E: E.memset(o, val), writes=[ob])

    def recip(self, ob, o, ib, i):
        return self.op("dve", lambda E: E.reciprocal(out=o, in_=i), reads=[ib], writes=[ob])

    phase_id = 0

    def begin_phase(self):
        self.phase_id += 1
        self.gstack = self.stack if not hasattr(self, "gstack") else self.gstack
        self.pstack = ExitStack()
        self.pstack.__enter__()
        self.alloc_stack = self.pstack

    def end_phase(self):
        sem = self.sem
        cnt = dict(self.cnt)
        fw = dict(self.final_waits)
        for e in ENGS:
            waits = [(o, cnt[o]) for o in ENGS if o != e and cnt[o] > self.waited[e].get(o, 0)]
            waits += [(k, v) for k, v in fw.items() if v > self.waited[e].get(k, 0)]
            for k, v in waits:
                self.waited[e][k] = v

            def emit(E, waits=waits):
                for k, v in waits:
                    E.wait_ge(sem[k], v)
            self.q[e].append(emit)
        self.finish()
        self.q = {e: [] for e in ENGS}
        for e in ENGS:
            self.sem[e] = self.stack.enter_context(self.nc.semaphore("s_%s_%d" % (e, self.phase_id)))
            self.cnt[e] = 0
            for x in ENGS:
                self.waited[x].pop(e, None)
        for b in self.global_bufs:
            if b.w is not None and b.w[0] in ENGS:
                b.w = None
            b.r = {k: v for k, v in b.r.items() if k not in ENGS}
        for b in self.phase_bufs:
            if b.dsem is not None:
                self.free_dsems[self.dsem_cls[b.dsem]].append((b.dsem, b.dcnt))
                b.dsem = None
        self.phase_bufs = []
        self.pstack.__exit__(None, None, None)
        self.alloc_stack = self.stack

    def finish(self):
        nc = self.nc
        sem = self.sem
        q = self.q
        fw = dict(self.final_waits)
        with nc.Block() as block:
            @block.tensor
            def _(E):
                for f in q["pe"]:
                    f(E)

            @block.scalar
            def _(E):
                for f in q["act"]:
                    f(E)

            @block.vector
            def _(E):
                for f in q["dve"]:
                    f(E)

            @block.gpsimd
            def _(E):
                for f in q["pool"]:
                    f(E)

            @block.sync
            def _(E):
                for f in q["sp"]:
                    f(E)
                for k, v in fw.items():
                    E.wait_ge(sem[k], v)


class Ring:
    def __init__(self, bufs):
        self.bufs = bufs
        self.i = 0

    def next(self):
        b = self.bufs[self.i % len(self.bufs)]
        self.i += 1
        return b


def dram(nc, name, shape, dt, kind):
    return nc.dram_tensor(name, list(shape), dt, kind=kind).ap()


def kcp(ap):
    return ap.rearrange("(kc p) n -> p kc n", p=128)


def rms_stats(S, C, xb, nch, n, ps_ring, sq_ring, rstd, tmp, nfeat, eps):
    pss = ps_ring.next()
    for kc in range(nch):
        sq = sq_ring.next()
        S.act(sq, sq[:, :n], xb, xb[:, kc, :n], AF.Square)
        S.mm(pss, pss[:, :n], C["ones"], C["ones"][:], sq, sq[:, :n], start=(kc == 0), stop=(kc == nch - 1))
    S.act(tmp, tmp[:, :n], pss, pss[:, :n], AF.Sqrt, bias=C["eps%g" % eps][:, 0:1], scale=1.0 / nfeat, rd=[C["eps%g" % eps]])
    S.recip(rstd, rstd[:, :n], tmp, tmp[:, :n])


def load_consts(S, nc, cst_ap, extra_eps=()):
    C = {}
    C["ones"] = S.sb("c_ones", [128, 128], F32)
    S.dma("sp", C["ones"][:], cst_ap[0], writes=[C["ones"]])
    for e in (EPS,) + tuple(extra_eps):
        b = S.sb("c_eps%g" % e, [128, 1], F32)
        S.memset("dve", b, b[:], float(e))
        C["eps%g" % e] = b
    return C


def build_tok(mode):
    nc = bass.Bass("TRN2", target_bir_lowering=False)
    A = {}
    A["xT"] = dram(nc, "xT", [DM, NT], F32, "ExternalInput")
    A["g_next"] = dram(nc, "g_next", [128, 8], F32, "ExternalInput")
    A["cst"] = dram(nc, "cst", [4, 128, 128], F32, "ExternalInput")
    if mode != "first":
        A["hT"] = dram(nc, "hT", [DM, NT], BF16, "ExternalInput")
        yT = dram(nc, "yT", [4096, NT], BF16, "ExternalInput")
        A["load_y"] = lambda S, yub, t: S.dma("act", yub[:], kcp(yT[:, t * TT:(t + 1) * TT]), writes=[yub])
        A["w_gate"] = dram(nc, "w_gate", [DM, 3072], F32, "ExternalInput")
        A["b_gate"] = dram(nc, "b_gate", [128, 24], F32, "ExternalInput")
        A["w_br"] = dram(nc, "w_br", [4096, DM], F32, "ExternalInput")
        A["w_out"] = dram(nc, "w_out", [DM, DM], F32, "ExternalInput")
        A["w_up"] = dram(nc, "w_up", [DM, 4096], F32, "ExternalInput")
        A["w_dn"] = dram(nc, "w_dn", [4096, DM], F32, "ExternalInput")
        A["g_mlp"] = dram(nc, "g_mlp", [128, 8], F32, "ExternalInput")
    if mode == "last":
        A["outT"] = dram(nc, "outT", [DM, NT], F32, "ExternalOutput")
    else:
        A["hT_out"] = dram(nc, "hT_out", [DM, NT], BF16, "ExternalOutput")
        A["hT_lo"] = dram(nc, "hT_lo", [DM, NT], BF16, "ExternalOutput")
        if mode == "mid":
            A["xT_out"] = dram(nc, "xT_out", [DM, NT], F32, "ExternalOutput")
    with ExitStack() as st:
        S = Sched(nc, st)
        S.begin_phase()
        emit_tok(nc, S, A, mode)
        S.end_phase()
    return nc


def emit_tok(nc, S, A, mode):
    xT, gn, cst = A["xT"], A["g_next"], A["cst"]
    if mode != "first":
        hT, w_gate, b_gate, w_br, w_out, w_up, w_dn, gm = (A[k] for k in ("hT", "w_gate", "b_gate", "w_br", "w_out", "w_up", "w_dn", "g_mlp"))
    if mode == "last":
        outT = A["outT"]
    else:
        hTo, hTlo = A["hT_out"], A["hT_lo"]
        if mode == "mid":
            xTo = A["xT_out"]
    if True:
        C = load_consts(S, nc, cst)
        gnb = S.sb("gnb", [128, 8], F32)
        S.dma("sp", gnb[:], gn, writes=[gnb])
        xb = S.sb("xb", [128, 8, TT], F32)
        ps_ring = Ring([S.ps("ps%d" % i, [128, TT], F32) for i in range(7)])
        sq_ring = Ring([S.sb("sq%d" % i, [128, TT], F32) for i in range(2)])
        rstd = S.sb("rstd", [128, TT], F32)
        rtmp = S.sb("rtmp", [128, TT], F32)
        hn_ring = Ring([S.sb("hn%d" % i, [128, 8, TT], BF16) for i in range(2)])
        hl_ring = Ring([S.sb("hl%d" % i, [128, 8, TT], BF16) for i in range(2)])
        h32_ring = Ring([S.sb("h32_%d" % i, [128, TT], F32) for i in range(2)])
        if mode == "last":
            oc_ring = Ring([S.sb("oc%d" % i, [128, TT], F32) for i in range(3)])
        if mode != "first":
            bgb = S.sb("bgb", [128, 24], F32)
            S.dma("sp", bgb[:], b_gate, writes=[bgb])
            gmb = S.sb("gmb", [128, 8], F32)
            S.dma("sp", gmb[:], gm, writes=[gmb])
            hb_ring = Ring([S.sb("hb%d" % i, [128, 8, TT], BF16) for i in range(2)])
            yub = S.sb("yub", [128, 32, TT], BF16)
            mixed = S.sb("mixed", [128, 8, TT], BF16)
            h2 = S.sb("h2", [128, 8, TT], BF16)
            g_ring = Ring([S.sb("g%d" % i, [128, TT], F32) for i in range(6)])
            acc_ring = Ring([S.sb("acc%d" % i, [128, TT], F32) for i in range(2)])
            tmp_ring = Ring([S.sb("tmp%d" % i, [128, TT], F32) for i in range(2)])
            r_ring = Ring([S.sb("r%d" % i, [128, TT], F32) for i in range(2)])
            w_ring = Ring([S.sb("w%d" % i, [128, 7168], BF16) for i in range(3)])

        for t in range(NT // TT):
            ts_ = slice(t * TT, (t + 1) * TT)
            S.dma("sp", xb[:], kcp(xT[:, ts_]), writes=[xb])
            if mode != "first":
                hb = hb_ring.next()
                S.dma("sp", hb[:], kcp(hT[:, ts_]), writes=[hb])
                A["load_y"](S, yub, t)
                for oc in range(8):
                    w = w_ring.next()
                    wg = w[:, 0:3072].rearrange("p (kc br j) -> p kc br j", kc=8, br=3)
                    wb = w[:, 3072:7168].rearrange("p (kc j) -> p kc j", kc=32)
                    for br in range(3):
                        c0 = br * 1024 + oc * 128
                        S.dma("pool", wg[:, :, br, :], kcp(w_gate[:, c0:c0 + 128]), writes=[w], group=(br > 0))
                    S.dma("pool", wb, kcp(w_br[:, oc * 128:(oc + 1) * 128]), writes=[w], group=True)
                    gts = []
                    for br in range(3):
                        pg = ps_ring.next()
                        for kc in range(8):
                            S.mm(pg, pg[:], w, wg[:, kc, br, :], hb, hb[:, kc, :], start=(kc == 0), stop=(kc == 7))
                        g = g_ring.next()
                        ch = br * 8 + oc
                        S.act(g, g[:], pg, pg[:], AF.Sigmoid, bias=bgb[:, ch:ch + 1], rd=[bgb])
                        gts.append(g)
                    acc = acc_ring.next()
                    koff = (0, 8, 24)
                    nk = (8, 16, 8)
                    for br in range(3):
                        pb = ps_ring.next()
                        for kc in range(nk[br]):
                            S.mm(pb, pb[:], w, wb[:, koff[br] + kc, :], yub, yub[:, koff[br] + kc, :], start=(kc == 0), stop=(kc == nk[br] - 1))
                        if br == 0:
                            S.tt("dve", acc, acc[:], pb, pb[:], gts[0], gts[0][:], ALU.mult)
                        else:
                            tmp = tmp_ring.next()
                            S.tt("dve", tmp, tmp[:], pb, pb[:], gts[br], gts[br][:], ALU.mult)
                            if br == 1:
                                S.tt("dve", acc, acc[:], acc, acc[:], tmp, tmp[:], ALU.add)
                            else:
                                S.tt("dve", mixed, mixed[:, oc, :], acc, acc[:], tmp, tmp[:], ALU.add)
                for half in range(2):
                    w = w_ring.next()
                    wo = w[:, 0:4096].rearrange("p (kc j) -> p kc j", kc=8)
                    S.dma("pool", wo, kcp(w_out[:, half * 512:(half + 1) * 512]), writes=[w])
                    for o4 in range(4):
                        oc = half * 4 + o4
                        po = ps_ring.next()
                        for kc in range(8):
                            S.mm(po, po[:], w, wo[:, kc, o4 * 128:(o4 + 1) * 128], mixed, mixed[:, kc, :], start=(kc == 0), stop=(kc == 7))
                        S.tt("dve", xb, xb[:, oc, :], xb, xb[:, oc, :], po, po[:], ALU.add)
                rms_stats(S, C, xb, 8, TT, ps_ring, sq_ring, rstd, rtmp, DM, EPS)
                for kc in range(8):
                    S.stt(h2, h2[:, kc, :], xb, xb[:, kc, :], gmb[:, kc:kc + 1], rstd, rstd[:], ALU.mult, ALU.mult, rd=[gmb])
                for o8 in range(8):
                    w = w_ring.next()
                    wu = w[:, 0:4096].rearrange("p (kc j) -> p kc j", kc=8)
                    S.dma("pool", wu, kcp(w_up[:, o8 * 512:(o8 + 1) * 512]), writes=[w])
                    for o4 in range(4):
                        oc = o8 * 4 + o4
                        pu = ps_ring.next()
                        for kc in range(8):
                            S.mm(pu, pu[:], w, wu[:, kc, o4 * 128:(o4 + 1) * 128], h2, h2[:, kc, :], start=(kc == 0), stop=(kc == 7))
                        r = r_ring.next()
                        S.act(r, r[:], pu, pu[:], AF.Relu)
                        S.tt("pool", yub, yub[:, oc, :], r, r[:], r, r[:], ALU.mult)
                for oc in range(8):
                    w = w_ring.next()
                    wd = w[:, 0:4096].rearrange("p (kc j) -> p kc j", kc=32)
                    S.dma("pool", wd, kcp(w_dn[:, oc * 128:(oc + 1) * 128]), writes=[w])
                    pd = ps_ring.next()
                    for kc in range(32):
                        S.mm(pd, pd[:], w, wd[:, kc, :], yub, yub[:, kc, :], start=(kc == 0), stop=(kc == 31))
                    S.tt("dve", xb, xb[:, oc, :], xb, xb[:, oc, :], pd, pd[:], ALU.add)
                if mode == "mid":
                    S.dma("sp", kcp(xTo[:, ts_]), xb[:], reads=[xb])
            rms_stats(S, C, xb, 8, TT, ps_ring, sq_ring, rstd, rtmp, DM, EPS)
            if mode == "last":
                for kc in range(8):
                    o = oc_ring.next()
                    S.stt(o, o[:], xb, xb[:, kc, :], gnb[:, kc:kc + 1], rstd, rstd[:], ALU.mult, ALU.mult, rd=[gnb])
                    S.dma("sp", outT[kc * 128:(kc + 1) * 128, ts_], o[:], reads=[o])
            else:
                hn = hn_ring.next()
                hl = hl_ring.next()
                for kc in range(8):
                    h32 = h32_ring.next()
                    S.stt(h32, h32[:], xb, xb[:, kc, :], gnb[:, kc:kc + 1], rstd, rstd[:], ALU.mult, ALU.mult, rd=[gnb])
                    S.copy("act", hn, hn[:, kc, :], h32, h32[:])
                    S.tt("pool", hl, hl[:, kc, :], h32, h32[:], hn, hn[:, kc, :], ALU.subtract)
                S.dma("sp", kcp(hTo[:, ts_]), hn[:], reads=[hn], writes=A.get("h_wr", []), sembuf=hn)
                S.dma("sp", kcp(hTlo[:, ts_]), hl[:], reads=[hl], writes=A.get("h_wr", []), sembuf=hl)


def make_consts():
    c = np.zeros((4, 128, 128), np.float32)
    c[0] = 1.0
    c[1] = np.triu(np.ones((128, 128), np.float32))
    c[2] = 1.0 - c[1]
    c[3] = np.eye(128, dtype=np.float32)
    return c


def pvec(v):
    v = np.asarray(v)
    return np.ascontiguousarray(v.reshape(-1, 128).T)


def small_views(S, name, nbanks, width):
    out = []
    per = 512 // width
    banks = [S.ps("%s%d" % (name, i), [128, 512], F32) for i in range(nbanks)]
    for j in range(per):
        for i in range(nbanks):
            out.append(VBuf("%s%d_%d" % (name, i, j), banks[i], banks[i].t[:, j * width:(j + 1) * width]))
    return out


def build_gla(ntiles=SEQ // TT):
    nc = bass.Bass("TRN2", target_bir_lowering=False)
    hT = dram(nc, "hT", [DM, SEQ], BF16, "ExternalInput")
    hTl = dram(nc, "hTlo", [DM, SEQ], BF16, "ExternalInput")
    A = {}
    A["w_gla"] = dram(nc, "w_gla", [DM, 784], F32, "ExternalInput")
    A["wgk2"] = dram(nc, "wgk2", [16, 128], F32, "ExternalInput")
    A["bgk"] = dram(nc, "bgk", [1, 128], F32, "ExternalInput")
    A["ng"] = dram(nc, "ng", [128, 2], F32, "ExternalInput")
    A["cst"] = dram(nc, "cst", [4, 128, 128], F32, "ExternalInput")
    yT = dram(nc, "yT", [256, SEQ], BF16, "ExternalOutput")
    A["load_h"] = lambda S, h, t: S.dma("sp", h[:], kcp(hT[:, t * TT:(t + 1) * TT]), writes=[h])
    A["load_hlo"] = lambda S, h, t: S.dma("act", h[:], kcp(hTl[:, t * TT:(t + 1) * TT]), writes=[h])
    A["store_y"] = lambda S, yo, t: S.dma("sp", yT[:, t * TT:(t + 1) * TT].rearrange("(ec p) n -> p ec n", p=128), yo[:], reads=[yo])
    with ExitStack() as st:
        S = Sched(nc, st)
        S.begin_phase()
        emit_gla(nc, S, A, ntiles)
        S.end_phase()
    return nc


def emit_gla(nc, S, A, ntiles=SEQ // TT):
    w_gla, wgk2, bgk, ngp, cst = A["w_gla"], A["wgk2"], A["bgk"], A["ng"], A["cst"]
    if True:
        C = load_consts(S, nc, cst)
        U = S.sb("U", [128, 128], F32)
        UC = S.sb("UC", [128, 128], F32)
        S.dma("sp", U[:], cst[1], writes=[U])
        S.dma("sp", UC[:], cst[2], writes=[UC])
        wa = S.sb("wa", [128, 8, 784], BF16)
        S.dma("pool", wa[:], kcp(w_gla), writes=[wa])
        wqk32 = S.sb("wqk32", [128, 8, 256], F32)
        S.dma("sp", wqk32[:, :, 0:128], kcp(w_gla[:, 0:128]), writes=[wqk32])
        S.dma("sp", wqk32[:, :, 128:256], kcp(w_gla[:, 384:512]), writes=[wqk32], group=True)
        wqk_hi = S.sb("wqk_hi", [128, 8, 256], BF16)
        wqk_lo = S.sb("wqk_lo", [128, 8, 256], BF16)
        S.copy("act", wqk_hi, wqk_hi[:], wqk32, wqk32[:])
        S.tt("dve", wqk_lo, wqk_lo[:], wqk32, wqk32[:], wqk_hi, wqk_hi[:], ALU.subtract)
        w2 = S.sb("w2", [16, 128], F32)
        S.dma("sp", w2[:], wgk2, writes=[w2])
        bgb = S.sb("bgb", [128, 128], F32)
        S.dma("sp", bgb[:], bgk.partition_broadcast(128), writes=[bgb])
        ng = S.sb("ng", [128, 2], F32)
        S.dma("sp", ng[:], ngp, writes=[ng])
        h_ring = Ring([S.sb("h%d" % i, [128, 8, TT], BF16) for i in range(2)])
        hlo_ring = Ring([S.sb("hlo%d" % i, [128, 8, TT], BF16) for i in range(2)])
        big = Ring([S.ps("pb%d" % i, [128, 512], F32) for i in range(2)])
        po = [S.ps("po%d" % i, [128, 512], F32) for i in range(2)]
        pss_ring = Ring([S.ps("pss", [128, 512], F32)])
        sm = Ring(small_views(S, "sm", 3, 256))
        qTs = S.sb("qTs", [128, TT], F32)
        kTs = S.sb("kTs", [128, TT], F32)
        sg = S.sb("sg", [128, 2, TT], F32)
        gkl = S.sb("gkl", [16, TT], F32)

        def ring(name, shape, dt, n=2):
            return Ring([S.sb("%s%d" % (name, i), shape, dt) for i in range(n)])
        ktok_r = ring("ktok", [128, 128], F32)
        vtok_r = ring("vtok", [128, 256], BF16)
        t1_r = ring("t1", [128, 128], F32)
        e_r = ring("e", [128, 128], F32)
        gk_r = ring("gk", [128, 128], F32)
        ebT_r = ring("ebT", [128, 128], F32)
        enbT_r = ring("enbT", [128, 128], F32)
        ed2_r = ring("ed2", [128, 128], F32)
        qp_r = ring("qp", [128, 128], BF16)
        qp32_r = ring("qp32", [128, 128], F32)
        kp32_r = ring("kp32", [128, 128], F32)
        kpp_r = ring("kpp", [128, 128], BF16)
        AT_r = ring("AT", [128, 128], BF16)
        Sst = S.sb("Sst", [128, 256], F32)
        Sbf = S.sb("Sbf", [128, 256], BF16)
        S.memset("dve", Sst, Sst[:], 0.0)
        S.memset("dve", Sbf, Sbf[:], 0.0)
        sq_ring = ring("sq", [128, TT], F32)
        rstd = S.sb("rstd", [128, TT], F32)
        rtmp = S.sb("rtmp", [128, TT], F32)
        tmp_r = ring("tmp", [128, TT], F32)
        yo_r = ring("yo", [128, 2, TT], BF16)

        for t in range(ntiles):
            ts_ = slice(t * TT, (t + 1) * TT)
            h = h_ring.next()
            A["load_h"](S, h, t)

            def proj(c0, m):
                p = big.next()
                for kc in range(8):
                    S.mm(p, p[:m, :], wa, wa[:, kc, c0:c0 + m], h, h[:, kc, :], start=(kc == 0), stop=(kc == 7))
                return p
            hlo = hlo_ring.next()
            A["load_hlo"](S, hlo, t)

            def proj3(c0):
                p = big.next()
                n = 0
                for (wb_, hb_) in ((wqk_hi, h), (wqk_hi, hlo), (wqk_lo, h)):
                    for kc in range(8):
                        S.mm(p, p[:], wb_, wb_[:, kc, c0:c0 + 128], hb_, hb_[:, kc, :], start=(n == 0), stop=(n == 23))
                        n += 1
                return p
            p = proj3(0)
            S.act(qTs, qTs[:], p, p[:], AF.Copy, scale=128.0 ** -0.5)
            p = proj3(128)
            S.act(kTs, kTs[:], p, p[:], AF.Copy)
            for ec in range(2):
                p = proj(128 + ec * 128, 128)
                S.act(sg, sg[:, ec, :], p, p[:], AF.Silu)
            p = proj(768, 16)
            S.copy("dve", gkl, gkl[:], p, p[:16, :])
            for c in range(4):
                cs = slice(c * 128, (c + 1) * 128)
                p = big.next()
                for kc in range(8):
                    S.mm(p, p[:, :384], h, h[:, kc, cs], wa, wa[:, kc, 384:768], start=(kc == 0), stop=(kc == 7))
                ktok = ktok_r.next()
                vtok = vtok_r.next()
                S.copy("act", ktok, ktok[:], p, p[:, 0:128])
                S.copy("act", vtok, vtok[:], p, p[:, 128:384])
                pg = sm.next()
                S.mm(pg, pg[:, :128], gkl, gkl[:, cs], w2, w2[:], start=True, stop=True)
                t1 = t1_r.next()
                S.tt("dve", t1, t1[:], pg, pg[:, :128], bgb, bgb[:], ALU.add)
                e = e_r.next()
                S.act(e, e[:], t1, t1[:], AF.Exp, scale=-1.0)
                S.act(e, e[:], e, e[:], AF.Ln, bias=1.0)
                gk = gk_r.next()
                S.ts("dve", gk, gk[:], e, e[:], -1.0 / 16.0, None, ALU.mult)
                pbT = sm.next()
                S.mm(pbT, pbT[:, :128], gk, gk[:], U, U[:])
                pd2 = sm.next()
                S.mm(pd2, pd2[:, :128], UC, UC[:], gk, gk[:])
                ebT = ebT_r.next()
                enbT = enbT_r.next()
                ed2 = ed2_r.next()
                S.act(ebT, ebT[:], pbT, pbT[:, :128], AF.Exp)
                S.act(enbT, enbT[:], pbT, pbT[:, :128], AF.Exp, scale=-1.0)
                S.act(ed2, ed2[:], pd2, pd2[:, :128], AF.Exp)
                qp = qp_r.next()
                qp32 = qp32_r.next()
                kp32 = kp32_r.next()
                kpp = kpp_r.next()
                S.tt("dve", qp32, qp32[:], qTs, qTs[:, cs], ebT, ebT[:], ALU.mult)
                S.tt("dve", kp32, kp32[:], kTs, kTs[:, cs], enbT, enbT[:], ALU.mult)
                S.copy("act", qp, qp[:], qp32, qp32[:])
                S.tt("pool", kpp, kpp[:], ktok, ktok[:], ed2, ed2[:], ALU.mult)
                pA = sm.next()
                S.mm(pA, pA[:, :128], kp32, kp32[:], qp32, qp32[:])
                AT = AT_r.next()
                S.tt("dve", AT, AT[:], pA, pA[:, :128], U, U[:], ALU.mult)
                for ec in range(2):
                    es = slice(ec * 128, (ec + 1) * 128)
                    S.mm(po[ec], po[ec][:, cs], vtok, vtok[:, es], AT, AT[:], start=True, stop=False)
                    S.mm(po[ec], po[ec][:, cs], Sbf, Sbf[:, es], qp, qp[:], start=False, stop=True)
                pkv = sm.next()
                S.mm(pkv, pkv[:, :256], kpp, kpp[:], vtok, vtok[:])
                S.stt(Sst, Sst[:], Sst, Sst[:], ebT[:, 127:128], pkv, pkv[:, :256], ALU.mult, ALU.add, rd=[ebT])
                S.copy("act", Sbf, Sbf[:], Sst, Sst[:])
            pss = pss_ring.next()
            for ec in range(2):
                sq = sq_ring.next()
                S.act(sq, sq[:], po[ec], po[ec][:], AF.Square)
                S.mm(pss, pss[:], C["ones"], C["ones"][:], sq, sq[:], start=(ec == 0), stop=(ec == 1))
            S.act(rtmp, rtmp[:], pss, pss[:], AF.Sqrt, bias=C["eps%g" % EPS][:, 0:1], scale=1.0 / 256, rd=[C["eps%g" % EPS]])
            S.recip(rstd, rstd[:], rtmp, rtmp[:])
            yo = yo_r.next()
            for ec in range(2):
                tmp = tmp_r.next()
                S.stt(tmp, tmp[:], po[ec], po[ec][:], ng[:, ec:ec + 1], rstd, rstd[:], ALU.mult, ALU.mult, rd=[ng])
                S.tt("pool", yo, yo[:, ec, :], tmp, tmp[:], sg, sg[:, ec, :], ALU.mult)
            A["store_y"](S, yo, t)


def gla_inputs(inp, l, g, hT_b, cst, hTlo_b=None):
    w = inp["w_in"][l]
    cols = np.concatenate([np.arange(g * 128, (g + 1) * 128),
                           2064 + np.arange(g * 256, (g + 1) * 256),
                           512 + np.arange(g * 128, (g + 1) * 128),
                           1024 + np.arange(g * 256, (g + 1) * 256),
                           2048 + np.arange(16)])
    if hTlo_b is None:
        hTlo_b = np.zeros_like(hT_b)
    elif isinstance(hTlo_b, int):
        hTlo_b = None
    return {"hT": hT_b, "hTlo": hTlo_b, "w_gla": np.ascontiguousarray(w[:, cols]),
            "wgk2": np.ascontiguousarray(inp["gla_w_gk2"][l][:, g * 128:(g + 1) * 128]),
            "bgk": np.ascontiguousarray(inp["gla_b_gk"][l][g * 128:(g + 1) * 128].reshape(1, 128)),
            "ng": pvec(inp["gla_norm_g"][l]), "cst": cst}


def build_ssm(ntiles=SEQ // TT):
    nc = bass.Bass("TRN2", target_bir_lowering=False)
    hT = dram(nc, "hT", [DM, SEQ], BF16, "ExternalInput")
    A = {}
    A["w_ssm"] = dram(nc, "w_ssm", [DM, 1288], F32, "ExternalInput")
    A["cw"] = dram(nc, "cw", [128, 6, 4], F32, "ExternalInput")
    A["cb"] = dram(nc, "cb", [128, 6], F32, "ExternalInput")
    A["dtb"] = dram(nc, "dtb", [1, 8], F32, "ExternalInput")
    A["alog"] = dram(nc, "alog", [1, 8], F32, "ExternalInput")
    A["dsk"] = dram(nc, "dsk", [1, 8], F32, "ExternalInput")
    A["ngs"] = dram(nc, "ngs", [1, 512], F32, "ExternalInput")
    A["cst"] = dram(nc, "cst", [4, 128, 128], F32, "ExternalInput")
    yT = dram(nc, "yT", [512, SEQ], BF16, "ExternalOutput")
    A["load_h"] = lambda S, h, t: S.dma("sp", h[:], kcp(hT[:, t * TT:(t + 1) * TT]), writes=[h])
    A["store_y"] = lambda S, yo, t: S.dma("sp", yT[:, t * TT:(t + 1) * TT].rearrange("(c p) n -> p c n", p=128), yo[:], reads=[yo])
    with ExitStack() as st:
        S = Sched(nc, st)
        S.begin_phase()
        emit_ssm(nc, S, A, ntiles)
        S.end_phase()
    return nc


def emit_ssm(nc, S, A, ntiles=SEQ // TT):
    w_ssm, cwp, cbp, dtbp, alogp, dskp, ngp, cst = (A[k] for k in ("w_ssm", "cw", "cb", "dtb", "alog", "dsk", "ngs", "cst"))
    if True:
        C = load_consts(S, nc, cst)
        U = S.sb("U", [128, 128], F32)
        UC = S.sb("UC", [128, 128], F32)
        idf = S.sb("idf", [128, 128], F32)
        idb = S.sb("idb", [128, 128], BF16)
        S.dma("sp", U[:], cst[1], writes=[U])
        S.dma("sp", UC[:], cst[2], writes=[UC])
        S.dma("sp", idf[:], cst[3], writes=[idf])
        S.dma("pool", idb[:], cst[3], writes=[idb])
        ws = S.sb("ws", [128, 8, 1288], BF16)
        S.dma("pool", ws[:], kcp(w_ssm), writes=[ws])
        cw = S.sb("cw", [128, 6, 4], F32)
        cb = S.sb("cb", [128, 6], F32)
        S.dma("sp", cw[:], cwp, writes=[cw])
        S.dma("sp", cb[:], cbp, writes=[cb])
        dtb = S.sb("dtb", [128, 8], F32)
        a_b = S.sb("a_b", [128, 8], F32)
        dsk = S.sb("dsk", [128, 8], F32)
        ngs = S.sb("ngs", [128, 512], F32)
        S.dma("sp", dtb[:], dtbp.partition_broadcast(128), writes=[dtb])
        S.dma("sp", a_b[:], alogp.partition_broadcast(128), writes=[a_b])
        S.dma("sp", dsk[:], dskp.partition_broadcast(128), writes=[dsk])
        S.dma("sp", ngs[:], ngp.partition_broadcast(128), writes=[ngs])
        S.act(a_b, a_b[:], a_b, a_b[:], AF.Exp)
        S.ts("dve", a_b, a_b[:], a_b, a_b[:], -1.0, None, ALU.mult)

        def ring(name, shape, dt, n=2):
            return Ring([S.sb("%s%d" % (name, i), shape, dt) for i in range(n)])
        h_ring = ring("h", [128, 8, TT], BF16)
        big = Ring([S.ps("pb%d" % i, [128, 512], F32) for i in range(6)])
        sm = Ring(small_views(S, "sm", 2, 128))
        raw = S.sb("raw", [128, 6, TT + 3], F32)
        S.memset("dve", raw, raw[:, :, 0:3], 0.0)
        cacc_r = ring("cacc", [128, TT], F32)
        xc = S.sb("xc", [128, 4, TT], F32)
        BT = S.sb("BT", [128, TT], BF16)
        CT = S.sb("CT", [128, TT], BF16)
        sz_r = ring("sz", [128, 512], F32)
        t8_r = ring("t8", [128, 8], F32)
        dt_r = ring("dt", [128, 8], F32)
        da_r = ring("da", [128, 8], F32)
        xdt_r = ring("xdt", [128, 512], BF16)
        xD_r = ring("xD", [128, 512], F32)
        Btok_r = ring("Btok", [128, 128], BF16)
        rda_r = ring("rda", [128, 8, 128], F32)
        eL_r = ring("eL", [128, 8, 128], F32)
        GU_r = ring("GU", [128, 128], F32)
        M_r = ring("M", [128, 8, 128], BF16)
        cs_r = ring("cs", [128, 16], F32)
        ec_r = ring("ec", [128, 24], F32)
        xdtd_r = ring("xdtd", [128, 512], BF16)
        Sst = S.sb("Sst", [128, 512], F32)
        Sbf = S.sb("Sbf", [128, 512], BF16)
        S.memset("dve", Sst, Sst[:], 0.0)
        S.memset("dve", Sbf, Sbf[:], 0.0)
        y1_r = ring("y1", [128, 512], F32)
        y2_r = ring("y2", [128, 512], F32)
        junk = S.sb("junk", [128, 512], F32)
        ss_r = ring("ss", [128, 2], F32)
        yn_r = ring("yn", [128, 512], BF16)
        yo_r = ring("yo", [128, 4, TT], BF16)
        eps = C["eps%g" % EPS]

        def b8(ap):
            return ap.unsqueeze(2).to_broadcast([128, 8, 64])

        def v8(ap):
            return ap.rearrange("p (h q) -> p h q", h=8)

        for t in range(ntiles):
            ts_ = slice(t * TT, (t + 1) * TT)
            h = h_ring.next()
            A["load_h"](S, h, t)
            for ch in range(6):
                p = big.next()
                for kc in range(8):
                    S.mm(p, p[:], ws, ws[:, kc, 512 + ch * 128:640 + ch * 128], h, h[:, kc, :], start=(kc == 0), stop=(kc == 7))
                S.act(raw, raw[:, ch, 3:TT + 3], p, p[:], AF.Copy)
            for ch in range(6):
                acc = cacc_r.next()
                S.ts("dve", acc, acc[:], raw, raw[:, ch, 0:TT], cw[:, ch, 0:1], cb[:, ch:ch + 1], ALU.mult, ALU.add, rd=[cw, cb])
                for i in range(1, 4):
                    S.stt(acc, acc[:], raw, raw[:, ch, i:i + TT], cw[:, ch, i:i + 1], acc, acc[:], ALU.mult, ALU.add, rd=[cw])
                if ch < 4:
                    S.act(xc, xc[:, ch, :], acc, acc[:], AF.Silu)
                elif ch == 4:
                    S.act(BT, BT[:], acc, acc[:], AF.Silu)
                else:
                    S.act(CT, CT[:], acc, acc[:], AF.Silu)
            S.copy("pool", raw, raw[:, :, 0:3], raw, raw[:, :, TT:TT + 3])
            yo = yo_r.next()
            for c in range(4):
                cs = slice(c * 128, (c + 1) * 128)
                pz = big.next()
                for kc in range(8):
                    S.mm(pz, pz[:], h, h[:, kc, cs], ws, ws[:, kc, 0:512], start=(kc == 0), stop=(kc == 7))
                sz = sz_r.next()
                S.act(sz, sz[:], pz, pz[:], AF.Silu)
                pdt = sm.next()
                for kc in range(8):
                    S.mm(pdt, pdt[:, 0:8], h, h[:, kc, cs], ws, ws[:, kc, 1280:1288], start=(kc == 0), stop=(kc == 7))
                t8 = t8_r.next()
                S.tt("dve", t8, t8[:], pdt, pdt[:, 0:8], dtb, dtb[:], ALU.add)
                S.act(t8, t8[:], t8, t8[:], AF.Exp)
                dt = dt_r.next()
                S.act(dt, dt[:], t8, t8[:], AF.Ln, bias=1.0)
                da = da_r.next()
                S.tt("dve", da, da[:], dt, dt[:], a_b, a_b[:], ALU.mult)
                px = big.next()
                for ch in range(4):
                    S.mm(px, px[:, ch * 128:(ch + 1) * 128], xc, xc[:, ch, cs], idf, idf[:])
                xdt = xdt_r.next()
                xD = xD_r.next()
                S.tt("dve", xdt, v8(xdt[:]), px, v8(px[:]), dt, b8(dt[:]), ALU.mult)
                S.tt("dve", xD, v8(xD[:]), px, v8(px[:]), dsk, b8(dsk[:]), ALU.mult)
                pB = sm.next()
                S.mm(pB, pB[:], BT, BT[:, cs], idb, idb[:])
                Btok = Btok_r.next()
                S.copy("act", Btok, Btok[:], pB, pB[:])
                rda = rda_r.next()
                S.tt("pool", rda, rda[:], U, U[:].unsqueeze(1).to_broadcast([128, 8, 128]), da, da[:].unsqueeze(2).to_broadcast([128, 8, 128]), ALU.mult)
                pD = [big.next(), big.next()]
                for hf in range(2):
                    S.mm(pD[hf], pD[hf][:], UC, UC[:], rda, rda[:, hf * 4:(hf + 1) * 4, :].rearrange("p a b -> p (a b)"))
                pc = sm.next()
                S.mm(pc, pc[:, 0:8], U, U[:], da, da[:])
                pc2 = sm.next()
                S.mm(pc2, pc2[:, 0:8], C["ones"], C["ones"][:], da, da[:])
                csb = cs_r.next()
                S.copy("dve", csb, csb[:, 0:8], pc, pc[:, 0:8])
                S.copy("dve", csb, csb[:, 8:16], pc2, pc2[:, 0:8])
                ec = ec_r.next()
                S.act(ec, ec[:, 0:16], csb, csb[:, 0:16], AF.Exp)
                S.tt("dve", csb, csb[:, 0:8], csb, csb[:, 8:16], csb, csb[:, 0:8], ALU.subtract)
                S.act(ec, ec[:, 16:24], csb, csb[:, 0:8], AF.Exp)
                eL = eL_r.next()
                for hf in range(2):
                    S.act(eL, eL[:, hf * 4:(hf + 1) * 4, :].rearrange("p a b -> p (a b)"), pD[hf], pD[hf][:], AF.Exp)
                pG = sm.next()
                S.mm(pG, pG[:], BT, BT[:, cs], CT, CT[:, cs])
                GU = GU_r.next()
                S.tt("dve", GU, GU[:], pG, pG[:], U, U[:], ALU.mult)
                M = M_r.next()
                S.tt("pool", M, M[:], eL, eL[:], GU, GU[:].unsqueeze(1).to_broadcast([128, 8, 128]), ALU.mult)
                xdtd = xdtd_r.next()
                S.tt("pool", xdtd, v8(xdtd[:]), xdt, v8(xdt[:]), ec, b8(ec[:, 16:24]), ALU.mult)
                pyd = big.next()
                for hh in range(8):
                    S.mm(pyd, pyd[:, hh * 64:(hh + 1) * 64], M, M[:, hh, :], xdt, xdt[:, hh * 64:(hh + 1) * 64])
                pyo = big.next()
                S.mm(pyo, pyo[:], CT, CT[:, cs], Sbf, Sbf[:])
                pst = big.next()
                S.mm(pst, pst[:], Btok, Btok[:], xdtd, xdtd[:])
                S.tt("dve", Sst, v8(Sst[:]), Sst, v8(Sst[:]), ec, b8(ec[:, 8:16]), ALU.mult)
                S.tt("dve", Sst, Sst[:], Sst, Sst[:], pst, pst[:], ALU.add)
                S.copy("act", Sbf, Sbf[:], Sst, Sst[:])
                y1 = y1_r.next()
                S.tt("dve", y1, v8(y1[:]), pyo, v8(pyo[:]), ec, b8(ec[:, 0:8]), ALU.mult)
                S.tt("dve", y1, y1[:], y1, y1[:], pyd, pyd[:], ALU.add)
                y2 = y2_r.next()
                S.tt("pool", y2, y2[:], y1, y1[:], xD, xD[:], ALU.add)
                S.tt("pool", y2, y2[:], y2, y2[:], sz, sz[:], ALU.mult)
                ss = ss_r.next()
                S.act(junk, junk[:], y2, y2[:], AF.Square, accum=ss[:, 0:1], wr=[ss])
                S.act(ss, ss[:, 1:2], ss, ss[:, 0:1], AF.Sqrt, bias=eps[:, 0:1], scale=1.0 / 512, rd=[eps])
                S.recip(ss, ss[:, 0:1], ss, ss[:, 1:2])
                yn = yn_r.next()
                S.stt(yn, yn[:], y2, y2[:], ss[:, 0:1], ngs, ngs[:], ALU.mult, ALU.mult, rd=[ss])
                pT = big.next()
                for ch in range(4):
                    S.mm(pT, pT[:, ch * 128:(ch + 1) * 128], yn, yn[:, ch * 128:(ch + 1) * 128], idb, idb[:])
                S.copy("act", yo, yo[:, :, cs], pT, pT[:].rearrange("p (c n) -> p c n", c=4))
            A["store_y"](S, yo, t)


def ssm_inputs(inp, l, g, hT_b, cst):
    w = inp["w_in"][l]
    xcols = 5136 + np.arange(g * 512, (g + 1) * 512)
    bcols = 7184 + np.arange(g * 128, (g + 1) * 128)
    ccols = 7696 + np.arange(g * 128, (g + 1) * 128)
    cols = np.concatenate([3088 + np.arange(g * 512, (g + 1) * 512), xcols, bcols, ccols, 8208 + np.arange(g * 8, (g + 1) * 8)])
    cc = np.concatenate([xcols, bcols, ccols]) - 5136
    cwv = inp["ssm_conv_w"][l][:, cc]
    cw = np.ascontiguousarray(cwv.reshape(4, 6, 128).transpose(2, 1, 0))
    cb = np.ascontiguousarray(inp["ssm_conv_b"][l][cc].reshape(6, 128).T)
    hs = slice(g * 8, (g + 1) * 8)
    return {"hT": hT_b, "w_ssm": np.ascontiguousarray(w[:, cols]), "cw": cw, "cb": cb,
            "dtb": np.ascontiguousarray(inp["ssm_dt_bias"][l][hs].reshape(1, 8)),
            "alog": np.ascontiguousarray(inp["ssm_a_log"][l][hs].reshape(1, 8)),
            "dsk": np.ascontiguousarray(inp["ssm_d"][l][hs].reshape(1, 8)),
            "ngs": np.ascontiguousarray(inp["ssm_norm_g"][l][g * 512:(g + 1) * 512].reshape(1, 512)),
            "cst": cst}


C1_2PI = 6.28125
C2_2PI = 2.0 * math.pi - 6.28125


def build_diff(l, ntiles=SEQ // TT):
    nc = bass.Bass("TRN2", target_bir_lowering=False)
    hT = dram(nc, "hT", [DM, SEQ], BF16, "ExternalInput")
    A = {}
    A["w_diff"] = dram(nc, "w_diff", [DM, 1280], F32, "ExternalInput")
    A["pos"] = dram(nc, "pos", [1, SEQ], I32, "ExternalInput")
    A["invf"] = dram(nc, "invf", [128, 2], F32, "ExternalInput")
    A["lqk"] = dram(nc, "lqk", [4, 64], F32, "ExternalInput")
    A["ngd"] = dram(nc, "ngd", [1, 128], F32, "ExternalInput")
    A["cst"] = dram(nc, "cst", [4, 128, 128], F32, "ExternalInput")
    yT = dram(nc, "yT", [256, SEQ], BF16, "ExternalOutput")
    A["load_h"] = lambda S, h, t: S.dma("sp", h[:], kcp(hT[:, t * TT:(t + 1) * TT]), writes=[h])
    A["store_y"] = lambda S, yo, t: S.dma("sp", yT[:, t * TT:(t + 1) * TT].rearrange("(c p) n -> p c n", p=128), yo[:], reads=[yo])
    with ExitStack() as st:
        S = Sched(nc, st)
        S.begin_phase()
        emit_diff(nc, S, A, l, ntiles)
        S.end_phase()
    return nc


def emit_diff(nc, S, A, l, ntiles=SEQ // TT):
    lambda_init = 0.8 - 0.6 * math.exp(-0.3 * l)
    w_diff, posd, invfp, lqk, ngp, cst = (A[k] for k in ("w_diff", "pos", "invf", "lqk", "ngd", "cst"))
    if True:
        C = load_consts(S, nc, cst, extra_eps=(1e-5,))
        idb = S.sb("idb", [128, 128], BF16)
        S.dma("pool", idb[:], cst[3], writes=[idb])
        wd = S.sb("wd", [128, 8, 1280], BF16)
        S.dma("pool", wd[:], kcp(w_diff), writes=[wd])
        invf = S.sb("invf", [128, 2], F32)
        S.dma("sp", invf[:], invfp, writes=[invf])
        ngd = S.sb("ngd", [128, 128], F32)
        S.dma("sp", ngd[:], ngp.partition_broadcast(128), writes=[ngd])
        S.ts("dve", ngd, ngd[:], ngd, ngd[:], 1.0 - lambda_init, None, ALU.mult)
        lq = S.sb("lq", [128, 4, 64], F32)
        for i in range(4):
            S.dma("sp", lq[:, i, :], lqk[i:i + 1, :].partition_broadcast(128), writes=[lq], group=(i > 0))
        lt = S.sb("lt", [128, 2, 64], F32)
        S.tt("dve", lt, lt[:, 0, :], lq, lq[:, 0, :], lq, lq[:, 1, :], ALU.mult)
        S.tt("dve", lt, lt[:, 1, :], lq, lq[:, 2, :], lq, lq[:, 3, :], ALU.mult)
        ls = S.sb("ls", [128, 4], F32)
        S.op("dve", lambda E: E.reduce_sum(out=ls[:, 0:2], in_=lt[:], axis=AX.X), reads=[lt], writes=[ls])
        S.act(ls, ls[:, 0:2], ls, ls[:, 0:2], AF.Exp)
        S.tt("dve", ls, ls[:, 2:3], ls, ls[:, 1:2], ls, ls[:, 0:1], ALU.subtract)
        S.ts("dve", ls, ls[:, 3:4], ls, ls[:, 2:3], -lambda_init, None, ALU.add)
        nlam = ls

        def ring(name, shape, dt, n=2):
            return Ring([S.sb("%s%d" % (name, i), shape, dt) for i in range(n)])
        h_ring = ring("h", [128, 8, TT], BF16)
        KT = S.sb("KT", [128, 2, SEQ], BF16)
        VA = S.sb("VA", [128, 2, SEQ // 128, 129], BF16)
        S.memset("pool", VA, VA[:, :, :, 128:129], 1.0)
        QT_r = ring("QT", [128, 2, TT], BF16)
        big = Ring([S.ps("pb%d" % i, [128, 512], F32) for i in range(3)])
        pob = [[S.ps("po%d_%d" % (s, hf), [128, 512], F32) for hf in range(2)] for s in range(2)]
        sm = Ring(small_views(S, "sm", 1, 128))
        posi = S.sb("posi", [128, TT], I32)
        ang = S.sb("ang", [128, TT], F32)
        ang2 = S.sb("ang2", [128, TT], F32)
        ki = S.sb("ki", [128, TT], I32)
        kf = S.sb("kf", [128, TT], F32)
        yr = S.sb("yr", [128, TT], F32)
        Cs = S.sb("Cs", [128, TT], F32)
        Sn = S.sb("Sn", [128, TT], F32)
        ta_r = ring("ta", [128, TT], F32)
        tb_r = ring("tb", [128, TT], F32)
        PT_r = ring("PT", [128, TT], BF16, 3)
        r_r = ring("r", [128, 4], F32)
        oa_r = ring("oa", [128, 128], F32)
        junk = S.sb("junk", [128, 128], F32)
        yn_r = ring("yn", [128, 128], BF16)
        yo_r = ring("yo", [128, 2, TT], BF16)
        eps5 = C["eps%g" % 1e-5]

        def reduce_sin(dst, src):
            S.ts("dve", ki, ki[:], src, src[:], 1.0 / (2.0 * math.pi), None, ALU.mult)
            S.copy("dve", kf, kf[:], ki, ki[:])
            S.stt(yr, yr[:], kf, kf[:], -C1_2PI, src, src[:], ALU.mult, ALU.add)
            S.stt(yr, yr[:], kf, kf[:], -C2_2PI, yr, yr[:], ALU.mult, ALU.add)
            S.ts("dve", yr, yr[:], yr, yr[:], -3.1415925, 3.1415925, ALU.max, ALU.min)
            S.act(dst, dst[:], yr, yr[:], AF.Sin)

        for t in range(ntiles):
            ts_ = slice(t * TT, (t + 1) * TT)
            h = h_ring.next()
            A["load_h"](S, h, t)
            S.dma("sp", posi[:], posd[0:1, ts_].partition_broadcast(128), writes=[posi])
            S.copy("dve", ang, ang[:], posi, posi[:])
            S.ts("dve", ang, ang[:], ang, ang[:], invf[:, 0:1], None, ALU.mult, rd=[invf])
            reduce_sin(Sn, ang)
            S.ts("dve", Sn, Sn[:], Sn, Sn[:], invf[:, 1:2], None, ALU.mult, rd=[invf])
            S.ts("dve", ang2, ang2[:], ang, ang[:], math.pi / 2.0, None, ALU.add)
            reduce_sin(Cs, ang2)
            QT = QT_r.next()
            for hd in range(2):
                for (c0, dstb, dst) in ((hd * 128, QT, QT[:, hd, :]), (512 + hd * 128, KT, KT[:, hd, ts_])):
                    p1 = big.next()
                    for kc in range(8):
                        S.mm(p1, p1[:], wd, wd[:, kc, c0:c0 + 128], h, h[:, kc, :], start=(kc == 0), stop=(kc == 7))
                    p2 = big.next()
                    for kc in range(8):
                        S.mm(p2, p2[:], wd, wd[:, kc, c0 + 256:c0 + 384], h, h[:, kc, :], start=(kc == 0), stop=(kc == 7))
                    ta = ta_r.next()
                    tb = tb_r.next()
                    S.tt("dve", ta, ta[:], p1, p1[:], Cs, Cs[:], ALU.mult)
                    S.tt("dve", tb, tb[:], p2, p2[:], Sn, Sn[:], ALU.mult)
                    S.tt("pool", dstb, dst, ta, ta[:], tb, tb[:], ALU.add)
            for c in range(4):
                cs = slice(c * 128, (c + 1) * 128)
                pv = big.next()
                for kc in range(8):
                    S.mm(pv, pv[:, 0:256], h, h[:, kc, cs], wd, wd[:, kc, 1024:1280], start=(kc == 0), stop=(kc == 7))
                S.copy("act", VA, VA[:, :, 4 * t + c, 0:128], pv, pv[:, 0:256].rearrange("p (a b) -> p a b", a=2))
            yo = yo_r.next()
            for hd in range(2):
                nkb = 4 * t + 4
                started = [[False, False], [False, False]]
                for kb in range(nkb):
                    r = kb - 4 * t
                    q0 = max(r, 0)
                    qlo = q0 * 128
                    n = TT - qlo
                    for s in range(2):
                        ps_ = slice(s * 64, (s + 1) * 64)
                        pS = big.next()
                        S.mm(pS, pS[:, :n], KT, KT[ps_, hd, kb * 128:(kb + 1) * 128], QT, QT[ps_, hd, qlo:TT])
                        PT = PT_r.next()
                        S.act(PT, PT[:, :n], pS, pS[:, :n], AF.Exp, scale=0.125)
                        if r >= 0:
                            S.memset("pool", PT, PT[64:128, 0:64], 0.0)
                        for qb in range(q0, 4):
                            col = qb * 128 - qlo
                            bank = pob[s][qb // 2]
                            o = bank[:, (qb % 2) * 129:(qb % 2) * 129 + 129]
                            first = not started[s][qb // 2]
                            started[s][qb // 2] = True
                            S.mm(bank, o, PT, PT[:, col:col + 128], VA, VA[:, hd, kb, :], start=first, stop=(kb == 4 * t + qb and qb % 2 == 1))
                for qb in range(4):
                    o1 = pob[0][qb // 2]
                    o2 = pob[1][qb // 2]
                    b0 = (qb % 2) * 129
                    rr = r_r.next()
                    S.recip(rr, rr[:, 0:1], o1, o1[:, b0 + 128:b0 + 129])
                    S.recip(rr, rr[:, 1:2], o2, o2[:, b0 + 128:b0 + 129])
                    S.tt("dve", rr, rr[:, 2:3], rr, rr[:, 1:2], nlam, nlam[:, 3:4], ALU.mult)
                    oa = oa_r.next()
                    S.ts("dve", oa, oa[:], o1, o1[:, b0:b0 + 128], rr[:, 0:1], None, ALU.mult, rd=[rr])
                    S.stt(oa, oa[:], o2, o2[:, b0:b0 + 128], rr[:, 2:3], oa, oa[:], ALU.mult, ALU.add, rd=[rr])
                    S.act(junk, junk[:], oa, oa[:], AF.Square, accum=rr[:, 3:4], wr=[rr])
                    S.act(rr, rr[:, 1:2], rr, rr[:, 3:4], AF.Sqrt, bias=eps5[:, 0:1], scale=1.0 / 128, rd=[eps5])
                    S.recip(rr, rr[:, 0:1], rr, rr[:, 1:2])
                    yn = yn_r.next()
                    S.stt(yn, yn[:], oa, oa[:], rr[:, 0:1], ngd, ngd[:], ALU.mult, ALU.mult, rd=[rr])
                    pT = sm.next()
                    S.mm(pT, pT[:], yn, yn[:], idb, idb[:])
                    S.copy("act", yo, yo[:, hd, qb * 128:(qb + 1) * 128], pT, pT[:])
            A["store_y"](S, yo, t)


def diff_inputs(inp, l, g, b, hT_b, cst):
    w = inp["w_in"][l]
    qc = 8240 + np.arange(g * 256, (g + 1) * 256)
    kc = 9264 + np.arange(g * 256, (g + 1) * 256)
    vc = 10288 + np.arange(g * 256, (g + 1) * 256)
    d = np.arange(256) % 64
    partner = np.arange(256) + np.where(d < 8, 8, np.where(d < 16, -8, 0))
    cols = np.concatenate([qc, qc[partner], kc, kc[partner], vc])
    p = np.arange(128) % 64
    invf = np.zeros((128, 2), np.float32)
    fr = (500000.0 ** (-np.arange(0, 16, 2, dtype=np.float32) / np.float32(16))).astype(np.float32)
    invf[:, 0] = np.where(p < 16, fr[p % 8], 0.0)
    invf[:, 1] = np.where(p < 8, -1.0, np.where(p < 16, 1.0, 0.0))
    lqk = np.stack([inp["diff_lq1"][l], inp["diff_lk1"][l], inp["diff_lq2"][l], inp["diff_lk2"][l]]).astype(np.float32)
    return {"hT": hT_b, "w_diff": np.ascontiguousarray(w[:, cols]),
            "pos": np.ascontiguousarray(inp["positions"][b].reshape(1, SEQ)), "invf": invf, "lqk": lqk,
            "ngd": np.ascontiguousarray(inp["diff_norm_g"][l].reshape(1, 128)), "cst": cst}


_PROGS = {}


def _prog(key, fn):
    if key not in _PROGS:
        _PROGS[key] = fn()
    return _PROGS[key]


def _run(nc, maps):
    return run_bass_kernel_spmd(nc, maps, core_ids=list(range(NCORES))).results


def tok_inputs(inp, l, xT_c, hT_c, yT_c, g_next, cst):
    w = inp["w_in"][l]
    return {"xT": xT_c, "g_next": pvec(g_next), "cst": cst, "hT": hT_c, "yT": yT_c,
            "w_gate": np.ascontiguousarray(w[:, 11312:14384]), "b_gate": pvec(inp["b_gate"][l]),
            "w_br": np.ascontiguousarray(np.concatenate([inp["w_br_gla"][l], inp["w_br_ssm"][l], inp["w_br_diff"][l]], axis=0)),
            "w_out": np.ascontiguousarray(inp["w_out"][l]), "w_up": np.ascontiguousarray(inp["w_mlp_up"][l]),
            "w_dn": np.ascontiguousarray(inp["w_mlp_down"][l]), "g_mlp": pvec(inp["norm_mlp_g"][l])}


def kernel_unfused(**inp):
    inp = {k: np.asarray(v) for k, v in inp.items()}
    cst = make_consts()
    x = inp["x"]
    xT = [np.ascontiguousarray(x[c // 4, (c % 4) * NT:(c % 4 + 1) * NT, :].T) for c in range(NCORES)]
    res = _run(_prog("first", lambda: build_tok("first")),
               [{"xT": xT[c], "g_next": pvec(inp["norm_mix_g"][0]), "cst": cst} for c in range(NCORES)])
    hT = [r["hT_out"] for r in res]
    hTl = [r["hT_lo"] for r in res]
    out = None
    for l in range(DEPTH):
        hTb = [np.ascontiguousarray(np.concatenate(hT[b * 4:(b + 1) * 4], axis=1)) for b in range(B)]
        hTlb = [np.ascontiguousarray(np.concatenate(hTl[b * 4:(b + 1) * 4], axis=1)) for b in range(B)]
        yg = _run(_prog("gla", build_gla), [gla_inputs(inp, l, c % 4, hTb[c // 4], cst, hTlb[c // 4]) for c in range(NCORES)])
        ys = _run(_prog("ssm", build_ssm), [ssm_inputs(inp, l, c % 4, hTb[c // 4], cst) for c in range(NCORES)])
        yd = _run(_prog(("diff", l), lambda: build_diff(l)), [diff_inputs(inp, l, c % 4, c // 4, hTb[c // 4], cst) for c in range(NCORES)])
        yTb = []
        for b in range(B):
            parts = [yg[b * 4 + g]["yT"] for g in range(4)] + [ys[b * 4 + g]["yT"] for g in range(4)] + [yd[b * 4 + g]["yT"] for g in range(4)]
            yTb.append(np.concatenate(parts, axis=0))
        last = (l == DEPTH - 1)
        g_next = inp["norm_final_g"] if last else inp["norm_mix_g"][l + 1]
        maps = [tok_inputs(inp, l, xT[c], hT[c], np.ascontiguousarray(yTb[c // 4][:, (c % 4) * NT:(c % 4 + 1) * NT]), g_next, cst)
                for c in range(NCORES)]
        mode = "last" if last else "mid"
        res = _run(_prog(mode, lambda: build_tok(mode)), maps)
        if last:
            out = np.empty((B, SEQ, DM), np.float32)
            for c in range(NCORES):
                out[c // 4, (c % 4) * NT:(c % 4 + 1) * NT, :] = res[c]["outT"].T
        else:
            xT = [r["xT_out"] for r in res]
            hT = [r["hT_out"] for r in res]
            hTl = [r["hT_lo"] for r in res]
    return out


GROUPS = [[0, 1, 2, 3], [4, 5, 6, 7]]


def build_fused(skip=()):
    nc = bass.Bass("TRN2", target_bir_lowering=False)
    ext = lambda name, shape, dt=F32: dram(nc, name, shape, dt, "ExternalInput")
    xT = ext("xT", [DM, NT])
    pos = ext("pos", [1, SEQ], I32)
    cst = ext("cst", [4, 128, 128])
    invf = ext("invf", [128, 2])
    g_mix = ext("g_mix", [DEPTH, 128, 8])
    g_fin = ext("g_fin", [128, 8])
    g_mlp = ext("g_mlp", [DEPTH, 128, 8])
    w_gate = ext("w_gate", [DEPTH, DM, 3072])
    b_gate = ext("b_gate", [DEPTH, 128, 24])
    w_br = ext("w_br", [DEPTH, 4096, DM])
    w_out = ext("w_out", [DEPTH, DM, DM])
    w_up = ext("w_up", [DEPTH, DM, 4096])
    w_dn = ext("w_dn", [DEPTH, 4096, DM])
    w_gla = ext("w_gla", [DEPTH, DM, 784])
    wgk2 = ext("wgk2", [DEPTH, 16, 128])
    bgk = ext("bgk", [DEPTH, 1, 128])
    ng = ext("ng", [DEPTH, 128, 2])
    w_ssm = ext("w_ssm", [DEPTH, DM, 1288])
    cw = ext("cw", [DEPTH, 128, 6, 4])
    cb = ext("cb", [DEPTH, 128, 6])
    dtb = ext("dtb", [DEPTH, 1, 8])
    alog = ext("alog", [DEPTH, 1, 8])
    dsk = ext("dsk", [DEPTH, 1, 8])
    ngs = ext("ngs", [DEPTH, 1, 512])
    w_diff = ext("w_diff", [DEPTH, DM, 1280])
    lqk = ext("lqk", [DEPTH, 4, 64])
    ngd = ext("ngd", [DEPTH, 1, 128])
    outT = dram(nc, "outT", [DM, NT], F32, "ExternalOutput")
    xs = nc.dram_tensor("xs_i", [DM, NT], F32).ap()
    hsrc = nc.dram_tensor("hsrc_i", [8, 256, NT], BF16).ap()
    hgat = nc.dram_tensor("hgat_i", [8, 1024, NT], BF16).ap()
    ysrc = nc.dram_tensor("ysrc_i", [4, 4, 256, NT], BF16).ap()
    ygat = nc.dram_tensor("ygat_i", [4, 4, 1024, NT], BF16).ap()
    hT_own = hsrc[0:4].rearrange("c r n -> (c r) n")
    hTlo_own = hsrc[4:8].rearrange("c r n -> (c r) n")
    ygat3 = ygat.rearrange("q a r n -> q (a r) n")

    with ExitStack() as st:
        S = Sched(nc, st)
        hsrc_b = Buf("hsrc_b")
        hgat_b = [Buf("hgat_b%d" % i) for i in range(8)]
        ysrc_b = [[Buf("ysrc_b%d_%d" % (q, a)) for a in range(4)] for q in range(4)]
        ygat_b = Buf("ygat_b")
        xs_b = Buf("xs_b")
        S.global_bufs += [hsrc_b, ygat_b, xs_b] + hgat_b + [b for row in ysrc_b for b in row]

        def gather_h():
            for i in range(8):
                S.collective("AllGather", hsrc_b, hsrc[i], hgat_b[i], hgat[i], GROUPS)

        def load_h_from(chunks):
            def f(S_, h, t):
                r, tl = t // (NT // TT), (t % (NT // TT)) * TT
                for j, i in enumerate(chunks):
                    S_.dma("sp", h[:, 2 * j:2 * j + 2, :], hgat[i][r * 256:(r + 1) * 256, tl:tl + TT].rearrange("(c p) n -> p c n", p=128),
                           reads=[hgat_b[i]], writes=[h], group=(j > 0))
            return f

        def store_y_parts(parts):
            def f(S_, yo, t):
                q, tl = t // (NT // TT), (t % (NT // TT)) * TT
                for j, a in enumerate(parts):
                    S_.dma("sp", ysrc[q, a][:, tl:tl + TT].rearrange("(c p) n -> p c n", p=128), yo[:, 2 * j:2 * j + 2, :],
                           reads=[yo], writes=[ysrc_b[q][a]], sembuf=yo, group=True)
                if t % (NT // TT) == NT // TT - 1:
                    for a in parts:
                        S_.collective("AllGather", ysrc_b[q][a], ysrc[q, a], ygat_b, ygat[q, a], GROUPS)
            return f

        qcache = {}

        def load_y(S_, yub, t):
            tl = t * TT
            ph = S_.phase_id

            def qval(E):
                if ph not in qcache:
                    qcache[ph] = E.snap(E.partition_id() % 4)
                return qcache[ph]
            src = (lambda E, tl=tl: ygat3[bass.ds(qval(E), 1), :, tl:tl + TT].rearrange("o (k p) n -> p (o k) n", p=128))
            S_.dma("sp", yub[:], src, reads=[ygat_b], writes=[yub])

        S.begin_phase()
        if "first" not in skip:
            emit_tok(nc, S, {"xT": xT, "g_next": g_mix[0], "cst": cst, "hT_out": hT_own, "hT_lo": hTlo_own, "h_wr": [hsrc_b]}, "first")
        if "gather" not in skip:
            gather_h()
        S.end_phase()
        for l in range(DEPTH):
            S.begin_phase()
            if "gla" not in skip:
              emit_gla(nc, S, {"w_gla": w_gla[l], "wgk2": wgk2[l], "bgk": bgk[l], "ng": ng[l], "cst": cst,
                             "load_h": load_h_from((0, 1, 2, 3)), "load_hlo": load_h_from((4, 5, 6, 7)), "store_y": store_y_parts((0,))}, SEQ // TT)
            S.end_phase()
            S.begin_phase()
            if "ssm" not in skip:
              emit_ssm(nc, S, {"w_ssm": w_ssm[l], "cw": cw[l], "cb": cb[l], "dtb": dtb[l], "alog": alog[l], "dsk": dsk[l], "ngs": ngs[l],
                             "cst": cst, "load_h": load_h_from((0, 1, 2, 3)), "store_y": store_y_parts((1, 2))}, SEQ // TT)
            S.end_phase()
            S.begin_phase()
            if "diff" not in skip:
              emit_diff(nc, S, {"w_diff": w_diff[l], "pos": pos, "invf": invf, "lqk": lqk[l], "ngd": ngd[l], "cst": cst,
                              "load_h": load_h_from((0, 1, 2, 3)), "store_y": store_y_parts((3,))}, l, SEQ // TT)
            S.end_phase()
            last = (l == DEPTH - 1)
            A = {"xT": (xT if l == 0 else xs), "g_next": (g_fin if last else g_mix[l + 1]), "cst": cst, "hT": hT_own, "load_y": load_y,
                 "w_gate": w_gate[l], "b_gate": b_gate[l], "w_br": w_br[l], "w_out": w_out[l], "w_up": w_up[l], "w_dn": w_dn[l], "g_mlp": g_mlp[l]}
            if last:
                A["outT"] = outT
            else:
                A.update({"hT_out": hT_own, "hT_lo": hTlo_own, "h_wr": [hsrc_b], "xT_out": xs})
            S.begin_phase()
            emit_tok(nc, S, A, "last" if last else "mid")
            if not last:
                gather_h()
            S.end_phase()
    return nc


def fused_y_row_order():
    rows = []
    for a in range(4):
        for r in range(4):
            j = np.arange(256)
            if a == 0:
                rows.append(r * 256 + j)
            elif a in (1, 2):
                rows.append(1024 + r * 512 + (a - 1) * 256 + j)
            else:
                rows.append(3072 + r * 256 + j)
    return np.concatenate(rows)


def fused_inputs(inp, c, cst):
    b, g = c // 4, c % 4
    x = inp["x"]
    m = {"xT": np.ascontiguousarray(x[b, g * NT:(g + 1) * NT, :].T), "cst": cst,
         "pos": np.ascontiguousarray(inp["positions"][b].reshape(1, SEQ)),
         "g_mix": np.stack([pvec(inp["norm_mix_g"][l]) for l in range(DEPTH)]), "g_fin": pvec(inp["norm_final_g"]),
         "g_mlp": np.stack([pvec(inp["norm_mlp_g"][l]) for l in range(DEPTH)]),
         "w_gate": np.ascontiguousarray(inp["w_in"][:, :, 11312:14384]),
         "b_gate": np.stack([pvec(inp["b_gate"][l]) for l in range(DEPTH)]),
         "w_br": np.ascontiguousarray(np.concatenate([inp["w_br_gla"], inp["w_br_ssm"], inp["w_br_diff"]], axis=1)[:, fused_y_row_order(), :]),
         "w_out": np.ascontiguousarray(inp["w_out"]), "w_up": np.ascontiguousarray(inp["w_mlp_up"]), "w_dn": np.ascontiguousarray(inp["w_mlp_down"])}
    per = {}
    for l in range(DEPTH):
        d = {}
        d.update(gla_inputs(inp, l, g, None, cst, 0))
        d.update(ssm_inputs(inp, l, g, None, cst))
        d.update(diff_inputs(inp, l, g, b, None, cst))
        for k, v in d.items():
            if k in ("hT", "hTlo", "cst", "pos", "invf"):
                continue
            per.setdefault(k, []).append(v)
        if l == 0:
            m["invf"] = d["invf"]
    for k, v in per.items():
        m[k] = np.ascontiguousarray(np.stack(v))
    return m


_FUSED = []


def kernel(**inp):
    inp = {k: np.asarray(v) for k, v in inp.items()}
    cst = make_consts()
    if not _FUSED:
        _FUSED.append(build_fused())
    maps = [fused_inputs(inp, c, cst) for c in range(NCORES)]
    res = run_bass_kernel_spmd(_FUSED[0], maps, core_ids=list(range(NCORES))).results
    out = np.empty((B, SEQ, DM), np.float32)
    for c in range(NCORES):
        out[c // 4, (c % 4) * NT:(c % 4 + 1) * NT, :] = res[c]["outT"].T
    return out
```

```python
import math
from contextlib import ExitStack
import numpy as np
import ml_dtypes
import concourse.bass as bass
import concourse.mybir as mybir
from concourse.bass_utils import run_bass_kernel_spmd

F32 = mybir.dt.float32
BF16 = mybir.dt.bfloat16
I32 = mybir.dt.int32
AF = mybir.ActivationFunctionType
ALU = mybir.AluOpType
AX = mybir.AxisListType

NCORES = 8
B, SEQ, DM, DEPTH = 2, 8192, 1024, 2
NT = 2048
TT = 512
EPS = 1e-6
ENGS = ("pe", "act", "dve", "pool", "sp")


class Buf:
    __slots__ = ("name", "t", "w", "r", "dsem", "dcnt")

    def __init__(self, name, t=None):
        self.name = name
        self.t = t
        self.w = None
        self.r = {}
        self.dsem = None
        self.dcnt = 0

    def __getitem__(self, idx):
        return self.t[idx]


class VBuf:
    def __init__(self, name, parent, ap):
        self.name = name
        self.parent = parent
        self.t = ap

    def __getitem__(self, idx):
        return self.t[idx]

    w = property(lambda s: s.parent.w, lambda s, v: setattr(s.parent, "w", v))
    r = property(lambda s: s.parent.r, lambda s, v: setattr(s.parent, "r", v))
    dsem = property(lambda s: s.parent.dsem, lambda s, v: setattr(s.parent, "dsem", v))
    dcnt = property(lambda s: s.parent.dcnt, lambda s, v: setattr(s.parent, "dcnt", v))


class Sched:
    def __init__(self, nc, stack):
        self.nc = nc
        self.stack = stack
        self.q = {e: [] for e in ENGS}
        self.sem = {}
        for e in ENGS:
            self.sem[e] = stack.enter_context(nc.semaphore("s_" + e))
        self.cnt = {e: 0 for e in ENGS}
        self.waited = {e: {} for e in ENGS}
        self.ndsem = 0
        self.final_waits = {}
        self.alloc_stack = stack
        self.nname = 0
        self.free_dsems = {}
        self.dsem_cls = {}
        self.phase_bufs = []
        self.global_bufs = []

    def sb(self, name, shape, dt):
        self.nname += 1
        t = self.alloc_stack.enter_context(self.nc.sbuf_tensor("sb%d_%s" % (self.nname, name), list(shape), dt))
        b = Buf(name, t)
        self.phase_bufs.append(b)
        return b

    def ps(self, name, shape, dt=F32):
        self.nname += 1
        t = self.alloc_stack.enter_context(self.nc.psum_tensor("ps%d_%s" % (self.nname, name), list(shape), dt))
        return Buf(name, t)

    def view(self, name, ap):
        return Buf(name, ap)

    def _dsem(self, b, eng="sp"):
        cls = {"pool": "sw", "cc": "cc"}.get(eng, "hw")
        if b.dsem is not None:
            assert self.dsem_cls[b.dsem] == cls, (b.name, cls)
        if b.dsem is None:
            pool = self.free_dsems.setdefault(cls, [])
            if pool:
                b.dsem, b.dcnt = pool.pop()
            else:
                b.dsem = "d%d" % self.ndsem
                self.ndsem += 1
                self.sem[b.dsem] = self.stack.enter_context(self.nc.semaphore(b.dsem))
                self.dsem_cls[b.dsem] = cls
        return b.dsem

    def _deps(self, eng, reads, writes, same_ok):
        deps = {}

        def add(k, v):
            if deps.get(k, 0) < v:
                deps[k] = v

        for b in reads:
            if b.w is not None:
                add(*b.w)
        for b in writes:
            if b.w is not None:
                add(*b.w)
            for k, v in b.r.items():
                add(k, v)
        waits = []
        wd = self.waited[eng]
        for k, v in deps.items():
            if k == eng and same_ok:
                continue
            if wd.get(k, 0) >= v:
                continue
            wd[k] = v
            waits.append((k, v))
        return waits

    def _commit(self, tk, reads, writes):
        k, v = tk
        for b in writes:
            b.w = tk
            b.r = {}
        for b in reads:
            if b.r.get(k, 0) < v:
                b.r[k] = v

    def op(self, eng, fn, reads=(), writes=()):
        waits = self._deps(eng, reads, writes, same_ok=(eng == "pe"))
        self.cnt[eng] += 1
        tk = (eng, self.cnt[eng])
        sem = self.sem
        me = sem[eng]

        def emit(E):
            for k, v in waits:
                E.wait_ge(sem[k], v)
            fn(E).then_inc(me, 1)

        self.q[eng].append(emit)
        self._commit(tk, reads, writes)
        return tk

    def dma(self, eng, out_ap, in_ap, reads=(), writes=(), sembuf=None, group=False):
        sb = sembuf if sembuf is not None else (writes[0] if writes else reads[0])
        dk = self._dsem(sb, eng)
        saved = None
        if group and writes and writes[0].w is not None and writes[0].w[0] == dk:
            saved = writes[0].w
            writes[0].w = None
        waits = self._deps(eng, reads, writes, same_ok=False)
        if saved is not None:
            writes[0].w = saved
        sb.dcnt += 16
        tk = (dk, sb.dcnt)
        sem = self.sem
        ds = sem[dk]

        def emit(E):
            for k, v in waits:
                E.wait_ge(sem[k], v)
            E.dma_start(out=out_ap, in_=(in_ap(E) if callable(in_ap) else in_ap)).then_inc(ds, 16)

        self.q[eng].append(emit)
        self._commit(tk, reads, writes)
        self.final_waits[dk] = sb.dcnt
        return tk

    def collective(self, kind, sb_, s_ap, db_, d_ap, groups):
        dk = self._dsem(db_, "cc")
        waits = self._deps("pool", [sb_], [db_], same_ok=False)
        db_.dcnt += 1
        tk = (dk, db_.dcnt)
        sem = self.sem
        ds = sem[dk]

        def emit(E):
            for k, v in waits:
                E.wait_ge(sem[k], v)
            E.collective_compute(kind, ALU.bypass, replica_groups=groups, ins=[s_ap], outs=[d_ap]).then_inc(ds, 1)

        self.q["pool"].append(emit)
        self._commit(tk, [sb_], [db_])
        self.final_waits[dk] = db_.dcnt
        return tk

    def mm(self, ob, o, lb, l, rb, r, start=True, stop=True):
        return self.op("pe", lambda E: E.matmul(o, lhsT=l, rhs=r, start=start, stop=stop), reads=[lb, rb], writes=[ob])

    def act(self, ob, o, ib, i, func, bias=None, scale=None, accum=None, rd=(), wr=()):
        kw = {}
        if bias is not None:
            kw["bias"] = bias
        if scale is not None:
            kw["scale"] = scale
        if accum is not None:
            kw["accum_out"] = accum
        return self.op("act", lambda E: E.activation(out=o, in_=i, func=func, **kw), reads=[ib] + list(rd), writes=[ob] + list(wr))

    def tt(self, eng, ob, o, ab, a, bb, b, op):
        return self.op(eng, lambda E: E.tensor_tensor(out=o, in0=a, in1=b, op=op), reads=[ab, bb], writes=[ob])

    def ts(self, eng, ob, o, ab, a, s1, s2, op0, op1=None, rd=()):
        if op1 is None:
            return self.op(eng, lambda E: E.tensor_scalar(out=o, in0=a, scalar1=s1, scalar2=None, op0=op0), reads=[ab] + list(rd), writes=[ob])
        return self.op(eng, lambda E: E.tensor_scalar(out=o, in0=a, scalar1=s1, scalar2=s2, op0=op0, op1=op1), reads=[ab] + list(rd), writes=[ob])

    def stt(self, ob, o, ab, a, sc, bb, b, op0, op1, rd=()):
        return self.op("dve", lambda E: E.scalar_tensor_tensor(out=o, in0=a, scalar=sc, in1=b, op0=op0, op1=op1), reads=[ab, bb] + list(rd), writes=[ob])

    def copy(self, eng, ob, o, ib, i):
        if eng == "act":
            return self.op("act", lambda E: E.copy(out=o, in_=i), reads=[ib], writes=[ob])
        return self.op(eng, lambda E: E.tensor_copy(out=o, in_=i), reads=[ib], writes=[ob])

    def memset(self, eng, ob, o, val):
        return self.op(eng, lambda E: E.memset(o, val), writes=[ob])

    def recip(self, ob, o, ib, i):
        return self.op("dve", lambda E: E.reciprocal(out=o, in_=i), reads=[ib], writes=[ob])

    phase_id = 0

    def begin_phase(self):
        self.phase_id += 1
        self.gstack = self.stack if not hasattr(self, "gstack") else self.gstack
        self.pstack = ExitStack()
        self.pstack.__enter__()
        self.alloc_stack = self.pstack

    def end_phase(self):
        sem = self.sem
        cnt = dict(self.cnt)
        fw = dict(self.final_waits)
        for e in ENGS:
            waits = [(o, cnt[o]) for o in ENGS if o != e and cnt[o] > self.waited[e].get(o, 0)]
            waits += [(k, v) for k, v in fw.items() if v > self.waited[e].get(k, 0)]
            for k, v in waits:
                self.waited[e][k] = v

            def emit(E, waits=waits):
                for k, v in waits:
                    E.wait_ge(sem[k], v)
            self.q[e].append(emit)
        self.finish()
        self.q = {e: [] for e in ENGS}
        for e in ENGS:
            self.sem[e] = self.stack.enter_context(self.nc.semaphore("s_%s_%d" % (e, self.phase_id)))
            self.cnt[e] = 0
            for x in ENGS:
                self.waited[x].pop(e, None)
        for b in self.global_bufs:
            if b.w is not None and b.w[0] in ENGS:
                b.w = None
            b.r = {k: v for k, v in b.r.items() if k not in ENGS}
        for b in self.phase_bufs:
            if b.dsem is not None:
                self.free_dsems[self.dsem_cls[b.dsem]].append((b.dsem, b.dcnt))
                b.dsem = None
        self.phase_bufs = []
        self.pstack.__exit__(None, None, None)
        self.alloc_stack = self.stack

    def finish(self):
        nc = self.nc
        sem = self.sem
        q = self.q
        fw = dict(self.final_waits)
        with nc.Block() as block:
            @block.tensor
            def _(E):
                for f in q["pe"]:
                    f(E)

            @block.scalar
            def _(E):
                for f in q["act"]:
                    f(E)

            @block.vector
            def _(E):
                for f in q["dve"]:
                    f(E)

            @block.gpsimd
            def _(E):
                for f in q["pool"]:
                    f(E)

            @block.sync
            def _(E):
                for f in q["sp"]:
                    f(E)
                for k, v in fw.items():
                    E.wait_ge(sem[k], v)


class Ring:
    def __init__(self, bufs):
        self.bufs = bufs
        self.i = 0

    def next(self):
        b = self.bufs[self.i % len(self.bufs)]
        self.i += 1
        return b


def dram(nc, name, shape, dt, kind):
    return nc.dram_tensor(name, list(shape), dt, kind=kind).ap()


def kcp(ap):
    return ap.rearrange("(kc p) n -> p kc n", p=128)


def rms_stats(S, C, xb, nch, n, ps_ring, sq_ring, rstd, tmp, nfeat, eps):
    pss = ps_ring.next()
    for kc in range(nch):
        sq = sq_ring.next()
        S.act(sq, sq[:, :n], xb, xb[:, kc, :n], AF.Square)
        S.mm(pss, pss[:, :n], C["ones"], C["ones"][:], sq, sq[:, :n], start=(kc == 0), stop=(kc == nch - 1))
    S.act(tmp, tmp[:, :n], pss, pss[:, :n], AF.Sqrt, bias=C["eps%g" % eps][:, 0:1], scale=1.0 / nfeat, rd=[C["eps%g" % eps]])
    S.recip(rstd, rstd[:, :n], tmp, tmp[:, :n])


def load_consts(S, nc, cst_ap, extra_eps=()):
    C = {}
    C["ones"] = S.sb("c_ones", [128, 128], F32)
    S.dma("sp", C["ones"][:], cst_ap[0], writes=[C["ones"]])
    for e in (EPS,) + tuple(extra_eps):
        b = S.sb("c_eps%g" % e, [128, 1], F32)
        S.memset("dve", b, b[:], float(e))
        C["eps%g" % e] = b
    return C


def build_tok(mode):
    nc = bass.Bass("TRN2", target_bir_lowering=False)
    A = {}
    A["xT"] = dram(nc, "xT", [DM, NT], F32, "ExternalInput")
    A["g_next"] = dram(nc, "g_next", [128, 8], F32, "ExternalInput")
    A["cst"] = dram(nc, "cst", [4, 128, 128], F32, "ExternalInput")
    if mode != "first":
        A["hT"] = dram(nc, "hT", [DM, NT], BF16, "ExternalInput")
        yT = dram(nc, "yT", [4096, NT], BF16, "ExternalInput")
        A["load_y"] = lambda S, yub, t: S.dma("act", yub[:], kcp(yT[:, t * TT:(t + 1) * TT]), writes=[yub])
        A["w_gate"] = dram(nc, "w_gate", [DM, 3072], F32, "ExternalInput")
        A["b_gate"] = dram(nc, "b_gate", [128, 24], F32, "ExternalInput")
        A["w_br"] = dram(nc, "w_br", [4096, DM], F32, "ExternalInput")
        A["w_out"] = dram(nc, "w_out", [DM, DM], F32, "ExternalInput")
        A["w_up"] = dram(nc, "w_up", [DM, 4096], F32, "ExternalInput")
        A["w_dn"] = dram(nc, "w_dn", [4096, DM], F32, "ExternalInput")
        A["g_mlp"] = dram(nc, "g_mlp", [128, 8], F32, "ExternalInput")
    if mode == "last":
        A["outT"] = dram(nc, "outT", [DM, NT], F32, "ExternalOutput")
    else:
        A["hT_out"] = dram(nc, "hT_out", [DM, NT], BF16, "ExternalOutput")
        A["hT_lo"] = dram(nc, "hT_lo", [DM, NT], BF16, "ExternalOutput")
        if mode == "mid":
            A["xT_out"] = dram(nc, "xT_out", [DM, NT], F32, "ExternalOutput")
    with ExitStack() as st:
        S = Sched(nc, st)
        S.begin_phase()
        emit_tok(nc, S, A, mode)
        S.end_phase()
    return nc


def emit_tok(nc, S, A, mode):
    xT, gn, cst = A["xT"], A["g_next"], A["cst"]
    if mode != "first":
        hT, w_gate, b_gate, w_br, w_out, w_up, w_dn, gm = (A[k] for k in ("hT", "w_gate", "b_gate", "w_br", "w_out", "w_up", "w_dn", "g_mlp"))
    if mode == "last":
        outT = A["outT"]
    else:
        hTo, hTlo = A["hT_out"], A["hT_lo"]
        if mode == "mid":
            xTo = A["xT_out"]
    if True:
        C = load_consts(S, nc, cst)
        gnb = S.sb("gnb", [128, 8], F32)
        S.dma("sp", gnb[:], gn, writes=[gnb])
        xb = S.sb("xb", [128, 8, TT], F32)
        ps_ring = Ring([S.ps("ps%d" % i, [128, TT], F32) for i in range(7)])
        sq_ring = Ring([S.sb("sq%d" % i, [128, TT], F32) for i in range(2)])
        rstd = S.sb("rstd", [128, TT], F32)
        rtmp = S.sb("rtmp", [128, TT], F32)
        hn_ring = Ring([S.sb("hn%d" % i, [128, 8, TT], BF16) for i in range(2)])
        hl_ring = Ring([S.sb("hl%d" % i, [128, 8, TT], BF16) for i in range(2)])
        h32_ring = Ring([S.sb("h32_%d" % i, [128, TT], F32) for i in range(2)])
        if mode == "last":
            oc_ring = Ring([S.sb("oc%d" % i, [128, TT], F32) for i in range(3)])
        if mode != "first":
            bgb = S.sb("bgb", [128, 24], F32)
            S.dma("sp", bgb[:], b_gate, writes=[bgb])
            gmb = S.sb("gmb", [128, 8], F32)
            S.dma("sp", gmb[:], gm, writes=[gmb])
            hb_ring = Ring([S.sb("hb%d" % i, [128, 8, TT], BF16) for i in range(2)])
            yub = S.sb("yub", [128, 32, TT], BF16)
            mixed = S.sb("mixed", [128, 8, TT], BF16)
            h2 = S.sb("h2", [128, 8, TT], BF16)
            g_ring = Ring([S.sb("g%d" % i, [128, TT], F32) for i in range(6)])
            acc_ring = Ring([S.sb("acc%d" % i, [128, TT], F32) for i in range(2)])
            tmp_ring = Ring([S.sb("tmp%d" % i, [128, TT], F32) for i in range(2)])
            r_ring = Ring([S.sb("r%d" % i, [128, TT], F32) for i in range(2)])
            w_ring = Ring([S.sb("w%d" % i, [128, 7168], BF16) for i in range(3)])

        for t in range(NT // TT):
            ts_ = slice(t * TT, (t + 1) * TT)
            S.dma("sp", xb[:], kcp(xT[:, ts_]), writes=[xb])
            if mode != "first":
                hb = hb_ring.next()
                S.dma("sp", hb[:], kcp(hT[:, ts_]), writes=[hb])
                A["load_y"](S, yub, t)
                for oc in range(8):
                    w = w_ring.next()
                    wg = w[:, 0:3072].rearrange("p (kc br j) -> p kc br j", kc=8, br=3)
                    wb = w[:, 3072:7168].rearrange("p (kc j) -> p kc j", kc=32)
                    for br in range(3):
                        c0 = br * 1024 + oc * 128
                        S.dma("pool", wg[:, :, br, :], kcp(w_gate[:, c0:c0 + 128]), writes=[w], group=(br > 0))
                    S.dma("pool", wb, kcp(w_br[:, oc * 128:(oc + 1) * 128]), writes=[w], group=True)
                    gts = []
                    for br in range(3):
                        pg = ps_ring.next()
                        for kc in range(8):
                            S.mm(pg, pg[:], w, wg[:, kc, br, :], hb, hb[:, kc, :], start=(kc == 0), stop=(kc == 7))
                        g = g_ring.next()
                        ch = br * 8 + oc
                        S.act(g, g[:], pg, pg[:], AF.Sigmoid, bias=bgb[:, ch:ch + 1], rd=[bgb])
                        gts.append(g)
                    acc = acc_ring.next()
                    koff = (0, 8, 24)
                    nk = (8, 16, 8)
                    for br in range(3):
                        pb = ps_ring.next()
                        for kc in range(nk[br]):
                            S.mm(pb, pb[:], w, wb[:, koff[br] + kc, :], yub, yub[:, koff[br] + kc, :], start=(kc == 0), stop=(kc == nk[br] - 1))
                        if br == 0:
                            S.tt("dve", acc, acc[:], pb, pb[:], gts[0], gts[0][:], ALU.mult)
                        else:
                            tmp = tmp_ring.next()
                            S.tt("dve", tmp, tmp[:], pb, pb[:], gts[br], gts[br][:], ALU.mult)
                            if br == 1:
                                S.tt("dve", acc, acc[:], acc, acc[:], tmp, tmp[:], ALU.add)
                            else:
                                S.tt("dve", mixed, mixed[:, oc, :], acc, acc[:], tmp, tmp[:], ALU.add)
                for half in range(2):
                    w = w_ring.next()
                    wo = w[:, 0:4096].rearrange("p (kc j) -> p kc j", kc=8)
                    S.dma("pool", wo, kcp(w_out[:, half * 512:(half + 1) * 512]), writes=[w])
                    for o4 in range(4):
                        oc = half * 4 + o4
                        po = ps_ring.next()
                        for kc in range(8):
                            S.mm(po, po[:], w, wo[:, kc, o4 * 128:(o4 + 1) * 128], mixed, mixed[:, kc, :], start=(kc == 0), stop=(kc == 7))
                        S.tt("dve", xb, xb[:, oc, :], xb, xb[:, oc, :], po, po[:], ALU.add)
                rms_stats(S, C, xb, 8, TT, ps_ring, sq_ring, rstd, rtmp, DM, EPS)
                for kc in range(8):
                    S.stt(h2, h2[:, kc, :], xb, xb[:, kc, :], gmb[:, kc:kc + 1], rstd, rstd[:], ALU.mult, ALU.mult, rd=[gmb])
                for o8 in range(8):
                    w = w_ring.next()
                    wu = w[:, 0:4096].rearrange("p (kc j) -> p kc j", kc=8)
                    S.dma("pool", wu, kcp(w_up[:, o8 * 512:(o8 + 1) * 512]), writes=[w])
                    for o4 in range(4):
                        oc = o8 * 4 + o4
                        pu = ps_ring.next()
                        for kc in range(8):
                            S.mm(pu, pu[:], w, wu[:, kc, o4 * 128:(o4 + 1) * 128], h2, h2[:, kc, :], start=(kc == 0), stop=(kc == 7))
                        r = r_ring.next()
                        S.act(r, r[:], pu, pu[:], AF.Relu)
                        S.tt("pool", yub, yub[:, oc, :], r, r[:], r, r[:], ALU.mult)
                for oc in range(8):
                    w = w_ring.next()
                    wd = w[:, 0:4096].rearrange("p (kc j) -> p kc j", kc=32)
                    S.dma("pool", wd, kcp(w_dn[:, oc * 128:(oc + 1) * 128]), writes=[w])
                    pd = ps_ring.next()
                    for kc in range(32):
                        S.mm(pd, pd[:], w, wd[:, kc, :], yub, yub[:, kc, :], start=(kc == 0), stop=(kc == 31))
                    S.tt("dve", xb, xb[:, oc, :], xb, xb[:, oc, :], pd, pd[:], ALU.add)
                if mode == "mid":
                    S.dma("sp", kcp(xTo[:, ts_]), xb[:], reads=[xb])
            rms_stats(S, C, xb, 8, TT, ps_ring, sq_ring, rstd, rtmp, DM, EPS)
            if mode == "last":
                for kc in range(8):
                    o = oc_ring.next()
                    S.stt(o, o[:], xb, xb[:, kc, :], gnb[:, kc:kc + 1], rstd, rstd[:], ALU.mult, ALU.mult, rd=[gnb])
                    S.dma("sp", outT[kc * 128:(kc + 1) * 128, ts_], o[:], reads=[o])
            else:
                hn = hn_ring.next()
                hl = hl_ring.next()
                for kc in range(8):
                    h32 = h32_ring.next()
                    S.stt(h32, h32[:], xb, xb[:, kc, :], gnb[:, kc:kc + 1], rstd, rstd[:], ALU.mult, ALU.mult, rd=[gnb])
                    S.copy("act", hn, hn[:, kc, :], h32, h32[:])
                    S.tt("pool", hl, hl[:, kc, :], h32, h32[:], hn, hn[:, kc, :], ALU.subtract)
                S.dma("sp", kcp(hTo[:, ts_]), hn[:], reads=[hn], writes=A.get("h_wr", []), sembuf=hn)
                S.dma("sp", kcp(hTlo[:, ts_]), hl[:], reads=[hl], writes=A.get("h_wr", []), sembuf=hl)


def make_consts():
    c = np.zeros((4, 128, 128), np.float32)
    c[0] = 1.0
    c[1] = np.triu(np.ones((128, 128), np.float32))
    c[2] = 1.0 - c[1]
    c[3] = np.eye(128, dtype=np.float32)
    return c


def pvec(v):
    v = np.asarray(v)
    return np.ascontiguousarray(v.reshape(-1, 128).T)


def small_views(S, name, nbanks, width):
    out = []
    per = 512 // width
    banks = [S.ps("%s%d" % (name, i), [128, 512], F32) for i in range(nbanks)]
    for j in range(per):
        for i in range(nbanks):
            out.append(VBuf("%s%d_%d" % (name, i, j), banks[i], banks[i].t[:, j * width:(j + 1) * width]))
    return out


def build_gla(ntiles=SEQ // TT):
    nc = bass.Bass("TRN2", target_bir_lowering=False)
    hT = dram(nc, "hT", [DM, SEQ], BF16, "ExternalInput")
    hTl = dram(nc, "hTlo", [DM, SEQ], BF16, "ExternalInput")
    A = {}
    A["w_gla"] = dram(nc, "w_gla", [DM, 784], F32, "ExternalInput")
    A["wgk2"] = dram(nc, "wgk2", [16, 128], F32, "ExternalInput")
    A["bgk"] = dram(nc, "bgk", [1, 128], F32, "ExternalInput")
    A["ng"] = dram(nc, "ng", [128, 2], F32, "ExternalInput")
    A["cst"] = dram(nc, "cst", [4, 128, 128], F32, "ExternalInput")
    yT = dram(nc, "yT", [256, SEQ], BF16, "ExternalOutput")
    A["load_h"] = lambda S, h, t: S.dma("sp", h[:], kcp(hT[:, t * TT:(t + 1) * TT]), writes=[h])
    A["load_hlo"] = lambda S, h, t: S.dma("act", h[:], kcp(hTl[:, t * TT:(t + 1) * TT]), writes=[h])
    A["store_y"] = lambda S, yo, t: S.dma("sp", yT[:, t * TT:(t + 1) * TT].rearrange("(ec p) n -> p ec n", p=128), yo[:], reads=[yo])
    with ExitStack() as st:
        S = Sched(nc, st)
        S.begin_phase()
        emit_gla(nc, S, A, ntiles)
        S.end_phase()
    return nc


def emit_gla(nc, S, A, ntiles=SEQ // TT):
    w_gla, wgk2, bgk, ngp, cst = A["w_gla"], A["wgk2"], A["bgk"], A["ng"], A["cst"]
    if True:
        C = load_consts(S, nc, cst)
        U = S.sb("U", [128, 128], F32)
        UC = S.sb("UC", [128, 128], F32)
        S.dma("sp", U[:], cst[1], writes=[U])
        S.dma("sp", UC[:], cst[2], writes=[UC])
        wa = S.sb("wa", [128, 8, 784], BF16)
        S.dma("pool", wa[:], kcp(w_gla), writes=[wa])
        wqk32 = S.sb("wqk32", [128, 8, 256], F32)
        S.dma("sp", wqk32[:, :, 0:128], kcp(w_gla[:, 0:128]), writes=[wqk32])
        S.dma("sp", wqk32[:, :, 128:256], kcp(w_gla[:, 384:512]), writes=[wqk32], group=True)
        wqk_hi = S.sb("wqk_hi", [128, 8, 256], BF16)
        wqk_lo = S.sb("wqk_lo", [128, 8, 256], BF16)
        S.copy("act", wqk_hi, wqk_hi[:], wqk32, wqk32[:])
        S.tt("dve", wqk_lo, wqk_lo[:], wqk32, wqk32[:], wqk_hi, wqk_hi[:], ALU.subtract)
        w2 = S.sb("w2", [16, 128], F32)
        S.dma("sp", w2[:], wgk2, writes=[w2])
        bgb = S.sb("bgb", [128, 128], F32)
        S.dma("sp", bgb[:], bgk.partition_broadcast(128), writes=[bgb])
        ng = S.sb("ng", [128, 2], F32)
        S.dma("sp", ng[:], ngp, writes=[ng])
        h_ring = Ring([S.sb("h%d" % i, [128, 8, TT], BF16) for i in range(2)])
        hlo_ring = Ring([S.sb("hlo%d" % i, [128, 8, TT], BF16) for i in range(2)])
        big = Ring([S.ps("pb%d" % i, [128, 512], F32) for i in range(2)])
        po = [S.ps("po%d" % i, [128, 512], F32) for i in range(2)]
        pss_ring = Ring([S.ps("pss", [128, 512], F32)])
        sm = Ring(small_views(S, "sm", 3, 256))
        qTs = S.sb("qTs", [128, TT], F32)
        kTs = S.sb("kTs", [128, TT], F32)
        sg = S.sb("sg", [128, 2, TT], F32)
        gkl = S.sb("gkl", [16, TT], F32)

        def ring(name, shape, dt, n=2):
            return Ring([S.sb("%s%d" % (name, i), shape, dt) for i in range(n)])
        ktok_r = ring("ktok", [128, 128], F32)
        vtok_r = ring("vtok", [128, 256], BF16)
        t1_r = ring("t1", [128, 128], F32)
        e_r = ring("e", [128, 128], F32)
        gk_r = ring("gk", [128, 128], F32)
        ebT_r = ring("ebT", [128, 128], F32)
        enbT_r = ring("enbT", [128, 128], F32)
        ed2_r = ring("ed2", [128, 128], F32)
        qp_r = ring("qp", [128, 128], BF16)
        qp32_r = ring("qp32", [128, 128], F32)
        kp32_r = ring("kp32", [128, 128], F32)
        kpp_r = ring("kpp", [128, 128], BF16)
        AT_r = ring("AT", [128, 128], BF16)
        Sst = S.sb("Sst", [128, 256], F32)
        Sbf = S.sb("Sbf", [128, 256], BF16)
        S.memset("dve", Sst, Sst[:], 0.0)
        S.memset("dve", Sbf, Sbf[:], 0.0)
        sq_ring = ring("sq", [128, TT], F32)
        rstd = S.sb("rstd", [128, TT], F32)
        rtmp = S.sb("rtmp", [128, TT], F32)
        tmp_r = ring("tmp", [128, TT], F32)
        yo_r = ring("yo", [128, 2, TT], BF16)

        for t in range(ntiles):
            ts_ = slice(t * TT, (t + 1) * TT)
            h = h_ring.next()
            A["load_h"](S, h, t)

            def proj(c0, m):
                p = big.next()
                for kc in range(8):
                    S.mm(p, p[:m, :], wa, wa[:, kc, c0:c0 + m], h, h[:, kc, :], start=(kc == 0), stop=(kc == 7))
                return p
            hlo = hlo_ring.next()
            A["load_hlo"](S, hlo, t)

            def proj3(c0):
                p = big.next()
                n = 0
                for (wb_, hb_) in ((wqk_hi, h), (wqk_hi, hlo), (wqk_lo, h)):
                    for kc in range(8):
                        S.mm(p, p[:], wb_, wb_[:, kc, c0:c0 + 128], hb_, hb_[:, kc, :], start=(n == 0), stop=(n == 23))
                        n += 1
                return p
            p = proj3(0)
            S.act(qTs, qTs[:], p, p[:], AF.Copy, scale=128.0 ** -0.5)
            p = proj3(128)
            S.act(kTs, kTs[:], p, p[:], AF.Copy)
            for ec in range(2):
                p = proj(128 + ec * 128, 128)
                S.act(sg, sg[:, ec, :], p, p[:], AF.Silu)
            p = proj(768, 16)
            S.copy("dve", gkl, gkl[:], p, p[:16, :])
            for c in range(4):
                cs = slice(c * 128, (c + 1) * 128)
                p = big.next()
                for kc in range(8):
                    S.mm(p, p[:, :384], h, h[:, kc, cs], wa, wa[:, kc, 384:768], start=(kc == 0), stop=(kc == 7))
                ktok = ktok_r.next()
                vtok = vtok_r.next()
                S.copy("act", ktok, ktok[:], p, p[:, 0:128])
                S.copy("act", vtok, vtok[:], p, p[:, 128:384])
                pg = sm.next()
                S.mm(pg, pg[:, :128], gkl, gkl[:, cs], w2, w2[:], start=True, stop=True)
                t1 = t1_r.next()
                S.tt("dve", t1, t1[:], pg, pg[:, :128], bgb, bgb[:], ALU.add)
                e = e_r.next()
                S.act(e, e[:], t1, t1[:], AF.Exp, scale=-1.0)
                S.act(e, e[:], e, e[:], AF.Ln, bias=1.0)
                gk = gk_r.next()
                S.ts("dve", gk, gk[:], e, e[:], -1.0 / 16.0, None, ALU.mult)
                pbT = sm.next()
                S.mm(pbT, pbT[:, :128], gk, gk[:], U, U[:])
                pd2 = sm.next()
                S.mm(pd2, pd2[:, :128], UC, UC[:], gk, gk[:])
                ebT = ebT_r.next()
                enbT = enbT_r.next()
                ed2 = ed2_r.next()
                S.act(ebT, ebT[:], pbT, pbT[:, :128], AF.Exp)
                S.act(enbT, enbT[:], pbT, pbT[:, :128], AF.Exp, scale=-1.0)
                S.act(ed2, ed2[:], pd2, pd2[:, :128], AF.Exp)
                qp = qp_r.next()
                qp32 = qp32_r.next()
                kp32 = kp32_r.next()
                kpp = kpp_r.next()
                S.tt("dve", qp32, qp32[:], qTs, qTs[:, cs], ebT, ebT[:], ALU.mult)
                S.tt("dve", kp32, kp32[:], kTs, kTs[:, cs], enbT, enbT[:], ALU.mult)
                S.copy("act", qp, qp[:], qp32, qp32[:])
                S.tt("pool", kpp, kpp[:], ktok, ktok[:], ed2, ed2[:], ALU.mult)
                pA = sm.next()
                S.mm(pA, pA[:, :128], kp32, kp32[:], qp32, qp32[:])
                AT = AT_r.next()
                S.tt("dve", AT, AT[:], pA, pA[:, :128], U, U[:], ALU.mult)
                for ec in range(2):
                    es = slice(ec * 128, (ec + 1) * 128)
                    S.mm(po[ec], po[ec][:, cs], vtok, vtok[:, es], AT, AT[:], start=True, stop=False)
                    S.mm(po[ec], po[ec][:, cs], Sbf, Sbf[:, es], qp, qp[:], start=False, stop=True)
                pkv = sm.next()
                S.mm(pkv, pkv[:, :256], kpp, kpp[:], vtok, vtok[:])
                S.stt(Sst, Sst[:], Sst, Sst[:], ebT[:, 127:128], pkv, pkv[:, :256], ALU.mult, ALU.add, rd=[ebT])
                S.copy("act", Sbf, Sbf[:], Sst, Sst[:])
            pss = pss_ring.next()
            for ec in range(2):
                sq = sq_ring.next()
                S.act(sq, sq[:], po[ec], po[ec][:], AF.Square)
                S.mm(pss, pss[:], C["ones"], C["ones"][:], sq, sq[:], start=(ec == 0), stop=(ec == 1))
            S.act(rtmp, rtmp[:], pss, pss[:], AF.Sqrt, bias=C["eps%g" % EPS][:, 0:1], scale=1.0 / 256, rd=[C["eps%g" % EPS]])
            S.recip(rstd, rstd[:], rtmp, rtmp[:])
            yo = yo_r.next()
            for ec in range(2):
                tmp = tmp_r.next()
                S.stt(tmp, tmp[:], po[ec], po[ec][:], ng[:, ec:ec + 1], rstd, rstd[:], ALU.mult, ALU.mult, rd=[ng])
                S.tt("pool", yo, yo[:, ec, :], tmp, tmp[:], sg, sg[:, ec, :], ALU.mult)
            A["store_y"](S, yo, t)


def gla_inputs(inp, l, g, hT_b, cst, hTlo_b=None):
    w = inp["w_in"][l]
    cols = np.concatenate([np.arange(g * 128, (g + 1) * 128),
                           2064 + np.arange(g * 256, (g + 1) * 256),
                           512 + np.arange(g * 128, (g + 1) * 128),
                           1024 + np.arange(g * 256, (g + 1) * 256),
                           2048 + np.arange(16)])
    if hTlo_b is None:
        hTlo_b = np.zeros_like(hT_b)
    elif isinstance(hTlo_b, int):
        hTlo_b = None
    return {"hT": hT_b, "hTlo": hTlo_b, "w_gla": np.ascontiguousarray(w[:, cols]),
            "wgk2": np.ascontiguousarray(inp["gla_w_gk2"][l][:, g * 128:(g + 1) * 128]),
            "bgk": np.ascontiguousarray(inp["gla_b_gk"][l][g * 128:(g + 1) * 128].reshape(1, 128)),
            "ng": pvec(inp["gla_norm_g"][l]), "cst": cst}


def build_ssm(ntiles=SEQ // TT):
    nc = bass.Bass("TRN2", target_bir_lowering=False)
    hT = dram(nc, "hT", [DM, SEQ], BF16, "ExternalInput")
    A = {}
    A["w_ssm"] = dram(nc, "w_ssm", [DM, 1288], F32, "ExternalInput")
    A["cw"] = dram(nc, "cw", [128, 6, 4], F32, "ExternalInput")
    A["cb"] = dram(nc, "cb", [128, 6], F32, "ExternalInput")
    A["dtb"] = dram(nc, "dtb", [1, 8], F32, "ExternalInput")
    A["alog"] = dram(nc, "alog", [1, 8], F32, "ExternalInput")
    A["dsk"] = dram(nc, "dsk", [1, 8], F32, "ExternalInput")
    A["ngs"] = dram(nc, "ngs", [1, 512], F32, "ExternalInput")
    A["cst"] = dram(nc, "cst", [4, 128, 128], F32, "ExternalInput")
    yT = dram(nc, "yT", [512, SEQ], BF16, "ExternalOutput")
    A["load_h"] = lambda S, h, t: S.dma("sp", h[:], kcp(hT[:, t * TT:(t + 1) * TT]), writes=[h])
    A["store_y"] = lambda S, yo, t: S.dma("sp", yT[:, t * TT:(t + 1) * TT].rearrange("(c p) n -> p c n", p=128), yo[:], reads=[yo])
    with ExitStack() as st:
        S = Sched(nc, st)
        S.begin_phase()
        emit_ssm(nc, S, A, ntiles)
        S.end_phase()
    return nc


def emit_ssm(nc, S, A, ntiles=SEQ // TT):
    w_ssm, cwp, cbp, dtbp, alogp, dskp, ngp, cst = (A[k] for k in ("w_ssm", "cw", "cb", "dtb", "alog", "dsk", "ngs", "cst"))
    if True:
        C = load_consts(S, nc, cst)
        U = S.sb("U", [128, 128], F32)
        UC = S.sb("UC", [128, 128], F32)
        idf = S.sb("idf", [128, 128], F32)
        idb = S.sb("idb", [128, 128], BF16)
        S.dma("sp", U[:], cst[1], writes=[U])
        S.dma("sp", UC[:], cst[2], writes=[UC])
        S.dma("sp", idf[:], cst[3], writes=[idf])
        S.dma("pool", idb[:], cst[3], writes=[idb])
        ws = S.sb("ws", [128, 8, 1288], BF16)
        S.dma("pool", ws[:], kcp(w_ssm), writes=[ws])
        cw = S.sb("cw", [128, 6, 4], F32)
        cb = S.sb("cb", [128, 6], F32)
        S.dma("sp", cw[:], cwp, writes=[cw])
        S.dma("sp", cb[:], cbp, writes=[cb])
        dtb = S.sb("dtb", [128, 8], F32)
        a_b = S.sb("a_b", [128, 8], F32)
        dsk = S.sb("dsk", [128, 8], F32)
        ngs = S.sb("ngs", [128, 512], F32)
        S.dma("sp", dtb[:], dtbp.partition_broadcast(128), writes=[dtb])
        S.dma("sp", a_b[:], alogp.partition_broadcast(128), writes=[a_b])
        S.dma("sp", dsk[:], dskp.partition_broadcast(128), writes=[dsk])
        S.dma("sp", ngs[:], ngp.partition_broadcast(128), writes=[ngs])
        S.act(a_b, a_b[:], a_b, a_b[:], AF.Exp)
        S.ts("dve", a_b, a_b[:], a_b, a_b[:], -1.0, None, ALU.mult)

        def ring(name, shape, dt, n=2):
            return Ring([S.sb("%s%d" % (name, i), shape, dt) for i in range(n)])
        h_ring = ring("h", [128, 8, TT], BF16)
        big = Ring([S.ps("pb%d" % i, [128, 512], F32) for i in range(6)])
        sm = Ring(small_views(S, "sm", 2, 128))
        raw = S.sb("raw", [128, 6, TT + 3], F32)
        S.memset("dve", raw, raw[:, :, 0:3], 0.0)
        cacc_r = ring("cacc", [128, TT], F32)
        xc = S.sb("xc", [128, 4, TT], F32)
        BT = S.sb("BT", [128, TT], BF16)
        CT = S.sb("CT", [128, TT], BF16)
        sz_r = ring("sz", [128, 512], F32)
        t8_r = ring("t8", [128, 8], F32)
        dt_r = ring("dt", [128, 8], F32)
        da_r = ring("da", [128, 8], F32)
        xdt_r = ring("xdt", [128, 512], BF16)
        xD_r = ring("xD", [128, 512], F32)
        Btok_r = ring("Btok", [128, 128], BF16)
        rda_r = ring("rda", [128, 8, 128], F32)
        eL_r = ring("eL", [128, 8, 128], F32)
        GU_r = ring("GU", [128, 128], F32)
        M_r = ring("M", [128, 8, 128], BF16)
        cs_r = ring("cs", [128, 16], F32)
        ec_r = ring("ec", [128, 24], F32)
        xdtd_r = ring("xdtd", [128, 512], BF16)
        Sst = S.sb("Sst", [128, 512], F32)
        Sbf = S.sb("Sbf", [128, 512], BF16)
        S.memset("dve", Sst, Sst[:], 0.0)
        S.memset("dve", Sbf, Sbf[:], 0.0)
        y1_r = ring("y1", [128, 512], F32)
        y2_r = ring("y2", [128, 512], F32)
        junk = S.sb("junk", [128, 512], F32)
        ss_r = ring("ss", [128, 2], F32)
        yn_r = ring("yn", [128, 512], BF16)
        yo_r = ring("yo", [128, 4, TT], BF16)
        eps = C["eps%g" % EPS]

        def b8(ap):
            return ap.unsqueeze(2).to_broadcast([128, 8, 64])

        def v8(ap):
            return ap.rearrange("p (h q) -> p h q", h=8)

        for t in range(ntiles):
            ts_ = slice(t * TT, (t + 1) * TT)
            h = h_ring.next()
            A["load_h"](S, h, t)
            for ch in range(6):
                p = big.next()
                for kc in range(8):
                    S.mm(p, p[:], ws, ws[:, kc, 512 + ch * 128:640 + ch * 128], h, h[:, kc, :], start=(kc == 0), stop=(kc == 7))
                S.act(raw, raw[:, ch, 3:TT + 3], p, p[:], AF.Copy)
            for ch in range(6):
                acc = cacc_r.next()
                S.ts("dve", acc, acc[:], raw, raw[:, ch, 0:TT], cw[:, ch, 0:1], cb[:, ch:ch + 1], ALU.mult, ALU.add, rd=[cw, cb])
                for i in range(1, 4):
                    S.stt(acc, acc[:], raw, raw[:, ch, i:i + TT], cw[:, ch, i:i + 1], acc, acc[:], ALU.mult, ALU.add, rd=[cw])
                if ch < 4:
                    S.act(xc, xc[:, ch, :], acc, acc[:], AF.Silu)
                elif ch == 4:
                    S.act(BT, BT[:], acc, acc[:], AF.Silu)
                else:
                    S.act(CT, CT[:], acc, acc[:], AF.Silu)
            S.copy("pool", raw, raw[:, :, 0:3], raw, raw[:, :, TT:TT + 3])
            yo = yo_r.next()
            for c in range(4):
                cs = slice(c * 128, (c + 1) * 128)
                pz = big.next()
                for kc in range(8):
                    S.mm(pz, pz[:], h, h[:, kc, cs], ws, ws[:, kc, 0:512], start=(kc == 0), stop=(kc == 7))
                sz = sz_r.next()
                S.act(sz, sz[:], pz, pz[:], AF.Silu)
                pdt = sm.next()
                for kc in range(8):
                    S.mm(pdt, pdt[:, 0:8], h, h[:, kc, cs], ws, ws[:, kc, 1280:1288], start=(kc == 0), stop=(kc == 7))
                t8 = t8_r.next()
                S.tt("dve", t8, t8[:], pdt, pdt[:, 0:8], dtb, dtb[:], ALU.add)
                S.act(t8, t8[:], t8, t8[:], AF.Exp)
                dt = dt_r.next()
                S.act(dt, dt[:], t8, t8[:], AF.Ln, bias=1.0)
                da = da_r.next()
                S.tt("dve", da, da[:], dt, dt[:], a_b, a_b[:], ALU.mult)
                px = big.next()
                for ch in range(4):
                    S.mm(px, px[:, ch * 128:(ch + 1) * 128], xc, xc[:, ch, cs], idf, idf[:])
                xdt = xdt_r.next()
                xD = xD_r.next()
                S.tt("dve", xdt, v8(xdt[:]), px, v8(px[:]), dt, b8(dt[:]), ALU.mult)
                S.tt("dve", xD, v8(xD[:]), px, v8(px[:]), dsk, b8(dsk[:]), ALU.mult)
                pB = sm.next()
                S.mm(pB, pB[:], BT, BT[:, cs], idb, idb[:])
                Btok = Btok_r.next()
                S.copy("act", Btok, Btok[:], pB, pB[:])
                rda = rda_r.next()
                S.tt("pool", rda, rda[:], U, U[:].unsqueeze(1).to_broadcast([128, 8, 128]), da, da[:].unsqueeze(2).to_broadcast([128, 8, 128]), ALU.mult)
                pD = [big.next(), big.next()]
                for hf in range(2):
                    S.mm(pD[hf], pD[hf][:], UC, UC[:], rda, rda[:, hf * 4:(hf + 1) * 4, :].rearrange("p a b -> p (a b)"))
                pc = sm.next()
                S.mm(pc, pc[:, 0:8], U, U[:], da, da[:])
                pc2 = sm.next()
                S.mm(pc2, pc2[:, 0:8], C["ones"], C["ones"][:], da, da[:])
                csb = cs_r.next()
                S.copy("dve", csb, csb[:, 0:8], pc, pc[:, 0:8])
                S.copy("dve", csb, csb[:, 8:16], pc2, pc2[:, 0:8])
                ec = ec_r.next()
                S.act(ec, ec[:, 0:16], csb, csb[:, 0:16], AF.Exp)
                S.tt("dve", csb, csb[:, 0:8], csb, csb[:, 8:16], csb, csb[:, 0:8], ALU.subtract)
                S.act(ec, ec[:, 16:24], csb, csb[:, 0:8], AF.Exp)
                eL = eL_r.next()
                for hf in range(2):
                    S.act(eL, eL[:, hf * 4:(hf + 1) * 4, :].rearrange("p a b -> p (a b)"), pD[hf], pD[hf][:], AF.Exp)
                pG = sm.next()
                S.mm(pG, pG[:], BT, BT[:, cs], CT, CT[:, cs])
                GU = GU_r.next()
                S.tt("dve", GU, GU[:], pG, pG[:], U, U[:], ALU.mult)
                M = M_r.next()
                S.tt("pool", M, M[:], eL, eL[:], GU, GU[:].unsqueeze(1).to_broadcast([128, 8, 128]), ALU.mult)
                xdtd = xdtd_r.next()
                S.tt("pool", xdtd, v8(xdtd[:]), xdt, v8(xdt[:]), ec, b8(ec[:, 16:24]), ALU.mult)
                pyd = big.next()
                for hh in range(8):
                    S.mm(pyd, pyd[:, hh * 64:(hh + 1) * 64], M, M[:, hh, :], xdt, xdt[:, hh * 64:(hh + 1) * 64])
                pyo = big.next()
                S.mm(pyo, pyo[:], CT, CT[:, cs], Sbf, Sbf[:])
                pst = big.next()
                S.mm(pst, pst[:], Btok, Btok[:], xdtd, xdtd[:])
                S.tt("dve", Sst, v8(Sst[:]), Sst, v8(Sst[:]), ec, b8(ec[:, 8:16]), ALU.mult)
                S.tt("dve", Sst, Sst[:], Sst, Sst[:], pst, pst[:], ALU.add)
                S.copy("act", Sbf, Sbf[:], Sst, Sst[:])
                y1 = y1_r.next()
                S.tt("dve", y1, v8(y1[:]), pyo, v8(pyo[:]), ec, b8(ec[:, 0:8]), ALU.mult)
                S.tt("dve", y1, y1[:], y1, y1[:], pyd, pyd[:], ALU.add)
                y2 = y2_r.next()
                S.tt("pool", y2, y2[:], y1, y1[:], xD, xD[:], ALU.add)
                S.tt("pool", y2, y2[:], y2, y2[:], sz, sz[:], ALU.mult)
                ss = ss_r.next()
                S.act(junk, junk[:], y2, y2[:], AF.Square, accum=ss[:, 0:1], wr=[ss])
                S.act(ss, ss[:, 1:2], ss, ss[:, 0:1], AF.Sqrt, bias=eps[:, 0:1], scale=1.0 / 512, rd=[eps])
                S.recip(ss, ss[:, 0:1], ss, ss[:, 1:2])
                yn = yn_r.next()
                S.stt(yn, yn[:], y2, y2[:], ss[:, 0:1], ngs, ngs[:], ALU.mult, ALU.mult, rd=[ss])
                pT = big.next()
                for ch in range(4):
                    S.mm(pT, pT[:, ch * 128:(ch + 1) * 128], yn, yn[:, ch * 128:(ch + 1) * 128], idb, idb[:])
                S.copy("act", yo, yo[:, :, cs], pT, pT[:].rearrange("p (c n) -> p c n", c=4))
            A["store_y"](S, yo, t)


def ssm_inputs(inp, l, g, hT_b, cst):
    w = inp["w_in"][l]
    xcols = 5136 + np.arange(g * 512, (g + 1) * 512)
    bcols = 7184 + np.arange(g * 128, (g + 1) * 128)
    ccols = 7696 + np.arange(g * 128, (g + 1) * 128)
    cols = np.concatenate([3088 + np.arange(g * 512, (g + 1) * 512), xcols, bcols, ccols, 8208 + np.arange(g * 8, (g + 1) * 8)])
    cc = np.concatenate([xcols, bcols, ccols]) - 5136
    cwv = inp["ssm_conv_w"][l][:, cc]
    cw = np.ascontiguousarray(cwv.reshape(4, 6, 128).transpose(2, 1, 0))
    cb = np.ascontiguousarray(inp["ssm_conv_b"][l][cc].reshape(6, 128).T)
    hs = slice(g * 8, (g + 1) * 8)
    return {"hT": hT_b, "w_ssm": np.ascontiguousarray(w[:, cols]), "cw": cw, "cb": cb,
            "dtb": np.ascontiguousarray(inp["ssm_dt_bias"][l][hs].reshape(1, 8)),
            "alog": np.ascontiguousarray(inp["ssm_a_log"][l][hs].reshape(1, 8)),
            "dsk": np.ascontiguousarray(inp["ssm_d"][l][hs].reshape(1, 8)),
            "ngs": np.ascontiguousarray(inp["ssm_norm_g"][l][g * 512:(g + 1) * 512].reshape(1, 512)),
            "cst": cst}


C1_2PI = 6.28125
C2_2PI = 2.0 * math.pi - 6.28125


def build_diff(l, ntiles=SEQ // TT):
    nc = bass.Bass("TRN2", target_bir_lowering=False)
    hT = dram(nc, "hT", [DM, SEQ], BF16, "ExternalInput")
    A = {}
    A["w_diff"] = dram(nc, "w_diff", [DM, 1280], F32, "ExternalInput")
    A["pos"] = dram(nc, "pos", [1, SEQ], I32, "ExternalInput")
    A["invf"] = dram(nc, "invf", [128, 2], F32, "ExternalInput")
    A["lqk"] = dram(nc, "lqk", [4, 64], F32, "ExternalInput")
    A["ngd"] = dram(nc, "ngd", [1, 128], F32, "ExternalInput")
    A["cst"] = dram(nc, "cst", [4, 128, 128], F32, "ExternalInput")
    yT = dram(nc, "yT", [256, SEQ], BF16, "ExternalOutput")
    A["load_h"] = lambda S, h, t: S.dma("sp", h[:], kcp(hT[:, t * TT:(t + 1) * TT]), writes=[h])
    A["store_y"] = lambda S, yo, t: S.dma("sp", yT[:, t * TT:(t + 1) * TT].rearrange("(c p) n -> p c n", p=128), yo[:], reads=[yo])
    with ExitStack() as st:
        S = Sched(nc, st)
        S.begin_phase()
        emit_diff(nc, S, A, l, ntiles)
        S.end_phase()
    return nc


def emit_diff(nc, S, A, l, ntiles=SEQ // TT):
    lambda_init = 0.8 - 0.6 * math.exp(-0.3 * l)
    w_diff, posd, invfp, lqk, ngp, cst = (A[k] for k in ("w_diff", "pos", "invf", "lqk", "ngd", "cst"))
    if True:
        C = load_consts(S, nc, cst, extra_eps=(1e-5,))
        idb = S.sb("idb", [128, 128], BF16)
        S.dma("pool", idb[:], cst[3], writes=[idb])
        wd = S.sb("wd", [128, 8, 1280], BF16)
        S.dma("pool", wd[:], kcp(w_diff), writes=[wd])
        invf = S.sb("invf", [128, 2], F32)
        S.dma("sp", invf[:], invfp, writes=[invf])
        ngd = S.sb("ngd", [128, 128], F32)
        S.dma("sp", ngd[:], ngp.partition_broadcast(128), writes=[ngd])
        S.ts("dve", ngd, ngd[:], ngd, ngd[:], 1.0 - lambda_init, None, ALU.mult)
        lq = S.sb("lq", [128, 4, 64], F32)
        for i in range(4):
            S.dma("sp", lq[:, i, :], lqk[i:i + 1, :].partition_broadcast(128), writes=[lq], group=(i > 0))
        lt = S.sb("lt", [128, 2, 64], F32)
        S.tt("dve", lt, lt[:, 0, :], lq, lq[:, 0, :], lq, lq[:, 1, :], ALU.mult)
        S.tt("dve", lt, lt[:, 1, :], lq, lq[:, 2, :], lq, lq[:, 3, :], ALU.mult)
        ls = S.sb("ls", [128, 4], F32)
        S.op("dve", lambda E: E.reduce_sum(out=ls[:, 0:2], in_=lt[:], axis=AX.X), reads=[lt], writes=[ls])
        S.act(ls, ls[:, 0:2], ls, ls[:, 0:2], AF.Exp)
        S.tt("dve", ls, ls[:, 2:3], ls, ls[:, 1:2], ls, ls[:, 0:1], ALU.subtract)
        S.ts("dve", ls, ls[:, 3:4], ls, ls[:, 2:3], -lambda_init, None, ALU.add)
        nlam = ls

        def ring(name, shape, dt, n=2):
            return Ring([S.sb("%s%d" % (name, i), shape, dt) for i in range(n)])
        h_ring = ring("h", [128, 8, TT], BF16)
        KT = S.sb("KT", [128, 2, SEQ], BF16)
        VA = S.sb("VA", [128, 2, SEQ // 128, 129], BF16)
        S.memset("pool", VA, VA[:, :, :, 128:129], 1.0)
        QT_r = ring("QT", [128, 2, TT], BF16)
        big = Ring([S.ps("pb%d" % i, [128, 512], F32) for i in range(3)])
        pob = [[S.ps("po%d_%d" % (s, hf), [128, 512], F32) for hf in range(2)] for s in range(2)]
        sm = Ring(small_views(S, "sm", 1, 128))
        posi = S.sb("posi", [128, TT], I32)
        ang = S.sb("ang", [128, TT], F32)
        ang2 = S.sb("ang2", [128, TT], F32)
        ki = S.sb("ki", [128, TT], I32)
        kf = S.sb("kf", [128, TT], F32)
        yr = S.sb("yr", [128, TT], F32)
        Cs = S.sb("Cs", [128, TT], F32)
        Sn = S.sb("Sn", [128, TT], F32)
        ta_r = ring("ta", [128, TT], F32)
        tb_r = ring("tb", [128, TT], F32)
        PT_r = ring("PT", [128, TT], BF16, 4)
        r_r = ring("r", [128, 4], F32)
        oa_r = ring("oa", [128, 128], F32)
        junk = S.sb("junk", [128, 128], F32)
        yn_r = ring("yn", [128, 128], BF16)
        yo_r = ring("yo", [128, 2, TT], BF16)
        eps5 = C["eps%g" % 1e-5]

        def reduce_sin(dst, src):
            S.ts("dve", ki, ki[:], src, src[:], 1.0 / (2.0 * math.pi), None, ALU.mult)
            S.copy("dve", kf, kf[:], ki, ki[:])
            S.stt(yr, yr[:], kf, kf[:], -C1_2PI, src, src[:], ALU.mult, ALU.add)
            S.stt(yr, yr[:], kf, kf[:], -C2_2PI, yr, yr[:], ALU.mult, ALU.add)
            S.ts("dve", yr, yr[:], yr, yr[:], -3.1415925, 3.1415925, ALU.max, ALU.min)
            S.act(dst, dst[:], yr, yr[:], AF.Sin)

        for t in range(ntiles):
            ts_ = slice(t * TT, (t + 1) * TT)
            h = h_ring.next()
            A["load_h"](S, h, t)
            S.dma("sp", posi[:], posd[0:1, ts_].partition_broadcast(128), writes=[posi])
            S.copy("dve", ang, ang[:], posi, posi[:])
            S.ts("dve", ang, ang[:], ang, ang[:], invf[:, 0:1], None, ALU.mult, rd=[invf])
            reduce_sin(Sn, ang)
            S.ts("dve", Sn, Sn[:], Sn, Sn[:], invf[:, 1:2], None, ALU.mult, rd=[invf])
            S.ts("dve", ang2, ang2[:], ang, ang[:], math.pi / 2.0, None, ALU.add)
            reduce_sin(Cs, ang2)
            QT = QT_r.next()
            for hd in range(2):
                for (c0, dstb, dst) in ((hd * 128, QT, QT[:, hd, :]), (512 + hd * 128, KT, KT[:, hd, ts_])):
                    p1 = big.next()
                    for kc in range(8):
                        S.mm(p1, p1[:], wd, wd[:, kc, c0:c0 + 128], h, h[:, kc, :], start=(kc == 0), stop=(kc == 7))
                    p2 = big.next()
                    for kc in range(8):
                        S.mm(p2, p2[:], wd, wd[:, kc, c0 + 256:c0 + 384], h, h[:, kc, :], start=(kc == 0), stop=(kc == 7))
                    ta = ta_r.next()
                    tb = tb_r.next()
                    S.tt("dve", ta, ta[:], p1, p1[:], Cs, Cs[:], ALU.mult)
                    S.tt("dve", tb, tb[:], p2, p2[:], Sn, Sn[:], ALU.mult)
                    S.tt("pool", dstb, dst, ta, ta[:], tb, tb[:], ALU.add)
            for c in range(4):
                cs = slice(c * 128, (c + 1) * 128)
                pv = big.next()
                for kc in range(8):
                    S.mm(pv, pv[:, 0:256], h, h[:, kc, cs], wd, wd[:, kc, 1024:1280], start=(kc == 0), stop=(kc == 7))
                S.copy("act", VA, VA[:, :, 4 * t + c, 0:128], pv, pv[:, 0:256].rearrange("p (a b) -> p a b", a=2))
            yo = yo_r.next()
            for hd in range(2):
                nkb = 4 * t + 4
                started = [[False, False], [False, False]]
                iters = [(kb, s_) for kb in range(nkb) for s_ in range(2)]
                pend = {}

                def emit_qk(i):
                    kb, s_ = iters[i]
                    r = kb - 4 * t
                    q0 = max(r, 0)
                    qlo = q0 * 128
                    n = TT - qlo
                    ps_ = slice(s_ * 64, (s_ + 1) * 64)
                    pS = big.next()
                    S.mm(pS, pS[:, :n], KT, KT[ps_, hd, kb * 128:(kb + 1) * 128], QT, QT[ps_, hd, qlo:TT])
                    PT = PT_r.next()
                    S.act(PT, PT[:, :n], pS, pS[:, :n], AF.Exp, scale=0.125)
                    if r >= 0:
                        S.memset("pool", PT, PT[64:128, 0:64], 0.0)
                    pend[i] = (PT, q0, qlo)

                def emit_pv(i):
                    kb, s_ = iters[i]
                    PT, q0, qlo = pend.pop(i)
                    for qb in range(q0, 4):
                        col = qb * 128 - qlo
                        bank = pob[s_][qb // 2]
                        o = bank[:, (qb % 2) * 129:(qb % 2) * 129 + 129]
                        first = not started[s_][qb // 2]
                        started[s_][qb // 2] = True
                        S.mm(bank, o, PT, PT[:, col:col + 128], VA, VA[:, hd, kb, :], start=first, stop=(kb == 4 * t + qb and qb % 2 == 1))

                LOOK = 2
                for i in range(len(iters) + LOOK):
                    if i < len(iters):
                        emit_qk(i)
                    if i >= LOOK:
                        emit_pv(i - LOOK)
                for qb in range(4):
                    o1 = pob[0][qb // 2]
                    o2 = pob[1][qb // 2]
                    b0 = (qb % 2) * 129
                    rr = r_r.next()
                    S.recip(rr, rr[:, 0:1], o1, o1[:, b0 + 128:b0 + 129])
                    S.recip(rr, rr[:, 1:2], o2, o2[:, b0 + 128:b0 + 129])
                    S.tt("dve", rr, rr[:, 2:3], rr, rr[:, 1:2], nlam, nlam[:, 3:4], ALU.mult)
                    oa = oa_r.next()
                    S.ts("dve", oa, oa[:], o1, o1[:, b0:b0 + 128], rr[:, 0:1], None, ALU.mult, rd=[rr])
                    S.stt(oa, oa[:], o2, o2[:, b0:b0 + 128], rr[:, 2:3], oa, oa[:], ALU.mult, ALU.add, rd=[rr])
                    S.act(junk, junk[:], oa, oa[:], AF.Square, accum=rr[:, 3:4], wr=[rr])
                    S.act(rr, rr[:, 1:2], rr, rr[:, 3:4], AF.Sqrt, bias=eps5[:, 0:1], scale=1.0 / 128, rd=[eps5])
                    S.recip(rr, rr[:, 0:1], rr, rr[:, 1:2])
                    yn = yn_r.next()
                    S.stt(yn, yn[:], oa, oa[:], rr[:, 0:1], ngd, ngd[:], ALU.mult, ALU.mult, rd=[rr])
                    pT = sm.next()
                    S.mm(pT, pT[:], yn, yn[:], idb, idb[:])
                    S.copy("act", yo, yo[:, hd, qb * 128:(qb + 1) * 128], pT, pT[:])
            A["store_y"](S, yo, t)


def diff_inputs(inp, l, g, b, hT_b, cst):
    w = inp["w_in"][l]
    qc = 8240 + np.arange(g * 256, (g + 1) * 256)
    kc = 9264 + np.arange(g * 256, (g + 1) * 256)
    vc = 10288 + np.arange(g * 256, (g + 1) * 256)
    d = np.arange(256) % 64
    partner = np.arange(256) + np.where(d < 8, 8, np.where(d < 16, -8, 0))
    cols = np.concatenate([qc, qc[partner], kc, kc[partner], vc])
    p = np.arange(128) % 64
    invf = np.zeros((128, 2), np.float32)
    fr = (500000.0 ** (-np.arange(0, 16, 2, dtype=np.float32) / np.float32(16))).astype(np.float32)
    invf[:, 0] = np.where(p < 16, fr[p % 8], 0.0)
    invf[:, 1] = np.where(p < 8, -1.0, np.where(p < 16, 1.0, 0.0))
    lqk = np.stack([inp["diff_lq1"][l], inp["diff_lk1"][l], inp["diff_lq2"][l], inp["diff_lk2"][l]]).astype(np.float32)
    return {"hT": hT_b, "w_diff": np.ascontiguousarray(w[:, cols]),
            "pos": np.ascontiguousarray(inp["positions"][b].reshape(1, SEQ)), "invf": invf, "lqk": lqk,
            "ngd": np.ascontiguousarray(inp["diff_norm_g"][l].reshape(1, 128)), "cst": cst}


_PROGS = {}


def _prog(key, fn):
    if key not in _PROGS:
        _PROGS[key] = fn()
    return _PROGS[key]


def _run(nc, maps):
    return run_bass_kernel_spmd(nc, maps, core_ids=list(range(NCORES))).results


def tok_inputs(inp, l, xT_c, hT_c, yT_c, g_next, cst):
    w = inp["w_in"][l]
    return {"xT": xT_c, "g_next": pvec(g_next), "cst": cst, "hT": hT_c, "yT": yT_c,
            "w_gate": np.ascontiguousarray(w[:, 11312:14384]), "b_gate": pvec(inp["b_gate"][l]),
            "w_br": np.ascontiguousarray(np.concatenate([inp["w_br_gla"][l], inp["w_br_ssm"][l], inp["w_br_diff"][l]], axis=0)),
            "w_out": np.ascontiguousarray(inp["w_out"][l]), "w_up": np.ascontiguousarray(inp["w_mlp_up"][l]),
            "w_dn": np.ascontiguousarray(inp["w_mlp_down"][l]), "g_mlp": pvec(inp["norm_mlp_g"][l])}


def kernel_unfused(**inp):
    inp = {k: np.asarray(v) for k, v in inp.items()}
    cst = make_consts()
    x = inp["x"]
    xT = [np.ascontiguousarray(x[c // 4, (c % 4) * NT:(c % 4 + 1) * NT, :].T) for c in range(NCORES)]
    res = _run(_prog("first", lambda: build_tok("first")),
               [{"xT": xT[c], "g_next": pvec(inp["norm_mix_g"][0]), "cst": cst} for c in range(NCORES)])
    hT = [r["hT_out"] for r in res]
    hTl = [r["hT_lo"] for r in res]
    out = None
    for l in range(DEPTH):
        hTb = [np.ascontiguousarray(np.concatenate(hT[b * 4:(b + 1) * 4], axis=1)) for b in range(B)]
        hTlb = [np.ascontiguousarray(np.concatenate(hTl[b * 4:(b + 1) * 4], axis=1)) for b in range(B)]
        yg = _run(_prog("gla", build_gla), [gla_inputs(inp, l, c % 4, hTb[c // 4], cst, hTlb[c // 4]) for c in range(NCORES)])
        ys = _run(_prog("ssm", build_ssm), [ssm_inputs(inp, l, c % 4, hTb[c // 4], cst) for c in range(NCORES)])
        yd = _run(_prog(("diff", l), lambda: build_diff(l)), [diff_inputs(inp, l, c % 4, c // 4, hTb[c // 4], cst) for c in range(NCORES)])
        yTb = []
        for b in range(B):
            parts = [yg[b * 4 + g]["yT"] for g in range(4)] + [ys[b * 4 + g]["yT"] for g in range(4)] + [yd[b * 4 + g]["yT"] for g in range(4)]
            yTb.append(np.concatenate(parts, axis=0))
        last = (l == DEPTH - 1)
        g_next = inp["norm_final_g"] if last else inp["norm_mix_g"][l + 1]
        maps = [tok_inputs(inp, l, xT[c], hT[c], np.ascontiguousarray(yTb[c // 4][:, (c % 4) * NT:(c % 4 + 1) * NT]), g_next, cst)
                for c in range(NCORES)]
        mode = "last" if last else "mid"
        res = _run(_prog(mode, lambda: build_tok(mode)), maps)
        if last:
            out = np.empty((B, SEQ, DM), np.float32)
            for c in range(NCORES):
                out[c // 4, (c % 4) * NT:(c % 4 + 1) * NT, :] = res[c]["outT"].T
        else:
            xT = [r["xT_out"] for r in res]
            hT = [r["hT_out"] for r in res]
            hTl = [r["hT_lo"] for r in res]
    return out


GROUPS = [[0, 1, 2, 3], [4, 5, 6, 7]]


def build_fused(skip=()):
    nc = bass.Bass("TRN2", target_bir_lowering=False)
    ext = lambda name, shape, dt=F32: dram(nc, name, shape, dt, "ExternalInput")
    xT = ext("xT", [DM, NT])
    pos = ext("pos", [1, SEQ], I32)
    cst = ext("cst", [4, 128, 128])
    invf = ext("invf", [128, 2])
    g_mix = ext("g_mix", [DEPTH, 128, 8])
    g_fin = ext("g_fin", [128, 8])
    g_mlp = ext("g_mlp", [DEPTH, 128, 8])
    w_gate = ext("w_gate", [DEPTH, DM, 3072])
    b_gate = ext("b_gate", [DEPTH, 128, 24])
    w_br = ext("w_br", [DEPTH, 4096, DM])
    w_out = ext("w_out", [DEPTH, DM, DM])
    w_up = ext("w_up", [DEPTH, DM, 4096])
    w_dn = ext("w_dn", [DEPTH, 4096, DM])
    w_gla = ext("w_gla", [DEPTH, DM, 784])
    wgk2 = ext("wgk2", [DEPTH, 16, 128])
    bgk = ext("bgk", [DEPTH, 1, 128])
    ng = ext("ng", [DEPTH, 128, 2])
    w_ssm = ext("w_ssm", [DEPTH, DM, 1288])
    cw = ext("cw", [DEPTH, 128, 6, 4])
    cb = ext("cb", [DEPTH, 128, 6])
    dtb = ext("dtb", [DEPTH, 1, 8])
    alog = ext("alog", [DEPTH, 1, 8])
    dsk = ext("dsk", [DEPTH, 1, 8])
    ngs = ext("ngs", [DEPTH, 1, 512])
    w_diff = ext("w_diff", [DEPTH, DM, 1280])
    lqk = ext("lqk", [DEPTH, 4, 64])
    ngd = ext("ngd", [DEPTH, 1, 128])
    outT = dram(nc, "outT", [DM, NT], F32, "ExternalOutput")
    xs = nc.dram_tensor("xs_i", [DM, NT], F32).ap()
    hsrc = nc.dram_tensor("hsrc_i", [8, 256, NT], BF16).ap()
    hgat = nc.dram_tensor("hgat_i", [8, 1024, NT], BF16).ap()
    ysrc = nc.dram_tensor("ysrc_i", [4, 4, 256, NT], BF16).ap()
    ygat = nc.dram_tensor("ygat_i", [4, 4, 1024, NT], BF16).ap()
    hT_own = hsrc[0:4].rearrange("c r n -> (c r) n")
    hTlo_own = hsrc[4:8].rearrange("c r n -> (c r) n")
    ygat3 = ygat.rearrange("q a r n -> q (a r) n")

    with ExitStack() as st:
        S = Sched(nc, st)
        hsrc_b = Buf("hsrc_b")
        hgat_b = [Buf("hgat_b%d" % i) for i in range(8)]
        ysrc_b = [[Buf("ysrc_b%d_%d" % (q, a)) for a in range(4)] for q in range(4)]
        ygat_b = Buf("ygat_b")
        xs_b = Buf("xs_b")
        S.global_bufs += [hsrc_b, ygat_b, xs_b] + hgat_b + [b for row in ysrc_b for b in row]

        def gather_h():
            for i in range(8):
                S.collective("AllGather", hsrc_b, hsrc[i], hgat_b[i], hgat[i], GROUPS)

        def load_h_from(chunks):
            def f(S_, h, t):
                r, tl = t // (NT // TT), (t % (NT // TT)) * TT
                for j, i in enumerate(chunks):
                    S_.dma("sp", h[:, 2 * j:2 * j + 2, :], hgat[i][r * 256:(r + 1) * 256, tl:tl + TT].rearrange("(c p) n -> p c n", p=128),
                           reads=[hgat_b[i]], writes=[h], group=(j > 0))
            return f

        def store_y_parts(parts):
            def f(S_, yo, t):
                q, tl = t // (NT // TT), (t % (NT // TT)) * TT
                for j, a in enumerate(parts):
                    S_.dma("sp", ysrc[q, a][:, tl:tl + TT].rearrange("(c p) n -> p c n", p=128), yo[:, 2 * j:2 * j + 2, :],
                           reads=[yo], writes=[ysrc_b[q][a]], sembuf=yo, group=True)
                if t % (NT // TT) == NT // TT - 1:
                    for a in parts:
                        S_.collective("AllGather", ysrc_b[q][a], ysrc[q, a], ygat_b, ygat[q, a], GROUPS)
            return f

        qcache = {}

        def load_y(S_, yub, t):
            tl = t * TT
            ph = S_.phase_id

            def qval(E):
                if ph not in qcache:
                    qcache[ph] = E.snap(E.partition_id() % 4)
                return qcache[ph]
            src = (lambda E, tl=tl: ygat3[bass.ds(qval(E), 1), :, tl:tl + TT].rearrange("o (k p) n -> p (o k) n", p=128))
            S_.dma("sp", yub[:], src, reads=[ygat_b], writes=[yub])

        S.begin_phase()
        if "first" not in skip:
            emit_tok(nc, S, {"xT": xT, "g_next": g_mix[0], "cst": cst, "hT_out": hT_own, "hT_lo": hTlo_own, "h_wr": [hsrc_b]}, "first")
        if "gather" not in skip:
            gather_h()
        S.end_phase()
        for l in range(DEPTH):
            S.begin_phase()
            if "gla" not in skip:
              emit_gla(nc, S, {"w_gla": w_gla[l], "wgk2": wgk2[l], "bgk": bgk[l], "ng": ng[l], "cst": cst,
                             "load_h": load_h_from((0, 1, 2, 3)), "load_hlo": load_h_from((4, 5, 6, 7)), "store_y": store_y_parts((0,))}, SEQ // TT)
            S.end_phase()
            S.begin_phase()
            if "ssm" not in skip:
              emit_ssm(nc, S, {"w_ssm": w_ssm[l], "cw": cw[l], "cb": cb[l], "dtb": dtb[l], "alog": alog[l], "dsk": dsk[l], "ngs": ngs[l],
                             "cst": cst, "load_h": load_h_from((0, 1, 2, 3)), "store_y": store_y_parts((1, 2))}, SEQ // TT)
            S.end_phase()
            S.begin_phase()
            if "diff" not in skip:
              emit_diff(nc, S, {"w_diff": w_diff[l], "pos": pos, "invf": invf, "lqk": lqk[l], "ngd": ngd[l], "cst": cst,
                              "load_h": load_h_from((0, 1, 2, 3)), "store_y": store_y_parts((3,))}, l, SEQ // TT)
            S.end_phase()
            last = (l == DEPTH - 1)
            A = {"xT": (xT if l == 0 else xs), "g_next": (g_fin if last else g_mix[l + 1]), "cst": cst, "hT": hT_own, "load_y": load_y,
                 "w_gate": w_gate[l], "b_gate": b_gate[l], "w_br": w_br[l], "w_out": w_out[l], "w_up": w_up[l], "w_dn": w_dn[l], "g_mlp": g_mlp[l]}
            if last:
                A["outT"] = outT
            else:
                A.update({"hT_out": hT_own, "hT_lo": hTlo_own, "h_wr": [hsrc_b], "xT_out": xs})
            S.begin_phase()
            emit_tok(nc, S, A, "last" if last else "mid")
            if not last:
                gather_h()
            S.end_phase()
    return nc


def fused_y_row_order():
    rows = []
    for a in range(4):
        for r in range(4):
            j = np.arange(256)
            if a == 0:
                rows.append(r * 256 + j)
            elif a in (1, 2):
                rows.append(1024 + r * 512 + (a - 1) * 256 + j)
            else:
                rows.append(3072 + r * 256 + j)
    return np.concatenate(rows)


def fused_inputs(inp, c, cst):
    b, g = c // 4, c % 4
    x = inp["x"]
    m = {"xT": np.ascontiguousarray(x[b, g * NT:(g + 1) * NT, :].T), "cst": cst,
         "pos": np.ascontiguousarray(inp["positions"][b].reshape(1, SEQ)),
         "g_mix": np.stack([pvec(inp["norm_mix_g"][l]) for l in range(DEPTH)]), "g_fin": pvec(inp["norm_final_g"]),
         "g_mlp": np.stack([pvec(inp["norm_mlp_g"][l]) for l in range(DEPTH)]),
         "w_gate": np.ascontiguousarray(inp["w_in"][:, :, 11312:14384]),
         "b_gate": np.stack([pvec(inp["b_gate"][l]) for l in range(DEPTH)]),
         "w_br": np.ascontiguousarray(np.concatenate([inp["w_br_gla"], inp["w_br_ssm"], inp["w_br_diff"]], axis=1)[:, fused_y_row_order(), :]),
         "w_out": np.ascontiguousarray(inp["w_out"]), "w_up": np.ascontiguousarray(inp["w_mlp_up"]), "w_dn": np.ascontiguousarray(inp["w_mlp_down"])}
    per = {}
    for l in range(DEPTH):
        d = {}
        d.update(gla_inputs(inp, l, g, None, cst, 0))
        d.update(ssm_inputs(inp, l, g, None, cst))
        d.update(diff_inputs(inp, l, g, b, None, cst))
        for k, v in d.items():
            if k in ("hT", "hTlo", "cst", "pos", "invf"):
                continue
            per.setdefault(k, []).append(v)
        if l == 0:
            m["invf"] = d["invf"]
    for k, v in per.items():
        m[k] = np.ascontiguousarray(np.stack(v))
    return m


_FUSED = []


def kernel(**inp):
    inp = {k: np.asarray(v) for k, v in inp.items()}
    cst = make_consts()
    if not _FUSED:
        _FUSED.append(build_fused())
    maps = [fused_inputs(inp, c, cst) for c in range(NCORES)]
    res = run_bass_kernel_spmd(_FUSED[0], maps, core_ids=list(range(NCORES))).results
    out = np.empty((B, SEQ, DM), np.float32)
    for c in range(NCORES):
        out[c // 4, (c % 4) * NT:(c % 4 + 1) * NT, :] = res[c]["outT"].T
    return out
```

```python
import math
from contextlib import ExitStack
import numpy as np
import ml_dtypes
import concourse.bass as bass
import concourse.mybir as mybir
from concourse.bass_utils import run_bass_kernel_spmd

F32 = mybir.dt.float32
BF16 = mybir.dt.bfloat16
I32 = mybir.dt.int32
AF = mybir.ActivationFunctionType
ALU = mybir.AluOpType
AX = mybir.AxisListType

NCORES = 8
B, SEQ, DM, DEPTH = 2, 8192, 1024, 2
NT = 2048
TT = 512
EPS = 1e-6
ENGS = ("pe", "act", "dve", "pool", "sp")


class Buf:
    __slots__ = ("name", "t", "w", "r", "dsem", "dcnt")

    def __init__(self, name, t=None):
        self.name = name
        self.t = t
        self.w = None
        self.r = {}
        self.dsem = None
        self.dcnt = 0

    def __getitem__(self, idx):
        return self.t[idx]


class VBuf:
    def __init__(self, name, parent, ap):
        self.name = name
        self.parent = parent
        self.t = ap

    def __getitem__(self, idx):
        return self.t[idx]

    w = property(lambda s: s.parent.w, lambda s, v: setattr(s.parent, "w", v))
    r = property(lambda s: s.parent.r, lambda s, v: setattr(s.parent, "r", v))
    dsem = property(lambda s: s.parent.dsem, lambda s, v: setattr(s.parent, "dsem", v))
    dcnt = property(lambda s: s.parent.dcnt, lambda s, v: setattr(s.parent, "dcnt", v))


class Sched:
    def __init__(self, nc, stack):
        self.nc = nc
        self.stack = stack
        self.q = {e: [] for e in ENGS}
        self.sem = {}
        for e in ENGS:
            self.sem[e] = stack.enter_context(nc.semaphore("s_" + e))
        self.cnt = {e: 0 for e in ENGS}
        self.waited = {e: {} for e in ENGS}
        self.ndsem = 0
        self.final_waits = {}
        self.alloc_stack = stack
        self.nname = 0
        self.free_dsems = {}
        self.dsem_cls = {}
        self.phase_bufs = []
        self.global_bufs = []

    def sb(self, name, shape, dt):
        self.nname += 1
        t = self.alloc_stack.enter_context(self.nc.sbuf_tensor("sb%d_%s" % (self.nname, name), list(shape), dt))
        b = Buf(name, t)
        self.phase_bufs.append(b)
        return b

    def ps(self, name, shape, dt=F32):
        self.nname += 1
        t = self.alloc_stack.enter_context(self.nc.psum_tensor("ps%d_%s" % (self.nname, name), list(shape), dt))
        return Buf(name, t)

    def view(self, name, ap):
        return Buf(name, ap)

    def _dsem(self, b, eng="sp"):
        cls = {"pool": "sw", "cc": "cc"}.get(eng, "hw")
        if b.dsem is not None:
            assert self.dsem_cls[b.dsem] == cls, (b.name, cls)
        if b.dsem is None:
            pool = self.free_dsems.setdefault(cls, [])
            if pool:
                b.dsem, b.dcnt = pool.pop()
            else:
                b.dsem = "d%d" % self.ndsem
                self.ndsem += 1
                self.sem[b.dsem] = self.stack.enter_context(self.nc.semaphore(b.dsem))
                self.dsem_cls[b.dsem] = cls
        return b.dsem

    def _deps(self, eng, reads, writes, same_ok):
        deps = {}

        def add(k, v):
            if deps.get(k, 0) < v:
                deps[k] = v

        for b in reads:
            if b.w is not None:
                add(*b.w)
        for b in writes:
            if b.w is not None:
                add(*b.w)
            for k, v in b.r.items():
                add(k, v)
        waits = []
        wd = self.waited[eng]
        for k, v in deps.items():
            if k == eng and same_ok:
                continue
            if wd.get(k, 0) >= v:
                continue
            wd[k] = v
            waits.append((k, v))
        return waits

    def _commit(self, tk, reads, writes):
        k, v = tk
        for b in writes:
            b.w = tk
            b.r = {}
        for b in reads:
            if b.r.get(k, 0) < v:
                b.r[k] = v

    def op(self, eng, fn, reads=(), writes=()):
        waits = self._deps(eng, reads, writes, same_ok=(eng == "pe"))
        self.cnt[eng] += 1
        tk = (eng, self.cnt[eng])
        sem = self.sem
        me = sem[eng]

        def emit(E):
            for k, v in waits:
                E.wait_ge(sem[k], v)
            fn(E).then_inc(me, 1)

        self.q[eng].append(emit)
        self._commit(tk, reads, writes)
        return tk

    def dma(self, eng, out_ap, in_ap, reads=(), writes=(), sembuf=None, group=False):
        sb = sembuf if sembuf is not None else (writes[0] if writes else reads[0])
        dk = self._dsem(sb, eng)
        saved = None
        if group and writes and writes[0].w is not None and writes[0].w[0] == dk:
            saved = writes[0].w
            writes[0].w = None
        waits = self._deps(eng, reads, writes, same_ok=False)
        if saved is not None:
            writes[0].w = saved
        sb.dcnt += 16
        tk = (dk, sb.dcnt)
        sem = self.sem
        ds = sem[dk]

        def emit(E):
            for k, v in waits:
                E.wait_ge(sem[k], v)
            E.dma_start(out=out_ap, in_=(in_ap(E) if callable(in_ap) else in_ap)).then_inc(ds, 16)

        self.q[eng].append(emit)
        self._commit(tk, reads, writes)
        self.final_waits[dk] = sb.dcnt
        return tk

    def collective(self, kind, sb_, s_ap, db_, d_ap, groups):
        dk = self._dsem(db_, "cc")
        waits = self._deps("pool", [sb_], [db_], same_ok=False)
        db_.dcnt += 1
        tk = (dk, db_.dcnt)
        sem = self.sem
        ds = sem[dk]

        def emit(E):
            for k, v in waits:
                E.wait_ge(sem[k], v)
            E.collective_compute(kind, ALU.bypass, replica_groups=groups, ins=[s_ap], outs=[d_ap]).then_inc(ds, 1)

        self.q["pool"].append(emit)
        self._commit(tk, [sb_], [db_])
        self.final_waits[dk] = db_.dcnt
        return tk

    def mm(self, ob, o, lb, l, rb, r, start=True, stop=True):
        return self.op("pe", lambda E: E.matmul(o, lhsT=l, rhs=r, start=start, stop=stop), reads=[lb, rb], writes=[ob])

    def act(self, ob, o, ib, i, func, bias=None, scale=None, accum=None, rd=(), wr=()):
        kw = {}
        if bias is not None:
            kw["bias"] = bias
        if scale is not None:
            kw["scale"] = scale
        if accum is not None:
            kw["accum_out"] = accum
        return self.op("act", lambda E: E.activation(out=o, in_=i, func=func, **kw), reads=[ib] + list(rd), writes=[ob] + list(wr))

    def tt(self, eng, ob, o, ab, a, bb, b, op):
        return self.op(eng, lambda E: E.tensor_tensor(out=o, in0=a, in1=b, op=op), reads=[ab, bb], writes=[ob])

    def ts(self, eng, ob, o, ab, a, s1, s2, op0, op1=None, rd=()):
        if op1 is None:
            return self.op(eng, lambda E: E.tensor_scalar(out=o, in0=a, scalar1=s1, scalar2=None, op0=op0), reads=[ab] + list(rd), writes=[ob])
        return self.op(eng, lambda E: E.tensor_scalar(out=o, in0=a, scalar1=s1, scalar2=s2, op0=op0, op1=op1), reads=[ab] + list(rd), writes=[ob])

    def stt(self, ob, o, ab, a, sc, bb, b, op0, op1, rd=()):
        return self.op("dve", lambda E: E.scalar_tensor_tensor(out=o, in0=a, scalar=sc, in1=b, op0=op0, op1=op1), reads=[ab, bb] + list(rd), writes=[ob])

    def copy(self, eng, ob, o, ib, i):
        if eng == "act":
            return self.op("act", lambda E: E.copy(out=o, in_=i), reads=[ib], writes=[ob])
        return self.op(eng, lambda E: E.tensor_copy(out=o, in_=i), reads=[ib], writes=[ob])

    def memset(self, eng, ob, o, val):
        return self.op(eng, lambda E: E.memset(o, val), writes=[ob])

    def recip(self, ob, o, ib, i):
        return self.op("dve", lambda E: E.reciprocal(out=o, in_=i), reads=[ib], writes=[ob])

    phase_id = 0

    def begin_phase(self):
        self.phase_id += 1
        self.gstack = self.stack if not hasattr(self, "gstack") else self.gstack
        self.pstack = ExitStack()
        self.pstack.__enter__()
        self.alloc_stack = self.pstack

    def end_phase(self):
        sem = self.sem
        cnt = dict(self.cnt)
        fw = dict(self.final_waits)
        for e in ENGS:
            waits = [(o, cnt[o]) for o in ENGS if o != e and cnt[o] > self.waited[e].get(o, 0)]
            waits += [(k, v) for k, v in fw.items() if v > self.waited[e].get(k, 0)]
            for k, v in waits:
                self.waited[e][k] = v

            def emit(E, waits=waits):
                for k, v in waits:
                    E.wait_ge(sem[k], v)
            self.q[e].append(emit)
        self.finish()
        self.q = {e: [] for e in ENGS}
        for e in ENGS:
            self.sem[e] = self.stack.enter_context(self.nc.semaphore("s_%s_%d" % (e, self.phase_id)))
            self.cnt[e] = 0
            for x in ENGS:
                self.waited[x].pop(e, None)
        for b in self.global_bufs:
            if b.w is not None and b.w[0] in ENGS:
                b.w = None
            b.r = {k: v for k, v in b.r.items() if k not in ENGS}
        for b in self.phase_bufs:
            if b.dsem is not None:
                self.free_dsems[self.dsem_cls[b.dsem]].append((b.dsem, b.dcnt))
                b.dsem = None
        self.phase_bufs = []
        self.pstack.__exit__(None, None, None)
        self.alloc_stack = self.stack

    def finish(self):
        nc = self.nc
        sem = self.sem
        q = self.q
        fw = dict(self.final_waits)
        with nc.Block() as block:
            @block.tensor
            def _(E):
                for f in q["pe"]:
                    f(E)

            @block.scalar
            def _(E):
                for f in q["act"]:
                    f(E)

            @block.vector
            def _(E):
                for f in q["dve"]:
                    f(E)

            @block.gpsimd
            def _(E):
                for f in q["pool"]:
                    f(E)

            @block.sync
            def _(E):
                for f in q["sp"]:
                    f(E)
                for k, v in fw.items():
                    E.wait_ge(sem[k], v)


class Ring:
    def __init__(self, bufs):
        self.bufs = bufs
        self.i = 0

    def next(self):
        b = self.bufs[self.i % len(self.bufs)]
        self.i += 1
        return b


def dram(nc, name, shape, dt, kind):
    return nc.dram_tensor(name, list(shape), dt, kind=kind).ap()


def kcp(ap):
    return ap.rearrange("(kc p) n -> p kc n", p=128)


def rms_stats(S, C, xb, nch, n, ps_ring, sq_ring, rstd, tmp, nfeat, eps):
    pss = ps_ring.next()
    for kc in range(nch):
        sq = sq_ring.next()
        S.act(sq, sq[:, :n], xb, xb[:, kc, :n], AF.Square)
        S.mm(pss, pss[:, :n], C["ones"], C["ones"][:], sq, sq[:, :n], start=(kc == 0), stop=(kc == nch - 1))
    S.act(tmp, tmp[:, :n], pss, pss[:, :n], AF.Sqrt, bias=C["eps%g" % eps][:, 0:1], scale=1.0 / nfeat, rd=[C["eps%g" % eps]])
    S.recip(rstd, rstd[:, :n], tmp, tmp[:, :n])


def load_consts(S, nc, cst_ap, extra_eps=()):
    C = {}
    C["ones"] = S.sb("c_ones", [128, 128], F32)
    S.dma("sp", C["ones"][:], cst_ap[0], writes=[C["ones"]])
    for e in (EPS,) + tuple(extra_eps):
        b = S.sb("c_eps%g" % e, [128, 1], F32)
        S.memset("dve", b, b[:], float(e))
        C["eps%g" % e] = b
    return C


def build_tok(mode):
    nc = bass.Bass("TRN2", target_bir_lowering=False)
    A = {}
    A["xT"] = dram(nc, "xT", [DM, NT], F32, "ExternalInput")
    A["g_next"] = dram(nc, "g_next", [128, 8], F32, "ExternalInput")
    A["cst"] = dram(nc, "cst", [4, 128, 128], F32, "ExternalInput")
    if mode != "first":
        A["hT"] = dram(nc, "hT", [DM, NT], BF16, "ExternalInput")
        yT = dram(nc, "yT", [4096, NT], BF16, "ExternalInput")
        A["load_y"] = lambda S, yub, t: S.dma("act", yub[:], kcp(yT[:, t * TT:(t + 1) * TT]), writes=[yub])
        A["w_gate"] = dram(nc, "w_gate", [DM, 3072], F32, "ExternalInput")
        A["b_gate"] = dram(nc, "b_gate", [128, 24], F32, "ExternalInput")
        A["w_br"] = dram(nc, "w_br", [4096, DM], F32, "ExternalInput")
        A["w_out"] = dram(nc, "w_out", [DM, DM], F32, "ExternalInput")
        A["w_up"] = dram(nc, "w_up", [DM, 4096], F32, "ExternalInput")
        A["w_dn"] = dram(nc, "w_dn", [4096, DM], F32, "ExternalInput")
        A["g_mlp"] = dram(nc, "g_mlp", [128, 8], F32, "ExternalInput")
    if mode == "last":
        A["outT"] = dram(nc, "outT", [DM, NT], F32, "ExternalOutput")
    else:
        A["hT_out"] = dram(nc, "hT_out", [DM, NT], BF16, "ExternalOutput")
        A["hT_lo"] = dram(nc, "hT_lo", [DM, NT], BF16, "ExternalOutput")
        if mode == "mid":
            A["xT_out"] = dram(nc, "xT_out", [DM, NT], F32, "ExternalOutput")
    with ExitStack() as st:
        S = Sched(nc, st)
        S.begin_phase()
        emit_tok(nc, S, A, mode)
        S.end_phase()
    return nc


def emit_tok(nc, S, A, mode):
    xT, gn, cst = A["xT"], A["g_next"], A["cst"]
    if mode != "first":
        hT, w_gate, b_gate, w_br, w_out, w_up, w_dn, gm = (A[k] for k in ("hT", "w_gate", "b_gate", "w_br", "w_out", "w_up", "w_dn", "g_mlp"))
    if mode == "last":
        outT = A["outT"]
    else:
        hTo, hTlo = A["hT_out"], A["hT_lo"]
        if mode == "mid":
            xTo = A["xT_out"]
    if True:
        C = load_consts(S, nc, cst)
        gnb = S.sb("gnb", [128, 8], F32)
        S.dma("sp", gnb[:], gn, writes=[gnb])
        xb = S.sb("xb", [128, 8, TT], F32)
        ps_ring = Ring([S.ps("ps%d" % i, [128, TT], F32) for i in range(7)])
        sq_ring = Ring([S.sb("sq%d" % i, [128, TT], F32) for i in range(2)])
        rstd = S.sb("rstd", [128, TT], F32)
        rtmp = S.sb("rtmp", [128, TT], F32)
        hn_ring = Ring([S.sb("hn%d" % i, [128, 8, TT], BF16) for i in range(2)])
        hl_ring = Ring([S.sb("hl%d" % i, [128, 8, TT], BF16) for i in range(2)])
        h32_ring = Ring([S.sb("h32_%d" % i, [128, TT], F32) for i in range(2)])
        if mode == "last":
            oc_ring = Ring([S.sb("oc%d" % i, [128, TT], F32) for i in range(3)])
        if mode != "first":
            bgb = S.sb("bgb", [128, 24], F32)
            S.dma("sp", bgb[:], b_gate, writes=[bgb])
            gmb = S.sb("gmb", [128, 8], F32)
            S.dma("sp", gmb[:], gm, writes=[gmb])
            hb_ring = Ring([S.sb("hb%d" % i, [128, 8, TT], BF16) for i in range(2)])
            yub = S.sb("yub", [128, 32, TT], BF16)
            mixed = S.sb("mixed", [128, 8, TT], BF16)
            h2 = S.sb("h2", [128, 8, TT], BF16)
            g_ring = Ring([S.sb("g%d" % i, [128, TT], F32) for i in range(6)])
            acc_ring = Ring([S.sb("acc%d" % i, [128, TT], F32) for i in range(2)])
            tmp_ring = Ring([S.sb("tmp%d" % i, [128, TT], F32) for i in range(2)])
            r_ring = Ring([S.sb("r%d" % i, [128, TT], F32) for i in range(2)])
            w_ring = Ring([S.sb("w%d" % i, [128, 7168], BF16) for i in range(3)])

        for t in range(NT // TT):
            ts_ = slice(t * TT, (t + 1) * TT)
            S.dma("sp", xb[:], kcp(xT[:, ts_]), writes=[xb])
            if mode != "first":
                hb = hb_ring.next()
                S.dma("sp", hb[:], kcp(hT[:, ts_]), writes=[hb])
                A["load_y"](S, yub, t)
                for oc in range(8):
                    w = w_ring.next()
                    wg = w[:, 0:3072].rearrange("p (kc br j) -> p kc br j", kc=8, br=3)
                    wb = w[:, 3072:7168].rearrange("p (kc j) -> p kc j", kc=32)
                    for br in range(3):
                        c0 = br * 1024 + oc * 128
                        S.dma("pool", wg[:, :, br, :], kcp(w_gate[:, c0:c0 + 128]), writes=[w], group=(br > 0))
                    S.dma("pool", wb, kcp(w_br[:, oc * 128:(oc + 1) * 128]), writes=[w], group=True)
                    gts = []
                    for br in range(3):
                        pg = ps_ring.next()
                        for kc in range(8):
                            S.mm(pg, pg[:], w, wg[:, kc, br, :], hb, hb[:, kc, :], start=(kc == 0), stop=(kc == 7))
                        g = g_ring.next()
                        ch = br * 8 + oc
                        S.act(g, g[:], pg, pg[:], AF.Sigmoid, bias=bgb[:, ch:ch + 1], rd=[bgb])
                        gts.append(g)
                    acc = acc_ring.next()
                    koff = (0, 8, 24)
                    nk = (8, 16, 8)
                    for br in range(3):
                        pb = ps_ring.next()
                        for kc in range(nk[br]):
                            S.mm(pb, pb[:], w, wb[:, koff[br] + kc, :], yub, yub[:, koff[br] + kc, :], start=(kc == 0), stop=(kc == nk[br] - 1))
                        if br == 0:
                            S.tt("dve", acc, acc[:], pb, pb[:], gts[0], gts[0][:], ALU.mult)
                        else:
                            tmp = tmp_ring.next()
                            S.tt("dve", tmp, tmp[:], pb, pb[:], gts[br], gts[br][:], ALU.mult)
                            if br == 1:
                                S.tt("dve", acc, acc[:], acc, acc[:], tmp, tmp[:], ALU.add)
                            else:
                                S.tt("dve", mixed, mixed[:, oc, :], acc, acc[:], tmp, tmp[:], ALU.add)
                for half in range(2):
                    w = w_ring.next()
                    wo = w[:, 0:4096].rearrange("p (kc j) -> p kc j", kc=8)
                    S.dma("pool", wo, kcp(w_out[:, half * 512:(half + 1) * 512]), writes=[w])
                    for o4 in range(4):
                        oc = half * 4 + o4
                        po = ps_ring.next()
                        for kc in range(8):
                            S.mm(po, po[:], w, wo[:, kc, o4 * 128:(o4 + 1) * 128], mixed, mixed[:, kc, :], start=(kc == 0), stop=(kc == 7))
                        S.tt("dve", xb, xb[:, oc, :], xb, xb[:, oc, :], po, po[:], ALU.add)
                rms_stats(S, C, xb, 8, TT, ps_ring, sq_ring, rstd, rtmp, DM, EPS)
                for kc in range(8):
                    S.stt(h2, h2[:, kc, :], xb, xb[:, kc, :], gmb[:, kc:kc + 1], rstd, rstd[:], ALU.mult, ALU.mult, rd=[gmb])
                for o8 in range(8):
                    w = w_ring.next()
                    wu = w[:, 0:4096].rearrange("p (kc j) -> p kc j", kc=8)
                    S.dma("pool", wu, kcp(w_up[:, o8 * 512:(o8 + 1) * 512]), writes=[w])
                    for o4 in range(4):
                        oc = o8 * 4 + o4
                        pu = ps_ring.next()
                        for kc in range(8):
                            S.mm(pu, pu[:], w, wu[:, kc, o4 * 128:(o4 + 1) * 128], h2, h2[:, kc, :], start=(kc == 0), stop=(kc == 7))
                        r = r_ring.next()
                        S.act(r, r[:], pu, pu[:], AF.Relu)
                        S.tt("pool", yub, yub[:, oc, :], r, r[:], r, r[:], ALU.mult)
                for oc in range(8):
                    w = w_ring.next()
                    wd = w[:, 0:4096].rearrange("p (kc j) -> p kc j", kc=32)
                    S.dma("pool", wd, kcp(w_dn[:, oc * 128:(oc + 1) * 128]), writes=[w])
                    pd = ps_ring.next()
                    for kc in range(32):
                        S.mm(pd, pd[:], w, wd[:, kc, :], yub, yub[:, kc, :], start=(kc == 0), stop=(kc == 31))
                    S.tt("dve", xb, xb[:, oc, :], xb, xb[:, oc, :], pd, pd[:], ALU.add)
                if mode == "mid":
                    S.dma("sp", kcp(xTo[:, ts_]), xb[:], reads=[xb])
            rms_stats(S, C, xb, 8, TT, ps_ring, sq_ring, rstd, rtmp, DM, EPS)
            if mode == "last":
                for kc in range(8):
                    o = oc_ring.next()
                    S.stt(o, o[:], xb, xb[:, kc, :], gnb[:, kc:kc + 1], rstd, rstd[:], ALU.mult, ALU.mult, rd=[gnb])
                    S.dma("sp", outT[kc * 128:(kc + 1) * 128, ts_], o[:], reads=[o])
            else:
                hn = hn_ring.next()
                hl = hl_ring.next()
                for kc in range(8):
                    h32 = h32_ring.next()
                    S.stt(h32, h32[:], xb, xb[:, kc, :], gnb[:, kc:kc + 1], rstd, rstd[:], ALU.mult, ALU.mult, rd=[gnb])
                    S.copy("act", hn, hn[:, kc, :], h32, h32[:])
                    S.tt("pool", hl, hl[:, kc, :], h32, h32[:], hn, hn[:, kc, :], ALU.subtract)
                S.dma("sp", kcp(hTo[:, ts_]), hn[:], reads=[hn], writes=A.get("h_wr", []), sembuf=hn)
                S.dma("sp", kcp(hTlo[:, ts_]), hl[:], reads=[hl], writes=A.get("h_wr", []), sembuf=hl)


def make_consts():
    c = np.zeros((4, 128, 128), np.float32)
    c[0] = 1.0
    c[1] = np.triu(np.ones((128, 128), np.float32))
    c[2] = 1.0 - c[1]
    c[3] = np.eye(128, dtype=np.float32)
    return c


def pvec(v):
    v = np.asarray(v)
    return np.ascontiguousarray(v.reshape(-1, 128).T)


def small_views(S, name, nbanks, width):
    out = []
    per = 512 // width
    banks = [S.ps("%s%d" % (name, i), [128, 512], F32) for i in range(nbanks)]
    for j in range(per):
        for i in range(nbanks):
            out.append(VBuf("%s%d_%d" % (name, i, j), banks[i], banks[i].t[:, j * width:(j + 1) * width]))
    return out


def build_gla(ntiles=SEQ // TT):
    nc = bass.Bass("TRN2", target_bir_lowering=False)
    hT = dram(nc, "hT", [DM, SEQ], BF16, "ExternalInput")
    hTl = dram(nc, "hTlo", [DM, SEQ], BF16, "ExternalInput")
    A = {}
    A["w_gla"] = dram(nc, "w_gla", [DM, 784], F32, "ExternalInput")
    A["wgk2"] = dram(nc, "wgk2", [16, 128], F32, "ExternalInput")
    A["bgk"] = dram(nc, "bgk", [1, 128], F32, "ExternalInput")
    A["ng"] = dram(nc, "ng", [128, 2], F32, "ExternalInput")
    A["cst"] = dram(nc, "cst", [4, 128, 128], F32, "ExternalInput")
    yT = dram(nc, "yT", [256, SEQ], BF16, "ExternalOutput")
    A["load_h"] = lambda S, h, t: S.dma("sp", h[:], kcp(hT[:, t * TT:(t + 1) * TT]), writes=[h])
    A["load_hlo"] = lambda S, h, t: S.dma("act", h[:], kcp(hTl[:, t * TT:(t + 1) * TT]), writes=[h])
    A["store_y"] = lambda S, yo, t: S.dma("sp", yT[:, t * TT:(t + 1) * TT].rearrange("(ec p) n -> p ec n", p=128), yo[:], reads=[yo])
    with ExitStack() as st:
        S = Sched(nc, st)
        S.begin_phase()
        emit_gla(nc, S, A, ntiles)
        S.end_phase()
    return nc


def emit_gla(nc, S, A, ntiles=SEQ // TT):
    w_gla, wgk2, bgk, ngp, cst = A["w_gla"], A["wgk2"], A["bgk"], A["ng"], A["cst"]
    if True:
        C = load_consts(S, nc, cst)
        U = S.sb("U", [128, 128], F32)
        UC = S.sb("UC", [128, 128], F32)
        S.dma("sp", U[:], cst[1], writes=[U])
        S.dma("sp", UC[:], cst[2], writes=[UC])
        wa = S.sb("wa", [128, 8, 784], BF16)
        S.dma("pool", wa[:], kcp(w_gla), writes=[wa])
        wqk32 = S.sb("wqk32", [128, 8, 256], F32)
        S.dma("sp", wqk32[:, :, 0:128], kcp(w_gla[:, 0:128]), writes=[wqk32])
        S.dma("sp", wqk32[:, :, 128:256], kcp(w_gla[:, 384:512]), writes=[wqk32], group=True)
        wqk_hi = S.sb("wqk_hi", [128, 8, 256], BF16)
        wqk_lo = S.sb("wqk_lo", [128, 8, 256], BF16)
        S.copy("act", wqk_hi, wqk_hi[:], wqk32, wqk32[:])
        S.tt("dve", wqk_lo, wqk_lo[:], wqk32, wqk32[:], wqk_hi, wqk_hi[:], ALU.subtract)
        w2 = S.sb("w2", [16, 128], F32)
        S.dma("sp", w2[:], wgk2, writes=[w2])
        bgb = S.sb("bgb", [128, 128], F32)
        S.dma("sp", bgb[:], bgk.partition_broadcast(128), writes=[bgb])
        ng = S.sb("ng", [128, 2], F32)
        S.dma("sp", ng[:], ngp, writes=[ng])
        h_ring = Ring([S.sb("h%d" % i, [128, 8, TT], BF16) for i in range(2)])
        hlo_ring = Ring([S.sb("hlo%d" % i, [128, 8, TT], BF16) for i in range(2)])
        big = Ring([S.ps("pb%d" % i, [128, 512], F32) for i in range(2)])
        po = [S.ps("po%d" % i, [128, 512], F32) for i in range(2)]
        pss_ring = Ring([S.ps("pss", [128, 512], F32)])
        sm = Ring(small_views(S, "sm", 3, 256))
        qTs = S.sb("qTs", [128, TT], F32)
        kTs = S.sb("kTs", [128, TT], F32)
        sg = S.sb("sg", [128, 2, TT], F32)
        gkl = S.sb("gkl", [16, TT], F32)

        def ring(name, shape, dt, n=2):
            return Ring([S.sb("%s%d" % (name, i), shape, dt) for i in range(n)])
        ktok_r = ring("ktok", [128, 128], F32)
        vtok_r = ring("vtok", [128, 256], BF16)
        t1_r = ring("t1", [128, 128], F32)
        e_r = ring("e", [128, 128], F32)
        gk_r = ring("gk", [128, 128], F32)
        ebT_r = ring("ebT", [128, 128], F32)
        enbT_r = ring("enbT", [128, 128], F32)
        ed2_r = ring("ed2", [128, 128], F32)
        qp_r = ring("qp", [128, 128], BF16)
        qp32_r = ring("qp32", [128, 128], F32)
        kp32_r = ring("kp32", [128, 128], F32)
        kpp_r = ring("kpp", [128, 128], BF16)
        AT_r = ring("AT", [128, 128], BF16)
        Sst = S.sb("Sst", [128, 256], F32)
        Sbf = S.sb("Sbf", [128, 256], BF16)
        S.memset("dve", Sst, Sst[:], 0.0)
        S.memset("dve", Sbf, Sbf[:], 0.0)
        sq_ring = ring("sq", [128, TT], F32)
        rstd = S.sb("rstd", [128, TT], F32)
        rtmp = S.sb("rtmp", [128, TT], F32)
        tmp_r = ring("tmp", [128, TT], F32)
        yo_r = ring("yo", [128, 2, TT], BF16)

        for t in range(ntiles):
            ts_ = slice(t * TT, (t + 1) * TT)
            h = h_ring.next()
            A["load_h"](S, h, t)

            def proj(c0, m):
                p = big.next()
                for kc in range(8):
                    S.mm(p, p[:m, :], wa, wa[:, kc, c0:c0 + m], h, h[:, kc, :], start=(kc == 0), stop=(kc == 7))
                return p
            hlo = hlo_ring.next()
            A["load_hlo"](S, hlo, t)

            def proj3(c0):
                p = big.next()
                n = 0
                for (wb_, hb_) in ((wqk_hi, h), (wqk_hi, hlo), (wqk_lo, h)):
                    for kc in range(8):
                        S.mm(p, p[:], wb_, wb_[:, kc, c0:c0 + 128], hb_, hb_[:, kc, :], start=(n == 0), stop=(n == 23))
                        n += 1
                return p
            p = proj3(0)
            S.act(qTs, qTs[:], p, p[:], AF.Copy, scale=128.0 ** -0.5)
            p = proj3(128)
            S.act(kTs, kTs[:], p, p[:], AF.Copy)
            for ec in range(2):
                p = proj(128 + ec * 128, 128)
                S.act(sg, sg[:, ec, :], p, p[:], AF.Silu)
            p = proj(768, 16)
            S.copy("dve", gkl, gkl[:], p, p[:16, :])
            for c in range(4):
                cs = slice(c * 128, (c + 1) * 128)
                p = big.next()
                for kc in range(8):
                    S.mm(p, p[:, :384], h, h[:, kc, cs], wa, wa[:, kc, 384:768], start=(kc == 0), stop=(kc == 7))
                ktok = ktok_r.next()
                vtok = vtok_r.next()
                S.copy("act", ktok, ktok[:], p, p[:, 0:128])
                S.copy("act", vtok, vtok[:], p, p[:, 128:384])
                pg = sm.next()
                S.mm(pg, pg[:, :128], gkl, gkl[:, cs], w2, w2[:], start=True, stop=True)
                t1 = t1_r.next()
                S.tt("dve", t1, t1[:], pg, pg[:, :128], bgb, bgb[:], ALU.add)
                e = e_r.next()
                S.act(e, e[:], t1, t1[:], AF.Exp, scale=-1.0)
                S.act(e, e[:], e, e[:], AF.Ln, bias=1.0)
                gk = gk_r.next()
                S.ts("dve", gk, gk[:], e, e[:], -1.0 / 16.0, None, ALU.mult)
                pbT = sm.next()
                S.mm(pbT, pbT[:, :128], gk, gk[:], U, U[:])
                pd2 = sm.next()
                S.mm(pd2, pd2[:, :128], UC, UC[:], gk, gk[:])
                ebT = ebT_r.next()
                enbT = enbT_r.next()
                ed2 = ed2_r.next()
                S.act(ebT, ebT[:], pbT, pbT[:, :128], AF.Exp)
                S.act(enbT, enbT[:], pbT, pbT[:, :128], AF.Exp, scale=-1.0)
                S.act(ed2, ed2[:], pd2, pd2[:, :128], AF.Exp)
                qp = qp_r.next()
                qp32 = qp32_r.next()
                kp32 = kp32_r.next()
                kpp = kpp_r.next()
                S.tt("dve", qp32, qp32[:], qTs, qTs[:, cs], ebT, ebT[:], ALU.mult)
                S.tt("dve", kp32, kp32[:], kTs, kTs[:, cs], enbT, enbT[:], ALU.mult)
                S.copy("act", qp, qp[:], qp32, qp32[:])
                S.tt("pool", kpp, kpp[:], ktok, ktok[:], ed2, ed2[:], ALU.mult)
                pA = sm.next()
                S.mm(pA, pA[:, :128], kp32, kp32[:], qp32, qp32[:])
                AT = AT_r.next()
                S.tt("dve", AT, AT[:], pA, pA[:, :128], U, U[:], ALU.mult)
                for ec in range(2):
                    es = slice(ec * 128, (ec + 1) * 128)
                    S.mm(po[ec], po[ec][:, cs], vtok, vtok[:, es], AT, AT[:], start=True, stop=False)
                    S.mm(po[ec], po[ec][:, cs], Sbf, Sbf[:, es], qp, qp[:], start=False, stop=True)
                pkv = sm.next()
                S.mm(pkv, pkv[:, :256], kpp, kpp[:], vtok, vtok[:])
                S.stt(Sst, Sst[:], Sst, Sst[:], ebT[:, 127:128], pkv, pkv[:, :256], ALU.mult, ALU.add, rd=[ebT])
                S.copy("act", Sbf, Sbf[:], Sst, Sst[:])
            pss = pss_ring.next()
            for ec in range(2):
                sq = sq_ring.next()
                S.act(sq, sq[:], po[ec], po[ec][:], AF.Square)
                S.mm(pss, pss[:], C["ones"], C["ones"][:], sq, sq[:], start=(ec == 0), stop=(ec == 1))
            S.act(rtmp, rtmp[:], pss, pss[:], AF.Sqrt, bias=C["eps%g" % EPS][:, 0:1], scale=1.0 / 256, rd=[C["eps%g" % EPS]])
            S.recip(rstd, rstd[:], rtmp, rtmp[:])
            yo = yo_r.next()
            for ec in range(2):
                tmp = tmp_r.next()
                S.stt(tmp, tmp[:], po[ec], po[ec][:], ng[:, ec:ec + 1], rstd, rstd[:], ALU.mult, ALU.mult, rd=[ng])
                S.tt("pool", yo, yo[:, ec, :], tmp, tmp[:], sg, sg[:, ec, :], ALU.mult)
            A["store_y"](S, yo, t)


def gla_inputs(inp, l, g, hT_b, cst, hTlo_b=None):
    w = inp["w_in"][l]
    cols = np.concatenate([np.arange(g * 128, (g + 1) * 128),
                           2064 + np.arange(g * 256, (g + 1) * 256),
                           512 + np.arange(g * 128, (g + 1) * 128),
                           1024 + np.arange(g * 256, (g + 1) * 256),
                           2048 + np.arange(16)])
    if hTlo_b is None:
        hTlo_b = np.zeros_like(hT_b)
    elif isinstance(hTlo_b, int):
        hTlo_b = None
    return {"hT": hT_b, "hTlo": hTlo_b, "w_gla": np.ascontiguousarray(w[:, cols]),
            "wgk2": np.ascontiguousarray(inp["gla_w_gk2"][l][:, g * 128:(g + 1) * 128]),
            "bgk": np.ascontiguousarray(inp["gla_b_gk"][l][g * 128:(g + 1) * 128].reshape(1, 128)),
            "ng": pvec(inp["gla_norm_g"][l]), "cst": cst}


def build_ssm(ntiles=SEQ // TT):
    nc = bass.Bass("TRN2", target_bir_lowering=False)
    hT = dram(nc, "hT", [DM, SEQ], BF16, "ExternalInput")
    A = {}
    A["w_ssm"] = dram(nc, "w_ssm", [DM, 1288], F32, "ExternalInput")
    A["cw"] = dram(nc, "cw", [128, 6, 4], F32, "ExternalInput")
    A["cb"] = dram(nc, "cb", [128, 6], F32, "ExternalInput")
    A["dtb"] = dram(nc, "dtb", [1, 8], F32, "ExternalInput")
    A["alog"] = dram(nc, "alog", [1, 8], F32, "ExternalInput")
    A["dsk"] = dram(nc, "dsk", [1, 8], F32, "ExternalInput")
    A["ngs"] = dram(nc, "ngs", [1, 512], F32, "ExternalInput")
    A["cst"] = dram(nc, "cst", [4, 128, 128], F32, "ExternalInput")
    yT = dram(nc, "yT", [512, SEQ], BF16, "ExternalOutput")
    A["load_h"] = lambda S, h, t: S.dma("sp", h[:], kcp(hT[:, t * TT:(t + 1) * TT]), writes=[h])
    A["store_y"] = lambda S, yo, t: S.dma("sp", yT[:, t * TT:(t + 1) * TT].rearrange("(c p) n -> p c n", p=128), yo[:], reads=[yo])
    with ExitStack() as st:
        S = Sched(nc, st)
        S.begin_phase()
        emit_ssm(nc, S, A, ntiles)
        S.end_phase()
    return nc


def emit_ssm(nc, S, A, ntiles=SEQ // TT):
    w_ssm, cwp, cbp, dtbp, alogp, dskp, ngp, cst = (A[k] for k in ("w_ssm", "cw", "cb", "dtb", "alog", "dsk", "ngs", "cst"))
    if True:
        C = load_consts(S, nc, cst)
        U = S.sb("U", [128, 128], F32)
        UC = S.sb("UC", [128, 128], F32)
        idf = S.sb("idf", [128, 128], F32)
        idb = S.sb("idb", [128, 128], BF16)
        S.dma("sp", U[:], cst[1], writes=[U])
        S.dma("sp", UC[:], cst[2], writes=[UC])
        S.dma("sp", idf[:], cst[3], writes=[idf])
        S.dma("pool", idb[:], cst[3], writes=[idb])
        ws = S.sb("ws", [128, 8, 1288], BF16)
        S.dma("pool", ws[:], kcp(w_ssm), writes=[ws])
        cw = S.sb("cw", [128, 6, 4], F32)
        cb = S.sb("cb", [128, 6], F32)
        S.dma("sp", cw[:], cwp, writes=[cw])
        S.dma("sp", cb[:], cbp, writes=[cb])
        dtb = S.sb("dtb", [128, 8], F32)
        a_b = S.sb("a_b", [128, 8], F32)
        dsk = S.sb("dsk", [128, 8], F32)
        ngs = S.sb("ngs", [128, 512], F32)
        S.dma("sp", dtb[:], dtbp.partition_broadcast(128), writes=[dtb])
        S.dma("sp", a_b[:], alogp.partition_broadcast(128), writes=[a_b])
        S.dma("sp", dsk[:], dskp.partition_broadcast(128), writes=[dsk])
        S.dma("sp", ngs[:], ngp.partition_broadcast(128), writes=[ngs])
        S.act(a_b, a_b[:], a_b, a_b[:], AF.Exp)
        S.ts("dve", a_b, a_b[:], a_b, a_b[:], -1.0, None, ALU.mult)

        def ring(name, shape, dt, n=2):
            return Ring([S.sb("%s%d" % (name, i), shape, dt) for i in range(n)])
        h_ring = ring("h", [128, 8, TT], BF16)
        big = Ring([S.ps("pb%d" % i, [128, 512], F32) for i in range(6)])
        sm = Ring(small_views(S, "sm", 2, 128))
        raw = S.sb("raw", [128, 6, TT + 3], F32)
        S.memset("dve", raw, raw[:, :, 0:3], 0.0)
        cacc_r = ring("cacc", [128, TT], F32)
        xc = S.sb("xc", [128, 4, TT], F32)
        BT = S.sb("BT", [128, TT], BF16)
        CT = S.sb("CT", [128, TT], BF16)
        sz_r = ring("sz", [128, 512], F32)
        t8_r = ring("t8", [128, 8], F32)
        dt_r = ring("dt", [128, 8], F32)
        da_r = ring("da", [128, 8], F32)
        xdt_r = ring("xdt", [128, 512], BF16)
        xD_r = ring("xD", [128, 512], F32)
        Btok_r = ring("Btok", [128, 128], BF16)
        rda_r = ring("rda", [128, 8, 128], F32)
        eL_r = ring("eL", [128, 8, 128], F32)
        GU_r = ring("GU", [128, 128], F32)
        M_r = ring("M", [128, 8, 128], BF16)
        cs_r = ring("cs", [128, 16], F32)
        ec_r = ring("ec", [128, 24], F32)
        xdtd_r = ring("xdtd", [128, 512], BF16)
        Sst = S.sb("Sst", [128, 512], F32)
        Sbf = S.sb("Sbf", [128, 512], BF16)
        S.memset("dve", Sst, Sst[:], 0.0)
        S.memset("dve", Sbf, Sbf[:], 0.0)
        y1_r = ring("y1", [128, 512], F32)
        y2_r = ring("y2", [128, 512], F32)
        junk = S.sb("junk", [128, 512], F32)
        ss_r = ring("ss", [128, 2], F32)
        yn_r = ring("yn", [128, 512], BF16)
        yo_r = ring("yo", [128, 4, TT], BF16)
        eps = C["eps%g" % EPS]

        def b8(ap):
            return ap.unsqueeze(2).to_broadcast([128, 8, 64])

        def v8(ap):
            return ap.rearrange("p (h q) -> p h q", h=8)

        for t in range(ntiles):
            ts_ = slice(t * TT, (t + 1) * TT)
            h = h_ring.next()
            A["load_h"](S, h, t)
            for ch in range(6):
                p = big.next()
                for kc in range(8):
                    S.mm(p, p[:], ws, ws[:, kc, 512 + ch * 128:640 + ch * 128], h, h[:, kc, :], start=(kc == 0), stop=(kc == 7))
                S.act(raw, raw[:, ch, 3:TT + 3], p, p[:], AF.Copy)
            for ch in range(6):
                acc = cacc_r.next()
                S.ts("dve", acc, acc[:], raw, raw[:, ch, 0:TT], cw[:, ch, 0:1], cb[:, ch:ch + 1], ALU.mult, ALU.add, rd=[cw, cb])
                for i in range(1, 4):
                    S.stt(acc, acc[:], raw, raw[:, ch, i:i + TT], cw[:, ch, i:i + 1], acc, acc[:], ALU.mult, ALU.add, rd=[cw])
                if ch < 4:
                    S.act(xc, xc[:, ch, :], acc, acc[:], AF.Silu)
                elif ch == 4:
                    S.act(BT, BT[:], acc, acc[:], AF.Silu)
                else:
                    S.act(CT, CT[:], acc, acc[:], AF.Silu)
            S.copy("pool", raw, raw[:, :, 0:3], raw, raw[:, :, TT:TT + 3])
            yo = yo_r.next()
            def stageA(c):
                cs = slice(c * 128, (c + 1) * 128)
                pz = big.next()
                for kc in range(8):
                    S.mm(pz, pz[:], h, h[:, kc, cs], ws, ws[:, kc, 0:512], start=(kc == 0), stop=(kc == 7))
                sz = sz_r.next()
                S.act(sz, sz[:], pz, pz[:], AF.Silu)
                pdt = sm.next()
                for kc in range(8):
                    S.mm(pdt, pdt[:, 0:8], h, h[:, kc, cs], ws, ws[:, kc, 1280:1288], start=(kc == 0), stop=(kc == 7))
                t8 = t8_r.next()
                S.tt("dve", t8, t8[:], pdt, pdt[:, 0:8], dtb, dtb[:], ALU.add)
                S.act(t8, t8[:], t8, t8[:], AF.Exp)
                dt = dt_r.next()
                S.act(dt, dt[:], t8, t8[:], AF.Ln, bias=1.0)
                da = da_r.next()
                S.tt("dve", da, da[:], dt, dt[:], a_b, a_b[:], ALU.mult)
                px = big.next()
                for ch in range(4):
                    S.mm(px, px[:, ch * 128:(ch + 1) * 128], xc, xc[:, ch, cs], idf, idf[:])
                xdt = xdt_r.next()
                xD = xD_r.next()
                S.tt("dve", xdt, v8(xdt[:]), px, v8(px[:]), dt, b8(dt[:]), ALU.mult)
                S.tt("dve", xD, v8(xD[:]), px, v8(px[:]), dsk, b8(dsk[:]), ALU.mult)
                pB = sm.next()
                S.mm(pB, pB[:], BT, BT[:, cs], idb, idb[:])
                Btok = Btok_r.next()
                S.copy("act", Btok, Btok[:], pB, pB[:])
                rda = rda_r.next()
                S.tt("pool", rda, rda[:], U, U[:].unsqueeze(1).to_broadcast([128, 8, 128]), da, da[:].unsqueeze(2).to_broadcast([128, 8, 128]), ALU.mult)
                pD = [big.next(), big.next()]
                for hf in range(2):
                    S.mm(pD[hf], pD[hf][:], UC, UC[:], rda, rda[:, hf * 4:(hf + 1) * 4, :].rearrange("p a b -> p (a b)"))
                pc = sm.next()
                S.mm(pc, pc[:, 0:8], U, U[:], da, da[:])
                pc2 = sm.next()
                S.mm(pc2, pc2[:, 0:8], C["ones"], C["ones"][:], da, da[:])
                csb = cs_r.next()
                S.copy("dve", csb, csb[:, 0:8], pc, pc[:, 0:8])
                S.copy("dve", csb, csb[:, 8:16], pc2, pc2[:, 0:8])
                ec = ec_r.next()
                S.act(ec, ec[:, 0:16], csb, csb[:, 0:16], AF.Exp)
                S.tt("dve", csb, csb[:, 0:8], csb, csb[:, 8:16], csb, csb[:, 0:8], ALU.subtract)
                S.act(ec, ec[:, 16:24], csb, csb[:, 0:8], AF.Exp)
                eL = eL_r.next()
                for hf in range(2):
                    S.act(eL, eL[:, hf * 4:(hf + 1) * 4, :].rearrange("p a b -> p (a b)"), pD[hf], pD[hf][:], AF.Exp)
                pG = sm.next()
                S.mm(pG, pG[:], BT, BT[:, cs], CT, CT[:, cs])
                GU = GU_r.next()
                S.tt("dve", GU, GU[:], pG, pG[:], U, U[:], ALU.mult)
                M = M_r.next()
                S.tt("pool", M, M[:], eL, eL[:], GU, GU[:].unsqueeze(1).to_broadcast([128, 8, 128]), ALU.mult)
                xdtd = xdtd_r.next()
                S.tt("pool", xdtd, v8(xdtd[:]), xdt, v8(xdt[:]), ec, b8(ec[:, 16:24]), ALU.mult)
                return dict(cs=cs, sz=sz, xdt=xdt, xD=xD, Btok=Btok, M=M, ec=ec, xdtd=xdtd)

            def stageB(c, v):
                cs, sz, xdt, xD, Btok, M, ec, xdtd = (v[k] for k in ('cs', 'sz', 'xdt', 'xD', 'Btok', 'M', 'ec', 'xdtd'))
                pyd = big.next()
                for hh in range(8):
                    S.mm(pyd, pyd[:, hh * 64:(hh + 1) * 64], M, M[:, hh, :], xdt, xdt[:, hh * 64:(hh + 1) * 64])
                pyo = big.next()
                S.mm(pyo, pyo[:], CT, CT[:, cs], Sbf, Sbf[:])
                pst = big.next()
                S.mm(pst, pst[:], Btok, Btok[:], xdtd, xdtd[:])
                S.tt("dve", Sst, v8(Sst[:]), Sst, v8(Sst[:]), ec, b8(ec[:, 8:16]), ALU.mult)
                S.tt("dve", Sst, Sst[:], Sst, Sst[:], pst, pst[:], ALU.add)
                S.copy("act", Sbf, Sbf[:], Sst, Sst[:])
                y1 = y1_r.next()
                S.tt("dve", y1, v8(y1[:]), pyo, v8(pyo[:]), ec, b8(ec[:, 0:8]), ALU.mult)
                S.tt("dve", y1, y1[:], y1, y1[:], pyd, pyd[:], ALU.add)
                y2 = y2_r.next()
                S.tt("pool", y2, y2[:], y1, y1[:], xD, xD[:], ALU.add)
                S.tt("pool", y2, y2[:], y2, y2[:], sz, sz[:], ALU.mult)
                ss = ss_r.next()
                S.act(junk, junk[:], y2, y2[:], AF.Square, accum=ss[:, 0:1], wr=[ss])
                S.act(ss, ss[:, 1:2], ss, ss[:, 0:1], AF.Sqrt, bias=eps[:, 0:1], scale=1.0 / 512, rd=[eps])
                S.recip(ss, ss[:, 0:1], ss, ss[:, 1:2])
                yn = yn_r.next()
                S.stt(yn, yn[:], y2, y2[:], ss[:, 0:1], ngs, ngs[:], ALU.mult, ALU.mult, rd=[ss])
                pT = big.next()
                for ch in range(4):
                    S.mm(pT, pT[:, ch * 128:(ch + 1) * 128], yn, yn[:, ch * 128:(ch + 1) * 128], idb, idb[:])
                S.copy("act", yo, yo[:, :, cs], pT, pT[:].rearrange("p (c n) -> p c n", c=4))

            va = {0: stageA(0)}
            for c in range(4):
                if c + 1 < 4:
                    va[c + 1] = stageA(c + 1)
                stageB(c, va.pop(c))
            A["store_y"](S, yo, t)


def ssm_inputs(inp, l, g, hT_b, cst):
    w = inp["w_in"][l]
    xcols = 5136 + np.arange(g * 512, (g + 1) * 512)
    bcols = 7184 + np.arange(g * 128, (g + 1) * 128)
    ccols = 7696 + np.arange(g * 128, (g + 1) * 128)
    cols = np.concatenate([3088 + np.arange(g * 512, (g + 1) * 512), xcols, bcols, ccols, 8208 + np.arange(g * 8, (g + 1) * 8)])
    cc = np.concatenate([xcols, bcols, ccols]) - 5136
    cwv = inp["ssm_conv_w"][l][:, cc]
    cw = np.ascontiguousarray(cwv.reshape(4, 6, 128).transpose(2, 1, 0))
    cb = np.ascontiguousarray(inp["ssm_conv_b"][l][cc].reshape(6, 128).T)
    hs = slice(g * 8, (g + 1) * 8)
    return {"hT": hT_b, "w_ssm": np.ascontiguousarray(w[:, cols]), "cw": cw, "cb": cb,
            "dtb": np.ascontiguousarray(inp["ssm_dt_bias"][l][hs].reshape(1, 8)),
            "alog": np.ascontiguousarray(inp["ssm_a_log"][l][hs].reshape(1, 8)),
            "dsk": np.ascontiguousarray(inp["ssm_d"][l][hs].reshape(1, 8)),
            "ngs": np.ascontiguousarray(inp["ssm_norm_g"][l][g * 512:(g + 1) * 512].reshape(1, 512)),
            "cst": cst}


C1_2PI = 6.28125
C2_2PI = 2.0 * math.pi - 6.28125


def build_diff(l, ntiles=SEQ // TT):
    nc = bass.Bass("TRN2", target_bir_lowering=False)
    hT = dram(nc, "hT", [DM, SEQ], BF16, "ExternalInput")
    A = {}
    A["w_diff"] = dram(nc, "w_diff", [DM, 1280], F32, "ExternalInput")
    A["pos"] = dram(nc, "pos", [1, SEQ], I32, "ExternalInput")
    A["invf"] = dram(nc, "invf", [128, 2], F32, "ExternalInput")
    A["lqk"] = dram(nc, "lqk", [4, 64], F32, "ExternalInput")
    A["ngd"] = dram(nc, "ngd", [1, 128], F32, "ExternalInput")
    A["cst"] = dram(nc, "cst", [4, 128, 128], F32, "ExternalInput")
    yT = dram(nc, "yT", [256, SEQ], BF16, "ExternalOutput")
    A["load_h"] = lambda S, h, t: S.dma("sp", h[:], kcp(hT[:, t * TT:(t + 1) * TT]), writes=[h])
    A["store_y"] = lambda S, yo, t: S.dma("sp", yT[:, t * TT:(t + 1) * TT].rearrange("(c p) n -> p c n", p=128), yo[:], reads=[yo])
    with ExitStack() as st:
        S = Sched(nc, st)
        S.begin_phase()
        emit_diff(nc, S, A, l, ntiles)
        S.end_phase()
    return nc


def emit_diff(nc, S, A, l, ntiles=SEQ // TT):
    lambda_init = 0.8 - 0.6 * math.exp(-0.3 * l)
    w_diff, posd, invfp, lqk, ngp, cst = (A[k] for k in ("w_diff", "pos", "invf", "lqk", "ngd", "cst"))
    if True:
        C = load_consts(S, nc, cst, extra_eps=(1e-5,))
        idb = S.sb("idb", [128, 128], BF16)
        S.dma("pool", idb[:], cst[3], writes=[idb])
        wd = S.sb("wd", [128, 8, 1280], BF16)
        S.dma("pool", wd[:], kcp(w_diff), writes=[wd])
        invf = S.sb("invf", [128, 2], F32)
        S.dma("sp", invf[:], invfp, writes=[invf])
        ngd = S.sb("ngd", [128, 128], F32)
        S.dma("sp", ngd[:], ngp.partition_broadcast(128), writes=[ngd])
        S.ts("dve", ngd, ngd[:], ngd, ngd[:], 1.0 - lambda_init, None, ALU.mult)
        lq = S.sb("lq", [128, 4, 64], F32)
        for i in range(4):
            S.dma("sp", lq[:, i, :], lqk[i:i + 1, :].partition_broadcast(128), writes=[lq], group=(i > 0))
        lt = S.sb("lt", [128, 2, 64], F32)
        S.tt("dve", lt, lt[:, 0, :], lq, lq[:, 0, :], lq, lq[:, 1, :], ALU.mult)
        S.tt("dve", lt, lt[:, 1, :], lq, lq[:, 2, :], lq, lq[:, 3, :], ALU.mult)
        ls = S.sb("ls", [128, 4], F32)
        S.op("dve", lambda E: E.reduce_sum(out=ls[:, 0:2], in_=lt[:], axis=AX.X), reads=[lt], writes=[ls])
        S.act(ls, ls[:, 0:2], ls, ls[:, 0:2], AF.Exp)
        S.tt("dve", ls, ls[:, 2:3], ls, ls[:, 1:2], ls, ls[:, 0:1], ALU.subtract)
        S.ts("dve", ls, ls[:, 3:4], ls, ls[:, 2:3], -lambda_init, None, ALU.add)
        nlam = ls

        def ring(name, shape, dt, n=2):
            return Ring([S.sb("%s%d" % (name, i), shape, dt) for i in range(n)])
        h_ring = ring("h", [128, 8, TT], BF16)
        KT = S.sb("KT", [128, 2, SEQ], BF16)
        VA = S.sb("VA", [128, 2, SEQ // 128, 129], BF16)
        S.memset("pool", VA, VA[:, :, :, 128:129], 1.0)
        QT_r = ring("QT", [128, 2, TT], BF16)
        big = Ring([S.ps("pb%d" % i, [128, 512], F32) for i in range(3)])
        pob = [[S.ps("po%d_%d" % (s, hf), [128, 512], F32) for hf in range(2)] for s in range(2)]
        sm = Ring(small_views(S, "sm", 1, 128))
        posi = S.sb("posi", [128, TT], I32)
        ang = S.sb("ang", [128, TT], F32)
        ang2 = S.sb("ang2", [128, TT], F32)
        ki = S.sb("ki", [128, TT], I32)
        kf = S.sb("kf", [128, TT], F32)
        yr = S.sb("yr", [128, TT], F32)
        Cs = S.sb("Cs", [128, TT], F32)
        Sn = S.sb("Sn", [128, TT], F32)
        ta_r = ring("ta", [128, TT], F32)
        tb_r = ring("tb", [128, TT], F32)
        PT_r = ring("PT", [128, TT], BF16, 5)
        r_r = ring("r", [128, 4], F32)
        oa_r = ring("oa", [128, 128], F32)
        junk = S.sb("junk", [128, 128], F32)
        yn_r = ring("yn", [128, 128], BF16)
        yo_r = ring("yo", [128, 2, TT], BF16)
        eps5 = C["eps%g" % 1e-5]

        def reduce_sin(dst, src):
            S.ts("dve", ki, ki[:], src, src[:], 1.0 / (2.0 * math.pi), None, ALU.mult)
            S.copy("dve", kf, kf[:], ki, ki[:])
            S.stt(yr, yr[:], kf, kf[:], -C1_2PI, src, src[:], ALU.mult, ALU.add)
            S.stt(yr, yr[:], kf, kf[:], -C2_2PI, yr, yr[:], ALU.mult, ALU.add)
            S.ts("dve", yr, yr[:], yr, yr[:], -3.1415925, 3.1415925, ALU.max, ALU.min)
            S.act(dst, dst[:], yr, yr[:], AF.Sin)

        for t in range(ntiles):
            ts_ = slice(t * TT, (t + 1) * TT)
            h = h_ring.next()
            A["load_h"](S, h, t)
            S.dma("sp", posi[:], posd[0:1, ts_].partition_broadcast(128), writes=[posi])
            S.copy("dve", ang, ang[:], posi, posi[:])
            S.ts("dve", ang, ang[:], ang, ang[:], invf[:, 0:1], None, ALU.mult, rd=[invf])
            reduce_sin(Sn, ang)
            S.ts("dve", Sn, Sn[:], Sn, Sn[:], invf[:, 1:2], None, ALU.mult, rd=[invf])
            S.ts("dve", ang2, ang2[:], ang, ang[:], math.pi / 2.0, None, ALU.add)
            reduce_sin(Cs, ang2)
            QT = QT_r.next()
            for hd in range(2):
                for (c0, dstb, dst) in ((hd * 128, QT, QT[:, hd, :]), (512 + hd * 128, KT, KT[:, hd, ts_])):
                    p1 = big.next()
                    for kc in range(8):
                        S.mm(p1, p1[:], wd, wd[:, kc, c0:c0 + 128], h, h[:, kc, :], start=(kc == 0), stop=(kc == 7))
                    p2 = big.next()
                    for kc in range(8):
                        S.mm(p2, p2[:], wd, wd[:, kc, c0 + 256:c0 + 384], h, h[:, kc, :], start=(kc == 0), stop=(kc == 7))
                    ta = ta_r.next()
                    tb = tb_r.next()
                    S.tt("dve", ta, ta[:], p1, p1[:], Cs, Cs[:], ALU.mult)
                    S.tt("dve", tb, tb[:], p2, p2[:], Sn, Sn[:], ALU.mult)
                    S.tt("pool", dstb, dst, ta, ta[:], tb, tb[:], ALU.add)
            for c in range(4):
                cs = slice(c * 128, (c + 1) * 128)
                pv = big.next()
                for kc in range(8):
                    S.mm(pv, pv[:, 0:256], h, h[:, kc, cs], wd, wd[:, kc, 1024:1280], start=(kc == 0), stop=(kc == 7))
                S.copy("act", VA, VA[:, :, 4 * t + c, 0:128], pv, pv[:, 0:256].rearrange("p (a b) -> p a b", a=2))
            yo = yo_r.next()
            for hd in range(2):
                nkb = 4 * t + 4
                started = [[False, False], [False, False]]
                iters = [(kb, s_) for kb in range(nkb) for s_ in range(2)]
                pend = {}

                def emit_qk(i):
                    kb, s_ = iters[i]
                    r = kb - 4 * t
                    q0 = max(r, 0)
                    qlo = q0 * 128
                    n = TT - qlo
                    ps_ = slice(s_ * 64, (s_ + 1) * 64)
                    pS = big.next()
                    S.mm(pS, pS[:, :n], KT, KT[ps_, hd, kb * 128:(kb + 1) * 128], QT, QT[ps_, hd, qlo:TT])
                    PT = PT_r.next()
                    S.act(PT, PT[:, :n], pS, pS[:, :n], AF.Exp, scale=0.125)
                    if r >= 0:
                        S.memset("pool", PT, PT[64:128, 0:64], 0.0)
                    pend[i] = (PT, q0, qlo)

                def emit_pv(i):
                    kb, s_ = iters[i]
                    PT, q0, qlo = pend.pop(i)
                    for qb in range(q0, 4):
                        col = qb * 128 - qlo
                        bank = pob[s_][qb // 2]
                        o = bank[:, (qb % 2) * 129:(qb % 2) * 129 + 129]
                        first = not started[s_][qb // 2]
                        started[s_][qb // 2] = True
                        S.mm(bank, o, PT, PT[:, col:col + 128], VA, VA[:, hd, kb, :], start=first, stop=(kb == 4 * t + qb and qb % 2 == 1))

                LOOK = 3
                for i in range(len(iters) + LOOK):
                    if i < len(iters):
                        emit_qk(i)
                    if i >= LOOK:
                        emit_pv(i - LOOK)
                for qb in range(4):
                    o1 = pob[0][qb // 2]
                    o2 = pob[1][qb // 2]
                    b0 = (qb % 2) * 129
                    rr = r_r.next()
                    S.recip(rr, rr[:, 0:1], o1, o1[:, b0 + 128:b0 + 129])
                    S.recip(rr, rr[:, 1:2], o2, o2[:, b0 + 128:b0 + 129])
                    S.tt("dve", rr, rr[:, 2:3], rr, rr[:, 1:2], nlam, nlam[:, 3:4], ALU.mult)
                    oa = oa_r.next()
                    S.ts("dve", oa, oa[:], o1, o1[:, b0:b0 + 128], rr[:, 0:1], None, ALU.mult, rd=[rr])
                    S.stt(oa, oa[:], o2, o2[:, b0:b0 + 128], rr[:, 2:3], oa, oa[:], ALU.mult, ALU.add, rd=[rr])
                    S.act(junk, junk[:], oa, oa[:], AF.Square, accum=rr[:, 3:4], wr=[rr])
                    S.act(rr, rr[:, 1:2], rr, rr[:, 3:4], AF.Sqrt, bias=eps5[:, 0:1], scale=1.0 / 128, rd=[eps5])
                    S.recip(rr, rr[:, 0:1], rr, rr[:, 1:2])
                    yn = yn_r.next()
                    S.stt(yn, yn[:], oa, oa[:], rr[:, 0:1], ngd, ngd[:], ALU.mult, ALU.mult, rd=[rr])
                    pT = sm.next()
                    S.mm(pT, pT[:], yn, yn[:], idb, idb[:])
                    S.copy("act", yo, yo[:, hd, qb * 128:(qb + 1) * 128], pT, pT[:])
            A["store_y"](S, yo, t)


def diff_inputs(inp, l, g, b, hT_b, cst):
    w = inp["w_in"][l]
    qc = 8240 + np.arange(g * 256, (g + 1) * 256)
    kc = 9264 + np.arange(g * 256, (g + 1) * 256)
    vc = 10288 + np.arange(g * 256, (g + 1) * 256)
    d = np.arange(256) % 64
    partner = np.arange(256) + np.where(d < 8, 8, np.where(d < 16, -8, 0))
    cols = np.concatenate([qc, qc[partner], kc, kc[partner], vc])
    p = np.arange(128) % 64
    invf = np.zeros((128, 2), np.float32)
    fr = (500000.0 ** (-np.arange(0, 16, 2, dtype=np.float32) / np.float32(16))).astype(np.float32)
    invf[:, 0] = np.where(p < 16, fr[p % 8], 0.0)
    invf[:, 1] = np.where(p < 8, -1.0, np.where(p < 16, 1.0, 0.0))
    lqk = np.stack([inp["diff_lq1"][l], inp["diff_lk1"][l], inp["diff_lq2"][l], inp["diff_lk2"][l]]).astype(np.float32)
    return {"hT": hT_b, "w_diff": np.ascontiguousarray(w[:, cols]),
            "pos": np.ascontiguousarray(inp["positions"][b].reshape(1, SEQ)), "invf": invf, "lqk": lqk,
            "ngd": np.ascontiguousarray(inp["diff_norm_g"][l].reshape(1, 128)), "cst": cst}


_PROGS = {}


def _prog(key, fn):
    if key not in _PROGS:
        _PROGS[key] = fn()
    return _PROGS[key]


def _run(nc, maps):
    return run_bass_kernel_spmd(nc, maps, core_ids=list(range(NCORES))).results


def tok_inputs(inp, l, xT_c, hT_c, yT_c, g_next, cst):
    w = inp["w_in"][l]
    return {"xT": xT_c, "g_next": pvec(g_next), "cst": cst, "hT": hT_c, "yT": yT_c,
            "w_gate": np.ascontiguousarray(w[:, 11312:14384]), "b_gate": pvec(inp["b_gate"][l]),
            "w_br": np.ascontiguousarray(np.concatenate([inp["w_br_gla"][l], inp["w_br_ssm"][l], inp["w_br_diff"][l]], axis=0)),
            "w_out": np.ascontiguousarray(inp["w_out"][l]), "w_up": np.ascontiguousarray(inp["w_mlp_up"][l]),
            "w_dn": np.ascontiguousarray(inp["w_mlp_down"][l]), "g_mlp": pvec(inp["norm_mlp_g"][l])}


def kernel_unfused(**inp):
    inp = {k: np.asarray(v) for k, v in inp.items()}
    cst = make_consts()
    x = inp["x"]
    xT = [np.ascontiguousarray(x[c // 4, (c % 4) * NT:(c % 4 + 1) * NT, :].T) for c in range(NCORES)]
    res = _run(_prog("first", lambda: build_tok("first")),
               [{"xT": xT[c], "g_next": pvec(inp["norm_mix_g"][0]), "cst": cst} for c in range(NCORES)])
    hT = [r["hT_out"] for r in res]
    hTl = [r["hT_lo"] for r in res]
    out = None
    for l in range(DEPTH):
        hTb = [np.ascontiguousarray(np.concatenate(hT[b * 4:(b + 1) * 4], axis=1)) for b in range(B)]
        hTlb = [np.ascontiguousarray(np.concatenate(hTl[b * 4:(b + 1) * 4], axis=1)) for b in range(B)]
        yg = _run(_prog("gla", build_gla), [gla_inputs(inp, l, c % 4, hTb[c // 4], cst, hTlb[c // 4]) for c in range(NCORES)])
        ys = _run(_prog("ssm", build_ssm), [ssm_inputs(inp, l, c % 4, hTb[c // 4], cst) for c in range(NCORES)])
        yd = _run(_prog(("diff", l), lambda: build_diff(l)), [diff_inputs(inp, l, c % 4, c // 4, hTb[c // 4], cst) for c in range(NCORES)])
        yTb = []
        for b in range(B):
            parts = [yg[b * 4 + g]["yT"] for g in range(4)] + [ys[b * 4 + g]["yT"] for g in range(4)] + [yd[b * 4 + g]["yT"] for g in range(4)]
            yTb.append(np.concatenate(parts, axis=0))
        last = (l == DEPTH - 1)
        g_next = inp["norm_final_g"] if last else inp["norm_mix_g"][l + 1]
        maps = [tok_inputs(inp, l, xT[c], hT[c], np.ascontiguousarray(yTb[c // 4][:, (c % 4) * NT:(c % 4 + 1) * NT]), g_next, cst)
                for c in range(NCORES)]
        mode = "last" if last else "mid"
        res = _run(_prog(mode, lambda: build_tok(mode)), maps)
        if last:
            out = np.empty((B, SEQ, DM), np.float32)
            for c in range(NCORES):
                out[c // 4, (c % 4) * NT:(c % 4 + 1) * NT, :] = res[c]["outT"].T
        else:
            xT = [r["xT_out"] for r in res]
            hT = [r["hT_out"] for r in res]
            hTl = [r["hT_lo"] for r in res]
    return out


GROUPS = [[0, 1, 2, 3], [4, 5, 6, 7]]


def build_fused(skip=()):
    nc = bass.Bass("TRN2", target_bir_lowering=False)
    ext = lambda name, shape, dt=F32: dram(nc, name, shape, dt, "ExternalInput")
    xT = ext("xT", [DM, NT])
    pos = ext("pos", [1, SEQ], I32)
    cst = ext("cst", [4, 128, 128])
    invf = ext("invf", [128, 2])
    g_mix = ext("g_mix", [DEPTH, 128, 8])
    g_fin = ext("g_fin", [128, 8])
    g_mlp = ext("g_mlp", [DEPTH, 128, 8])
    w_gate = ext("w_gate", [DEPTH, DM, 3072])
    b_gate = ext("b_gate", [DEPTH, 128, 24])
    w_br = ext("w_br", [DEPTH, 4096, DM])
    w_out = ext("w_out", [DEPTH, DM, DM])
    w_up = ext("w_up", [DEPTH, DM, 4096])
    w_dn = ext("w_dn", [DEPTH, 4096, DM])
    w_gla = ext("w_gla", [DEPTH, DM, 784])
    wgk2 = ext("wgk2", [DEPTH, 16, 128])
    bgk = ext("bgk", [DEPTH, 1, 128])
    ng = ext("ng", [DEPTH, 128, 2])
    w_ssm = ext("w_ssm", [DEPTH, DM, 1288])
    cw = ext("cw", [DEPTH, 128, 6, 4])
    cb = ext("cb", [DEPTH, 128, 6])
    dtb = ext("dtb", [DEPTH, 1, 8])
    alog = ext("alog", [DEPTH, 1, 8])
    dsk = ext("dsk", [DEPTH, 1, 8])
    ngs = ext("ngs", [DEPTH, 1, 512])
    w_diff = ext("w_diff", [DEPTH, DM, 1280])
    lqk = ext("lqk", [DEPTH, 4, 64])
    ngd = ext("ngd", [DEPTH, 1, 128])
    outT = dram(nc, "outT", [DM, NT], F32, "ExternalOutput")
    xs = nc.dram_tensor("xs_i", [DM, NT], F32).ap()
    hsrc = nc.dram_tensor("hsrc_i", [8, 256, NT], BF16).ap()
    hgat = nc.dram_tensor("hgat_i", [8, 1024, NT], BF16).ap()
    ysrc = nc.dram_tensor("ysrc_i", [4, 4, 256, NT], BF16).ap()
    ygat = nc.dram_tensor("ygat_i", [4, 4, 1024, NT], BF16).ap()
    hT_own = hsrc[0:4].rearrange("c r n -> (c r) n")
    hTlo_own = hsrc[4:8].rearrange("c r n -> (c r) n")
    ygat3 = ygat.rearrange("q a r n -> q (a r) n")

    with ExitStack() as st:
        S = Sched(nc, st)
        hsrc_b = Buf("hsrc_b")
        hgat_b = [Buf("hgat_b%d" % i) for i in range(8)]
        ysrc_b = [[Buf("ysrc_b%d_%d" % (q, a)) for a in range(4)] for q in range(4)]
        ygat_b = Buf("ygat_b")
        xs_b = Buf("xs_b")
        S.global_bufs += [hsrc_b, ygat_b, xs_b] + hgat_b + [b for row in ysrc_b for b in row]

        def gather_h():
            for i in range(8):
                S.collective("AllGather", hsrc_b, hsrc[i], hgat_b[i], hgat[i], GROUPS)

        def load_h_from(chunks):
            def f(S_, h, t):
                r, tl = t // (NT // TT), (t % (NT // TT)) * TT
                for j, i in enumerate(chunks):
                    S_.dma("sp", h[:, 2 * j:2 * j + 2, :], hgat[i][r * 256:(r + 1) * 256, tl:tl + TT].rearrange("(c p) n -> p c n", p=128),
                           reads=[hgat_b[i]], writes=[h], group=(j > 0))
            return f

        def store_y_parts(parts):
            def f(S_, yo, t):
                q, tl = t // (NT // TT), (t % (NT // TT)) * TT
                for j, a in enumerate(parts):
                    S_.dma("sp", ysrc[q, a][:, tl:tl + TT].rearrange("(c p) n -> p c n", p=128), yo[:, 2 * j:2 * j + 2, :],
                           reads=[yo], writes=[ysrc_b[q][a]], sembuf=yo, group=True)
                if t % (NT // TT) == NT // TT - 1:
                    for a in parts:
                        S_.collective("AllGather", ysrc_b[q][a], ysrc[q, a], ygat_b, ygat[q, a], GROUPS)
            return f

        qcache = {}

        def load_y(S_, yub, t):
            tl = t * TT
            ph = S_.phase_id

            def qval(E):
                if ph not in qcache:
                    qcache[ph] = E.snap(E.partition_id() % 4)
                return qcache[ph]
            src = (lambda E, tl=tl: ygat3[bass.ds(qval(E), 1), :, tl:tl + TT].rearrange("o (k p) n -> p (o k) n", p=128))
            S_.dma("sp", yub[:], src, reads=[ygat_b], writes=[yub])

        S.begin_phase()
        if "first" not in skip:
            emit_tok(nc, S, {"xT": xT, "g_next": g_mix[0], "cst": cst, "hT_out": hT_own, "hT_lo": hTlo_own, "h_wr": [hsrc_b]}, "first")
        if "gather" not in skip:
            gather_h()
        S.end_phase()
        for l in range(DEPTH):
            S.begin_phase()
            if "gla" not in skip:
              emit_gla(nc, S, {"w_gla": w_gla[l], "wgk2": wgk2[l], "bgk": bgk[l], "ng": ng[l], "cst": cst,
                             "load_h": load_h_from((0, 1, 2, 3)), "load_hlo": load_h_from((4, 5, 6, 7)), "store_y": store_y_parts((0,))}, SEQ // TT)
            S.end_phase()
            S.begin_phase()
            if "ssm" not in skip:
              emit_ssm(nc, S, {"w_ssm": w_ssm[l], "cw": cw[l], "cb": cb[l], "dtb": dtb[l], "alog": alog[l], "dsk": dsk[l], "ngs": ngs[l],
                             "cst": cst, "load_h": load_h_from((0, 1, 2, 3)), "store_y": store_y_parts((1, 2))}, SEQ // TT)
            S.end_phase()
            S.begin_phase()
            if "diff" not in skip:
              emit_diff(nc, S, {"w_diff": w_diff[l], "pos": pos, "invf": invf, "lqk": lqk[l], "ngd": ngd[l], "cst": cst,
                              "load_h": load_h_from((0, 1, 2, 3)), "store_y": store_y_parts((3,))}, l, SEQ // TT)
            S.end_phase()
            last = (l == DEPTH - 1)
            A = {"xT": (xT if l == 0 else xs), "g_next": (g_fin if last else g_mix[l + 1]), "cst": cst, "hT": hT_own, "load_y": load_y,
                 "w_gate": w_gate[l], "b_gate": b_gate[l], "w_br": w_br[l], "w_out": w_out[l], "w_up": w_up[l], "w_dn": w_dn[l], "g_mlp": g_mlp[l]}
            if last:
                A["outT"] = outT
            else:
                A.update({"hT_out": hT_own, "hT_lo": hTlo_own, "h_wr": [hsrc_b], "xT_out": xs})
            S.begin_phase()
            emit_tok(nc, S, A, "last" if last else "mid")
            if not last:
                gather_h()
            S.end_phase()
    return nc


def fused_y_row_order():
    rows = []
    for a in range(4):
        for r in range(4):
            j = np.arange(256)
            if a == 0:
                rows.append(r * 256 + j)
            elif a in (1, 2):
                rows.append(1024 + r * 512 + (a - 1) * 256 + j)
            else:
                rows.append(3072 + r * 256 + j)
    return np.concatenate(rows)


def fused_inputs(inp, c, cst):
    b, g = c // 4, c % 4
    x = inp["x"]
    m = {"xT": np.ascontiguousarray(x[b, g * NT:(g + 1) * NT, :].T), "cst": cst,
         "pos": np.ascontiguousarray(inp["positions"][b].reshape(1, SEQ)),
         "g_mix": np.stack([pvec(inp["norm_mix_g"][l]) for l in range(DEPTH)]), "g_fin": pvec(inp["norm_final_g"]),
         "g_mlp": np.stack([pvec(inp["norm_mlp_g"][l]) for l in range(DEPTH)]),
         "w_gate": np.ascontiguousarray(inp["w_in"][:, :, 11312:14384]),
         "b_gate": np.stack([pvec(inp["b_gate"][l]) for l in range(DEPTH)]),
         "w_br": np.ascontiguousarray(np.concatenate([inp["w_br_gla"], inp["w_br_ssm"], inp["w_br_diff"]], axis=1)[:, fused_y_row_order(), :]),
         "w_out": np.ascontiguousarray(inp["w_out"]), "w_up": np.ascontiguousarray(inp["w_mlp_up"]), "w_dn": np.ascontiguousarray(inp["w_mlp_down"])}
    per = {}
    for l in range(DEPTH):
        d = {}
        d.update(gla_inputs(inp, l, g, None, cst, 0))
        d.update(ssm_inputs(inp, l, g, None, cst))
        d.update(diff_inputs(inp, l, g, b, None, cst))
        for k, v in d.items():
            if k in ("hT", "hTlo", "cst", "pos", "invf"):
                continue
            per.setdefault(k, []).append(v)
        if l == 0:
            m["invf"] = d["invf"]
    for k, v in per.items():
        m[k] = np.ascontiguousarray(np.stack(v))
    return m


_FUSED = []


def kernel(**inp):
    inp = {k: np.asarray(v) for k, v in inp.items()}
    cst = make_consts()
    if not _FUSED:
        _FUSED.append(build_fused())
    maps = [fused_inputs(inp, c, cst) for c in range(NCORES)]
    res = run_bass_kernel_spmd(_FUSED[0], maps, core_ids=list(range(NCORES))).results
    out = np.empty((B, SEQ, DM), np.float32)
    for c in range(NCORES):
        out[c // 4, (c % 4) * NT:(c % 4 + 1) * NT, :] = res[c]["outT"].T
    return out
```

```python
import math
from contextlib import ExitStack
import numpy as np
import ml_dtypes
import concourse.bass as bass
import concourse.mybir as mybir
from concourse.bass_utils import run_bass_kernel_spmd

F32 = mybir.dt.float32
BF16 = mybir.dt.bfloat16
I32 = mybir.dt.int32
AF = mybir.ActivationFunctionType
ALU = mybir.AluOpType
AX = mybir.AxisListType

NCORES = 8
B, SEQ, DM, DEPTH = 2, 8192, 1024, 2
NT = 2048
TT = 512
EPS = 1e-6
ENGS = ("pe", "act", "dve", "pool", "sp")


class Buf:
    __slots__ = ("name", "t", "w", "r", "dsem", "dcnt")

    def __init__(self, name, t=None):
        self.name = name
        self.t = t
        self.w = None
        self.r = {}
        self.dsem = None
        self.dcnt = 0

    def __getitem__(self, idx):
        return self.t[idx]


class VBuf:
    def __init__(self, name, parent, ap):
        self.name = name
        self.parent = parent
        self.t = ap

    def __getitem__(self, idx):
        return self.t[idx]

    w = property(lambda s: s.parent.w, lambda s, v: setattr(s.parent, "w", v))
    r = property(lambda s: s.parent.r, lambda s, v: setattr(s.parent, "r", v))
    dsem = property(lambda s: s.parent.dsem, lambda s, v: setattr(s.parent, "dsem", v))
    dcnt = property(lambda s: s.parent.dcnt, lambda s, v: setattr(s.parent, "dcnt", v))


class Sched:
    def __init__(self, nc, stack):
        self.nc = nc
        self.stack = stack
        self.q = {e: [] for e in ENGS}
        self.sem = {}
        for e in ENGS:
            self.sem[e] = stack.enter_context(nc.semaphore("s_" + e))
        self.cnt = {e: 0 for e in ENGS}
        self.waited = {e: {} for e in ENGS}
        self.ndsem = 0
        self.final_waits = {}
        self.alloc_stack = stack
        self.nname = 0
        self.free_dsems = {}
        self.dsem_cls = {}
        self.phase_bufs = []
        self.global_bufs = []

    def sb(self, name, shape, dt):
        self.nname += 1
        t = self.alloc_stack.enter_context(self.nc.sbuf_tensor("sb%d_%s" % (self.nname, name), list(shape), dt))
        b = Buf(name, t)
        self.phase_bufs.append(b)
        return b

    def ps(self, name, shape, dt=F32):
        self.nname += 1
        t = self.alloc_stack.enter_context(self.nc.psum_tensor("ps%d_%s" % (self.nname, name), list(shape), dt))
        return Buf(name, t)

    def view(self, name, ap):
        return Buf(name, ap)

    def _dsem(self, b, eng="sp"):
        cls = {"pool": "sw", "cc": "cc"}.get(eng, "hw")
        if b.dsem is not None:
            assert self.dsem_cls[b.dsem] == cls, (b.name, cls)
        if b.dsem is None:
            pool = self.free_dsems.setdefault(cls, [])
            if pool:
                b.dsem, b.dcnt = pool.pop()
            else:
                b.dsem = "d%d" % self.ndsem
                self.ndsem += 1
                self.sem[b.dsem] = self.stack.enter_context(self.nc.semaphore(b.dsem))
                self.dsem_cls[b.dsem] = cls
        return b.dsem

    def _deps(self, eng, reads, writes, same_ok):
        deps = {}

        def add(k, v):
            if deps.get(k, 0) < v:
                deps[k] = v

        for b in reads:
            if b.w is not None:
                add(*b.w)
        for b in writes:
            if b.w is not None:
                add(*b.w)
            for k, v in b.r.items():
                add(k, v)
        waits = []
        wd = self.waited[eng]
        for k, v in deps.items():
            if k == eng and same_ok:
                continue
            if wd.get(k, 0) >= v:
                continue
            wd[k] = v
            waits.append((k, v))
        return waits

    def _commit(self, tk, reads, writes):
        k, v = tk
        for b in writes:
            b.w = tk
            b.r = {}
        for b in reads:
            if b.r.get(k, 0) < v:
                b.r[k] = v

    def op(self, eng, fn, reads=(), writes=()):
        waits = self._deps(eng, reads, writes, same_ok=(eng == "pe"))
        self.cnt[eng] += 1
        tk = (eng, self.cnt[eng])
        sem = self.sem
        me = sem[eng]

        def emit(E):
            for k, v in waits:
                E.wait_ge(sem[k], v)
            fn(E).then_inc(me, 1)

        self.q[eng].append(emit)
        self._commit(tk, reads, writes)
        return tk

    def dma(self, eng, out_ap, in_ap, reads=(), writes=(), sembuf=None, group=False):
        sb = sembuf if sembuf is not None else (writes[0] if writes else reads[0])
        dk = self._dsem(sb, eng)
        saved = None
        if group and writes and writes[0].w is not None and writes[0].w[0] == dk:
            saved = writes[0].w
            writes[0].w = None
        waits = self._deps(eng, reads, writes, same_ok=False)
        if saved is not None:
            writes[0].w = saved
        sb.dcnt += 16
        tk = (dk, sb.dcnt)
        sem = self.sem
        ds = sem[dk]

        def emit(E):
            for k, v in waits:
                E.wait_ge(sem[k], v)
            E.dma_start(out=out_ap, in_=(in_ap(E) if callable(in_ap) else in_ap)).then_inc(ds, 16)

        self.q[eng].append(emit)
        self._commit(tk, reads, writes)
        self.final_waits[dk] = sb.dcnt
        return tk

    def collective(self, kind, sb_, s_ap, db_, d_ap, groups):
        dk = self._dsem(db_, "cc")
        waits = self._deps("pool", [sb_], [db_], same_ok=False)
        db_.dcnt += 1
        tk = (dk, db_.dcnt)
        sem = self.sem
        ds = sem[dk]

        def emit(E):
            for k, v in waits:
                E.wait_ge(sem[k], v)
            E.collective_compute(kind, ALU.bypass, replica_groups=groups, ins=[s_ap], outs=[d_ap]).then_inc(ds, 1)

        self.q["pool"].append(emit)
        self._commit(tk, [sb_], [db_])
        self.final_waits[dk] = db_.dcnt
        return tk

    def mm(self, ob, o, lb, l, rb, r, start=True, stop=True):
        return self.op("pe", lambda E: E.matmul(o, lhsT=l, rhs=r, start=start, stop=stop), reads=[lb, rb], writes=[ob])

    def act(self, ob, o, ib, i, func, bias=None, scale=None, accum=None, rd=(), wr=()):
        kw = {}
        if bias is not None:
            kw["bias"] = bias
        if scale is not None:
            kw["scale"] = scale
        if accum is not None:
            kw["accum_out"] = accum
        return self.op("act", lambda E: E.activation(out=o, in_=i, func=func, **kw), reads=[ib] + list(rd), writes=[ob] + list(wr))

    def tt(self, eng, ob, o, ab, a, bb, b, op):
        return self.op(eng, lambda E: E.tensor_tensor(out=o, in0=a, in1=b, op=op), reads=[ab, bb], writes=[ob])

    def ts(self, eng, ob, o, ab, a, s1, s2, op0, op1=None, rd=()):
        if op1 is None:
            return self.op(eng, lambda E: E.tensor_scalar(out=o, in0=a, scalar1=s1, scalar2=None, op0=op0), reads=[ab] + list(rd), writes=[ob])
        return self.op(eng, lambda E: E.tensor_scalar(out=o, in0=a, scalar1=s1, scalar2=s2, op0=op0, op1=op1), reads=[ab] + list(rd), writes=[ob])

    def stt(self, ob, o, ab, a, sc, bb, b, op0, op1, rd=()):
        return self.op("dve", lambda E: E.scalar_tensor_tensor(out=o, in0=a, scalar=sc, in1=b, op0=op0, op1=op1), reads=[ab, bb] + list(rd), writes=[ob])

    def copy(self, eng, ob, o, ib, i):
        if eng == "act":
            return self.op("act", lambda E: E.copy(out=o, in_=i), reads=[ib], writes=[ob])
        return self.op(eng, lambda E: E.tensor_copy(out=o, in_=i), reads=[ib], writes=[ob])

    def memset(self, eng, ob, o, val):
        return self.op(eng, lambda E: E.memset(o, val), writes=[ob])

    def recip(self, ob, o, ib, i):
        return self.op("dve", lambda E: E.reciprocal(out=o, in_=i), reads=[ib], writes=[ob])

    phase_id = 0

    def begin_phase(self):
        self.phase_id += 1
        self.gstack = self.stack if not hasattr(self, "gstack") else self.gstack
        self.pstack = ExitStack()
        self.pstack.__enter__()
        self.alloc_stack = self.pstack

    def end_phase(self):
        sem = self.sem
        cnt = dict(self.cnt)
        fw = dict(self.final_waits)
        for e in ENGS:
            waits = [(o, cnt[o]) for o in ENGS if o != e and cnt[o] > self.waited[e].get(o, 0)]
            waits += [(k, v) for k, v in fw.items() if v > self.waited[e].get(k, 0)]
            for k, v in waits:
                self.waited[e][k] = v

            def emit(E, waits=waits):
                for k, v in waits:
                    E.wait_ge(sem[k], v)
            self.q[e].append(emit)
        self.finish()
        self.q = {e: [] for e in ENGS}
        for e in ENGS:
            self.sem[e] = self.stack.enter_context(self.nc.semaphore("s_%s_%d" % (e, self.phase_id)))
            self.cnt[e] = 0
            for x in ENGS:
                self.waited[x].pop(e, None)
        for b in self.global_bufs:
            if b.w is not None and b.w[0] in ENGS:
                b.w = None
            b.r = {k: v for k, v in b.r.items() if k not in ENGS}
        for b in self.phase_bufs:
            if b.dsem is not None:
                self.free_dsems[self.dsem_cls[b.dsem]].append((b.dsem, b.dcnt))
                b.dsem = None
        self.phase_bufs = []
        self.pstack.__exit__(None, None, None)
        self.alloc_stack = self.stack

    def finish(self):
        nc = self.nc
        sem = self.sem
        q = self.q
        fw = dict(self.final_waits)
        with nc.Block() as block:
            @block.tensor
            def _(E):
                for f in q["pe"]:
                    f(E)

            @block.scalar
            def _(E):
                for f in q["act"]:
                    f(E)

            @block.vector
            def _(E):
                for f in q["dve"]:
                    f(E)

            @block.gpsimd
            def _(E):
                for f in q["pool"]:
                    f(E)

            @block.sync
            def _(E):
                for f in q["sp"]:
                    f(E)
                for k, v in fw.items():
                    E.wait_ge(sem[k], v)


class Ring:
    def __init__(self, bufs):
        self.bufs = bufs
        self.i = 0

    def next(self):
        b = self.bufs[self.i % len(self.bufs)]
        self.i += 1
        return b


def dram(nc, name, shape, dt, kind):
    return nc.dram_tensor(name, list(shape), dt, kind=kind).ap()


def kcp(ap):
    return ap.rearrange("(kc p) n -> p kc n", p=128)


def rms_stats(S, C, xb, nch, n, ps_ring, sq_ring, rstd, tmp, nfeat, eps):
    pss = ps_ring.next()
    for kc in range(nch):
        sq = sq_ring.next()
        S.act(sq, sq[:, :n], xb, xb[:, kc, :n], AF.Square)
        S.mm(pss, pss[:, :n], C["ones"], C["ones"][:], sq, sq[:, :n], start=(kc == 0), stop=(kc == nch - 1))
    S.act(tmp, tmp[:, :n], pss, pss[:, :n], AF.Sqrt, bias=C["eps%g" % eps][:, 0:1], scale=1.0 / nfeat, rd=[C["eps%g" % eps]])
    S.recip(rstd, rstd[:, :n], tmp, tmp[:, :n])


def load_consts(S, nc, cst_ap, extra_eps=()):
    C = {}
    C["ones"] = S.sb("c_ones", [128, 128], F32)
    S.dma("sp", C["ones"][:], cst_ap[0], writes=[C["ones"]])
    for e in (EPS,) + tuple(extra_eps):
        b = S.sb("c_eps%g" % e, [128, 1], F32)
        S.memset("dve", b, b[:], float(e))
        C["eps%g" % e] = b
    return C


def build_tok(mode):
    nc = bass.Bass("TRN2", target_bir_lowering=False)
    A = {}
    A["xT"] = dram(nc, "xT", [DM, NT], F32, "ExternalInput")
    A["g_next"] = dram(nc, "g_next", [128, 8], F32, "ExternalInput")
    A["cst"] = dram(nc, "cst", [4, 128, 128], F32, "ExternalInput")
    if mode != "first":
        A["hT"] = dram(nc, "hT", [DM, NT], BF16, "ExternalInput")
        yT = dram(nc, "yT", [4096, NT], BF16, "ExternalInput")
        A["load_y"] = lambda S, yub, t: S.dma("act", yub[:], kcp(yT[:, t * TT:(t + 1) * TT]), writes=[yub])
        A["w_gate"] = dram(nc, "w_gate", [DM, 3072], F32, "ExternalInput")
        A["b_gate"] = dram(nc, "b_gate", [128, 24], F32, "ExternalInput")
        A["w_br"] = dram(nc, "w_br", [4096, DM], F32, "ExternalInput")
        A["w_out"] = dram(nc, "w_out", [DM, DM], F32, "ExternalInput")
        A["w_up"] = dram(nc, "w_up", [DM, 4096], F32, "ExternalInput")
        A["w_dn"] = dram(nc, "w_dn", [4096, DM], F32, "ExternalInput")
        A["g_mlp"] = dram(nc, "g_mlp", [128, 8], F32, "ExternalInput")
    if mode == "last":
        A["outT"] = dram(nc, "outT", [DM, NT], F32, "ExternalOutput")
    else:
        A["hT_out"] = dram(nc, "hT_out", [DM, NT], BF16, "ExternalOutput")
        A["hT_lo"] = dram(nc, "hT_lo", [DM, NT], BF16, "ExternalOutput")
        if mode == "mid":
            A["xT_out"] = dram(nc, "xT_out", [DM, NT], F32, "ExternalOutput")
    with ExitStack() as st:
        S = Sched(nc, st)
        S.begin_phase()
        emit_tok(nc, S, A, mode)
        S.end_phase()
    return nc


def emit_tok(nc, S, A, mode):
    xT, gn, cst = A["xT"], A["g_next"], A["cst"]
    if mode != "first":
        hT, w_gate, b_gate, w_br, w_out, w_up, w_dn, gm = (A[k] for k in ("hT", "w_gate", "b_gate", "w_br", "w_out", "w_up", "w_dn", "g_mlp"))
    if mode == "last":
        outT = A["outT"]
    else:
        hTo, hTlo = A["hT_out"], A["hT_lo"]
        if mode == "mid":
            xTo = A["xT_out"]
    if True:
        C = load_consts(S, nc, cst)
        gnb = S.sb("gnb", [128, 8], F32)
        S.dma("sp", gnb[:], gn, writes=[gnb])
        xb = S.sb("xb", [128, 8, TT], F32)
        ps_ring = Ring([S.ps("ps%d" % i, [128, TT], F32) for i in range(7)])
        sq_ring = Ring([S.sb("sq%d" % i, [128, TT], F32) for i in range(2)])
        rstd = S.sb("rstd", [128, TT], F32)
        rtmp = S.sb("rtmp", [128, TT], F32)
        hn_ring = Ring([S.sb("hn%d" % i, [128, 8, TT], BF16) for i in range(2)])
        hl_ring = Ring([S.sb("hl%d" % i, [128, 8, TT], BF16) for i in range(2)])
        h32_ring = Ring([S.sb("h32_%d" % i, [128, TT], F32) for i in range(2)])
        if mode == "last":
            oc_ring = Ring([S.sb("oc%d" % i, [128, TT], F32) for i in range(3)])
        if mode != "first":
            bgb = S.sb("bgb", [128, 24], F32)
            S.dma("sp", bgb[:], b_gate, writes=[bgb])
            gmb = S.sb("gmb", [128, 8], F32)
            S.dma("sp", gmb[:], gm, writes=[gmb])
            hb_ring = Ring([S.sb("hb%d" % i, [128, 8, TT], BF16) for i in range(2)])
            yub = S.sb("yub", [128, 32, TT], BF16)
            mixed = S.sb("mixed", [128, 8, TT], BF16)
            h2 = S.sb("h2", [128, 8, TT], BF16)
            g_ring = Ring([S.sb("g%d" % i, [128, TT], F32) for i in range(6)])
            acc_ring = Ring([S.sb("acc%d" % i, [128, TT], F32) for i in range(2)])
            tmp_ring = Ring([S.sb("tmp%d" % i, [128, TT], F32) for i in range(2)])
            r_ring = Ring([S.sb("r%d" % i, [128, TT], F32) for i in range(2)])
            w_ring = Ring([S.sb("w%d" % i, [128, 7168], BF16) for i in range(3)])

        for t in range(NT // TT):
            ts_ = slice(t * TT, (t + 1) * TT)
            S.dma("sp", xb[:], kcp(xT[:, ts_]), writes=[xb])
            if mode != "first":
                hb = hb_ring.next()
                S.dma("sp", hb[:], kcp(hT[:, ts_]), writes=[hb])
                A["load_y"](S, yub, t)
                for oc in range(8):
                    w = w_ring.next()
                    wg = w[:, 0:3072].rearrange("p (kc br j) -> p kc br j", kc=8, br=3)
                    wb = w[:, 3072:7168].rearrange("p (kc j) -> p kc j", kc=32)
                    for br in range(3):
                        c0 = br * 1024 + oc * 128
                        S.dma("pool", wg[:, :, br, :], kcp(w_gate[:, c0:c0 + 128]), writes=[w], group=(br > 0))
                    S.dma("pool", wb, kcp(w_br[:, oc * 128:(oc + 1) * 128]), writes=[w], group=True)
                    gts = []
                    for br in range(3):
                        pg = ps_ring.next()
                        for kc in range(8):
                            S.mm(pg, pg[:], w, wg[:, kc, br, :], hb, hb[:, kc, :], start=(kc == 0), stop=(kc == 7))
                        g = g_ring.next()
                        ch = br * 8 + oc
                        S.act(g, g[:], pg, pg[:], AF.Sigmoid, bias=bgb[:, ch:ch + 1], rd=[bgb])
                        gts.append(g)
                    acc = acc_ring.next()
                    koff = (0, 8, 24)
                    nk = (8, 16, 8)
                    for br in range(3):
                        pb = ps_ring.next()
                        for kc in range(nk[br]):
                            S.mm(pb, pb[:], w, wb[:, koff[br] + kc, :], yub, yub[:, koff[br] + kc, :], start=(kc == 0), stop=(kc == nk[br] - 1))
                        if br == 0:
                            S.tt("dve", acc, acc[:], pb, pb[:], gts[0], gts[0][:], ALU.mult)
                        else:
                            tmp = tmp_ring.next()
                            S.tt("dve", tmp, tmp[:], pb, pb[:], gts[br], gts[br][:], ALU.mult)
                            if br == 1:
                                S.tt("dve", acc, acc[:], acc, acc[:], tmp, tmp[:], ALU.add)
                            else:
                                S.tt("dve", mixed, mixed[:, oc, :], acc, acc[:], tmp, tmp[:], ALU.add)
                for half in range(2):
                    w = w_ring.next()
                    wo = w[:, 0:4096].rearrange("p (kc j) -> p kc j", kc=8)
                    S.dma("pool", wo, kcp(w_out[:, half * 512:(half + 1) * 512]), writes=[w])
                    for o4 in range(4):
                        oc = half * 4 + o4
                        po = ps_ring.next()
                        for kc in range(8):
                            S.mm(po, po[:], w, wo[:, kc, o4 * 128:(o4 + 1) * 128], mixed, mixed[:, kc, :], start=(kc == 0), stop=(kc == 7))
                        S.tt("dve", xb, xb[:, oc, :], xb, xb[:, oc, :], po, po[:], ALU.add)
                rms_stats(S, C, xb, 8, TT, ps_ring, sq_ring, rstd, rtmp, DM, EPS)
                for kc in range(8):
                    S.stt(h2, h2[:, kc, :], xb, xb[:, kc, :], gmb[:, kc:kc + 1], rstd, rstd[:], ALU.mult, ALU.mult, rd=[gmb])
                for o8 in range(8):
                    w = w_ring.next()
                    wu = w[:, 0:4096].rearrange("p (kc j) -> p kc j", kc=8)
                    S.dma("pool", wu, kcp(w_up[:, o8 * 512:(o8 + 1) * 512]), writes=[w])
                    for o4 in range(4):
                        oc = o8 * 4 + o4
                        pu = ps_ring.next()
                        for kc in range(8):
                            S.mm(pu, pu[:], w, wu[:, kc, o4 * 128:(o4 + 1) * 128], h2, h2[:, kc, :], start=(kc == 0), stop=(kc == 7))
                        r = r_ring.next()
                        S.act(r, r[:], pu, pu[:], AF.Relu)
                        S.tt("dve", yub, yub[:, oc, :], r, r[:], r, r[:], ALU.mult)
                for oc in range(8):
                    w = w_ring.next()
                    wd = w[:, 0:4096].rearrange("p (kc j) -> p kc j", kc=32)
                    S.dma("pool", wd, kcp(w_dn[:, oc * 128:(oc + 1) * 128]), writes=[w])
                    pd = ps_ring.next()
                    for kc in range(32):
                        S.mm(pd, pd[:], w, wd[:, kc, :], yub, yub[:, kc, :], start=(kc == 0), stop=(kc == 31))
                    S.tt("dve", xb, xb[:, oc, :], xb, xb[:, oc, :], pd, pd[:], ALU.add)
                if mode == "mid":
                    S.dma("sp", kcp(xTo[:, ts_]), xb[:], reads=[xb])
            rms_stats(S, C, xb, 8, TT, ps_ring, sq_ring, rstd, rtmp, DM, EPS)
            if mode == "last":
                for kc in range(8):
                    o = oc_ring.next()
                    S.stt(o, o[:], xb, xb[:, kc, :], gnb[:, kc:kc + 1], rstd, rstd[:], ALU.mult, ALU.mult, rd=[gnb])
                    S.dma("sp", outT[kc * 128:(kc + 1) * 128, ts_], o[:], reads=[o])
            else:
                hn = hn_ring.next()
                hl = hl_ring.next()
                for kc in range(8):
                    h32 = h32_ring.next()
                    S.stt(h32, h32[:], xb, xb[:, kc, :], gnb[:, kc:kc + 1], rstd, rstd[:], ALU.mult, ALU.mult, rd=[gnb])
                    S.copy("act", hn, hn[:, kc, :], h32, h32[:])
                    S.tt("pool", hl, hl[:, kc, :], h32, h32[:], hn, hn[:, kc, :], ALU.subtract)
                S.dma("sp", kcp(hTo[:, ts_]), hn[:], reads=[hn], writes=A.get("h_wr", []), sembuf=hn)
                S.dma("sp", kcp(hTlo[:, ts_]), hl[:], reads=[hl], writes=A.get("h_wr", []), sembuf=hl)


def make_consts():
    c = np.zeros((4, 128, 128), np.float32)
    c[0] = 1.0
    c[1] = np.triu(np.ones((128, 128), np.float32))
    c[2] = 1.0 - c[1]
    c[3] = np.eye(128, dtype=np.float32)
    return c


def pvec(v):
    v = np.asarray(v)
    return np.ascontiguousarray(v.reshape(-1, 128).T)


def small_views(S, name, nbanks, width):
    out = []
    per = 512 // width
    banks = [S.ps("%s%d" % (name, i), [128, 512], F32) for i in range(nbanks)]
    for j in range(per):
        for i in range(nbanks):
            out.append(VBuf("%s%d_%d" % (name, i, j), banks[i], banks[i].t[:, j * width:(j + 1) * width]))
    return out


def build_gla(ntiles=SEQ // TT):
    nc = bass.Bass("TRN2", target_bir_lowering=False)
    hT = dram(nc, "hT", [DM, SEQ], BF16, "ExternalInput")
    hTl = dram(nc, "hTlo", [DM, SEQ], BF16, "ExternalInput")
    A = {}
    A["w_gla"] = dram(nc, "w_gla", [DM, 784], F32, "ExternalInput")
    A["wgk2"] = dram(nc, "wgk2", [16, 128], F32, "ExternalInput")
    A["bgk"] = dram(nc, "bgk", [1, 128], F32, "ExternalInput")
    A["ng"] = dram(nc, "ng", [128, 2], F32, "ExternalInput")
    A["cst"] = dram(nc, "cst", [4, 128, 128], F32, "ExternalInput")
    yT = dram(nc, "yT", [256, SEQ], BF16, "ExternalOutput")
    A["load_h"] = lambda S, h, t: S.dma("sp", h[:], kcp(hT[:, t * TT:(t + 1) * TT]), writes=[h])
    A["load_hlo"] = lambda S, h, t: S.dma("act", h[:], kcp(hTl[:, t * TT:(t + 1) * TT]), writes=[h])
    A["store_y"] = lambda S, yo, t: S.dma("sp", yT[:, t * TT:(t + 1) * TT].rearrange("(ec p) n -> p ec n", p=128), yo[:], reads=[yo])
    with ExitStack() as st:
        S = Sched(nc, st)
        S.begin_phase()
        emit_gla(nc, S, A, ntiles)
        S.end_phase()
    return nc


def emit_gla(nc, S, A, ntiles=SEQ // TT):
    w_gla, wgk2, bgk, ngp, cst = A["w_gla"], A["wgk2"], A["bgk"], A["ng"], A["cst"]
    if True:
        C = load_consts(S, nc, cst)
        U = S.sb("U", [128, 128], F32)
        UC = S.sb("UC", [128, 128], F32)
        S.dma("sp", U[:], cst[1], writes=[U])
        S.dma("sp", UC[:], cst[2], writes=[UC])
        wa = S.sb("wa", [128, 8, 784], BF16)
        S.dma("pool", wa[:], kcp(w_gla), writes=[wa])
        wqk32 = S.sb("wqk32", [128, 8, 256], F32)
        S.dma("sp", wqk32[:, :, 0:128], kcp(w_gla[:, 0:128]), writes=[wqk32])
        S.dma("sp", wqk32[:, :, 128:256], kcp(w_gla[:, 384:512]), writes=[wqk32], group=True)
        wqk_hi = S.sb("wqk_hi", [128, 8, 256], BF16)
        wqk_lo = S.sb("wqk_lo", [128, 8, 256], BF16)
        S.copy("act", wqk_hi, wqk_hi[:], wqk32, wqk32[:])
        S.tt("dve", wqk_lo, wqk_lo[:], wqk32, wqk32[:], wqk_hi, wqk_hi[:], ALU.subtract)
        w2 = S.sb("w2", [16, 128], F32)
        S.dma("sp", w2[:], wgk2, writes=[w2])
        bgb = S.sb("bgb", [128, 128], F32)
        S.dma("sp", bgb[:], bgk.partition_broadcast(128), writes=[bgb])
        ng = S.sb("ng", [128, 2], F32)
        S.dma("sp", ng[:], ngp, writes=[ng])
        h_ring = Ring([S.sb("h%d" % i, [128, 8, TT], BF16) for i in range(2)])
        hlo_ring = Ring([S.sb("hlo%d" % i, [128, 8, TT], BF16) for i in range(2)])
        big = Ring([S.ps("pb%d" % i, [128, 512], F32) for i in range(2)])
        po = [S.ps("po%d" % i, [128, 512], F32) for i in range(2)]
        pss_ring = Ring([S.ps("pss", [128, 512], F32)])
        sm = Ring(small_views(S, "sm", 3, 256))
        qTs = S.sb("qTs", [128, TT], F32)
        kTs = S.sb("kTs", [128, TT], F32)
        sg = S.sb("sg", [128, 2, TT], F32)
        gkl = S.sb("gkl", [16, TT], F32)

        def ring(name, shape, dt, n=2):
            return Ring([S.sb("%s%d" % (name, i), shape, dt) for i in range(n)])
        ktok_r = ring("ktok", [128, 128], F32)
        vtok_r = ring("vtok", [128, 256], BF16)
        t1_r = ring("t1", [128, 128], F32)
        e_r = ring("e", [128, 128], F32)
        gk_r = ring("gk", [128, 128], F32)
        ebT_r = ring("ebT", [128, 128], F32)
        enbT_r = ring("enbT", [128, 128], F32)
        ed2_r = ring("ed2", [128, 128], F32)
        qp_r = ring("qp", [128, 128], BF16)
        qp32_r = ring("qp32", [128, 128], F32)
        kp32_r = ring("kp32", [128, 128], F32)
        kpp_r = ring("kpp", [128, 128], BF16)
        AT_r = ring("AT", [128, 128], BF16)
        Sst = S.sb("Sst", [128, 256], F32)
        Sbf = S.sb("Sbf", [128, 256], BF16)
        S.memset("dve", Sst, Sst[:], 0.0)
        S.memset("dve", Sbf, Sbf[:], 0.0)
        sq_ring = ring("sq", [128, TT], F32)
        rstd = S.sb("rstd", [128, TT], F32)
        rtmp = S.sb("rtmp", [128, TT], F32)
        tmp_r = ring("tmp", [128, TT], F32)
        yo_r = ring("yo", [128, 2, TT], BF16)

        for t in range(ntiles):
            ts_ = slice(t * TT, (t + 1) * TT)
            h = h_ring.next()
            A["load_h"](S, h, t)

            def proj(c0, m):
                p = big.next()
                for kc in range(8):
                    S.mm(p, p[:m, :], wa, wa[:, kc, c0:c0 + m], h, h[:, kc, :], start=(kc == 0), stop=(kc == 7))
                return p
            hlo = hlo_ring.next()
            A["load_hlo"](S, hlo, t)

            def proj3(c0):
                p = big.next()
                n = 0
                for (wb_, hb_) in ((wqk_hi, h), (wqk_hi, hlo), (wqk_lo, h)):
                    for kc in range(8):
                        S.mm(p, p[:], wb_, wb_[:, kc, c0:c0 + 128], hb_, hb_[:, kc, :], start=(n == 0), stop=(n == 23))
                        n += 1
                return p
            p = proj3(0)
            S.act(qTs, qTs[:], p, p[:], AF.Copy, scale=128.0 ** -0.5)
            p = proj3(128)
            S.act(kTs, kTs[:], p, p[:], AF.Copy)
            for ec in range(2):
                p = proj(128 + ec * 128, 128)
                S.act(sg, sg[:, ec, :], p, p[:], AF.Silu)
            p = proj(768, 16)
            S.copy("dve", gkl, gkl[:], p, p[:16, :])
            def stageA(c):
                cs = slice(c * 128, (c + 1) * 128)
                p = big.next()
                for kc in range(8):
                    S.mm(p, p[:, :384], h, h[:, kc, cs], wa, wa[:, kc, 384:768], start=(kc == 0), stop=(kc == 7))
                ktok = ktok_r.next()
                vtok = vtok_r.next()
                S.copy("act", ktok, ktok[:], p, p[:, 0:128])
                S.copy("act", vtok, vtok[:], p, p[:, 128:384])
                pg = sm.next()
                S.mm(pg, pg[:, :128], gkl, gkl[:, cs], w2, w2[:], start=True, stop=True)
                t1 = t1_r.next()
                S.tt("dve", t1, t1[:], pg, pg[:, :128], bgb, bgb[:], ALU.add)
                e = e_r.next()
                S.act(e, e[:], t1, t1[:], AF.Exp, scale=-1.0)
                S.act(e, e[:], e, e[:], AF.Ln, bias=1.0)
                gk = gk_r.next()
                S.ts("dve", gk, gk[:], e, e[:], -1.0 / 16.0, None, ALU.mult)
                pbT = sm.next()
                S.mm(pbT, pbT[:, :128], gk, gk[:], U, U[:])
                pd2 = sm.next()
                S.mm(pd2, pd2[:, :128], UC, UC[:], gk, gk[:])
                ebT = ebT_r.next()
                enbT = enbT_r.next()
                ed2 = ed2_r.next()
                S.act(ebT, ebT[:], pbT, pbT[:, :128], AF.Exp)
                S.act(enbT, enbT[:], pbT, pbT[:, :128], AF.Exp, scale=-1.0)
                S.act(ed2, ed2[:], pd2, pd2[:, :128], AF.Exp)
                qp = qp_r.next()
                qp32 = qp32_r.next()
                kp32 = kp32_r.next()
                kpp = kpp_r.next()
                S.tt("dve", qp32, qp32[:], qTs, qTs[:, cs], ebT, ebT[:], ALU.mult)
                S.tt("dve", kp32, kp32[:], kTs, kTs[:, cs], enbT, enbT[:], ALU.mult)
                S.copy("act", qp, qp[:], qp32, qp32[:])
                S.tt("pool", kpp, kpp[:], ktok, ktok[:], ed2, ed2[:], ALU.mult)
                pA = sm.next()
                S.mm(pA, pA[:, :128], kp32, kp32[:], qp32, qp32[:])
                AT = AT_r.next()
                S.tt("dve", AT, AT[:], pA, pA[:, :128], U, U[:], ALU.mult)
                return dict(cs=cs, vtok=vtok, AT=AT, qp=qp, kpp=kpp, ebT=ebT)

            def stageB(c, v):
                cs, vtok, AT, qp, kpp, ebT = (v[k] for k in ('cs', 'vtok', 'AT', 'qp', 'kpp', 'ebT'))
                for ec in range(2):
                    es = slice(ec * 128, (ec + 1) * 128)
                    S.mm(po[ec], po[ec][:, cs], vtok, vtok[:, es], AT, AT[:], start=True, stop=False)
                    S.mm(po[ec], po[ec][:, cs], Sbf, Sbf[:, es], qp, qp[:], start=False, stop=True)
                pkv = sm.next()
                S.mm(pkv, pkv[:, :256], kpp, kpp[:], vtok, vtok[:])
                S.stt(Sst, Sst[:], Sst, Sst[:], ebT[:, 127:128], pkv, pkv[:, :256], ALU.mult, ALU.add, rd=[ebT])
                S.copy("act", Sbf, Sbf[:], Sst, Sst[:])

            va = {0: stageA(0)}
            for c in range(4):
                if c + 1 < 4:
                    va[c + 1] = stageA(c + 1)
                stageB(c, va.pop(c))
            pss = pss_ring.next()
            for ec in range(2):
                sq = sq_ring.next()
                S.act(sq, sq[:], po[ec], po[ec][:], AF.Square)
                S.mm(pss, pss[:], C["ones"], C["ones"][:], sq, sq[:], start=(ec == 0), stop=(ec == 1))
            S.act(rtmp, rtmp[:], pss, pss[:], AF.Sqrt, bias=C["eps%g" % EPS][:, 0:1], scale=1.0 / 256, rd=[C["eps%g" % EPS]])
            S.recip(rstd, rstd[:], rtmp, rtmp[:])
            yo = yo_r.next()
            for ec in range(2):
                tmp = tmp_r.next()
                S.stt(tmp, tmp[:], po[ec], po[ec][:], ng[:, ec:ec + 1], rstd, rstd[:], ALU.mult, ALU.mult, rd=[ng])
                S.tt("pool", yo, yo[:, ec, :], tmp, tmp[:], sg, sg[:, ec, :], ALU.mult)
            A["store_y"](S, yo, t)


def gla_inputs(inp, l, g, hT_b, cst, hTlo_b=None):
    w = inp["w_in"][l]
    cols = np.concatenate([np.arange(g * 128, (g + 1) * 128),
                           2064 + np.arange(g * 256, (g + 1) * 256),
                           512 + np.arange(g * 128, (g + 1) * 128),
                           1024 + np.arange(g * 256, (g + 1) * 256),
                           2048 + np.arange(16)])
    if hTlo_b is None:
        hTlo_b = np.zeros_like(hT_b)
    elif isinstance(hTlo_b, int):
        hTlo_b = None
    return {"hT": hT_b, "hTlo": hTlo_b, "w_gla": np.ascontiguousarray(w[:, cols]),
            "wgk2": np.ascontiguousarray(inp["gla_w_gk2"][l][:, g * 128:(g + 1) * 128]),
            "bgk": np.ascontiguousarray(inp["gla_b_gk"][l][g * 128:(g + 1) * 128].reshape(1, 128)),
            "ng": pvec(inp["gla_norm_g"][l]), "cst": cst}


def build_ssm(ntiles=SEQ // TT):
    nc = bass.Bass("TRN2", target_bir_lowering=False)
    hT = dram(nc, "hT", [DM, SEQ], BF16, "ExternalInput")
    A = {}
    A["w_ssm"] = dram(nc, "w_ssm", [DM, 1288], F32, "ExternalInput")
    A["cw"] = dram(nc, "cw", [128, 6, 4], F32, "ExternalInput")
    A["cb"] = dram(nc, "cb", [128, 6], F32, "ExternalInput")
    A["dtb"] = dram(nc, "dtb", [1, 8], F32, "ExternalInput")
    A["alog"] = dram(nc, "alog", [1, 8], F32, "ExternalInput")
    A["dsk"] = dram(nc, "dsk", [1, 8], F32, "ExternalInput")
    A["ngs"] = dram(nc, "ngs", [1, 512], F32, "ExternalInput")
    A["cst"] = dram(nc, "cst", [4, 128, 128], F32, "ExternalInput")
    yT = dram(nc, "yT", [512, SEQ], BF16, "ExternalOutput")
    A["load_h"] = lambda S, h, t: S.dma("sp", h[:], kcp(hT[:, t * TT:(t + 1) * TT]), writes=[h])
    A["store_y"] = lambda S, yo, t: S.dma("sp", yT[:, t * TT:(t + 1) * TT].rearrange("(c p) n -> p c n", p=128), yo[:], reads=[yo])
    with ExitStack() as st:
        S = Sched(nc, st)
        S.begin_phase()
        emit_ssm(nc, S, A, ntiles)
        S.end_phase()
    return nc


def emit_ssm(nc, S, A, ntiles=SEQ // TT):
    w_ssm, cwp, cbp, dtbp, alogp, dskp, ngp, cst = (A[k] for k in ("w_ssm", "cw", "cb", "dtb", "alog", "dsk", "ngs", "cst"))
    if True:
        C = load_consts(S, nc, cst)
        U = S.sb("U", [128, 128], F32)
        UC = S.sb("UC", [128, 128], F32)
        idf = S.sb("idf", [128, 128], F32)
        idb = S.sb("idb", [128, 128], BF16)
        S.dma("sp", U[:], cst[1], writes=[U])
        S.dma("sp", UC[:], cst[2], writes=[UC])
        S.dma("sp", idf[:], cst[3], writes=[idf])
        S.dma("pool", idb[:], cst[3], writes=[idb])
        ws = S.sb("ws", [128, 8, 1288], BF16)
        S.dma("pool", ws[:], kcp(w_ssm), writes=[ws])
        cw = S.sb("cw", [128, 6, 4], F32)
        cb = S.sb("cb", [128, 6], F32)
        S.dma("sp", cw[:], cwp, writes=[cw])
        S.dma("sp", cb[:], cbp, writes=[cb])
        dtb = S.sb("dtb", [128, 8], F32)
        a_b = S.sb("a_b", [128, 8], F32)
        dsk = S.sb("dsk", [128, 8], F32)
        ngs = S.sb("ngs", [128, 512], F32)
        S.dma("sp", dtb[:], dtbp.partition_broadcast(128), writes=[dtb])
        S.dma("sp", a_b[:], alogp.partition_broadcast(128), writes=[a_b])
        S.dma("sp", dsk[:], dskp.partition_broadcast(128), writes=[dsk])
        S.dma("sp", ngs[:], ngp.partition_broadcast(128), writes=[ngs])
        S.act(a_b, a_b[:], a_b, a_b[:], AF.Exp)
        S.ts("dve", a_b, a_b[:], a_b, a_b[:], -1.0, None, ALU.mult)

        def ring(name, shape, dt, n=2):
            return Ring([S.sb("%s%d" % (name, i), shape, dt) for i in range(n)])
        h_ring = ring("h", [128, 8, TT], BF16)
        big = Ring([S.ps("pb%d" % i, [128, 512], F32) for i in range(6)])
        sm = Ring(small_views(S, "sm", 2, 128))
        raw = S.sb("raw", [128, 6, TT + 3], F32)
        S.memset("dve", raw, raw[:, :, 0:3], 0.0)
        cacc_r = ring("cacc", [128, TT], F32)
        xc = S.sb("xc", [128, 4, TT], F32)
        BT = S.sb("BT", [128, TT], BF16)
        CT = S.sb("CT", [128, TT], BF16)
        sz_r = ring("sz", [128, 512], F32)
        t8_r = ring("t8", [128, 8], F32)
        dt_r = ring("dt", [128, 8], F32)
        da_r = ring("da", [128, 8], F32)
        xdt_r = ring("xdt", [128, 512], BF16)
        xD_r = ring("xD", [128, 512], F32)
        Btok_r = ring("Btok", [128, 128], BF16)
        rda_r = ring("rda", [128, 8, 128], F32)
        eL_r = ring("eL", [128, 8, 128], F32)
        GU_r = ring("GU", [128, 128], F32)
        M_r = ring("M", [128, 8, 128], BF16)
        cs_r = ring("cs", [128, 16], F32)
        ec_r = ring("ec", [128, 24], F32)
        xdtd_r = ring("xdtd", [128, 512], BF16)
        Sst = S.sb("Sst", [128, 512], F32)
        Sbf = S.sb("Sbf", [128, 512], BF16)
        S.memset("dve", Sst, Sst[:], 0.0)
        S.memset("dve", Sbf, Sbf[:], 0.0)
        y1_r = ring("y1", [128, 512], F32)
        y2_r = ring("y2", [128, 512], F32)
        junk = S.sb("junk", [128, 512], F32)
        ss_r = ring("ss", [128, 2], F32)
        yn_r = ring("yn", [128, 512], BF16)
        yo_r = ring("yo", [128, 4, TT], BF16)
        eps = C["eps%g" % EPS]

        def b8(ap):
            return ap.unsqueeze(2).to_broadcast([128, 8, 64])

        def v8(ap):
            return ap.rearrange("p (h q) -> p h q", h=8)

        for t in range(ntiles):
            ts_ = slice(t * TT, (t + 1) * TT)
            h = h_ring.next()
            A["load_h"](S, h, t)
            for ch in range(6):
                p = big.next()
                for kc in range(8):
                    S.mm(p, p[:], ws, ws[:, kc, 512 + ch * 128:640 + ch * 128], h, h[:, kc, :], start=(kc == 0), stop=(kc == 7))
                S.act(raw, raw[:, ch, 3:TT + 3], p, p[:], AF.Copy)
            for ch in range(6):
                acc = cacc_r.next()
                S.ts("dve", acc, acc[:], raw, raw[:, ch, 0:TT], cw[:, ch, 0:1], cb[:, ch:ch + 1], ALU.mult, ALU.add, rd=[cw, cb])
                for i in range(1, 4):
                    S.stt(acc, acc[:], raw, raw[:, ch, i:i + TT], cw[:, ch, i:i + 1], acc, acc[:], ALU.mult, ALU.add, rd=[cw])
                if ch < 4:
                    S.act(xc, xc[:, ch, :], acc, acc[:], AF.Silu)
                elif ch == 4:
                    S.act(BT, BT[:], acc, acc[:], AF.Silu)
                else:
                    S.act(CT, CT[:], acc, acc[:], AF.Silu)
            S.copy("pool", raw, raw[:, :, 0:3], raw, raw[:, :, TT:TT + 3])
            yo = yo_r.next()
            def stageA(c):
                cs = slice(c * 128, (c + 1) * 128)
                pz = big.next()
                for kc in range(8):
                    S.mm(pz, pz[:], h, h[:, kc, cs], ws, ws[:, kc, 0:512], start=(kc == 0), stop=(kc == 7))
                sz = sz_r.next()
                S.act(sz, sz[:], pz, pz[:], AF.Silu)
                pdt = sm.next()
                for kc in range(8):
                    S.mm(pdt, pdt[:, 0:8], h, h[:, kc, cs], ws, ws[:, kc, 1280:1288], start=(kc == 0), stop=(kc == 7))
                t8 = t8_r.next()
                S.tt("dve", t8, t8[:], pdt, pdt[:, 0:8], dtb, dtb[:], ALU.add)
                S.act(t8, t8[:], t8, t8[:], AF.Exp)
                dt = dt_r.next()
                S.act(dt, dt[:], t8, t8[:], AF.Ln, bias=1.0)
                da = da_r.next()
                S.tt("dve", da, da[:], dt, dt[:], a_b, a_b[:], ALU.mult)
                px = big.next()
                for ch in range(4):
                    S.mm(px, px[:, ch * 128:(ch + 1) * 128], xc, xc[:, ch, cs], idf, idf[:])
                xdt = xdt_r.next()
                xD = xD_r.next()
                S.tt("dve", xdt, v8(xdt[:]), px, v8(px[:]), dt, b8(dt[:]), ALU.mult)
                S.tt("dve", xD, v8(xD[:]), px, v8(px[:]), dsk, b8(dsk[:]), ALU.mult)
                pB = sm.next()
                S.mm(pB, pB[:], BT, BT[:, cs], idb, idb[:])
                Btok = Btok_r.next()
                S.copy("act", Btok, Btok[:], pB, pB[:])
                rda = rda_r.next()
                S.tt("pool", rda, rda[:], U, U[:].unsqueeze(1).to_broadcast([128, 8, 128]), da, da[:].unsqueeze(2).to_broadcast([128, 8, 128]), ALU.mult)
                pD = [big.next(), big.next()]
                for hf in range(2):
                    S.mm(pD[hf], pD[hf][:], UC, UC[:], rda, rda[:, hf * 4:(hf + 1) * 4, :].rearrange("p a b -> p (a b)"))
                pc = sm.next()
                S.mm(pc, pc[:, 0:8], U, U[:], da, da[:])
                pc2 = sm.next()
                S.mm(pc2, pc2[:, 0:8], C["ones"], C["ones"][:], da, da[:])
                csb = cs_r.next()
                S.copy("dve", csb, csb[:, 0:8], pc, pc[:, 0:8])
                S.copy("dve", csb, csb[:, 8:16], pc2, pc2[:, 0:8])
                ec = ec_r.next()
                S.act(ec, ec[:, 0:16], csb, csb[:, 0:16], AF.Exp)
                S.tt("dve", csb, csb[:, 0:8], csb, csb[:, 8:16], csb, csb[:, 0:8], ALU.subtract)
                S.act(ec, ec[:, 16:24], csb, csb[:, 0:8], AF.Exp)
                eL = eL_r.next()
                for hf in range(2):
                    S.act(eL, eL[:, hf * 4:(hf + 1) * 4, :].rearrange("p a b -> p (a b)"), pD[hf], pD[hf][:], AF.Exp)
                pG = sm.next()
                S.mm(pG, pG[:], BT, BT[:, cs], CT, CT[:, cs])
                GU = GU_r.next()
                S.tt("dve", GU, GU[:], pG, pG[:], U, U[:], ALU.mult)
                M = M_r.next()
                S.tt("pool", M, M[:], eL, eL[:], GU, GU[:].unsqueeze(1).to_broadcast([128, 8, 128]), ALU.mult)
                xdtd = xdtd_r.next()
                S.tt("pool", xdtd, v8(xdtd[:]), xdt, v8(xdt[:]), ec, b8(ec[:, 16:24]), ALU.mult)
                return dict(cs=cs, sz=sz, xdt=xdt, xD=xD, Btok=Btok, M=M, ec=ec, xdtd=xdtd)

            def stageB(c, v):
                cs, sz, xdt, xD, Btok, M, ec, xdtd = (v[k] for k in ('cs', 'sz', 'xdt', 'xD', 'Btok', 'M', 'ec', 'xdtd'))
                pyd = big.next()
                for hh in range(8):
                    S.mm(pyd, pyd[:, hh * 64:(hh + 1) * 64], M, M[:, hh, :], xdt, xdt[:, hh * 64:(hh + 1) * 64])
                pyo = big.next()
                S.mm(pyo, pyo[:], CT, CT[:, cs], Sbf, Sbf[:])
                pst = big.next()
                S.mm(pst, pst[:], Btok, Btok[:], xdtd, xdtd[:])
                S.tt("dve", Sst, v8(Sst[:]), Sst, v8(Sst[:]), ec, b8(ec[:, 8:16]), ALU.mult)
                S.tt("dve", Sst, Sst[:], Sst, Sst[:], pst, pst[:], ALU.add)
                S.copy("act", Sbf, Sbf[:], Sst, Sst[:])
                y1 = y1_r.next()
                S.tt("dve", y1, v8(y1[:]), pyo, v8(pyo[:]), ec, b8(ec[:, 0:8]), ALU.mult)
                S.tt("dve", y1, y1[:], y1, y1[:], pyd, pyd[:], ALU.add)
                y2 = y2_r.next()
                S.tt("pool", y2, y2[:], y1, y1[:], xD, xD[:], ALU.add)
                S.tt("pool", y2, y2[:], y2, y2[:], sz, sz[:], ALU.mult)
                ss = ss_r.next()
                S.act(junk, junk[:], y2, y2[:], AF.Square, accum=ss[:, 0:1], wr=[ss])
                S.act(ss, ss[:, 1:2], ss, ss[:, 0:1], AF.Sqrt, bias=eps[:, 0:1], scale=1.0 / 512, rd=[eps])
                S.recip(ss, ss[:, 0:1], ss, ss[:, 1:2])
                yn = yn_r.next()
                S.stt(yn, yn[:], y2, y2[:], ss[:, 0:1], ngs, ngs[:], ALU.mult, ALU.mult, rd=[ss])
                pT = big.next()
                for ch in range(4):
                    S.mm(pT, pT[:, ch * 128:(ch + 1) * 128], yn, yn[:, ch * 128:(ch + 1) * 128], idb, idb[:])
                S.copy("act", yo, yo[:, :, cs], pT, pT[:].rearrange("p (c n) -> p c n", c=4))

            va = {0: stageA(0)}
            for c in range(4):
                if c + 1 < 4:
                    va[c + 1] = stageA(c + 1)
                stageB(c, va.pop(c))
            A["store_y"](S, yo, t)


def ssm_inputs(inp, l, g, hT_b, cst):
    w = inp["w_in"][l]
    xcols = 5136 + np.arange(g * 512, (g + 1) * 512)
    bcols = 7184 + np.arange(g * 128, (g + 1) * 128)
    ccols = 7696 + np.arange(g * 128, (g + 1) * 128)
    cols = np.concatenate([3088 + np.arange(g * 512, (g + 1) * 512), xcols, bcols, ccols, 8208 + np.arange(g * 8, (g + 1) * 8)])
    cc = np.concatenate([xcols, bcols, ccols]) - 5136
    cwv = inp["ssm_conv_w"][l][:, cc]
    cw = np.ascontiguousarray(cwv.reshape(4, 6, 128).transpose(2, 1, 0))
    cb = np.ascontiguousarray(inp["ssm_conv_b"][l][cc].reshape(6, 128).T)
    hs = slice(g * 8, (g + 1) * 8)
    return {"hT": hT_b, "w_ssm": np.ascontiguousarray(w[:, cols]), "cw": cw, "cb": cb,
            "dtb": np.ascontiguousarray(inp["ssm_dt_bias"][l][hs].reshape(1, 8)),
            "alog": np.ascontiguousarray(inp["ssm_a_log"][l][hs].reshape(1, 8)),
            "dsk": np.ascontiguousarray(inp["ssm_d"][l][hs].reshape(1, 8)),
            "ngs": np.ascontiguousarray(inp["ssm_norm_g"][l][g * 512:(g + 1) * 512].reshape(1, 512)),
            "cst": cst}


C1_2PI = 6.28125
C2_2PI = 2.0 * math.pi - 6.28125


def build_diff(l, ntiles=SEQ // TT):
    nc = bass.Bass("TRN2", target_bir_lowering=False)
    hT = dram(nc, "hT", [DM, SEQ], BF16, "ExternalInput")
    A = {}
    A["w_diff"] = dram(nc, "w_diff", [DM, 1280], F32, "ExternalInput")
    A["pos"] = dram(nc, "pos", [1, SEQ], I32, "ExternalInput")
    A["invf"] = dram(nc, "invf", [128, 2], F32, "ExternalInput")
    A["lqk"] = dram(nc, "lqk", [4, 64], F32, "ExternalInput")
    A["ngd"] = dram(nc, "ngd", [1, 128], F32, "ExternalInput")
    A["cst"] = dram(nc, "cst", [4, 128, 128], F32, "ExternalInput")
    yT = dram(nc, "yT", [256, SEQ], BF16, "ExternalOutput")
    A["load_h"] = lambda S, h, t: S.dma("sp", h[:], kcp(hT[:, t * TT:(t + 1) * TT]), writes=[h])
    A["store_y"] = lambda S, yo, t: S.dma("sp", yT[:, t * TT:(t + 1) * TT].rearrange("(c p) n -> p c n", p=128), yo[:], reads=[yo])
    with ExitStack() as st:
        S = Sched(nc, st)
        S.begin_phase()
        emit_diff(nc, S, A, l, ntiles)
        S.end_phase()
    return nc


def emit_diff(nc, S, A, l, ntiles=SEQ // TT):
    lambda_init = 0.8 - 0.6 * math.exp(-0.3 * l)
    w_diff, posd, invfp, lqk, ngp, cst = (A[k] for k in ("w_diff", "pos", "invf", "lqk", "ngd", "cst"))
    if True:
        C = load_consts(S, nc, cst, extra_eps=(1e-5,))
        idb = S.sb("idb", [128, 128], BF16)
        S.dma("pool", idb[:], cst[3], writes=[idb])
        wd = S.sb("wd", [128, 8, 1280], BF16)
        S.dma("pool", wd[:], kcp(w_diff), writes=[wd])
        invf = S.sb("invf", [128, 2], F32)
        S.dma("sp", invf[:], invfp, writes=[invf])
        ngd = S.sb("ngd", [128, 128], F32)
        S.dma("sp", ngd[:], ngp.partition_broadcast(128), writes=[ngd])
        S.ts("dve", ngd, ngd[:], ngd, ngd[:], 1.0 - lambda_init, None, ALU.mult)
        lq = S.sb("lq", [128, 4, 64], F32)
        for i in range(4):
            S.dma("sp", lq[:, i, :], lqk[i:i + 1, :].partition_broadcast(128), writes=[lq], group=(i > 0))
        lt = S.sb("lt", [128, 2, 64], F32)
        S.tt("dve", lt, lt[:, 0, :], lq, lq[:, 0, :], lq, lq[:, 1, :], ALU.mult)
        S.tt("dve", lt, lt[:, 1, :], lq, lq[:, 2, :], lq, lq[:, 3, :], ALU.mult)
        ls = S.sb("ls", [128, 4], F32)
        S.op("dve", lambda E: E.reduce_sum(out=ls[:, 0:2], in_=lt[:], axis=AX.X), reads=[lt], writes=[ls])
        S.act(ls, ls[:, 0:2], ls, ls[:, 0:2], AF.Exp)
        S.tt("dve", ls, ls[:, 2:3], ls, ls[:, 1:2], ls, ls[:, 0:1], ALU.subtract)
        S.ts("dve", ls, ls[:, 3:4], ls, ls[:, 2:3], -lambda_init, None, ALU.add)
        nlam = ls

        def ring(name, shape, dt, n=2):
            return Ring([S.sb("%s%d" % (name, i), shape, dt) for i in range(n)])
        h_ring = ring("h", [128, 8, TT], BF16)
        KT = S.sb("KT", [128, 2, SEQ], BF16)
        VA = S.sb("VA", [128, 2, SEQ // 128, 129], BF16)
        S.memset("pool", VA, VA[:, :, :, 128:129], 1.0)
        QT_r = ring("QT", [128, 2, TT], BF16)
        big = Ring([S.ps("pb%d" % i, [128, 512], F32) for i in range(3)])
        pob = [[S.ps("po%d_%d" % (s, hf), [128, 512], F32) for hf in range(2)] for s in range(2)]
        sm = Ring(small_views(S, "sm", 1, 128))
        posi = S.sb("posi", [128, TT], I32)
        ang = S.sb("ang", [128, TT], F32)
        ang2 = S.sb("ang2", [128, TT], F32)
        ki = S.sb("ki", [128, TT], I32)
        kf = S.sb("kf", [128, TT], F32)
        yr = S.sb("yr", [128, TT], F32)
        Cs = S.sb("Cs", [128, TT], F32)
        Sn = S.sb("Sn", [128, TT], F32)
        ta_r = ring("ta", [128, TT], F32)
        tb_r = ring("tb", [128, TT], F32)
        PT_r = ring("PT", [128, TT], BF16, 5)
        r_r = ring("r", [128, 4], F32)
        oa_r = ring("oa", [128, 128], F32)
        junk = S.sb("junk", [128, 128], F32)
        yn_r = ring("yn", [128, 128], BF16)
        yo_r = ring("yo", [128, 2, TT], BF16)
        eps5 = C["eps%g" % 1e-5]

        def reduce_sin(dst, src):
            S.ts("dve", ki, ki[:], src, src[:], 1.0 / (2.0 * math.pi), None, ALU.mult)
            S.copy("dve", kf, kf[:], ki, ki[:])
            S.stt(yr, yr[:], kf, kf[:], -C1_2PI, src, src[:], ALU.mult, ALU.add)
            S.stt(yr, yr[:], kf, kf[:], -C2_2PI, yr, yr[:], ALU.mult, ALU.add)
            S.ts("dve", yr, yr[:], yr, yr[:], -3.1415925, 3.1415925, ALU.max, ALU.min)
            S.act(dst, dst[:], yr, yr[:], AF.Sin)

        for t in range(ntiles):
            ts_ = slice(t * TT, (t + 1) * TT)
            h = h_ring.next()
            A["load_h"](S, h, t)
            S.dma("sp", posi[:], posd[0:1, ts_].partition_broadcast(128), writes=[posi])
            S.copy("dve", ang, ang[:], posi, posi[:])
            S.ts("dve", ang, ang[:], ang, ang[:], invf[:, 0:1], None, ALU.mult, rd=[invf])
            reduce_sin(Sn, ang)
            S.ts("dve", Sn, Sn[:], Sn, Sn[:], invf[:, 1:2], None, ALU.mult, rd=[invf])
            S.ts("dve", ang2, ang2[:], ang, ang[:], math.pi / 2.0, None, ALU.add)
            reduce_sin(Cs, ang2)
            QT = QT_r.next()
            for hd in range(2):
                for (c0, dstb, dst) in ((hd * 128, QT, QT[:, hd, :]), (512 + hd * 128, KT, KT[:, hd, ts_])):
                    p1 = big.next()
                    for kc in range(8):
                        S.mm(p1, p1[:], wd, wd[:, kc, c0:c0 + 128], h, h[:, kc, :], start=(kc == 0), stop=(kc == 7))
                    p2 = big.next()
                    for kc in range(8):
                        S.mm(p2, p2[:], wd, wd[:, kc, c0 + 256:c0 + 384], h, h[:, kc, :], start=(kc == 0), stop=(kc == 7))
                    ta = ta_r.next()
                    tb = tb_r.next()
                    S.tt("dve", ta, ta[:], p1, p1[:], Cs, Cs[:], ALU.mult)
                    S.tt("dve", tb, tb[:], p2, p2[:], Sn, Sn[:], ALU.mult)
                    S.tt("pool", dstb, dst, ta, ta[:], tb, tb[:], ALU.add)
            for c in range(4):
                cs = slice(c * 128, (c + 1) * 128)
                pv = big.next()
                for kc in range(8):
                    S.mm(pv, pv[:, 0:256], h, h[:, kc, cs], wd, wd[:, kc, 1024:1280], start=(kc == 0), stop=(kc == 7))
                S.copy("act", VA, VA[:, :, 4 * t + c, 0:128], pv, pv[:, 0:256].rearrange("p (a b) -> p a b", a=2))
            yo = yo_r.next()
            for hd in range(2):
                nkb = 4 * t + 4
                started = [[False, False], [False, False]]
                iters = [(kb, s_) for kb in range(nkb) for s_ in range(2)]
                pend = {}

                def emit_qk(i):
                    kb, s_ = iters[i]
                    r = kb - 4 * t
                    q0 = max(r, 0)
                    qlo = q0 * 128
                    n = TT - qlo
                    ps_ = slice(s_ * 64, (s_ + 1) * 64)
                    pS = big.next()
                    S.mm(pS, pS[:, :n], KT, KT[ps_, hd, kb * 128:(kb + 1) * 128], QT, QT[ps_, hd, qlo:TT])
                    PT = PT_r.next()
                    S.act(PT, PT[:, :n], pS, pS[:, :n], AF.Exp, scale=0.125)
                    if r >= 0:
                        S.memset("pool", PT, PT[64:128, 0:64], 0.0)
                    pend[i] = (PT, q0, qlo)

                def emit_pv(i):
                    kb, s_ = iters[i]
                    PT, q0, qlo = pend.pop(i)
                    for qb in range(q0, 4):
                        col = qb * 128 - qlo
                        bank = pob[s_][qb // 2]
                        o = bank[:, (qb % 2) * 129:(qb % 2) * 129 + 129]
                        first = not started[s_][qb // 2]
                        started[s_][qb // 2] = True
                        S.mm(bank, o, PT, PT[:, col:col + 128], VA, VA[:, hd, kb, :], start=first, stop=(kb == 4 * t + qb and qb % 2 == 1))

                LOOK = 3
                for i in range(len(iters) + LOOK):
                    if i < len(iters):
                        emit_qk(i)
                    if i >= LOOK:
                        emit_pv(i - LOOK)
                for qb in range(4):
                    o1 = pob[0][qb // 2]
                    o2 = pob[1][qb // 2]
                    b0 = (qb % 2) * 129
                    rr = r_r.next()
                    S.recip(rr, rr[:, 0:1], o1, o1[:, b0 + 128:b0 + 129])
                    S.recip(rr, rr[:, 1:2], o2, o2[:, b0 + 128:b0 + 129])
                    S.tt("dve", rr, rr[:, 2:3], rr, rr[:, 1:2], nlam, nlam[:, 3:4], ALU.mult)
                    oa = oa_r.next()
                    S.ts("dve", oa, oa[:], o1, o1[:, b0:b0 + 128], rr[:, 0:1], None, ALU.mult, rd=[rr])
                    S.stt(oa, oa[:], o2, o2[:, b0:b0 + 128], rr[:, 2:3], oa, oa[:], ALU.mult, ALU.add, rd=[rr])
                    S.act(junk, junk[:], oa, oa[:], AF.Square, accum=rr[:, 3:4], wr=[rr])
                    S.act(rr, rr[:, 1:2], rr, rr[:, 3:4], AF.Sqrt, bias=eps5[:, 0:1], scale=1.0 / 128, rd=[eps5])
                    S.recip(rr, rr[:, 0:1], rr, rr[:, 1:2])
                    yn = yn_r.next()
                    S.stt(yn, yn[:], oa, oa[:], rr[:, 0:1], ngd, ngd[:], ALU.mult, ALU.mult, rd=[rr])
                    pT = sm.next()
                    S.mm(pT, pT[:], yn, yn[:], idb, idb[:])
                    S.copy("act", yo, yo[:, hd, qb * 128:(qb + 1) * 128], pT, pT[:])
            A["store_y"](S, yo, t)


def diff_inputs(inp, l, g, b, hT_b, cst):
    w = inp["w_in"][l]
    qc = 8240 + np.arange(g * 256, (g + 1) * 256)
    kc = 9264 + np.arange(g * 256, (g + 1) * 256)
    vc = 10288 + np.arange(g * 256, (g + 1) * 256)
    d = np.arange(256) % 64
    partner = np.arange(256) + np.where(d < 8, 8, np.where(d < 16, -8, 0))
    cols = np.concatenate([qc, qc[partner], kc, kc[partner], vc])
    p = np.arange(128) % 64
    invf = np.zeros((128, 2), np.float32)
    fr = (500000.0 ** (-np.arange(0, 16, 2, dtype=np.float32) / np.float32(16))).astype(np.float32)
    invf[:, 0] = np.where(p < 16, fr[p % 8], 0.0)
    invf[:, 1] = np.where(p < 8, -1.0, np.where(p < 16, 1.0, 0.0))
    lqk = np.stack([inp["diff_lq1"][l], inp["diff_lk1"][l], inp["diff_lq2"][l], inp["diff_lk2"][l]]).astype(np.float32)
    return {"hT": hT_b, "w_diff": np.ascontiguousarray(w[:, cols]),
            "pos": np.ascontiguousarray(inp["positions"][b].reshape(1, SEQ)), "invf": invf, "lqk": lqk,
            "ngd": np.ascontiguousarray(inp["diff_norm_g"][l].reshape(1, 128)), "cst": cst}


_PROGS = {}


def _prog(key, fn):
    if key not in _PROGS:
        _PROGS[key] = fn()
    return _PROGS[key]


def _run(nc, maps):
    return run_bass_kernel_spmd(nc, maps, core_ids=list(range(NCORES))).results


def tok_inputs(inp, l, xT_c, hT_c, yT_c, g_next, cst):
    w = inp["w_in"][l]
    return {"xT": xT_c, "g_next": pvec(g_next), "cst": cst, "hT": hT_c, "yT": yT_c,
            "w_gate": np.ascontiguousarray(w[:, 11312:14384]), "b_gate": pvec(inp["b_gate"][l]),
            "w_br": np.ascontiguousarray(np.concatenate([inp["w_br_gla"][l], inp["w_br_ssm"][l], inp["w_br_diff"][l]], axis=0)),
            "w_out": np.ascontiguousarray(inp["w_out"][l]), "w_up": np.ascontiguousarray(inp["w_mlp_up"][l]),
            "w_dn": np.ascontiguousarray(inp["w_mlp_down"][l]), "g_mlp": pvec(inp["norm_mlp_g"][l])}


def kernel_unfused(**inp):
    inp = {k: np.asarray(v) for k, v in inp.items()}
    cst = make_consts()
    x = inp["x"]
    xT = [np.ascontiguousarray(x[c // 4, (c % 4) * NT:(c % 4 + 1) * NT, :].T) for c in range(NCORES)]
    res = _run(_prog("first", lambda: build_tok("first")),
               [{"xT": xT[c], "g_next": pvec(inp["norm_mix_g"][0]), "cst": cst} for c in range(NCORES)])
    hT = [r["hT_out"] for r in res]
    hTl = [r["hT_lo"] for r in res]
    out = None
    for l in range(DEPTH):
        hTb = [np.ascontiguousarray(np.concatenate(hT[b * 4:(b + 1) * 4], axis=1)) for b in range(B)]
        hTlb = [np.ascontiguousarray(np.concatenate(hTl[b * 4:(b + 1) * 4], axis=1)) for b in range(B)]
        yg = _run(_prog("gla", build_gla), [gla_inputs(inp, l, c % 4, hTb[c // 4], cst, hTlb[c // 4]) for c in range(NCORES)])
        ys = _run(_prog("ssm", build_ssm), [ssm_inputs(inp, l, c % 4, hTb[c // 4], cst) for c in range(NCORES)])
        yd = _run(_prog(("diff", l), lambda: build_diff(l)), [diff_inputs(inp, l, c % 4, c // 4, hTb[c // 4], cst) for c in range(NCORES)])
        yTb = []
        for b in range(B):
            parts = [yg[b * 4 + g]["yT"] for g in range(4)] + [ys[b * 4 + g]["yT"] for g in range(4)] + [yd[b * 4 + g]["yT"] for g in range(4)]
            yTb.append(np.concatenate(parts, axis=0))
        last = (l == DEPTH - 1)
        g_next = inp["norm_final_g"] if last else inp["norm_mix_g"][l + 1]
        maps = [tok_inputs(inp, l, xT[c], hT[c], np.ascontiguousarray(yTb[c // 4][:, (c % 4) * NT:(c % 4 + 1) * NT]), g_next, cst)
                for c in range(NCORES)]
        mode = "last" if last else "mid"
        res = _run(_prog(mode, lambda: build_tok(mode)), maps)
        if last:
            out = np.empty((B, SEQ, DM), np.float32)
            for c in range(NCORES):
                out[c // 4, (c % 4) * NT:(c % 4 + 1) * NT, :] = res[c]["outT"].T
        else:
            xT = [r["xT_out"] for r in res]
            hT = [r["hT_out"] for r in res]
            hTl = [r["hT_lo"] for r in res]
    return out


GROUPS = [[0, 1, 2, 3], [4, 5, 6, 7]]


def build_fused(skip=()):
    nc = bass.Bass("TRN2", target_bir_lowering=False)
    ext = lambda name, shape, dt=F32: dram(nc, name, shape, dt, "ExternalInput")
    xT = ext("xT", [DM, NT])
    pos = ext("pos", [1, SEQ], I32)
    cst = ext("cst", [4, 128, 128])
    invf = ext("invf", [128, 2])
    g_mix = ext("g_mix", [DEPTH, 128, 8])
    g_fin = ext("g_fin", [128, 8])
    g_mlp = ext("g_mlp", [DEPTH, 128, 8])
    w_gate = ext("w_gate", [DEPTH, DM, 3072])
    b_gate = ext("b_gate", [DEPTH, 128, 24])
    w_br = ext("w_br", [DEPTH, 4096, DM])
    w_out = ext("w_out", [DEPTH, DM, DM])
    w_up = ext("w_up", [DEPTH, DM, 4096])
    w_dn = ext("w_dn", [DEPTH, 4096, DM])
    w_gla = ext("w_gla", [DEPTH, DM, 784])
    wgk2 = ext("wgk2", [DEPTH, 16, 128])
    bgk = ext("bgk", [DEPTH, 1, 128])
    ng = ext("ng", [DEPTH, 128, 2])
    w_ssm = ext("w_ssm", [DEPTH, DM, 1288])
    cw = ext("cw", [DEPTH, 128, 6, 4])
    cb = ext("cb", [DEPTH, 128, 6])
    dtb = ext("dtb", [DEPTH, 1, 8])
    alog = ext("alog", [DEPTH, 1, 8])
    dsk = ext("dsk", [DEPTH, 1, 8])
    ngs = ext("ngs", [DEPTH, 1, 512])
    w_diff = ext("w_diff", [DEPTH, DM, 1280])
    lqk = ext("lqk", [DEPTH, 4, 64])
    ngd = ext("ngd", [DEPTH, 1, 128])
    outT = dram(nc, "outT", [DM, NT], F32, "ExternalOutput")
    xs = nc.dram_tensor("xs_i", [DM, NT], F32).ap()
    hsrc = nc.dram_tensor("hsrc_i", [8, 256, NT], BF16).ap()
    hgat = nc.dram_tensor("hgat_i", [8, 1024, NT], BF16).ap()
    ysrc = nc.dram_tensor("ysrc_i", [4, 4, 256, NT], BF16).ap()
    ygat = nc.dram_tensor("ygat_i", [4, 4, 1024, NT], BF16).ap()
    hT_own = hsrc[0:4].rearrange("c r n -> (c r) n")
    hTlo_own = hsrc[4:8].rearrange("c r n -> (c r) n")
    ygat3 = ygat.rearrange("q a r n -> q (a r) n")

    with ExitStack() as st:
        S = Sched(nc, st)
        hsrc_b = Buf("hsrc_b")
        hgat_b = [Buf("hgat_b%d" % i) for i in range(8)]
        ysrc_b = [[Buf("ysrc_b%d_%d" % (q, a)) for a in range(4)] for q in range(4)]
        ygat_b = Buf("ygat_b")
        xs_b = Buf("xs_b")
        S.global_bufs += [hsrc_b, ygat_b, xs_b] + hgat_b + [b for row in ysrc_b for b in row]

        def gather_h():
            for i in range(8):
                S.collective("AllGather", hsrc_b, hsrc[i], hgat_b[i], hgat[i], GROUPS)

        def load_h_from(chunks):
            def f(S_, h, t):
                r, tl = t // (NT // TT), (t % (NT // TT)) * TT
                for j, i in enumerate(chunks):
                    S_.dma("sp", h[:, 2 * j:2 * j + 2, :], hgat[i][r * 256:(r + 1) * 256, tl:tl + TT].rearrange("(c p) n -> p c n", p=128),
                           reads=[hgat_b[i]], writes=[h], group=(j > 0))
            return f

        def store_y_parts(parts):
            def f(S_, yo, t):
                q, tl = t // (NT // TT), (t % (NT // TT)) * TT
                for j, a in enumerate(parts):
                    S_.dma("sp", ysrc[q, a][:, tl:tl + TT].rearrange("(c p) n -> p c n", p=128), yo[:, 2 * j:2 * j + 2, :],
                           reads=[yo], writes=[ysrc_b[q][a]], sembuf=yo, group=True)
                if t % (NT // TT) == NT // TT - 1:
                    for a in parts:
                        S_.collective("AllGather", ysrc_b[q][a], ysrc[q, a], ygat_b, ygat[q, a], GROUPS)
            return f

        qcache = {}

        def load_y(S_, yub, t):
            tl = t * TT
            ph = S_.phase_id

            def qval(E):
                if ph not in qcache:
                    qcache[ph] = E.snap(E.partition_id() % 4)
                return qcache[ph]
            src = (lambda E, tl=tl: ygat3[bass.ds(qval(E), 1), :, tl:tl + TT].rearrange("o (k p) n -> p (o k) n", p=128))
            S_.dma("sp", yub[:], src, reads=[ygat_b], writes=[yub])

        S.begin_phase()
        if "first" not in skip:
            emit_tok(nc, S, {"xT": xT, "g_next": g_mix[0], "cst": cst, "hT_out": hT_own, "hT_lo": hTlo_own, "h_wr": [hsrc_b]}, "first")
        if "gather" not in skip:
            gather_h()
        S.end_phase()
        for l in range(DEPTH):
            S.begin_phase()
            if "gla" not in skip:
              emit_gla(nc, S, {"w_gla": w_gla[l], "wgk2": wgk2[l], "bgk": bgk[l], "ng": ng[l], "cst": cst,
                             "load_h": load_h_from((0, 1, 2, 3)), "load_hlo": load_h_from((4, 5, 6, 7)), "store_y": store_y_parts((0,))}, SEQ // TT)
            S.end_phase()
            S.begin_phase()
            if "ssm" not in skip:
              emit_ssm(nc, S, {"w_ssm": w_ssm[l], "cw": cw[l], "cb": cb[l], "dtb": dtb[l], "alog": alog[l], "dsk": dsk[l], "ngs": ngs[l],
                             "cst": cst, "load_h": load_h_from((0, 1, 2, 3)), "store_y": store_y_parts((1, 2))}, SEQ // TT)
            S.end_phase()
            S.begin_phase()
            if "diff" not in skip:
              emit_diff(nc, S, {"w_diff": w_diff[l], "pos": pos, "invf": invf, "lqk": lqk[l], "ngd": ngd[l], "cst": cst,
                              "load_h": load_h_from((0, 1, 2, 3)), "store_y": store_y_parts((3,))}, l, SEQ // TT)
            S.end_phase()
            last = (l == DEPTH - 1)
            A = {"xT": (xT if l == 0 else xs), "g_next": (g_fin if last else g_mix[l + 1]), "cst": cst, "hT": hT_own, "load_y": load_y,
                 "w_gate": w_gate[l], "b_gate": b_gate[l], "w_br": w_br[l], "w_out": w_out[l], "w_up": w_up[l], "w_dn": w_dn[l], "g_mlp": g_mlp[l]}
            if last:
                A["outT"] = outT
            else:
                A.update({"hT_out": hT_own, "hT_lo": hTlo_own, "h_wr": [hsrc_b], "xT_out": xs})
            S.begin_phase()
            emit_tok(nc, S, A, "last" if last else "mid")
            if not last:
                gather_h()
            S.end_phase()
    return nc


def fused_y_row_order():
    rows = []
    for a in range(4):
        for r in range(4):
            j = np.arange(256)
            if a == 0:
                rows.append(r * 256 + j)
            elif a in (1, 2):
                rows.append(1024 + r * 512 + (a - 1) * 256 + j)
            else:
                rows.append(3072 + r * 256 + j)
    return np.concatenate(rows)


def fused_inputs(inp, c, cst):
    b, g = c // 4, c % 4
    x = inp["x"]
    m = {"xT": np.ascontiguousarray(x[b, g * NT:(g + 1) * NT, :].T), "cst": cst,
         "pos": np.ascontiguousarray(inp["positions"][b].reshape(1, SEQ)),
         "g_mix": np.stack([pvec(inp["norm_mix_g"][l]) for l in range(DEPTH)]), "g_fin": pvec(inp["norm_final_g"]),
         "g_mlp": np.stack([pvec(inp["norm_mlp_g"][l]) for l in range(DEPTH)]),
         "w_gate": np.ascontiguousarray(inp["w_in"][:, :, 11312:14384]),
         "b_gate": np.stack([pvec(inp["b_gate"][l]) for l in range(DEPTH)]),
         "w_br": np.ascontiguousarray(np.concatenate([inp["w_br_gla"], inp["w_br_ssm"], inp["w_br_diff"]], axis=1)[:, fused_y_row_order(), :]),
         "w_out": np.ascontiguousarray(inp["w_out"]), "w_up": np.ascontiguousarray(inp["w_mlp_up"]), "w_dn": np.ascontiguousarray(inp["w_mlp_down"])}
    per = {}
    for l in range(DEPTH):
        d = {}
        d.update(gla_inputs(inp, l, g, None, cst, 0))
        d.update(ssm_inputs(inp, l, g, None, cst))
        d.update(diff_inputs(inp, l, g, b, None, cst))
        for k, v in d.items():
            if k in ("hT", "hTlo", "cst", "pos", "invf"):
                continue
            per.setdefault(k, []).append(v)
        if l == 0:
            m["invf"] = d["invf"]
    for k, v in per.items():
        m[k] = np.ascontiguousarray(np.stack(v))
    return m


_FUSED = []


def kernel(**inp):
    inp = {k: np.asarray(v) for k, v in inp.items()}
    cst = make_consts()
    if not _FUSED:
        _FUSED.append(build_fused())
    maps = [fused_inputs(inp, c, cst) for c in range(NCORES)]
    res = run_bass_kernel_spmd(_FUSED[0], maps, core_ids=list(range(NCORES))).results
    out = np.empty((B, SEQ, DM), np.float32)
    for c in range(NCORES):
        out[c // 4, (c % 4) * NT:(c % 4 + 1) * NT, :] = res[c]["outT"].T
    return out
```

```python
import math
from contextlib import ExitStack
import numpy as np
import ml_dtypes
import concourse.bass as bass
import concourse.mybir as mybir
from concourse.bass_utils import run_bass_kernel_spmd

F32 = mybir.dt.float32
BF16 = mybir.dt.bfloat16
I32 = mybir.dt.int32
AF = mybir.ActivationFunctionType
ALU = mybir.AluOpType
AX = mybir.AxisListType

NCORES = 8
B, SEQ, DM, DEPTH = 2, 8192, 1024, 2
NT = 2048
TT = 512
EPS = 1e-6
ENGS = ("pe", "act", "dve", "pool", "sp")


class Buf:
    __slots__ = ("name", "t", "w", "r", "dsem", "dcnt")

    def __init__(self, name, t=None):
        self.name = name
        self.t = t
        self.w = None
        self.r = {}
        self.dsem = None
        self.dcnt = 0

    def __getitem__(self, idx):
        return self.t[idx]


class VBuf:
    def __init__(self, name, parent, ap):
        self.name = name
        self.parent = parent
        self.t = ap

    def __getitem__(self, idx):
        return self.t[idx]

    w = property(lambda s: s.parent.w, lambda s, v: setattr(s.parent, "w", v))
    r = property(lambda s: s.parent.r, lambda s, v: setattr(s.parent, "r", v))
    dsem = property(lambda s: s.parent.dsem, lambda s, v: setattr(s.parent, "dsem", v))
    dcnt = property(lambda s: s.parent.dcnt, lambda s, v: setattr(s.parent, "dcnt", v))


class Sched:
    def __init__(self, nc, stack):
        self.nc = nc
        self.stack = stack
        self.q = {e: [] for e in ENGS}
        self.sem = {}
        for e in ENGS:
            self.sem[e] = stack.enter_context(nc.semaphore("s_" + e))
        self.cnt = {e: 0 for e in ENGS}
        self.waited = {e: {} for e in ENGS}
        self.ndsem = 0
        self.final_waits = {}
        self.alloc_stack = stack
        self.nname = 0
        self.free_dsems = {}
        self.dsem_cls = {}
        self.phase_bufs = []
        self.global_bufs = []

    def sb(self, name, shape, dt):
        self.nname += 1
        t = self.alloc_stack.enter_context(self.nc.sbuf_tensor("sb%d_%s" % (self.nname, name), list(shape), dt))
        b = Buf(name, t)
        self.phase_bufs.append(b)
        return b

    def ps(self, name, shape, dt=F32):
        self.nname += 1
        t = self.alloc_stack.enter_context(self.nc.psum_tensor("ps%d_%s" % (self.nname, name), list(shape), dt))
        return Buf(name, t)

    def view(self, name, ap):
        return Buf(name, ap)

    def _dsem(self, b, eng="sp"):
        cls = {"pool": "sw", "cc": "cc"}.get(eng, "hw")
        if b.dsem is not None:
            assert self.dsem_cls[b.dsem] == cls, (b.name, cls)
        if b.dsem is None:
            pool = self.free_dsems.setdefault(cls, [])
            if pool:
                b.dsem, b.dcnt = pool.pop()
            else:
                b.dsem = "d%d" % self.ndsem
                self.ndsem += 1
                self.sem[b.dsem] = self.stack.enter_context(self.nc.semaphore(b.dsem))
                self.dsem_cls[b.dsem] = cls
        return b.dsem

    def _deps(self, eng, reads, writes, same_ok):
        deps = {}

        def add(k, v):
            if deps.get(k, 0) < v:
                deps[k] = v

        for b in reads:
            if b.w is not None:
                add(*b.w)
        for b in writes:
            if b.w is not None:
                add(*b.w)
            for k, v in b.r.items():
                add(k, v)
        waits = []
        wd = self.waited[eng]
        for k, v in deps.items():
            if k == eng and same_ok:
                continue
            if wd.get(k, 0) >= v:
                continue
            wd[k] = v
            waits.append((k, v))
        return waits

    def _commit(self, tk, reads, writes):
        k, v = tk
        for b in writes:
            b.w = tk
            b.r = {}
        for b in reads:
            if b.r.get(k, 0) < v:
                b.r[k] = v

    def op(self, eng, fn, reads=(), writes=()):
        waits = self._deps(eng, reads, writes, same_ok=(eng == "pe"))
        self.cnt[eng] += 1
        tk = (eng, self.cnt[eng])
        sem = self.sem
        me = sem[eng]

        def emit(E):
            for k, v in waits:
                E.wait_ge(sem[k], v)
            fn(E).then_inc(me, 1)

        self.q[eng].append(emit)
        self._commit(tk, reads, writes)
        return tk

    def dma(self, eng, out_ap, in_ap, reads=(), writes=(), sembuf=None, group=False):
        sb = sembuf if sembuf is not None else (writes[0] if writes else reads[0])
        dk = self._dsem(sb, eng)
        saved = None
        if group and writes and writes[0].w is not None and writes[0].w[0] == dk:
            saved = writes[0].w
            writes[0].w = None
        waits = self._deps(eng, reads, writes, same_ok=False)
        if saved is not None:
            writes[0].w = saved
        sb.dcnt += 16
        tk = (dk, sb.dcnt)
        sem = self.sem
        ds = sem[dk]

        def emit(E):
            for k, v in waits:
                E.wait_ge(sem[k], v)
            E.dma_start(out=out_ap, in_=(in_ap(E) if callable(in_ap) else in_ap)).then_inc(ds, 16)

        self.q[eng].append(emit)
        self._commit(tk, reads, writes)
        self.final_waits[dk] = sb.dcnt
        return tk

    def collective(self, kind, sb_, s_ap, db_, d_ap, groups):
        dk = self._dsem(db_, "cc")
        waits = self._deps("pool", [sb_], [db_], same_ok=False)
        db_.dcnt += 1
        tk = (dk, db_.dcnt)
        sem = self.sem
        ds = sem[dk]

        def emit(E):
            for k, v in waits:
                E.wait_ge(sem[k], v)
            E.collective_compute(kind, ALU.bypass, replica_groups=groups, ins=[s_ap], outs=[d_ap]).then_inc(ds, 1)

        self.q["pool"].append(emit)
        self._commit(tk, [sb_], [db_])
        self.final_waits[dk] = db_.dcnt
        return tk

    def mm(self, ob, o, lb, l, rb, r, start=True, stop=True):
        return self.op("pe", lambda E: E.matmul(o, lhsT=l, rhs=r, start=start, stop=stop), reads=[lb, rb], writes=[ob])

    def act(self, ob, o, ib, i, func, bias=None, scale=None, accum=None, rd=(), wr=()):
        kw = {}
        if bias is not None:
            kw["bias"] = bias
        if scale is not None:
            kw["scale"] = scale
        if accum is not None:
            kw["accum_out"] = accum
        return self.op("act", lambda E: E.activation(out=o, in_=i, func=func, **kw), reads=[ib] + list(rd), writes=[ob] + list(wr))

    def tt(self, eng, ob, o, ab, a, bb, b, op):
        return self.op(eng, lambda E: E.tensor_tensor(out=o, in0=a, in1=b, op=op), reads=[ab, bb], writes=[ob])

    def ts(self, eng, ob, o, ab, a, s1, s2, op0, op1=None, rd=()):
        if op1 is None:
            return self.op(eng, lambda E: E.tensor_scalar(out=o, in0=a, scalar1=s1, scalar2=None, op0=op0), reads=[ab] + list(rd), writes=[ob])
        return self.op(eng, lambda E: E.tensor_scalar(out=o, in0=a, scalar1=s1, scalar2=s2, op0=op0, op1=op1), reads=[ab] + list(rd), writes=[ob])

    def stt(self, ob, o, ab, a, sc, bb, b, op0, op1, rd=()):
        return self.op("dve", lambda E: E.scalar_tensor_tensor(out=o, in0=a, scalar=sc, in1=b, op0=op0, op1=op1), reads=[ab, bb] + list(rd), writes=[ob])

    def copy(self, eng, ob, o, ib, i):
        if eng == "act":
            return self.op("act", lambda E: E.copy(out=o, in_=i), reads=[ib], writes=[ob])
        return self.op(eng, lambda E: E.tensor_copy(out=o, in_=i), reads=[ib], writes=[ob])

    def memset(self, eng, ob, o, val):
        return self.op(eng, lambda E: E.memset(o, val), writes=[ob])

    def recip(self, ob, o, ib, i):
        return self.op("dve", lambda E: E.reciprocal(out=o, in_=i), reads=[ib], writes=[ob])

    phase_id = 0

    def begin_phase(self):
        self.phase_id += 1
        self.gstack = self.stack if not hasattr(self, "gstack") else self.gstack
        self.pstack = ExitStack()
        self.pstack.__enter__()
        self.alloc_stack = self.pstack

    def end_phase(self):
        sem = self.sem
        cnt = dict(self.cnt)
        fw = dict(self.final_waits)
        for e in ENGS:
            waits = [(o, cnt[o]) for o in ENGS if o != e and cnt[o] > self.waited[e].get(o, 0)]
            waits += [(k, v) for k, v in fw.items() if v > self.waited[e].get(k, 0)]
            for k, v in waits:
                self.waited[e][k] = v

            def emit(E, waits=waits):
                for k, v in waits:
                    E.wait_ge(sem[k], v)
            self.q[e].append(emit)
        self.finish()
        self.q = {e: [] for e in ENGS}
        for e in ENGS:
            self.sem[e] = self.stack.enter_context(self.nc.semaphore("s_%s_%d" % (e, self.phase_id)))
            self.cnt[e] = 0
            for x in ENGS:
                self.waited[x].pop(e, None)
        for b in self.global_bufs:
            if b.w is not None and b.w[0] in ENGS:
                b.w = None
            b.r = {k: v for k, v in b.r.items() if k not in ENGS}
        for b in self.phase_bufs:
            if b.dsem is not None:
                self.free_dsems[self.dsem_cls[b.dsem]].append((b.dsem, b.dcnt))
                b.dsem = None
        self.phase_bufs = []
        self.pstack.__exit__(None, None, None)
        self.alloc_stack = self.stack

    def finish(self):
        nc = self.nc
        sem = self.sem
        q = self.q
        fw = dict(self.final_waits)
        with nc.Block() as block:
            @block.tensor
            def _(E):
                for f in q["pe"]:
                    f(E)

            @block.scalar
            def _(E):
                for f in q["act"]:
                    f(E)

            @block.vector
            def _(E):
                for f in q["dve"]:
                    f(E)

            @block.gpsimd
            def _(E):
                for f in q["pool"]:
                    f(E)

            @block.sync
            def _(E):
                for f in q["sp"]:
                    f(E)
                for k, v in fw.items():
                    E.wait_ge(sem[k], v)


class Ring:
    def __init__(self, bufs):
        self.bufs = bufs
        self.i = 0

    def next(self):
        b = self.bufs[self.i % len(self.bufs)]
        self.i += 1
        return b


def dram(nc, name, shape, dt, kind):
    return nc.dram_tensor(name, list(shape), dt, kind=kind).ap()


def kcp(ap):
    return ap.rearrange("(kc p) n -> p kc n", p=128)


def rms_stats(S, C, xb, nch, n, ps_ring, sq_ring, rstd, tmp, nfeat, eps):
    pss = ps_ring.next()
    for kc in range(nch):
        sq = sq_ring.next()
        S.act(sq, sq[:, :n], xb, xb[:, kc, :n], AF.Square)
        S.mm(pss, pss[:, :n], C["ones"], C["ones"][:], sq, sq[:, :n], start=(kc == 0), stop=(kc == nch - 1))
    S.act(tmp, tmp[:, :n], pss, pss[:, :n], AF.Sqrt, bias=C["eps%g" % eps][:, 0:1], scale=1.0 / nfeat, rd=[C["eps%g" % eps]])
    S.recip(rstd, rstd[:, :n], tmp, tmp[:, :n])


def load_consts(S, nc, cst_ap, extra_eps=()):
    C = {}
    C["ones"] = S.sb("c_ones", [128, 128], F32)
    S.dma("sp", C["ones"][:], cst_ap[0], writes=[C["ones"]])
    for e in (EPS,) + tuple(extra_eps):
        b = S.sb("c_eps%g" % e, [128, 1], F32)
        S.memset("dve", b, b[:], float(e))
        C["eps%g" % e] = b
    return C


def build_tok(mode):
    nc = bass.Bass("TRN2", target_bir_lowering=False)
    A = {}
    A["xT"] = dram(nc, "xT", [DM, NT], F32, "ExternalInput")
    A["g_next"] = dram(nc, "g_next", [128, 8], F32, "ExternalInput")
    A["cst"] = dram(nc, "cst", [4, 128, 128], F32, "ExternalInput")
    if mode != "first":
        A["hT"] = dram(nc, "hT", [DM, NT], BF16, "ExternalInput")
        yT = dram(nc, "yT", [4096, NT], BF16, "ExternalInput")
        A["load_y"] = lambda S, yub, t: S.dma("act", yub[:], kcp(yT[:, t * TT:(t + 1) * TT]), writes=[yub])
        A["w_gate"] = dram(nc, "w_gate", [DM, 3072], F32, "ExternalInput")
        A["b_gate"] = dram(nc, "b_gate", [128, 24], F32, "ExternalInput")
        A["w_br"] = dram(nc, "w_br", [4096, DM], F32, "ExternalInput")
        A["w_out"] = dram(nc, "w_out", [DM, DM], F32, "ExternalInput")
        A["w_up"] = dram(nc, "w_up", [DM, 4096], F32, "ExternalInput")
        A["w_dn"] = dram(nc, "w_dn", [4096, DM], F32, "ExternalInput")
        A["g_mlp"] = dram(nc, "g_mlp", [128, 8], F32, "ExternalInput")
    if mode == "last":
        A["outT"] = dram(nc, "outT", [DM, NT], F32, "ExternalOutput")
    else:
        A["hT_out"] = dram(nc, "hT_out", [DM, NT], BF16, "ExternalOutput")
        A["hT_lo"] = dram(nc, "hT_lo", [DM, NT], BF16, "ExternalOutput")
        if mode == "mid":
            A["xT_out"] = dram(nc, "xT_out", [DM, NT], F32, "ExternalOutput")
    with ExitStack() as st:
        S = Sched(nc, st)
        S.begin_phase()
        emit_tok(nc, S, A, mode)
        S.end_phase()
    return nc


def emit_tok(nc, S, A, mode):
    xT, gn, cst = A["xT"], A["g_next"], A["cst"]
    if mode != "first":
        hT, w_gate, b_gate, w_br, w_out, w_up, w_dn, gm = (A[k] for k in ("hT", "w_gate", "b_gate", "w_br", "w_out", "w_up", "w_dn", "g_mlp"))
    if mode == "last":
        outT = A["outT"]
    else:
        hTo, hTlo = A["hT_out"], A["hT_lo"]
        if mode == "mid":
            xTo = A["xT_out"]
    if True:
        C = load_consts(S, nc, cst)
        gnb = S.sb("gnb", [128, 8], F32)
        S.dma("sp", gnb[:], gn, writes=[gnb])
        xb = S.sb("xb", [128, 8, TT], F32)
        ps_ring = Ring([S.ps("ps%d" % i, [128, TT], F32) for i in range(7)])
        sq_ring = Ring([S.sb("sq%d" % i, [128, TT], F32) for i in range(2)])
        rstd = S.sb("rstd", [128, TT], F32)
        rtmp = S.sb("rtmp", [128, TT], F32)
        hn_ring = Ring([S.sb("hn%d" % i, [128, 8, TT], BF16) for i in range(2)])
        hl_ring = Ring([S.sb("hl%d" % i, [128, 8, TT], BF16) for i in range(2)])
        h32_ring = Ring([S.sb("h32_%d" % i, [128, TT], F32) for i in range(2)])
        if mode == "last":
            oc_ring = Ring([S.sb("oc%d" % i, [128, TT], F32) for i in range(3)])
        if mode != "first":
            bgb = S.sb("bgb", [128, 24], F32)
            S.dma("sp", bgb[:], b_gate, writes=[bgb])
            gmb = S.sb("gmb", [128, 8], F32)
            S.dma("sp", gmb[:], gm, writes=[gmb])
            hb_ring = Ring([S.sb("hb%d" % i, [128, 8, TT], BF16) for i in range(2)])
            yub = S.sb("yub", [128, 32, TT], BF16)
            mixed = S.sb("mixed", [128, 8, TT], BF16)
            h2 = S.sb("h2", [128, 8, TT], BF16)
            g_ring = Ring([S.sb("g%d" % i, [128, TT], F32) for i in range(6)])
            acc_ring = Ring([S.sb("acc%d" % i, [128, TT], F32) for i in range(2)])
            tmp_ring = Ring([S.sb("tmp%d" % i, [128, TT], F32) for i in range(2)])
            r_ring = Ring([S.sb("r%d" % i, [128, TT], F32) for i in range(2)])
            w_ring = Ring([S.sb("w%d" % i, [128, 7168], BF16) for i in range(3)])

        for t in range(NT // TT):
            ts_ = slice(t * TT, (t + 1) * TT)
            S.dma("sp", xb[:], kcp(xT[:, ts_]), writes=[xb])
            if mode != "first":
                hb = hb_ring.next()
                S.dma("sp", hb[:], kcp(hT[:, ts_]), writes=[hb])
                A["load_y"](S, yub, t)
                for oc in range(8):
                    w = w_ring.next()
                    wg = w[:, 0:3072].rearrange("p (kc br j) -> p kc br j", kc=8, br=3)
                    wb = w[:, 3072:7168].rearrange("p (kc j) -> p kc j", kc=32)
                    for br in range(3):
                        c0 = br * 1024 + oc * 128
                        S.dma("pool", wg[:, :, br, :], kcp(w_gate[:, c0:c0 + 128]), writes=[w], group=(br > 0))
                    S.dma("pool", wb, kcp(w_br[:, oc * 128:(oc + 1) * 128]), writes=[w], group=True)
                    gts = []
                    for br in range(3):
                        pg = ps_ring.next()
                        for kc in range(8):
                            S.mm(pg, pg[:], w, wg[:, kc, br, :], hb, hb[:, kc, :], start=(kc == 0), stop=(kc == 7))
                        g = g_ring.next()
                        ch = br * 8 + oc
                        S.act(g, g[:], pg, pg[:], AF.Sigmoid, bias=bgb[:, ch:ch + 1], rd=[bgb])
                        gts.append(g)
                    acc = acc_ring.next()
                    koff = (0, 8, 24)
                    nk = (8, 16, 8)
                    for br in range(3):
                        pb = ps_ring.next()
                        for kc in range(nk[br]):
                            S.mm(pb, pb[:], w, wb[:, koff[br] + kc, :], yub, yub[:, koff[br] + kc, :], start=(kc == 0), stop=(kc == nk[br] - 1))
                        if br == 0:
                            S.tt("dve", acc, acc[:], pb, pb[:], gts[0], gts[0][:], ALU.mult)
                        else:
                            tmp = tmp_ring.next()
                            S.tt("dve", tmp, tmp[:], pb, pb[:], gts[br], gts[br][:], ALU.mult)
                            if br == 1:
                                S.tt("dve", acc, acc[:], acc, acc[:], tmp, tmp[:], ALU.add)
                            else:
                                S.tt("dve", mixed, mixed[:, oc, :], acc, acc[:], tmp, tmp[:], ALU.add)
                for half in range(2):
                    w = w_ring.next()
                    wo = w[:, 0:4096].rearrange("p (kc j) -> p kc j", kc=8)
                    S.dma("pool", wo, kcp(w_out[:, half * 512:(half + 1) * 512]), writes=[w])
                    for o4 in range(4):
                        oc = half * 4 + o4
                        po = ps_ring.next()
                        for kc in range(8):
                            S.mm(po, po[:], w, wo[:, kc, o4 * 128:(o4 + 1) * 128], mixed, mixed[:, kc, :], start=(kc == 0), stop=(kc == 7))
                        S.tt("dve", xb, xb[:, oc, :], xb, xb[:, oc, :], po, po[:], ALU.add)
                rms_stats(S, C, xb, 8, TT, ps_ring, sq_ring, rstd, rtmp, DM, EPS)
                for kc in range(8):
                    S.stt(h2, h2[:, kc, :], xb, xb[:, kc, :], gmb[:, kc:kc + 1], rstd, rstd[:], ALU.mult, ALU.mult, rd=[gmb])
                for o8 in range(8):
                    w = w_ring.next()
                    wu = w[:, 0:4096].rearrange("p (kc j) -> p kc j", kc=8)
                    S.dma("pool", wu, kcp(w_up[:, o8 * 512:(o8 + 1) * 512]), writes=[w])
                    for o4 in range(4):
                        oc = o8 * 4 + o4
                        pu = ps_ring.next()
                        for kc in range(8):
                            S.mm(pu, pu[:], w, wu[:, kc, o4 * 128:(o4 + 1) * 128], h2, h2[:, kc, :], start=(kc == 0), stop=(kc == 7))
                        r = r_ring.next()
                        S.act(r, r[:], pu, pu[:], AF.Relu)
                        S.tt("dve", yub, yub[:, oc, :], r, r[:], r, r[:], ALU.mult)
                for oc in range(8):
                    w = w_ring.next()
                    wd = w[:, 0:4096].rearrange("p (kc j) -> p kc j", kc=32)
                    S.dma("pool", wd, kcp(w_dn[:, oc * 128:(oc + 1) * 128]), writes=[w])
                    pd = ps_ring.next()
                    for kc in range(32):
                        S.mm(pd, pd[:], w, wd[:, kc, :], yub, yub[:, kc, :], start=(kc == 0), stop=(kc == 31))
                    S.tt("dve", xb, xb[:, oc, :], xb, xb[:, oc, :], pd, pd[:], ALU.add)
                if mode == "mid":
                    S.dma("sp", kcp(xTo[:, ts_]), xb[:], reads=[xb])
            rms_stats(S, C, xb, 8, TT, ps_ring, sq_ring, rstd, rtmp, DM, EPS)
            if mode == "last":
                for kc in range(8):
                    o = oc_ring.next()
                    S.stt(o, o[:], xb, xb[:, kc, :], gnb[:, kc:kc + 1], rstd, rstd[:], ALU.mult, ALU.mult, rd=[gnb])
                    S.dma("sp", outT[kc * 128:(kc + 1) * 128, ts_], o[:], reads=[o])
            else:
                hn = hn_ring.next()
                hl = hl_ring.next()
                for kc in range(8):
                    h32 = h32_ring.next()
                    S.stt(h32, h32[:], xb, xb[:, kc, :], gnb[:, kc:kc + 1], rstd, rstd[:], ALU.mult, ALU.mult, rd=[gnb])
                    S.copy("act", hn, hn[:, kc, :], h32, h32[:])
                    S.tt("pool", hl, hl[:, kc, :], h32, h32[:], hn, hn[:, kc, :], ALU.subtract)
                S.dma("sp", kcp(hTo[:, ts_]), hn[:], reads=[hn], writes=A.get("h_wr", []), sembuf=hn)
                S.dma("sp", kcp(hTlo[:, ts_]), hl[:], reads=[hl], writes=A.get("h_wr", []), sembuf=hl)


def make_consts():
    c = np.zeros((4, 128, 128), np.float32)
    c[0] = 1.0
    c[1] = np.triu(np.ones((128, 128), np.float32))
    c[2] = 1.0 - c[1]
    c[3] = np.eye(128, dtype=np.float32)
    return c


def pvec(v):
    v = np.asarray(v)
    return np.ascontiguousarray(v.reshape(-1, 128).T)


def small_views(S, name, nbanks, width):
    out = []
    per = 512 // width
    banks = [S.ps("%s%d" % (name, i), [128, 512], F32) for i in range(nbanks)]
    for j in range(per):
        for i in range(nbanks):
            out.append(VBuf("%s%d_%d" % (name, i, j), banks[i], banks[i].t[:, j * width:(j + 1) * width]))
    return out


def build_gla(ntiles=SEQ // TT):
    nc = bass.Bass("TRN2", target_bir_lowering=False)
    hT = dram(nc, "hT", [DM, SEQ], BF16, "ExternalInput")
    hTl = dram(nc, "hTlo", [DM, SEQ], BF16, "ExternalInput")
    A = {}
    A["w_gla"] = dram(nc, "w_gla", [DM, 784], F32, "ExternalInput")
    A["wgk2"] = dram(nc, "wgk2", [16, 128], F32, "ExternalInput")
    A["bgk"] = dram(nc, "bgk", [1, 128], F32, "ExternalInput")
    A["ng"] = dram(nc, "ng", [128, 2], F32, "ExternalInput")
    A["cst"] = dram(nc, "cst", [4, 128, 128], F32, "ExternalInput")
    yT = dram(nc, "yT", [256, SEQ], BF16, "ExternalOutput")
    A["load_h"] = lambda S, h, t: S.dma("sp", h[:], kcp(hT[:, t * TT:(t + 1) * TT]), writes=[h])
    A["load_hlo"] = lambda S, h, t: S.dma("act", h[:], kcp(hTl[:, t * TT:(t + 1) * TT]), writes=[h])
    A["store_y"] = lambda S, yo, t: S.dma("sp", yT[:, t * TT:(t + 1) * TT].rearrange("(ec p) n -> p ec n", p=128), yo[:], reads=[yo])
    with ExitStack() as st:
        S = Sched(nc, st)
        S.begin_phase()
        emit_gla(nc, S, A, ntiles)
        S.end_phase()
    return nc


def emit_gla(nc, S, A, ntiles=SEQ // TT):
    w_gla, wgk2, bgk, ngp, cst = A["w_gla"], A["wgk2"], A["bgk"], A["ng"], A["cst"]
    if True:
        C = load_consts(S, nc, cst)
        U = S.sb("U", [128, 128], F32)
        UC = S.sb("UC", [128, 128], F32)
        S.dma("sp", U[:], cst[1], writes=[U])
        S.dma("sp", UC[:], cst[2], writes=[UC])
        wa = S.sb("wa", [128, 8, 784], BF16)
        S.dma("pool", wa[:], kcp(w_gla), writes=[wa])
        wqk32 = S.sb("wqk32", [128, 8, 256], F32)
        S.dma("sp", wqk32[:, :, 0:128], kcp(w_gla[:, 0:128]), writes=[wqk32])
        S.dma("sp", wqk32[:, :, 128:256], kcp(w_gla[:, 384:512]), writes=[wqk32], group=True)
        wqk_hi = S.sb("wqk_hi", [128, 8, 256], BF16)
        wqk_lo = S.sb("wqk_lo", [128, 8, 256], BF16)
        S.copy("act", wqk_hi, wqk_hi[:], wqk32, wqk32[:])
        S.tt("dve", wqk_lo, wqk_lo[:], wqk32, wqk32[:], wqk_hi, wqk_hi[:], ALU.subtract)
        w2 = S.sb("w2", [16, 128], F32)
        S.dma("sp", w2[:], wgk2, writes=[w2])
        bgb = S.sb("bgb", [128, 128], F32)
        S.dma("sp", bgb[:], bgk.partition_broadcast(128), writes=[bgb])
        ng = S.sb("ng", [128, 2], F32)
        S.dma("sp", ng[:], ngp, writes=[ng])
        h_ring = Ring([S.sb("h%d" % i, [128, 8, TT], BF16) for i in range(2)])
        hlo_ring = Ring([S.sb("hlo%d" % i, [128, 8, TT], BF16) for i in range(2)])
        big = Ring([S.ps("pb%d" % i, [128, 512], F32) for i in range(2)])
        po = [S.ps("po%d" % i, [128, 512], F32) for i in range(2)]
        pss_ring = Ring([S.ps("pss", [128, 512], F32)])
        sm = Ring(small_views(S, "sm", 3, 256))
        qTs = S.sb("qTs", [128, TT], F32)
        kTs = S.sb("kTs", [128, TT], F32)
        sg = S.sb("sg", [128, 2, TT], F32)
        gkl = S.sb("gkl", [16, TT], F32)

        def ring(name, shape, dt, n=2):
            return Ring([S.sb("%s%d" % (name, i), shape, dt) for i in range(n)])
        ktok_r = ring("ktok", [128, 128], F32)
        vtok_r = ring("vtok", [128, 256], BF16)
        t1_r = ring("t1", [128, 128], F32)
        e_r = ring("e", [128, 128], F32)
        gk_r = ring("gk", [128, 128], F32)
        ebT_r = ring("ebT", [128, 128], F32)
        enbT_r = ring("enbT", [128, 128], F32)
        ed2_r = ring("ed2", [128, 128], F32)
        qp_r = ring("qp", [128, 128], BF16)
        qp32_r = ring("qp32", [128, 128], F32)
        kp32_r = ring("kp32", [128, 128], F32)
        kpp_r = ring("kpp", [128, 128], BF16)
        AT_r = ring("AT", [128, 128], BF16)
        Sst = S.sb("Sst", [128, 256], F32)
        Sbf = S.sb("Sbf", [128, 256], BF16)
        S.memset("dve", Sst, Sst[:], 0.0)
        S.memset("dve", Sbf, Sbf[:], 0.0)
        sq_ring = ring("sq", [128, TT], F32)
        rstd = S.sb("rstd", [128, TT], F32)
        rtmp = S.sb("rtmp", [128, TT], F32)
        tmp_r = ring("tmp", [128, TT], F32)
        yo_r = ring("yo", [128, 2, TT], BF16)

        for t in range(ntiles):
            ts_ = slice(t * TT, (t + 1) * TT)
            h = h_ring.next()
            A["load_h"](S, h, t)

            def proj(c0, m):
                p = big.next()
                for kc in range(8):
                    S.mm(p, p[:m, :], wa, wa[:, kc, c0:c0 + m], h, h[:, kc, :], start=(kc == 0), stop=(kc == 7))
                return p
            hlo = hlo_ring.next()
            A["load_hlo"](S, hlo, t)

            def proj3(c0):
                p = big.next()
                n = 0
                for (wb_, hb_) in ((wqk_hi, h), (wqk_hi, hlo), (wqk_lo, h)):
                    for kc in range(8):
                        S.mm(p, p[:], wb_, wb_[:, kc, c0:c0 + 128], hb_, hb_[:, kc, :], start=(n == 0), stop=(n == 23))
                        n += 1
                return p
            p = proj3(0)
            S.act(qTs, qTs[:], p, p[:], AF.Copy, scale=128.0 ** -0.5)
            p = proj3(128)
            S.act(kTs, kTs[:], p, p[:], AF.Copy)
            for ec in range(2):
                p = proj(128 + ec * 128, 128)
                S.act(sg, sg[:, ec, :], p, p[:], AF.Silu)
            p = proj(768, 16)
            S.copy("dve", gkl, gkl[:], p, p[:16, :])
            def stageA(c):
                cs = slice(c * 128, (c + 1) * 128)
                p = big.next()
                for kc in range(8):
                    S.mm(p, p[:, :384], h, h[:, kc, cs], wa, wa[:, kc, 384:768], start=(kc == 0), stop=(kc == 7))
                ktok = ktok_r.next()
                vtok = vtok_r.next()
                S.copy("act", ktok, ktok[:], p, p[:, 0:128])
                S.copy("act", vtok, vtok[:], p, p[:, 128:384])
                pg = sm.next()
                S.mm(pg, pg[:, :128], gkl, gkl[:, cs], w2, w2[:], start=True, stop=True)
                t1 = t1_r.next()
                S.tt("dve", t1, t1[:], pg, pg[:, :128], bgb, bgb[:], ALU.add)
                e = e_r.next()
                S.act(e, e[:], t1, t1[:], AF.Exp, scale=-1.0)
                S.act(e, e[:], e, e[:], AF.Ln, bias=1.0)
                gk = gk_r.next()
                S.ts("dve", gk, gk[:], e, e[:], -1.0 / 16.0, None, ALU.mult)
                pbT = sm.next()
                S.mm(pbT, pbT[:, :128], gk, gk[:], U, U[:])
                pd2 = sm.next()
                S.mm(pd2, pd2[:, :128], UC, UC[:], gk, gk[:])
                ebT = ebT_r.next()
                enbT = enbT_r.next()
                ed2 = ed2_r.next()
                S.act(ebT, ebT[:], pbT, pbT[:, :128], AF.Exp)
                S.act(enbT, enbT[:], pbT, pbT[:, :128], AF.Exp, scale=-1.0)
                S.act(ed2, ed2[:], pd2, pd2[:, :128], AF.Exp)
                qp = qp_r.next()
                qp32 = qp32_r.next()
                kp32 = kp32_r.next()
                kpp = kpp_r.next()
                S.tt("dve", qp32, qp32[:], qTs, qTs[:, cs], ebT, ebT[:], ALU.mult)
                S.tt("dve", kp32, kp32[:], kTs, kTs[:, cs], enbT, enbT[:], ALU.mult)
                S.copy("act", qp, qp[:], qp32, qp32[:])
                S.tt("pool", kpp, kpp[:], ktok, ktok[:], ed2, ed2[:], ALU.mult)
                pA = sm.next()
                S.mm(pA, pA[:, :128], kp32, kp32[:], qp32, qp32[:])
                AT = AT_r.next()
                S.tt("dve", AT, AT[:], pA, pA[:, :128], U, U[:], ALU.mult)
                return dict(cs=cs, vtok=vtok, AT=AT, qp=qp, kpp=kpp, ebT=ebT)

            def stageB(c, v):
                cs, vtok, AT, qp, kpp, ebT = (v[k] for k in ('cs', 'vtok', 'AT', 'qp', 'kpp', 'ebT'))
                for ec in range(2):
                    es = slice(ec * 128, (ec + 1) * 128)
                    S.mm(po[ec], po[ec][:, cs], vtok, vtok[:, es], AT, AT[:], start=True, stop=False)
                    S.mm(po[ec], po[ec][:, cs], Sbf, Sbf[:, es], qp, qp[:], start=False, stop=True)
                pkv = sm.next()
                S.mm(pkv, pkv[:, :256], kpp, kpp[:], vtok, vtok[:])
                S.stt(Sst, Sst[:], Sst, Sst[:], ebT[:, 127:128], pkv, pkv[:, :256], ALU.mult, ALU.add, rd=[ebT])
                S.copy("act", Sbf, Sbf[:], Sst, Sst[:])

            va = {0: stageA(0)}
            for c in range(4):
                if c + 1 < 4:
                    va[c + 1] = stageA(c + 1)
                stageB(c, va.pop(c))
            pss = pss_ring.next()
            for ec in range(2):
                sq = sq_ring.next()
                S.act(sq, sq[:], po[ec], po[ec][:], AF.Square)
                S.mm(pss, pss[:], C["ones"], C["ones"][:], sq, sq[:], start=(ec == 0), stop=(ec == 1))
            S.act(rtmp, rtmp[:], pss, pss[:], AF.Sqrt, bias=C["eps%g" % EPS][:, 0:1], scale=1.0 / 256, rd=[C["eps%g" % EPS]])
            S.recip(rstd, rstd[:], rtmp, rtmp[:])
            yo = yo_r.next()
            for ec in range(2):
                tmp = tmp_r.next()
                S.stt(tmp, tmp[:], po[ec], po[ec][:], ng[:, ec:ec + 1], rstd, rstd[:], ALU.mult, ALU.mult, rd=[ng])
                S.tt("pool", yo, yo[:, ec, :], tmp, tmp[:], sg, sg[:, ec, :], ALU.mult)
            A["store_y"](S, yo, t)


def gla_inputs(inp, l, g, hT_b, cst, hTlo_b=None):
    w = inp["w_in"][l]
    cols = np.concatenate([np.arange(g * 128, (g + 1) * 128),
                           2064 + np.arange(g * 256, (g + 1) * 256),
                           512 + np.arange(g * 128, (g + 1) * 128),
                           1024 + np.arange(g * 256, (g + 1) * 256),
                           2048 + np.arange(16)])
    if hTlo_b is None:
        hTlo_b = np.zeros_like(hT_b)
    elif isinstance(hTlo_b, int):
        hTlo_b = None
    return {"hT": hT_b, "hTlo": hTlo_b, "w_gla": np.ascontiguousarray(w[:, cols]),
            "wgk2": np.ascontiguousarray(inp["gla_w_gk2"][l][:, g * 128:(g + 1) * 128]),
            "bgk": np.ascontiguousarray(inp["gla_b_gk"][l][g * 128:(g + 1) * 128].reshape(1, 128)),
            "ng": pvec(inp["gla_norm_g"][l]), "cst": cst}


def build_ssm(ntiles=SEQ // TT):
    nc = bass.Bass("TRN2", target_bir_lowering=False)
    hT = dram(nc, "hT", [DM, SEQ], BF16, "ExternalInput")
    A = {}
    A["w_ssm"] = dram(nc, "w_ssm", [DM, 1288], F32, "ExternalInput")
    A["cw"] = dram(nc, "cw", [128, 6, 4], F32, "ExternalInput")
    A["cb"] = dram(nc, "cb", [128, 6], F32, "ExternalInput")
    A["dtb"] = dram(nc, "dtb", [1, 8], F32, "ExternalInput")
    A["alog"] = dram(nc, "alog", [1, 8], F32, "ExternalInput")
    A["dsk"] = dram(nc, "dsk", [1, 8], F32, "ExternalInput")
    A["ngs"] = dram(nc, "ngs", [1, 512], F32, "ExternalInput")
    A["cst"] = dram(nc, "cst", [4, 128, 128], F32, "ExternalInput")
    yT = dram(nc, "yT", [512, SEQ], BF16, "ExternalOutput")
    A["load_h"] = lambda S, h, t: S.dma("sp", h[:], kcp(hT[:, t * TT:(t + 1) * TT]), writes=[h])
    A["store_y"] = lambda S, yo, t: S.dma("sp", yT[:, t * TT:(t + 1) * TT].rearrange("(c p) n -> p c n", p=128), yo[:], reads=[yo])
    with ExitStack() as st:
        S = Sched(nc, st)
        S.begin_phase()
        emit_ssm(nc, S, A, ntiles)
        S.end_phase()
    return nc


def emit_ssm(nc, S, A, ntiles=SEQ // TT):
    w_ssm, cwp, cbp, dtbp, alogp, dskp, ngp, cst = (A[k] for k in ("w_ssm", "cw", "cb", "dtb", "alog", "dsk", "ngs", "cst"))
    if True:
        C = load_consts(S, nc, cst)
        U = S.sb("U", [128, 128], F32)
        UC = S.sb("UC", [128, 128], F32)
        idf = S.sb("idf", [128, 128], F32)
        idb = S.sb("idb", [128, 128], BF16)
        S.dma("sp", U[:], cst[1], writes=[U])
        S.dma("sp", UC[:], cst[2], writes=[UC])
        S.dma("sp", idf[:], cst[3], writes=[idf])
        S.dma("pool", idb[:], cst[3], writes=[idb])
        ws = S.sb("ws", [128, 8, 1288], BF16)
        S.dma("pool", ws[:], kcp(w_ssm), writes=[ws])
        cw = S.sb("cw", [128, 6, 4], F32)
        cb = S.sb("cb", [128, 6], F32)
        S.dma("sp", cw[:], cwp, writes=[cw])
        S.dma("sp", cb[:], cbp, writes=[cb])
        dtb = S.sb("dtb", [128, 8], F32)
        a_b = S.sb("a_b", [128, 8], F32)
        dsk = S.sb("dsk", [128, 8], F32)
        ngs = S.sb("ngs", [128, 512], F32)
        S.dma("sp", dtb[:], dtbp.partition_broadcast(128), writes=[dtb])
        S.dma("sp", a_b[:], alogp.partition_broadcast(128), writes=[a_b])
        S.dma("sp", dsk[:], dskp.partition_broadcast(128), writes=[dsk])
        S.dma("sp", ngs[:], ngp.partition_broadcast(128), writes=[ngs])
        S.act(a_b, a_b[:], a_b, a_b[:], AF.Exp)
        S.ts("dve", a_b, a_b[:], a_b, a_b[:], -1.0, None, ALU.mult)

        def ring(name, shape, dt, n=2):
            return Ring([S.sb("%s%d" % (name, i), shape, dt) for i in range(n)])
        h_ring = ring("h", [128, 8, TT], BF16)
        big = Ring([S.ps("pb%d" % i, [128, 512], F32) for i in range(6)])
        sm = Ring(small_views(S, "sm", 2, 128))
        raw = S.sb("raw", [128, 6, TT + 3], F32)
        S.memset("dve", raw, raw[:, :, 0:3], 0.0)
        cacc_r = ring("cacc", [128, TT], F32)
        xc = S.sb("xc", [128, 4, TT], F32)
        BT = S.sb("BT", [128, TT], BF16)
        CT = S.sb("CT", [128, TT], BF16)
        sz_r = ring("sz", [128, 512], F32)
        t8_r = ring("t8", [128, 8], F32)
        dt_r = ring("dt", [128, 8], F32)
        da_r = ring("da", [128, 8], F32)
        xdt_r = ring("xdt", [128, 512], BF16)
        xD_r = ring("xD", [128, 512], F32)
        Btok_r = ring("Btok", [128, 128], BF16)
        rda_r = ring("rda", [128, 8, 128], F32)
        eL_r = ring("eL", [128, 8, 128], F32)
        GU_r = ring("GU", [128, 128], F32)
        M_r = ring("M", [128, 8, 128], BF16)
        cs_r = ring("cs", [128, 16], F32)
        ec_r = ring("ec", [128, 24], F32)
        xdtd_r = ring("xdtd", [128, 512], BF16)
        Sst = S.sb("Sst", [128, 512], F32)
        Sbf = S.sb("Sbf", [128, 512], BF16)
        S.memset("dve", Sst, Sst[:], 0.0)
        S.memset("dve", Sbf, Sbf[:], 0.0)
        y1_r = ring("y1", [128, 512], F32)
        y2_r = ring("y2", [128, 512], F32)
        junk = S.sb("junk", [128, 512], F32)
        ss_r = ring("ss", [128, 2], F32)
        yn_r = ring("yn", [128, 512], BF16)
        yo_r = ring("yo", [128, 4, TT], BF16)
        eps = C["eps%g" % EPS]

        def b8(ap):
            return ap.unsqueeze(2).to_broadcast([128, 8, 64])

        def v8(ap):
            return ap.rearrange("p (h q) -> p h q", h=8)

        for t in range(ntiles):
            ts_ = slice(t * TT, (t + 1) * TT)
            h = h_ring.next()
            A["load_h"](S, h, t)
            for ch in range(6):
                p = big.next()
                for kc in range(8):
                    S.mm(p, p[:], ws, ws[:, kc, 512 + ch * 128:640 + ch * 128], h, h[:, kc, :], start=(kc == 0), stop=(kc == 7))
                S.act(raw, raw[:, ch, 3:TT + 3], p, p[:], AF.Copy)
            for ch in range(6):
                acc = cacc_r.next()
                S.ts("dve", acc, acc[:], raw, raw[:, ch, 0:TT], cw[:, ch, 0:1], cb[:, ch:ch + 1], ALU.mult, ALU.add, rd=[cw, cb])
                for i in range(1, 4):
                    S.stt(acc, acc[:], raw, raw[:, ch, i:i + TT], cw[:, ch, i:i + 1], acc, acc[:], ALU.mult, ALU.add, rd=[cw])
                if ch < 4:
                    S.act(xc, xc[:, ch, :], acc, acc[:], AF.Silu)
                elif ch == 4:
                    S.act(BT, BT[:], acc, acc[:], AF.Silu)
                else:
                    S.act(CT, CT[:], acc, acc[:], AF.Silu)
            S.copy("pool", raw, raw[:, :, 0:3], raw, raw[:, :, TT:TT + 3])
            yo = yo_r.next()
            def stageA(c):
                cs = slice(c * 128, (c + 1) * 128)
                pz = big.next()
                for kc in range(8):
                    S.mm(pz, pz[:], h, h[:, kc, cs], ws, ws[:, kc, 0:512], start=(kc == 0), stop=(kc == 7))
                sz = sz_r.next()
                S.act(sz, sz[:], pz, pz[:], AF.Silu)
                pdt = sm.next()
                for kc in range(8):
                    S.mm(pdt, pdt[:, 0:8], h, h[:, kc, cs], ws, ws[:, kc, 1280:1288], start=(kc == 0), stop=(kc == 7))
                t8 = t8_r.next()
                S.tt("dve", t8, t8[:], pdt, pdt[:, 0:8], dtb, dtb[:], ALU.add)
                S.act(t8, t8[:], t8, t8[:], AF.Exp)
                dt = dt_r.next()
                S.act(dt, dt[:], t8, t8[:], AF.Ln, bias=1.0)
                da = da_r.next()
                S.tt("dve", da, da[:], dt, dt[:], a_b, a_b[:], ALU.mult)
                px = big.next()
                for ch in range(4):
                    S.mm(px, px[:, ch * 128:(ch + 1) * 128], xc, xc[:, ch, cs], idf, idf[:])
                xdt = xdt_r.next()
                xD = xD_r.next()
                S.tt("dve", xdt, v8(xdt[:]), px, v8(px[:]), dt, b8(dt[:]), ALU.mult)
                S.tt("dve", xD, v8(xD[:]), px, v8(px[:]), dsk, b8(dsk[:]), ALU.mult)
                pB = sm.next()
                S.mm(pB, pB[:], BT, BT[:, cs], idb, idb[:])
                Btok = Btok_r.next()
                S.copy("act", Btok, Btok[:], pB, pB[:])
                rda = rda_r.next()
                S.tt("pool", rda, rda[:], U, U[:].unsqueeze(1).to_broadcast([128, 8, 128]), da, da[:].unsqueeze(2).to_broadcast([128, 8, 128]), ALU.mult)
                pD = [big.next(), big.next()]
                for hf in range(2):
                    S.mm(pD[hf], pD[hf][:], UC, UC[:], rda, rda[:, hf * 4:(hf + 1) * 4, :].rearrange("p a b -> p (a b)"))
                pc = sm.next()
                S.mm(pc, pc[:, 0:8], U, U[:], da, da[:])
                pc2 = sm.next()
                S.mm(pc2, pc2[:, 0:8], C["ones"], C["ones"][:], da, da[:])
                csb = cs_r.next()
                S.copy("dve", csb, csb[:, 0:8], pc, pc[:, 0:8])
                S.copy("dve", csb, csb[:, 8:16], pc2, pc2[:, 0:8])
                ec = ec_r.next()
                S.act(ec, ec[:, 0:16], csb, csb[:, 0:16], AF.Exp)
                S.tt("dve", csb, csb[:, 0:8], csb, csb[:, 8:16], csb, csb[:, 0:8], ALU.subtract)
                S.act(ec, ec[:, 16:24], csb, csb[:, 0:8], AF.Exp)
                eL = eL_r.next()
                for hf in range(2):
                    S.act(eL, eL[:, hf * 4:(hf + 1) * 4, :].rearrange("p a b -> p (a b)"), pD[hf], pD[hf][:], AF.Exp)
                pG = sm.next()
                S.mm(pG, pG[:], BT, BT[:, cs], CT, CT[:, cs])
                GU = GU_r.next()
                S.tt("dve", GU, GU[:], pG, pG[:], U, U[:], ALU.mult)
                M = M_r.next()
                S.tt("pool", M, M[:], eL, eL[:], GU, GU[:].unsqueeze(1).to_broadcast([128, 8, 128]), ALU.mult)
                xdtd = xdtd_r.next()
                S.tt("pool", xdtd, v8(xdtd[:]), xdt, v8(xdt[:]), ec, b8(ec[:, 16:24]), ALU.mult)
                return dict(cs=cs, sz=sz, xdt=xdt, xD=xD, Btok=Btok, M=M, ec=ec, xdtd=xdtd)

            def stageB(c, v):
                cs, sz, xdt, xD, Btok, M, ec, xdtd = (v[k] for k in ('cs', 'sz', 'xdt', 'xD', 'Btok', 'M', 'ec', 'xdtd'))
                pyd = big.next()
                for hh in range(8):
                    S.mm(pyd, pyd[:, hh * 64:(hh + 1) * 64], M, M[:, hh, :], xdt, xdt[:, hh * 64:(hh + 1) * 64])
                pyo = big.next()
                S.mm(pyo, pyo[:], CT, CT[:, cs], Sbf, Sbf[:])
                pst = big.next()
                S.mm(pst, pst[:], Btok, Btok[:], xdtd, xdtd[:])
                S.tt("dve", Sst, v8(Sst[:]), Sst, v8(Sst[:]), ec, b8(ec[:, 8:16]), ALU.mult)
                S.tt("dve", Sst, Sst[:], Sst, Sst[:], pst, pst[:], ALU.add)
                S.copy("act", Sbf, Sbf[:], Sst, Sst[:])
                y1 = y1_r.next()
                S.tt("dve", y1, v8(y1[:]), pyo, v8(pyo[:]), ec, b8(ec[:, 0:8]), ALU.mult)
                S.tt("dve", y1, y1[:], y1, y1[:], pyd, pyd[:], ALU.add)
                y2 = y2_r.next()
                S.tt("pool", y2, y2[:], y1, y1[:], xD, xD[:], ALU.add)
                S.tt("pool", y2, y2[:], y2, y2[:], sz, sz[:], ALU.mult)
                ss = ss_r.next()
                S.act(junk, junk[:], y2, y2[:], AF.Square, accum=ss[:, 0:1], wr=[ss])
                S.act(ss, ss[:, 1:2], ss, ss[:, 0:1], AF.Sqrt, bias=eps[:, 0:1], scale=1.0 / 512, rd=[eps])
                S.recip(ss, ss[:, 0:1], ss, ss[:, 1:2])
                yn = yn_r.next()
                S.stt(yn, yn[:], y2, y2[:], ss[:, 0:1], ngs, ngs[:], ALU.mult, ALU.mult, rd=[ss])
                pT = big.next()
                for ch in range(4):
                    S.mm(pT, pT[:, ch * 128:(ch + 1) * 128], yn, yn[:, ch * 128:(ch + 1) * 128], idb, idb[:])
                S.copy("act", yo, yo[:, :, cs], pT, pT[:].rearrange("p (c n) -> p c n", c=4))

            va = {0: stageA(0)}
            for c in range(4):
                if c + 1 < 4:
                    va[c + 1] = stageA(c + 1)
                stageB(c, va.pop(c))
            A["store_y"](S, yo, t)


def ssm_inputs(inp, l, g, hT_b, cst):
    w = inp["w_in"][l]
    xcols = 5136 + np.arange(g * 512, (g + 1) * 512)
    bcols = 7184 + np.arange(g * 128, (g + 1) * 128)
    ccols = 7696 + np.arange(g * 128, (g + 1) * 128)
    cols = np.concatenate([3088 + np.arange(g * 512, (g + 1) * 512), xcols, bcols, ccols, 8208 + np.arange(g * 8, (g + 1) * 8)])
    cc = np.concatenate([xcols, bcols, ccols]) - 5136
    cwv = inp["ssm_conv_w"][l][:, cc]
    cw = np.ascontiguousarray(cwv.reshape(4, 6, 128).transpose(2, 1, 0))
    cb = np.ascontiguousarray(inp["ssm_conv_b"][l][cc].reshape(6, 128).T)
    hs = slice(g * 8, (g + 1) * 8)
    return {"hT": hT_b, "w_ssm": np.ascontiguousarray(w[:, cols]), "cw": cw, "cb": cb,
            "dtb": np.ascontiguousarray(inp["ssm_dt_bias"][l][hs].reshape(1, 8)),
            "alog": np.ascontiguousarray(inp["ssm_a_log"][l][hs].reshape(1, 8)),
            "dsk": np.ascontiguousarray(inp["ssm_d"][l][hs].reshape(1, 8)),
            "ngs": np.ascontiguousarray(inp["ssm_norm_g"][l][g * 512:(g + 1) * 512].reshape(1, 512)),
            "cst": cst}


C1_2PI = 6.28125
C2_2PI = 2.0 * math.pi - 6.28125


def build_diff(l, ntiles=SEQ // TT):
    nc = bass.Bass("TRN2", target_bir_lowering=False)
    hT = dram(nc, "hT", [DM, SEQ], BF16, "ExternalInput")
    A = {}
    A["w_diff"] = dram(nc, "w_diff", [DM, 1280], F32, "ExternalInput")
    A["pos"] = dram(nc, "pos", [1, SEQ], I32, "ExternalInput")
    A["invf"] = dram(nc, "invf", [128, 2], F32, "ExternalInput")
    A["lqk"] = dram(nc, "lqk", [4, 64], F32, "ExternalInput")
    A["ngd"] = dram(nc, "ngd", [1, 128], F32, "ExternalInput")
    A["cst"] = dram(nc, "cst", [4, 128, 128], F32, "ExternalInput")
    yT = dram(nc, "yT", [256, SEQ], BF16, "ExternalOutput")
    A["load_h"] = lambda S, h, t: S.dma("sp", h[:], kcp(hT[:, t * TT:(t + 1) * TT]), writes=[h])
    A["store_y"] = lambda S, yo, t: S.dma("sp", yT[:, t * TT:(t + 1) * TT].rearrange("(c p) n -> p c n", p=128), yo[:], reads=[yo])
    with ExitStack() as st:
        S = Sched(nc, st)
        S.begin_phase()
        emit_diff(nc, S, A, l, ntiles)
        S.end_phase()
    return nc


def emit_diff(nc, S, A, l, ntiles=SEQ // TT):
    lambda_init = 0.8 - 0.6 * math.exp(-0.3 * l)
    w_diff, posd, invfp, lqk, ngp, cst = (A[k] for k in ("w_diff", "pos", "invf", "lqk", "ngd", "cst"))
    if True:
        C = load_consts(S, nc, cst, extra_eps=(1e-5,))
        idb = S.sb("idb", [128, 128], BF16)
        S.dma("pool", idb[:], cst[3], writes=[idb])
        wd = S.sb("wd", [128, 8, 1280], BF16)
        S.dma("pool", wd[:], kcp(w_diff), writes=[wd])
        invf = S.sb("invf", [128, 2], F32)
        S.dma("sp", invf[:], invfp, writes=[invf])
        ngd = S.sb("ngd", [128, 128], F32)
        S.dma("sp", ngd[:], ngp.partition_broadcast(128), writes=[ngd])
        S.ts("dve", ngd, ngd[:], ngd, ngd[:], 1.0 - lambda_init, None, ALU.mult)
        lq = S.sb("lq", [128, 4, 64], F32)
        for i in range(4):
            S.dma("sp", lq[:, i, :], lqk[i:i + 1, :].partition_broadcast(128), writes=[lq], group=(i > 0))
        lt = S.sb("lt", [128, 2, 64], F32)
        S.tt("dve", lt, lt[:, 0, :], lq, lq[:, 0, :], lq, lq[:, 1, :], ALU.mult)
        S.tt("dve", lt, lt[:, 1, :], lq, lq[:, 2, :], lq, lq[:, 3, :], ALU.mult)
        ls = S.sb("ls", [128, 4], F32)
        S.op("dve", lambda E: E.reduce_sum(out=ls[:, 0:2], in_=lt[:], axis=AX.X), reads=[lt], writes=[ls])
        S.act(ls, ls[:, 0:2], ls, ls[:, 0:2], AF.Exp)
        S.tt("dve", ls, ls[:, 2:3], ls, ls[:, 1:2], ls, ls[:, 0:1], ALU.subtract)
        S.ts("dve", ls, ls[:, 3:4], ls, ls[:, 2:3], -lambda_init, None, ALU.add)
        nlam = ls

        def ring(name, shape, dt, n=2):
            return Ring([S.sb("%s%d" % (name, i), shape, dt) for i in range(n)])
        h_ring = ring("h", [128, 8, TT], BF16)
        KT = S.sb("KT", [128, 2, SEQ], BF16)
        VA = S.sb("VA", [128, 2, SEQ // 128, 129], BF16)
        S.memset("pool", VA, VA[:, :, :, 128:129], 1.0)
        QT_r = ring("QT", [128, 2, TT], BF16)
        big = Ring([S.ps("pb%d" % i, [128, 512], F32) for i in range(3)])
        pob = [[S.ps("po%d_%d" % (s, hf), [128, 512], F32) for hf in range(2)] for s in range(2)]
        sm = Ring(small_views(S, "sm", 1, 128))
        posi = S.sb("posi", [128, TT], I32)
        ang = S.sb("ang", [128, TT], F32)
        ang2 = S.sb("ang2", [128, TT], F32)
        ki = S.sb("ki", [128, TT], I32)
        kf = S.sb("kf", [128, TT], F32)
        yr = S.sb("yr", [128, TT], F32)
        Cs = S.sb("Cs", [128, TT], F32)
        Sn = S.sb("Sn", [128, TT], F32)
        ta_r = ring("ta", [128, TT], F32)
        tb_r = ring("tb", [128, TT], F32)
        PT_r = ring("PT", [128, TT], BF16, 5)
        r_r = ring("r", [128, 4], F32, 4)
        oa_r = ring("oa", [128, 128], F32, 4)
        junk_r = ring("junk", [128, 128], F32, 4)
        yn_r = ring("yn", [128, 128], BF16, 4)
        yo_r = ring("yo", [128, 2, TT], BF16)
        eps5 = C["eps%g" % 1e-5]

        def reduce_sin(dst, src):
            S.ts("dve", ki, ki[:], src, src[:], 1.0 / (2.0 * math.pi), None, ALU.mult)
            S.copy("dve", kf, kf[:], ki, ki[:])
            S.stt(yr, yr[:], kf, kf[:], -C1_2PI, src, src[:], ALU.mult, ALU.add)
            S.stt(yr, yr[:], kf, kf[:], -C2_2PI, yr, yr[:], ALU.mult, ALU.add)
            S.ts("dve", yr, yr[:], yr, yr[:], -3.1415925, 3.1415925, ALU.max, ALU.min)
            S.act(dst, dst[:], yr, yr[:], AF.Sin)

        for t in range(ntiles):
            ts_ = slice(t * TT, (t + 1) * TT)
            h = h_ring.next()
            A["load_h"](S, h, t)
            S.dma("sp", posi[:], posd[0:1, ts_].partition_broadcast(128), writes=[posi])
            S.copy("dve", ang, ang[:], posi, posi[:])
            S.ts("dve", ang, ang[:], ang, ang[:], invf[:, 0:1], None, ALU.mult, rd=[invf])
            reduce_sin(Sn, ang)
            S.ts("dve", Sn, Sn[:], Sn, Sn[:], invf[:, 1:2], None, ALU.mult, rd=[invf])
            S.ts("dve", ang2, ang2[:], ang, ang[:], math.pi / 2.0, None, ALU.add)
            reduce_sin(Cs, ang2)
            QT = QT_r.next()
            for hd in range(2):
                for (c0, dstb, dst) in ((hd * 128, QT, QT[:, hd, :]), (512 + hd * 128, KT, KT[:, hd, ts_])):
                    p1 = big.next()
                    for kc in range(8):
                        S.mm(p1, p1[:], wd, wd[:, kc, c0:c0 + 128], h, h[:, kc, :], start=(kc == 0), stop=(kc == 7))
                    p2 = big.next()
                    for kc in range(8):
                        S.mm(p2, p2[:], wd, wd[:, kc, c0 + 256:c0 + 384], h, h[:, kc, :], start=(kc == 0), stop=(kc == 7))
                    ta = ta_r.next()
                    tb = tb_r.next()
                    S.tt("dve", ta, ta[:], p1, p1[:], Cs, Cs[:], ALU.mult)
                    S.tt("dve", tb, tb[:], p2, p2[:], Sn, Sn[:], ALU.mult)
                    S.tt("pool", dstb, dst, ta, ta[:], tb, tb[:], ALU.add)
            for c in range(4):
                cs = slice(c * 128, (c + 1) * 128)
                pv = big.next()
                for kc in range(8):
                    S.mm(pv, pv[:, 0:256], h, h[:, kc, cs], wd, wd[:, kc, 1024:1280], start=(kc == 0), stop=(kc == 7))
                S.copy("act", VA, VA[:, :, 4 * t + c, 0:128], pv, pv[:, 0:256].rearrange("p (a b) -> p a b", a=2))
            yo = yo_r.next()
            for hd in range(2):
                nkb = 4 * t + 4
                started = [[False, False], [False, False]]
                iters = [(kb, s_) for kb in range(nkb) for s_ in range(2)]
                pend = {}

                def emit_qk(i):
                    kb, s_ = iters[i]
                    r = kb - 4 * t
                    q0 = max(r, 0)
                    qlo = q0 * 128
                    n = TT - qlo
                    ps_ = slice(s_ * 64, (s_ + 1) * 64)
                    pS = big.next()
                    S.mm(pS, pS[:, :n], KT, KT[ps_, hd, kb * 128:(kb + 1) * 128], QT, QT[ps_, hd, qlo:TT])
                    PT = PT_r.next()
                    S.act(PT, PT[:, :n], pS, pS[:, :n], AF.Exp, scale=0.125)
                    if r >= 0:
                        S.memset("pool", PT, PT[64:128, 0:64], 0.0)
                    pend[i] = (PT, q0, qlo)

                def emit_pv(i):
                    kb, s_ = iters[i]
                    PT, q0, qlo = pend.pop(i)
                    for qb in range(q0, 4):
                        col = qb * 128 - qlo
                        bank = pob[s_][qb // 2]
                        o = bank[:, (qb % 2) * 129:(qb % 2) * 129 + 129]
                        first = not started[s_][qb // 2]
                        started[s_][qb // 2] = True
                        S.mm(bank, o, PT, PT[:, col:col + 128], VA, VA[:, hd, kb, :], start=first, stop=(kb == 4 * t + qb and qb % 2 == 1))

                LOOK = 3
                for i in range(len(iters) + LOOK):
                    if i < len(iters):
                        emit_qk(i)
                    if i >= LOOK:
                        emit_pv(i - LOOK)
                QB = range(4)
                o1 = [pob[0][qb // 2] for qb in QB]
                o2 = [pob[1][qb // 2] for qb in QB]
                b0 = [(qb % 2) * 129 for qb in QB]
                rr = [r_r.next() for qb in QB]
                oa = [oa_r.next() for qb in QB]
                yn = [yn_r.next() for qb in QB]
                jk = [junk_r.next() for qb in QB]
                for qb in QB:
                    S.recip(rr[qb], rr[qb][:, 0:1], o1[qb], o1[qb][:, b0[qb] + 128:b0[qb] + 129])
                for qb in QB:
                    S.recip(rr[qb], rr[qb][:, 1:2], o2[qb], o2[qb][:, b0[qb] + 128:b0[qb] + 129])
                for qb in QB:
                    S.tt("dve", rr[qb], rr[qb][:, 2:3], rr[qb], rr[qb][:, 1:2], nlam, nlam[:, 3:4], ALU.mult)
                for qb in QB:
                    S.ts("dve", oa[qb], oa[qb][:], o1[qb], o1[qb][:, b0[qb]:b0[qb] + 128], rr[qb][:, 0:1], None, ALU.mult, rd=[rr[qb]])
                for qb in QB:
                    S.stt(oa[qb], oa[qb][:], o2[qb], o2[qb][:, b0[qb]:b0[qb] + 128], rr[qb][:, 2:3], oa[qb], oa[qb][:], ALU.mult, ALU.add, rd=[rr[qb]])
                for qb in QB:
                    S.act(jk[qb], jk[qb][:], oa[qb], oa[qb][:], AF.Square, accum=rr[qb][:, 3:4], wr=[rr[qb]])
                for qb in QB:
                    S.act(rr[qb], rr[qb][:, 1:2], rr[qb], rr[qb][:, 3:4], AF.Sqrt, bias=eps5[:, 0:1], scale=1.0 / 128, rd=[eps5])
                for qb in QB:
                    S.recip(rr[qb], rr[qb][:, 0:1], rr[qb], rr[qb][:, 1:2])
                for qb in QB:
                    S.stt(yn[qb], yn[qb][:], oa[qb], oa[qb][:], rr[qb][:, 0:1], ngd, ngd[:], ALU.mult, ALU.mult, rd=[rr[qb]])
                for qb in QB:
                    pT = sm.next()
                    S.mm(pT, pT[:], yn[qb], yn[qb][:], idb, idb[:])
                    S.copy("act", yo, yo[:, hd, qb * 128:(qb + 1) * 128], pT, pT[:])
            A["store_y"](S, yo, t)


def diff_inputs(inp, l, g, b, hT_b, cst):
    w = inp["w_in"][l]
    qc = 8240 + np.arange(g * 256, (g + 1) * 256)
    kc = 9264 + np.arange(g * 256, (g + 1) * 256)
    vc = 10288 + np.arange(g * 256, (g + 1) * 256)
    d = np.arange(256) % 64
    partner = np.arange(256) + np.where(d < 8, 8, np.where(d < 16, -8, 0))
    cols = np.concatenate([qc, qc[partner], kc, kc[partner], vc])
    p = np.arange(128) % 64
    invf = np.zeros((128, 2), np.float32)
    fr = (500000.0 ** (-np.arange(0, 16, 2, dtype=np.float32) / np.float32(16))).astype(np.float32)
    invf[:, 0] = np.where(p < 16, fr[p % 8], 0.0)
    invf[:, 1] = np.where(p < 8, -1.0, np.where(p < 16, 1.0, 0.0))
    lqk = np.stack([inp["diff_lq1"][l], inp["diff_lk1"][l], inp["diff_lq2"][l], inp["diff_lk2"][l]]).astype(np.float32)
    return {"hT": hT_b, "w_diff": np.ascontiguousarray(w[:, cols]),
            "pos": np.ascontiguousarray(inp["positions"][b].reshape(1, SEQ)), "invf": invf, "lqk": lqk,
            "ngd": np.ascontiguousarray(inp["diff_norm_g"][l].reshape(1, 128)), "cst": cst}


_PROGS = {}


def _prog(key, fn):
    if key not in _PROGS:
        _PROGS[key] = fn()
    return _PROGS[key]


def _run(nc, maps):
    return run_bass_kernel_spmd(nc, maps, core_ids=list(range(NCORES))).results


def tok_inputs(inp, l, xT_c, hT_c, yT_c, g_next, cst):
    w = inp["w_in"][l]
    return {"xT": xT_c, "g_next": pvec(g_next), "cst": cst, "hT": hT_c, "yT": yT_c,
            "w_gate": np.ascontiguousarray(w[:, 11312:14384]), "b_gate": pvec(inp["b_gate"][l]),
            "w_br": np.ascontiguousarray(np.concatenate([inp["w_br_gla"][l], inp["w_br_ssm"][l], inp["w_br_diff"][l]], axis=0)),
            "w_out": np.ascontiguousarray(inp["w_out"][l]), "w_up": np.ascontiguousarray(inp["w_mlp_up"][l]),
            "w_dn": np.ascontiguousarray(inp["w_mlp_down"][l]), "g_mlp": pvec(inp["norm_mlp_g"][l])}


def kernel_unfused(**inp):
    inp = {k: np.asarray(v) for k, v in inp.items()}
    cst = make_consts()
    x = inp["x"]
    xT = [np.ascontiguousarray(x[c // 4, (c % 4) * NT:(c % 4 + 1) * NT, :].T) for c in range(NCORES)]
    res = _run(_prog("first", lambda: build_tok("first")),
               [{"xT": xT[c], "g_next": pvec(inp["norm_mix_g"][0]), "cst": cst} for c in range(NCORES)])
    hT = [r["hT_out"] for r in res]
    hTl = [r["hT_lo"] for r in res]
    out = None
    for l in range(DEPTH):
        hTb = [np.ascontiguousarray(np.concatenate(hT[b * 4:(b + 1) * 4], axis=1)) for b in range(B)]
        hTlb = [np.ascontiguousarray(np.concatenate(hTl[b * 4:(b + 1) * 4], axis=1)) for b in range(B)]
        yg = _run(_prog("gla", build_gla), [gla_inputs(inp, l, c % 4, hTb[c // 4], cst, hTlb[c // 4]) for c in range(NCORES)])
        ys = _run(_prog("ssm", build_ssm), [ssm_inputs(inp, l, c % 4, hTb[c // 4], cst) for c in range(NCORES)])
        yd = _run(_prog(("diff", l), lambda: build_diff(l)), [diff_inputs(inp, l, c % 4, c // 4, hTb[c // 4], cst) for c in range(NCORES)])
        yTb = []
        for b in range(B):
            parts = [yg[b * 4 + g]["yT"] for g in range(4)] + [ys[b * 4 + g]["yT"] for g in range(4)] + [yd[b * 4 + g]["yT"] for g in range(4)]
            yTb.append(np.concatenate(parts, axis=0))
        last = (l == DEPTH - 1)
        g_next = inp["norm_final_g"] if last else inp["norm_mix_g"][l + 1]
        maps = [tok_inputs(inp, l, xT[c], hT[c], np.ascontiguousarray(yTb[c // 4][:, (c % 4) * NT:(c % 4 + 1) * NT]), g_next, cst)
                for c in range(NCORES)]
        mode = "last" if last else "mid"
        res = _run(_prog(mode, lambda: build_tok(mode)), maps)
        if last:
            out = np.empty((B, SEQ, DM), np.float32)
            for c in range(NCORES):
                out[c // 4, (c % 4) * NT:(c % 4 + 1) * NT, :] = res[c]["outT"].T
        else:
            xT = [r["xT_out"] for r in res]
            hT = [r["hT_out"] for r in res]
            hTl = [r["hT_lo"] for r in res]
    return out


GROUPS = [[0, 1, 2, 3], [4, 5, 6, 7]]


def build_fused(skip=()):
    nc = bass.Bass("TRN2", target_bir_lowering=False)
    ext = lambda name, shape, dt=F32: dram(nc, name, shape, dt, "ExternalInput")
    xT = ext("xT", [DM, NT])
    pos = ext("pos", [1, SEQ], I32)
    cst = ext("cst", [4, 128, 128])
    invf = ext("invf", [128, 2])
    g_mix = ext("g_mix", [DEPTH, 128, 8])
    g_fin = ext("g_fin", [128, 8])
    g_mlp = ext("g_mlp", [DEPTH, 128, 8])
    w_gate = ext("w_gate", [DEPTH, DM, 3072])
    b_gate = ext("b_gate", [DEPTH, 128, 24])
    w_br = ext("w_br", [DEPTH, 4096, DM])
    w_out = ext("w_out", [DEPTH, DM, DM])
    w_up = ext("w_up", [DEPTH, DM, 4096])
    w_dn = ext("w_dn", [DEPTH, 4096, DM])
    w_gla = ext("w_gla", [DEPTH, DM, 784])
    wgk2 = ext("wgk2", [DEPTH, 16, 128])
    bgk = ext("bgk", [DEPTH, 1, 128])
    ng = ext("ng", [DEPTH, 128, 2])
    w_ssm = ext("w_ssm", [DEPTH, DM, 1288])
    cw = ext("cw", [DEPTH, 128, 6, 4])
    cb = ext("cb", [DEPTH, 128, 6])
    dtb = ext("dtb", [DEPTH, 1, 8])
    alog = ext("alog", [DEPTH, 1, 8])
    dsk = ext("dsk", [DEPTH, 1, 8])
    ngs = ext("ngs", [DEPTH, 1, 512])
    w_diff = ext("w_diff", [DEPTH, DM, 1280])
    lqk = ext("lqk", [DEPTH, 4, 64])
    ngd = ext("ngd", [DEPTH, 1, 128])
    outT = dram(nc, "outT", [DM, NT], F32, "ExternalOutput")
    xs = nc.dram_tensor("xs_i", [DM, NT], F32).ap()
    hsrc = nc.dram_tensor("hsrc_i", [8, 256, NT], BF16).ap()
    hgat = nc.dram_tensor("hgat_i", [8, 1024, NT], BF16).ap()
    ysrc = nc.dram_tensor("ysrc_i", [4, 4, 256, NT], BF16).ap()
    ygat = nc.dram_tensor("ygat_i", [4, 4, 1024, NT], BF16).ap()
    hT_own = hsrc[0:4].rearrange("c r n -> (c r) n")
    hTlo_own = hsrc[4:8].rearrange("c r n -> (c r) n")
    ygat3 = ygat.rearrange("q a r n -> q (a r) n")

    with ExitStack() as st:
        S = Sched(nc, st)
        hsrc_b = Buf("hsrc_b")
        hgat_b = [Buf("hgat_b%d" % i) for i in range(8)]
        ysrc_b = [[Buf("ysrc_b%d_%d" % (q, a)) for a in range(4)] for q in range(4)]
        ygat_b = Buf("ygat_b")
        xs_b = Buf("xs_b")
        S.global_bufs += [hsrc_b, ygat_b, xs_b] + hgat_b + [b for row in ysrc_b for b in row]

        def gather_h():
            for i in range(8):
                S.collective("AllGather", hsrc_b, hsrc[i], hgat_b[i], hgat[i], GROUPS)

        def load_h_from(chunks):
            def f(S_, h, t):
                r, tl = t // (NT // TT), (t % (NT // TT)) * TT
                for j, i in enumerate(chunks):
                    S_.dma("sp", h[:, 2 * j:2 * j + 2, :], hgat[i][r * 256:(r + 1) * 256, tl:tl + TT].rearrange("(c p) n -> p c n", p=128),
                           reads=[hgat_b[i]], writes=[h], group=(j > 0))
            return f

        def store_y_parts(parts):
            def f(S_, yo, t):
                q, tl = t // (NT // TT), (t % (NT // TT)) * TT
                for j, a in enumerate(parts):
                    S_.dma("sp", ysrc[q, a][:, tl:tl + TT].rearrange("(c p) n -> p c n", p=128), yo[:, 2 * j:2 * j + 2, :],
                           reads=[yo], writes=[ysrc_b[q][a]], sembuf=yo, group=True)
                if t % (NT // TT) == NT // TT - 1:
                    for a in parts:
                        S_.collective("AllGather", ysrc_b[q][a], ysrc[q, a], ygat_b, ygat[q, a], GROUPS)
            return f

        qcache = {}

        def load_y(S_, yub, t):
            tl = t * TT
            ph = S_.phase_id

            def qval(E):
                if ph not in qcache:
                    qcache[ph] = E.snap(E.partition_id() % 4)
                return qcache[ph]
            src = (lambda E, tl=tl: ygat3[bass.ds(qval(E), 1), :, tl:tl + TT].rearrange("o (k p) n -> p (o k) n", p=128))
            S_.dma("sp", yub[:], src, reads=[ygat_b], writes=[yub])

        S.begin_phase()
        if "first" not in skip:
            emit_tok(nc, S, {"xT": xT, "g_next": g_mix[0], "cst": cst, "hT_out": hT_own, "hT_lo": hTlo_own, "h_wr": [hsrc_b]}, "first")
        if "gather" not in skip:
            gather_h()
        S.end_phase()
        for l in range(DEPTH):
            S.begin_phase()
            if "gla" not in skip:
              emit_gla(nc, S, {"w_gla": w_gla[l], "wgk2": wgk2[l], "bgk": bgk[l], "ng": ng[l], "cst": cst,
                             "load_h": load_h_from((0, 1, 2, 3)), "load_hlo": load_h_from((4, 5, 6, 7)), "store_y": store_y_parts((0,))}, SEQ // TT)
            S.end_phase()
            S.begin_phase()
            if "ssm" not in skip:
              emit_ssm(nc, S, {"w_ssm": w_ssm[l], "cw": cw[l], "cb": cb[l], "dtb": dtb[l], "alog": alog[l], "dsk": dsk[l], "ngs": ngs[l],
                             "cst": cst, "load_h": load_h_from((0, 1, 2, 3)), "store_y": store_y_parts((1, 2))}, SEQ // TT)
            S.end_phase()
            S.begin_phase()
            if "diff" not in skip:
              emit_diff(nc, S, {"w_diff": w_diff[l], "pos": pos, "invf": invf, "lqk": lqk[l], "ngd": ngd[l], "cst": cst,
                              "load_h": load_h_from((0, 1, 2, 3)), "store_y": store_y_parts((3,))}, l, SEQ // TT)
            S.end_phase()
            last = (l == DEPTH - 1)
            A = {"xT": (xT if l == 0 else xs), "g_next": (g_fin if last else g_mix[l + 1]), "cst": cst, "hT": hT_own, "load_y": load_y,
                 "w_gate": w_gate[l], "b_gate": b_gate[l], "w_br": w_br[l], "w_out": w_out[l], "w_up": w_up[l], "w_dn": w_dn[l], "g_mlp": g_mlp[l]}
            if last:
                A["outT"] = outT
            else:
                A.update({"hT_out": hT_own, "hT_lo": hTlo_own, "h_wr": [hsrc_b], "xT_out": xs})
            S.begin_phase()
            emit_tok(nc, S, A, "last" if last else "mid")
            if not last:
                gather_h()
            S.end_phase()
    return nc


def fused_y_row_order():
    rows = []
    for a in range(4):
        for r in range(4):
            j = np.arange(256)
            if a == 0:
                rows.append(r * 256 + j)
            elif a in (1, 2):
                rows.append(1024 + r * 512 + (a - 1) * 256 + j)
            else:
                rows.append(3072 + r * 256 + j)
    return np.concatenate(rows)


def fused_inputs(inp, c, cst):
    b, g = c // 4, c % 4
    x = inp["x"]
    m = {"xT": np.ascontiguousarray(x[b, g * NT:(g + 1) * NT, :].T), "cst": cst,
         "pos": np.ascontiguousarray(inp["positions"][b].reshape(1, SEQ)),
         "g_mix": np.stack([pvec(inp["norm_mix_g"][l]) for l in range(DEPTH)]), "g_fin": pvec(inp["norm_final_g"]),
         "g_mlp": np.stack([pvec(inp["norm_mlp_g"][l]) for l in range(DEPTH)]),
         "w_gate": np.ascontiguousarray(inp["w_in"][:, :, 11312:14384]),
         "b_gate": np.stack([pvec(inp["b_gate"][l]) for l in range(DEPTH)]),
         "w_br": np.ascontiguousarray(np.concatenate([inp["w_br_gla"], inp["w_br_ssm"], inp["w_br_diff"]], axis=1)[:, fused_y_row_order(), :]),
         "w_out": np.ascontiguousarray(inp["w_out"]), "w_up": np.ascontiguousarray(inp["w_mlp_up"]), "w_dn": np.ascontiguousarray(inp["w_mlp_down"])}
    per = {}
    for l in range(DEPTH):
        d = {}
        d.update(gla_inputs(inp, l, g, None, cst, 0))
        d.update(ssm_inputs(inp, l, g, None, cst))
        d.update(diff_inputs(inp, l, g, b, None, cst))
        for k, v in d.items():
            if k in ("hT", "hTlo", "cst", "pos", "invf"):
                continue
            per.setdefault(k, []).append(v)
        if l == 0:
            m["invf"] = d["invf"]
    for k, v in per.items():
        m[k] = np.ascontiguousarray(np.stack(v))
    return m


_FUSED = []


def kernel(**inp):
    inp = {k: np.asarray(v) for k, v in inp.items()}
    cst = make_consts()
    if not _FUSED:
        _FUSED.append(build_fused())
    maps = [fused_inputs(inp, c, cst) for c in range(NCORES)]
    res = run_bass_kernel_spmd(_FUSED[0], maps, core_ids=list(range(NCORES))).results
    out = np.empty((B, SEQ, DM), np.float32)
    for c in range(NCORES):
        out[c // 4, (c % 4) * NT:(c % 4 + 1) * NT, :] = res[c]["outT"].T
    return out
```

```python
import math
from contextlib import ExitStack
import numpy as np
import ml_dtypes
import concourse.bass as bass
import concourse.mybir as mybir
from concourse.bass_utils import run_bass_kernel_spmd

F32 = mybir.dt.float32
BF16 = mybir.dt.bfloat16
I32 = mybir.dt.int32
AF = mybir.ActivationFunctionType
ALU = mybir.AluOpType
AX = mybir.AxisListType

NCORES = 8
B, SEQ, DM, DEPTH = 2, 8192, 1024, 2
NT = 2048
TT = 512
EPS = 1e-6
ENGS = ("pe", "act", "dve", "pool", "sp")


class Buf:
    __slots__ = ("name", "t", "w", "r", "dsem", "dcnt")

    def __init__(self, name, t=None):
        self.name = name
        self.t = t
        self.w = None
        self.r = {}
        self.dsem = None
        self.dcnt = 0

    def __getitem__(self, idx):
        return self.t[idx]


class VBuf:
    def __init__(self, name, parent, ap):
        self.name = name
        self.parent = parent
        self.t = ap

    def __getitem__(self, idx):
        return self.t[idx]

    w = property(lambda s: s.parent.w, lambda s, v: setattr(s.parent, "w", v))
    r = property(lambda s: s.parent.r, lambda s, v: setattr(s.parent, "r", v))
    dsem = property(lambda s: s.parent.dsem, lambda s, v: setattr(s.parent, "dsem", v))
    dcnt = property(lambda s: s.parent.dcnt, lambda s, v: setattr(s.parent, "dcnt", v))


class Sched:
    def __init__(self, nc, stack):
        self.nc = nc
        self.stack = stack
        self.q = {e: [] for e in ENGS}
        self.sem = {}
        for e in ENGS:
            self.sem[e] = stack.enter_context(nc.semaphore("s_" + e))
        self.cnt = {e: 0 for e in ENGS}
        self.waited = {e: {} for e in ENGS}
        self.ndsem = 0
        self.final_waits = {}
        self.alloc_stack = stack
        self.nname = 0
        self.free_dsems = {}
        self.dsem_cls = {}
        self.phase_bufs = []
        self.global_bufs = []

    def sb(self, name, shape, dt):
        self.nname += 1
        t = self.alloc_stack.enter_context(self.nc.sbuf_tensor("sb%d_%s" % (self.nname, name), list(shape), dt))
        b = Buf(name, t)
        self.phase_bufs.append(b)
        return b

    def ps(self, name, shape, dt=F32):
        self.nname += 1
        t = self.alloc_stack.enter_context(self.nc.psum_tensor("ps%d_%s" % (self.nname, name), list(shape), dt))
        return Buf(name, t)

    def view(self, name, ap):
        return Buf(name, ap)

    def _dsem(self, b, eng="sp"):
        cls = {"pool": "sw", "cc": "cc"}.get(eng, "hw")
        if b.dsem is not None:
            assert self.dsem_cls[b.dsem] == cls, (b.name, cls)
        if b.dsem is None:
            pool = self.free_dsems.setdefault(cls, [])
            if pool:
                b.dsem, b.dcnt = pool.pop()
            else:
                b.dsem = "d%d" % self.ndsem
                self.ndsem += 1
                self.sem[b.dsem] = self.stack.enter_context(self.nc.semaphore(b.dsem))
                self.dsem_cls[b.dsem] = cls
        return b.dsem

    def _deps(self, eng, reads, writes, same_ok):
        deps = {}

        def add(k, v):
            if deps.get(k, 0) < v:
                deps[k] = v

        for b in reads:
            if b.w is not None:
                add(*b.w)
        for b in writes:
            if b.w is not None:
                add(*b.w)
            for k, v in b.r.items():
                add(k, v)
        waits = []
        wd = self.waited[eng]
        for k, v in deps.items():
            if k == eng and same_ok:
                continue
            if wd.get(k, 0) >= v:
                continue
            wd[k] = v
            waits.append((k, v))
        return waits

    def _commit(self, tk, reads, writes):
        k, v = tk
        for b in writes:
            b.w = tk
            b.r = {}
        for b in reads:
            if b.r.get(k, 0) < v:
                b.r[k] = v

    def op(self, eng, fn, reads=(), writes=()):
        waits = self._deps(eng, reads, writes, same_ok=(eng == "pe"))
        self.cnt[eng] += 1
        tk = (eng, self.cnt[eng])
        sem = self.sem
        me = sem[eng]

        def emit(E):
            for k, v in waits:
                E.wait_ge(sem[k], v)
            fn(E).then_inc(me, 1)

        self.q[eng].append(emit)
        self._commit(tk, reads, writes)
        return tk

    def dma(self, eng, out_ap, in_ap, reads=(), writes=(), sembuf=None, group=False):
        sb = sembuf if sembuf is not None else (writes[0] if writes else reads[0])
        dk = self._dsem(sb, eng)
        saved = None
        if group and writes and writes[0].w is not None and writes[0].w[0] == dk:
            saved = writes[0].w
            writes[0].w = None
        waits = self._deps(eng, reads, writes, same_ok=False)
        if saved is not None:
            writes[0].w = saved
        sb.dcnt += 16
        tk = (dk, sb.dcnt)
        sem = self.sem
        ds = sem[dk]

        def emit(E):
            for k, v in waits:
                E.wait_ge(sem[k], v)
            E.dma_start(out=out_ap, in_=(in_ap(E) if callable(in_ap) else in_ap)).then_inc(ds, 16)

        self.q[eng].append(emit)
        self._commit(tk, reads, writes)
        self.final_waits[dk] = sb.dcnt
        return tk

    def collective(self, kind, sb_, s_ap, db_, d_ap, groups):
        dk = self._dsem(db_, "cc")
        waits = self._deps("pool", [sb_], [db_], same_ok=False)
        db_.dcnt += 1
        tk = (dk, db_.dcnt)
        sem = self.sem
        ds = sem[dk]

        def emit(E):
            for k, v in waits:
                E.wait_ge(sem[k], v)
            E.collective_compute(kind, ALU.bypass, replica_groups=groups, ins=[s_ap], outs=[d_ap]).then_inc(ds, 1)

        self.q["pool"].append(emit)
        self._commit(tk, [sb_], [db_])
        self.final_waits[dk] = db_.dcnt
        return tk

    def mm(self, ob, o, lb, l, rb, r, start=True, stop=True):
        return self.op("pe", lambda E: E.matmul(o, lhsT=l, rhs=r, start=start, stop=stop), reads=[lb, rb], writes=[ob])

    def act(self, ob, o, ib, i, func, bias=None, scale=None, accum=None, rd=(), wr=()):
        kw = {}
        if bias is not None:
            kw["bias"] = bias
        if scale is not None:
            kw["scale"] = scale
        if accum is not None:
            kw["accum_out"] = accum
        return self.op("act", lambda E: E.activation(out=o, in_=i, func=func, **kw), reads=[ib] + list(rd), writes=[ob] + list(wr))

    def tt(self, eng, ob, o, ab, a, bb, b, op):
        return self.op(eng, lambda E: E.tensor_tensor(out=o, in0=a, in1=b, op=op), reads=[ab, bb], writes=[ob])

    def ts(self, eng, ob, o, ab, a, s1, s2, op0, op1=None, rd=()):
        if op1 is None:
            return self.op(eng, lambda E: E.tensor_scalar(out=o, in0=a, scalar1=s1, scalar2=None, op0=op0), reads=[ab] + list(rd), writes=[ob])
        return self.op(eng, lambda E: E.tensor_scalar(out=o, in0=a, scalar1=s1, scalar2=s2, op0=op0, op1=op1), reads=[ab] + list(rd), writes=[ob])

    def stt(self, ob, o, ab, a, sc, bb, b, op0, op1, rd=()):
        return self.op("dve", lambda E: E.scalar_tensor_tensor(out=o, in0=a, scalar=sc, in1=b, op0=op0, op1=op1), reads=[ab, bb] + list(rd), writes=[ob])

    def copy(self, eng, ob, o, ib, i):
        if eng == "act":
            return self.op("act", lambda E: E.copy(out=o, in_=i), reads=[ib], writes=[ob])
        return self.op(eng, lambda E: E.tensor_copy(out=o, in_=i), reads=[ib], writes=[ob])

    def memset(self, eng, ob, o, val):
        return self.op(eng, lambda E: E.memset(o, val), writes=[ob])

    def recip(self, ob, o, ib, i):
        return self.op("dve", lambda E: E.reciprocal(out=o, in_=i), reads=[ib], writes=[ob])

    phase_id = 0

    def begin_phase(self):
        self.phase_id += 1
        self.gstack = self.stack if not hasattr(self, "gstack") else self.gstack
        self.pstack = ExitStack()
        self.pstack.__enter__()
        self.alloc_stack = self.pstack

    def end_phase(self):
        sem = self.sem
        cnt = dict(self.cnt)
        fw = dict(self.final_waits)
        for e in ENGS:
            waits = [(o, cnt[o]) for o in ENGS if o != e and cnt[o] > self.waited[e].get(o, 0)]
            waits += [(k, v) for k, v in fw.items() if v > self.waited[e].get(k, 0)]
            for k, v in waits:
                self.waited[e][k] = v

            def emit(E, waits=waits):
                for k, v in waits:
                    E.wait_ge(sem[k], v)
            self.q[e].append(emit)
        self.finish()
        self.q = {e: [] for e in ENGS}
        for e in ENGS:
            self.sem[e] = self.stack.enter_context(self.nc.semaphore("s_%s_%d" % (e, self.phase_id)))
            self.cnt[e] = 0
            for x in ENGS:
                self.waited[x].pop(e, None)
        for b in self.global_bufs:
            if b.w is not None and b.w[0] in ENGS:
                b.w = None
            b.r = {k: v for k, v in b.r.items() if k not in ENGS}
        for b in self.phase_bufs:
            if b.dsem is not None:
                self.free_dsems[self.dsem_cls[b.dsem]].append((b.dsem, b.dcnt))
                b.dsem = None
        self.phase_bufs = []
        self.pstack.__exit__(None, None, None)
        self.alloc_stack = self.stack

    def finish(self):
        nc = self.nc
        sem = self.sem
        q = self.q
        fw = dict(self.final_waits)
        with nc.Block() as block:
            @block.tensor
            def _(E):
                for f in q["pe"]:
                    f(E)

            @block.scalar
            def _(E):
                for f in q["act"]:
                    f(E)

            @block.vector
            def _(E):
                for f in q["dve"]:
                    f(E)

            @block.gpsimd
            def _(E):
                for f in q["pool"]:
                    f(E)

            @block.sync
            def _(E):
                for f in q["sp"]:
                    f(E)
                for k, v in fw.items():
                    E.wait_ge(sem[k], v)


class Ring:
    def __init__(self, bufs):
        self.bufs = bufs
        self.i = 0

    def next(self):
        b = self.bufs[self.i % len(self.bufs)]
        self.i += 1
        return b


def dram(nc, name, shape, dt, kind):
    return nc.dram_tensor(name, list(shape), dt, kind=kind).ap()


def kcp(ap):
    return ap.rearrange("(kc p) n -> p kc n", p=128)


def rms_stats(S, C, xb, nch, n, ps_ring, sq_ring, rstd, tmp, nfeat, eps):
    pss = ps_ring.next()
    for kc in range(nch):
        sq = sq_ring.next()
        S.act(sq, sq[:, :n], xb, xb[:, kc, :n], AF.Square)
        S.mm(pss, pss[:, :n], C["ones"], C["ones"][:], sq, sq[:, :n], start=(kc == 0), stop=(kc == nch - 1))
    S.act(tmp, tmp[:, :n], pss, pss[:, :n], AF.Sqrt, bias=C["eps%g" % eps][:, 0:1], scale=1.0 / nfeat, rd=[C["eps%g" % eps]])
    S.recip(rstd, rstd[:, :n], tmp, tmp[:, :n])


def load_consts(S, nc, cst_ap, extra_eps=()):
    C = {}
    C["ones"] = S.sb("c_ones", [128, 128], F32)
    S.dma("sp", C["ones"][:], cst_ap[0], writes=[C["ones"]])
    for e in (EPS,) + tuple(extra_eps):
        b = S.sb("c_eps%g" % e, [128, 1], F32)
        S.memset("dve", b, b[:], float(e))
        C["eps%g" % e] = b
    return C


def build_tok(mode):
    nc = bass.Bass("TRN2", target_bir_lowering=False)
    A = {}
    A["xT"] = dram(nc, "xT", [DM, NT], F32, "ExternalInput")
    A["g_next"] = dram(nc, "g_next", [128, 8], F32, "ExternalInput")
    A["cst"] = dram(nc, "cst", [4, 128, 128], F32, "ExternalInput")
    if mode != "first":
        A["hT"] = dram(nc, "hT", [DM, NT], BF16, "ExternalInput")
        yT = dram(nc, "yT", [4096, NT], BF16, "ExternalInput")
        A["load_y"] = lambda S, yub, t: S.dma("act", yub[:], kcp(yT[:, t * TT:(t + 1) * TT]), writes=[yub])
        A["w_gate"] = dram(nc, "w_gate", [DM, 3072], F32, "ExternalInput")
        A["b_gate"] = dram(nc, "b_gate", [128, 24], F32, "ExternalInput")
        A["w_br"] = dram(nc, "w_br", [4096, DM], F32, "ExternalInput")
        A["w_out"] = dram(nc, "w_out", [DM, DM], F32, "ExternalInput")
        A["w_up"] = dram(nc, "w_up", [DM, 4096], F32, "ExternalInput")
        A["w_dn"] = dram(nc, "w_dn", [4096, DM], F32, "ExternalInput")
        A["g_mlp"] = dram(nc, "g_mlp", [128, 8], F32, "ExternalInput")
    if mode == "last":
        A["outT"] = dram(nc, "outT", [DM, NT], F32, "ExternalOutput")
    else:
        A["hT_out"] = dram(nc, "hT_out", [DM, NT], BF16, "ExternalOutput")
        A["hT_lo"] = dram(nc, "hT_lo", [DM, NT], BF16, "ExternalOutput")
        if mode == "mid":
            A["xT_out"] = dram(nc, "xT_out", [DM, NT], F32, "ExternalOutput")
    with ExitStack() as st:
        S = Sched(nc, st)
        S.begin_phase()
        emit_tok(nc, S, A, mode)
        S.end_phase()
    return nc


def emit_tok(nc, S, A, mode):
    xT, gn, cst = A["xT"], A["g_next"], A["cst"]
    if mode != "first":
        w_gate, b_gate, w_br, w_out, w_up, w_dn, gm = (A[k] for k in ("w_gate", "b_gate", "w_br", "w_out", "w_up", "w_dn", "g_mlp"))
        h_in = A["h_in"] if "h_in" in A else (lambda t: A["hT"][:, t * TT:(t + 1) * TT])
    if mode == "last":
        outT = A["outT"]
    else:
        h_out = A["h_out"] if "h_out" in A else (lambda t, which: (A["hT_out"], A["hT_lo"])[which][:, t * TT:(t + 1) * TT])
        after_h = A.get("after_h", None)
        if mode == "mid":
            xTo = A["xT_out"]
    if True:
        C = load_consts(S, nc, cst)
        gnb = S.sb("gnb", [128, 8], F32)
        S.dma("sp", gnb[:], gn, writes=[gnb])
        xb = S.sb("xb", [128, 8, TT], F32)
        ps_ring = Ring([S.ps("ps%d" % i, [128, TT], F32) for i in range(7)])
        sq_ring = Ring([S.sb("sq%d" % i, [128, TT], F32) for i in range(2)])
        rstd = S.sb("rstd", [128, TT], F32)
        rtmp = S.sb("rtmp", [128, TT], F32)
        hn_ring = Ring([S.sb("hn%d" % i, [128, 8, TT], BF16) for i in range(2)])
        hl_ring = Ring([S.sb("hl%d" % i, [128, 8, TT], BF16) for i in range(2)])
        h32_ring = Ring([S.sb("h32_%d" % i, [128, TT], F32) for i in range(2)])
        if mode == "last":
            oc_ring = Ring([S.sb("oc%d" % i, [128, TT], F32) for i in range(3)])
        if mode != "first":
            bgb = S.sb("bgb", [128, 24], F32)
            S.dma("sp", bgb[:], b_gate, writes=[bgb])
            gmb = S.sb("gmb", [128, 8], F32)
            S.dma("sp", gmb[:], gm, writes=[gmb])
            hb_ring = Ring([S.sb("hb%d" % i, [128, 8, TT], BF16) for i in range(2)])
            yub = S.sb("yub", [128, 32, TT], BF16)
            mixed = S.sb("mixed", [128, 8, TT], BF16)
            h2 = S.sb("h2", [128, 8, TT], BF16)
            g_ring = Ring([S.sb("g%d" % i, [128, TT], F32) for i in range(6)])
            acc_ring = Ring([S.sb("acc%d" % i, [128, TT], F32) for i in range(2)])
            tmp_ring = Ring([S.sb("tmp%d" % i, [128, TT], F32) for i in range(2)])
            r_ring = Ring([S.sb("r%d" % i, [128, TT], F32) for i in range(2)])
            w_ring = Ring([S.sb("w%d" % i, [128, 7168], BF16) for i in range(3)])

        for t in range(NT // TT):
            ts_ = slice(t * TT, (t + 1) * TT)
            S.dma("sp", xb[:], kcp(xT[:, ts_]), writes=[xb])
            if mode != "first":
                hb = hb_ring.next()
                S.dma("sp", hb[:], kcp(h_in(t)), writes=[hb])
                A["load_y"](S, yub, t)
                for oc in range(8):
                    w = w_ring.next()
                    wg = w[:, 0:3072].rearrange("p (kc br j) -> p kc br j", kc=8, br=3)
                    wb = w[:, 3072:7168].rearrange("p (kc j) -> p kc j", kc=32)
                    for br in range(3):
                        c0 = br * 1024 + oc * 128
                        S.dma("pool", wg[:, :, br, :], kcp(w_gate[:, c0:c0 + 128]), writes=[w], group=(br > 0))
                    S.dma("pool", wb, kcp(w_br[:, oc * 128:(oc + 1) * 128]), writes=[w], group=True)
                    gts = []
                    for br in range(3):
                        pg = ps_ring.next()
                        for kc in range(8):
                            S.mm(pg, pg[:], w, wg[:, kc, br, :], hb, hb[:, kc, :], start=(kc == 0), stop=(kc == 7))
                        g = g_ring.next()
                        ch = br * 8 + oc
                        S.act(g, g[:], pg, pg[:], AF.Sigmoid, bias=bgb[:, ch:ch + 1], rd=[bgb])
                        gts.append(g)
                    acc = acc_ring.next()
                    koff = (0, 8, 24)
                    nk = (8, 16, 8)
                    for br in range(3):
                        pb = ps_ring.next()
                        for kc in range(nk[br]):
                            S.mm(pb, pb[:], w, wb[:, koff[br] + kc, :], yub, yub[:, koff[br] + kc, :], start=(kc == 0), stop=(kc == nk[br] - 1))
                        if br == 0:
                            S.tt("dve", acc, acc[:], pb, pb[:], gts[0], gts[0][:], ALU.mult)
                        else:
                            tmp = tmp_ring.next()
                            S.tt("dve", tmp, tmp[:], pb, pb[:], gts[br], gts[br][:], ALU.mult)
                            if br == 1:
                                S.tt("dve", acc, acc[:], acc, acc[:], tmp, tmp[:], ALU.add)
                            else:
                                S.tt("dve", mixed, mixed[:, oc, :], acc, acc[:], tmp, tmp[:], ALU.add)
                if t > 0 and mode == "mid" and after_h is not None:
                    after_h(S, t - 1)
                for half in range(2):
                    w = w_ring.next()
                    wo = w[:, 0:4096].rearrange("p (kc j) -> p kc j", kc=8)
                    S.dma("pool", wo, kcp(w_out[:, half * 512:(half + 1) * 512]), writes=[w])
                    for o4 in range(4):
                        oc = half * 4 + o4
                        po = ps_ring.next()
                        for kc in range(8):
                            S.mm(po, po[:], w, wo[:, kc, o4 * 128:(o4 + 1) * 128], mixed, mixed[:, kc, :], start=(kc == 0), stop=(kc == 7))
                        S.tt("dve", xb, xb[:, oc, :], xb, xb[:, oc, :], po, po[:], ALU.add)
                rms_stats(S, C, xb, 8, TT, ps_ring, sq_ring, rstd, rtmp, DM, EPS)
                for kc in range(8):
                    S.stt(h2, h2[:, kc, :], xb, xb[:, kc, :], gmb[:, kc:kc + 1], rstd, rstd[:], ALU.mult, ALU.mult, rd=[gmb])
                for o8 in range(8):
                    w = w_ring.next()
                    wu = w[:, 0:4096].rearrange("p (kc j) -> p kc j", kc=8)
                    S.dma("pool", wu, kcp(w_up[:, o8 * 512:(o8 + 1) * 512]), writes=[w])
                    for o4 in range(4):
                        oc = o8 * 4 + o4
                        pu = ps_ring.next()
                        for kc in range(8):
                            S.mm(pu, pu[:], w, wu[:, kc, o4 * 128:(o4 + 1) * 128], h2, h2[:, kc, :], start=(kc == 0), stop=(kc == 7))
                        r = r_ring.next()
                        S.act(r, r[:], pu, pu[:], AF.Relu)
                        S.tt("dve", yub, yub[:, oc, :], r, r[:], r, r[:], ALU.mult)
                for oc in range(8):
                    w = w_ring.next()
                    wd = w[:, 0:4096].rearrange("p (kc j) -> p kc j", kc=32)
                    S.dma("pool", wd, kcp(w_dn[:, oc * 128:(oc + 1) * 128]), writes=[w])
                    pd = ps_ring.next()
                    for kc in range(32):
                        S.mm(pd, pd[:], w, wd[:, kc, :], yub, yub[:, kc, :], start=(kc == 0), stop=(kc == 31))
                    S.tt("dve", xb, xb[:, oc, :], xb, xb[:, oc, :], pd, pd[:], ALU.add)
                if mode == "mid":
                    S.dma("sp", kcp(xTo[:, ts_]), xb[:], reads=[xb])
            rms_stats(S, C, xb, 8, TT, ps_ring, sq_ring, rstd, rtmp, DM, EPS)
            if mode == "last":
                for kc in range(8):
                    o = oc_ring.next()
                    S.stt(o, o[:], xb, xb[:, kc, :], gnb[:, kc:kc + 1], rstd, rstd[:], ALU.mult, ALU.mult, rd=[gnb])
                    S.dma("sp", outT[kc * 128:(kc + 1) * 128, ts_], o[:], reads=[o])
            else:
                hn = hn_ring.next()
                hl = hl_ring.next()
                for kc in range(8):
                    h32 = h32_ring.next()
                    S.stt(h32, h32[:], xb, xb[:, kc, :], gnb[:, kc:kc + 1], rstd, rstd[:], ALU.mult, ALU.mult, rd=[gnb])
                    S.copy("act", hn, hn[:, kc, :], h32, h32[:])
                    S.tt("pool", hl, hl[:, kc, :], h32, h32[:], hn, hn[:, kc, :], ALU.subtract)
                wr = A["h_wr"](t) if "h_wr" in A else [[], []]
                S.dma("sp", kcp(h_out(t, 0)), hn[:], reads=[hn], writes=wr[0], sembuf=hn)
                S.dma("sp", kcp(h_out(t, 1)), hl[:], reads=[hl], writes=wr[1], sembuf=hl)
                if after_h is not None and (mode == "first" or t == NT // TT - 1):
                    after_h(S, t)


def make_consts():
    c = np.zeros((4, 128, 128), np.float32)
    c[0] = 1.0
    c[1] = np.triu(np.ones((128, 128), np.float32))
    c[2] = 1.0 - c[1]
    c[3] = np.eye(128, dtype=np.float32)
    return c


def pvec(v):
    v = np.asarray(v)
    return np.ascontiguousarray(v.reshape(-1, 128).T)


def small_views(S, name, nbanks, width):
    out = []
    per = 512 // width
    banks = [S.ps("%s%d" % (name, i), [128, 512], F32) for i in range(nbanks)]
    for j in range(per):
        for i in range(nbanks):
            out.append(VBuf("%s%d_%d" % (name, i, j), banks[i], banks[i].t[:, j * width:(j + 1) * width]))
    return out


def build_gla(ntiles=SEQ // TT):
    nc = bass.Bass("TRN2", target_bir_lowering=False)
    hT = dram(nc, "hT", [DM, SEQ], BF16, "ExternalInput")
    hTl = dram(nc, "hTlo", [DM, SEQ], BF16, "ExternalInput")
    A = {}
    A["w_gla"] = dram(nc, "w_gla", [DM, 784], F32, "ExternalInput")
    A["wgk2"] = dram(nc, "wgk2", [16, 128], F32, "ExternalInput")
    A["bgk"] = dram(nc, "bgk", [1, 128], F32, "ExternalInput")
    A["ng"] = dram(nc, "ng", [128, 2], F32, "ExternalInput")
    A["cst"] = dram(nc, "cst", [4, 128, 128], F32, "ExternalInput")
    yT = dram(nc, "yT", [256, SEQ], BF16, "ExternalOutput")
    A["load_h"] = lambda S, h, t: S.dma("sp", h[:], kcp(hT[:, t * TT:(t + 1) * TT]), writes=[h])
    A["load_hlo"] = lambda S, h, t: S.dma("act", h[:], kcp(hTl[:, t * TT:(t + 1) * TT]), writes=[h])
    A["store_y"] = lambda S, yo, t: S.dma("sp", yT[:, t * TT:(t + 1) * TT].rearrange("(ec p) n -> p ec n", p=128), yo[:], reads=[yo])
    with ExitStack() as st:
        S = Sched(nc, st)
        S.begin_phase()
        emit_gla(nc, S, A, ntiles)
        S.end_phase()
    return nc


def emit_gla(nc, S, A, ntiles=SEQ // TT):
    w_gla, wgk2, bgk, ngp, cst = A["w_gla"], A["wgk2"], A["bgk"], A["ng"], A["cst"]
    if True:
        C = load_consts(S, nc, cst)
        U = S.sb("U", [128, 128], F32)
        UC = S.sb("UC", [128, 128], F32)
        S.dma("sp", U[:], cst[1], writes=[U])
        S.dma("sp", UC[:], cst[2], writes=[UC])
        wa = S.sb("wa", [128, 8, 784], BF16)
        S.dma("pool", wa[:], kcp(w_gla), writes=[wa])
        wqk32 = S.sb("wqk32", [128, 8, 256], F32)
        S.dma("sp", wqk32[:, :, 0:128], kcp(w_gla[:, 0:128]), writes=[wqk32])
        S.dma("sp", wqk32[:, :, 128:256], kcp(w_gla[:, 384:512]), writes=[wqk32], group=True)
        wqk_hi = S.sb("wqk_hi", [128, 8, 256], BF16)
        wqk_lo = S.sb("wqk_lo", [128, 8, 256], BF16)
        S.copy("act", wqk_hi, wqk_hi[:], wqk32, wqk32[:])
        S.tt("dve", wqk_lo, wqk_lo[:], wqk32, wqk32[:], wqk_hi, wqk_hi[:], ALU.subtract)
        w2 = S.sb("w2", [16, 128], F32)
        S.dma("sp", w2[:], wgk2, writes=[w2])
        bgb = S.sb("bgb", [128, 128], F32)
        S.dma("sp", bgb[:], bgk.partition_broadcast(128), writes=[bgb])
        ng = S.sb("ng", [128, 2], F32)
        S.dma("sp", ng[:], ngp, writes=[ng])
        h_ring = Ring([S.sb("h%d" % i, [128, 8, TT], BF16) for i in range(2)])
        hlo_ring = Ring([S.sb("hlo%d" % i, [128, 8, TT], BF16) for i in range(2)])
        big = Ring([S.ps("pb%d" % i, [128, 512], F32) for i in range(2)])
        po = [S.ps("po%d" % i, [128, 512], F32) for i in range(2)]
        pss_ring = Ring([S.ps("pss", [128, 512], F32)])
        sm = Ring(small_views(S, "sm", 3, 256))
        qTs = S.sb("qTs", [128, TT], F32)
        kTs = S.sb("kTs", [128, TT], F32)
        sg = S.sb("sg", [128, 2, TT], F32)
        gkl = S.sb("gkl", [16, TT], F32)

        def ring(name, shape, dt, n=2):
            return Ring([S.sb("%s%d" % (name, i), shape, dt) for i in range(n)])
        ktok_r = ring("ktok", [128, 128], F32)
        vtok_r = ring("vtok", [128, 256], BF16)
        t1_r = ring("t1", [128, 128], F32)
        e_r = ring("e", [128, 128], F32)
        gk_r = ring("gk", [128, 128], F32)
        ebT_r = ring("ebT", [128, 128], F32)
        enbT_r = ring("enbT", [128, 128], F32)
        ed2_r = ring("ed2", [128, 128], F32)
        qp_r = ring("qp", [128, 128], BF16)
        qp32_r = ring("qp32", [128, 128], F32)
        kp32_r = ring("kp32", [128, 128], F32)
        kpp_r = ring("kpp", [128, 128], BF16)
        AT_r = ring("AT", [128, 128], BF16)
        Sst = S.sb("Sst", [128, 256], F32)
        Sbf = S.sb("Sbf", [128, 256], BF16)
        S.memset("dve", Sst, Sst[:], 0.0)
        S.memset("dve", Sbf, Sbf[:], 0.0)
        sq_ring = ring("sq", [128, TT], F32)
        rstd = S.sb("rstd", [128, TT], F32)
        rtmp = S.sb("rtmp", [128, TT], F32)
        tmp_r = ring("tmp", [128, TT], F32)
        yo_r = ring("yo", [128, 2, TT], BF16)

        for t in range(ntiles):
            ts_ = slice(t * TT, (t + 1) * TT)
            h = h_ring.next()
            A["load_h"](S, h, t)

            def proj(c0, m):
                p = big.next()
                for kc in range(8):
                    S.mm(p, p[:m, :], wa, wa[:, kc, c0:c0 + m], h, h[:, kc, :], start=(kc == 0), stop=(kc == 7))
                return p
            hlo = hlo_ring.next()
            A["load_hlo"](S, hlo, t)

            def proj3(c0):
                p = big.next()
                n = 0
                for (wb_, hb_) in ((wqk_hi, h), (wqk_hi, hlo), (wqk_lo, h)):
                    for kc in range(8):
                        S.mm(p, p[:], wb_, wb_[:, kc, c0:c0 + 128], hb_, hb_[:, kc, :], start=(n == 0), stop=(n == 23))
                        n += 1
                return p
            p = proj3(0)
            S.act(qTs, qTs[:], p, p[:], AF.Copy, scale=128.0 ** -0.5)
            p = proj3(128)
            S.act(kTs, kTs[:], p, p[:], AF.Copy)
            for ec in range(2):
                p = proj(128 + ec * 128, 128)
                S.act(sg, sg[:, ec, :], p, p[:], AF.Silu)
            p = proj(768, 16)
            S.copy("dve", gkl, gkl[:], p, p[:16, :])
            def stageA(c):
                cs = slice(c * 128, (c + 1) * 128)
                p = big.next()
                for kc in range(8):
                    S.mm(p, p[:, :384], h, h[:, kc, cs], wa, wa[:, kc, 384:768], start=(kc == 0), stop=(kc == 7))
                ktok = ktok_r.next()
                vtok = vtok_r.next()
                S.copy("act", ktok, ktok[:], p, p[:, 0:128])
                S.copy("act", vtok, vtok[:], p, p[:, 128:384])
                pg = sm.next()
                S.mm(pg, pg[:, :128], gkl, gkl[:, cs], w2, w2[:], start=True, stop=True)
                t1 = t1_r.next()
                S.tt("dve", t1, t1[:], pg, pg[:, :128], bgb, bgb[:], ALU.add)
                e = e_r.next()
                S.act(e, e[:], t1, t1[:], AF.Exp, scale=-1.0)
                S.act(e, e[:], e, e[:], AF.Ln, bias=1.0)
                gk = gk_r.next()
                S.ts("dve", gk, gk[:], e, e[:], -1.0 / 16.0, None, ALU.mult)
                pbT = sm.next()
                S.mm(pbT, pbT[:, :128], gk, gk[:], U, U[:])
                pd2 = sm.next()
                S.mm(pd2, pd2[:, :128], UC, UC[:], gk, gk[:])
                ebT = ebT_r.next()
                enbT = enbT_r.next()
                ed2 = ed2_r.next()
                S.act(ebT, ebT[:], pbT, pbT[:, :128], AF.Exp)
                S.act(enbT, enbT[:], pbT, pbT[:, :128], AF.Exp, scale=-1.0)
                S.act(ed2, ed2[:], pd2, pd2[:, :128], AF.Exp)
                qp = qp_r.next()
                qp32 = qp32_r.next()
                kp32 = kp32_r.next()
                kpp = kpp_r.next()
                S.tt("dve", qp32, qp32[:], qTs, qTs[:, cs], ebT, ebT[:], ALU.mult)
                S.tt("dve", kp32, kp32[:], kTs, kTs[:, cs], enbT, enbT[:], ALU.mult)
                S.copy("act", qp, qp[:], qp32, qp32[:])
                S.tt("pool", kpp, kpp[:], ktok, ktok[:], ed2, ed2[:], ALU.mult)
                pA = sm.next()
                S.mm(pA, pA[:, :128], kp32, kp32[:], qp32, qp32[:])
                AT = AT_r.next()
                S.tt("dve", AT, AT[:], pA, pA[:, :128], U, U[:], ALU.mult)
                return dict(cs=cs, vtok=vtok, AT=AT, qp=qp, kpp=kpp, ebT=ebT)

            def stageB(c, v):
                cs, vtok, AT, qp, kpp, ebT = (v[k] for k in ('cs', 'vtok', 'AT', 'qp', 'kpp', 'ebT'))
                for ec in range(2):
                    es = slice(ec * 128, (ec + 1) * 128)
                    S.mm(po[ec], po[ec][:, cs], vtok, vtok[:, es], AT, AT[:], start=True, stop=False)
                    S.mm(po[ec], po[ec][:, cs], Sbf, Sbf[:, es], qp, qp[:], start=False, stop=True)
                pkv = sm.next()
                S.mm(pkv, pkv[:, :256], kpp, kpp[:], vtok, vtok[:])
                S.stt(Sst, Sst[:], Sst, Sst[:], ebT[:, 127:128], pkv, pkv[:, :256], ALU.mult, ALU.add, rd=[ebT])
                S.copy("act", Sbf, Sbf[:], Sst, Sst[:])

            va = {0: stageA(0)}
            for c in range(4):
                if c + 1 < 4:
                    va[c + 1] = stageA(c + 1)
                stageB(c, va.pop(c))
            pss = pss_ring.next()
            for ec in range(2):
                sq = sq_ring.next()
                S.act(sq, sq[:], po[ec], po[ec][:], AF.Square)
                S.mm(pss, pss[:], C["ones"], C["ones"][:], sq, sq[:], start=(ec == 0), stop=(ec == 1))
            S.act(rtmp, rtmp[:], pss, pss[:], AF.Sqrt, bias=C["eps%g" % EPS][:, 0:1], scale=1.0 / 256, rd=[C["eps%g" % EPS]])
            S.recip(rstd, rstd[:], rtmp, rtmp[:])
            yo = yo_r.next()
            for ec in range(2):
                tmp = tmp_r.next()
                S.stt(tmp, tmp[:], po[ec], po[ec][:], ng[:, ec:ec + 1], rstd, rstd[:], ALU.mult, ALU.mult, rd=[ng])
                S.tt("pool", yo, yo[:, ec, :], tmp, tmp[:], sg, sg[:, ec, :], ALU.mult)
            A["store_y"](S, yo, t)


def gla_inputs(inp, l, g, hT_b, cst, hTlo_b=None):
    w = inp["w_in"][l]
    cols = np.concatenate([np.arange(g * 128, (g + 1) * 128),
                           2064 + np.arange(g * 256, (g + 1) * 256),
                           512 + np.arange(g * 128, (g + 1) * 128),
                           1024 + np.arange(g * 256, (g + 1) * 256),
                           2048 + np.arange(16)])
    if hTlo_b is None:
        hTlo_b = np.zeros_like(hT_b)
    elif isinstance(hTlo_b, int):
        hTlo_b = None
    return {"hT": hT_b, "hTlo": hTlo_b, "w_gla": np.ascontiguousarray(w[:, cols]),
            "wgk2": np.ascontiguousarray(inp["gla_w_gk2"][l][:, g * 128:(g + 1) * 128]),
            "bgk": np.ascontiguousarray(inp["gla_b_gk"][l][g * 128:(g + 1) * 128].reshape(1, 128)),
            "ng": pvec(inp["gla_norm_g"][l]), "cst": cst}


def build_ssm(ntiles=SEQ // TT):
    nc = bass.Bass("TRN2", target_bir_lowering=False)
    hT = dram(nc, "hT", [DM, SEQ], BF16, "ExternalInput")
    A = {}
    A["w_ssm"] = dram(nc, "w_ssm", [DM, 1288], F32, "ExternalInput")
    A["cw"] = dram(nc, "cw", [128, 6, 4], F32, "ExternalInput")
    A["cb"] = dram(nc, "cb", [128, 6], F32, "ExternalInput")
    A["dtb"] = dram(nc, "dtb", [1, 8], F32, "ExternalInput")
    A["alog"] = dram(nc, "alog", [1, 8], F32, "ExternalInput")
    A["dsk"] = dram(nc, "dsk", [1, 8], F32, "ExternalInput")
    A["ngs"] = dram(nc, "ngs", [1, 512], F32, "ExternalInput")
    A["cst"] = dram(nc, "cst", [4, 128, 128], F32, "ExternalInput")
    yT = dram(nc, "yT", [512, SEQ], BF16, "ExternalOutput")
    A["load_h"] = lambda S, h, t: S.dma("sp", h[:], kcp(hT[:, t * TT:(t + 1) * TT]), writes=[h])
    A["store_y"] = lambda S, yo, t: S.dma("sp", yT[:, t * TT:(t + 1) * TT].rearrange("(c p) n -> p c n", p=128), yo[:], reads=[yo])
    with ExitStack() as st:
        S = Sched(nc, st)
        S.begin_phase()
        emit_ssm(nc, S, A, ntiles)
        S.end_phase()
    return nc


def emit_ssm(nc, S, A, ntiles=SEQ // TT):
    w_ssm, cwp, cbp, dtbp, alogp, dskp, ngp, cst = (A[k] for k in ("w_ssm", "cw", "cb", "dtb", "alog", "dsk", "ngs", "cst"))
    if True:
        C = load_consts(S, nc, cst)
        U = S.sb("U", [128, 128], F32)
        UC = S.sb("UC", [128, 128], F32)
        idf = S.sb("idf", [128, 128], F32)
        idb = S.sb("idb", [128, 128], BF16)
        S.dma("sp", U[:], cst[1], writes=[U])
        S.dma("sp", UC[:], cst[2], writes=[UC])
        S.dma("sp", idf[:], cst[3], writes=[idf])
        S.dma("pool", idb[:], cst[3], writes=[idb])
        ws = S.sb("ws", [128, 8, 1288], BF16)
        S.dma("pool", ws[:], kcp(w_ssm), writes=[ws])
        cw = S.sb("cw", [128, 6, 4], F32)
        cb = S.sb("cb", [128, 6], F32)
        S.dma("sp", cw[:], cwp, writes=[cw])
        S.dma("sp", cb[:], cbp, writes=[cb])
        dtb = S.sb("dtb", [128, 8], F32)
        a_b = S.sb("a_b", [128, 8], F32)
        dsk = S.sb("dsk", [128, 8], F32)
        ngs = S.sb("ngs", [128, 512], F32)
        S.dma("sp", dtb[:], dtbp.partition_broadcast(128), writes=[dtb])
        S.dma("sp", a_b[:], alogp.partition_broadcast(128), writes=[a_b])
        S.dma("sp", dsk[:], dskp.partition_broadcast(128), writes=[dsk])
        S.dma("sp", ngs[:], ngp.partition_broadcast(128), writes=[ngs])
        S.act(a_b, a_b[:], a_b, a_b[:], AF.Exp)
        S.ts("dve", a_b, a_b[:], a_b, a_b[:], -1.0, None, ALU.mult)

        def ring(name, shape, dt, n=2):
            return Ring([S.sb("%s%d" % (name, i), shape, dt) for i in range(n)])
        h_ring = ring("h", [128, 8, TT], BF16)
        big = Ring([S.ps("pb%d" % i, [128, 512], F32) for i in range(6)])
        sm = Ring(small_views(S, "sm", 2, 128))
        raw = S.sb("raw", [128, 6, TT + 3], F32)
        S.memset("dve", raw, raw[:, :, 0:3], 0.0)
        cacc_r = ring("cacc", [128, TT], F32)
        xc = S.sb("xc", [128, 4, TT], F32)
        BT = S.sb("BT", [128, TT], BF16)
        CT = S.sb("CT", [128, TT], BF16)
        sz_r = ring("sz", [128, 512], F32)
        t8_r = ring("t8", [128, 8], F32)
        dt_r = ring("dt", [128, 8], F32)
        da_r = ring("da", [128, 8], F32)
        xdt_r = ring("xdt", [128, 512], BF16)
        xD_r = ring("xD", [128, 512], F32)
        Btok_r = ring("Btok", [128, 128], BF16)
        rda_r = ring("rda", [128, 8, 128], F32)
        eL_r = ring("eL", [128, 8, 128], F32)
        GU_r = ring("GU", [128, 128], F32)
        M_r = ring("M", [128, 8, 128], BF16)
        cs_r = ring("cs", [128, 16], F32)
        ec_r = ring("ec", [128, 24], F32)
        xdtd_r = ring("xdtd", [128, 512], BF16)
        Sst = S.sb("Sst", [128, 512], F32)
        Sbf = S.sb("Sbf", [128, 512], BF16)
        S.memset("dve", Sst, Sst[:], 0.0)
        S.memset("dve", Sbf, Sbf[:], 0.0)
        y1_r = ring("y1", [128, 512], F32)
        y2_r = ring("y2", [128, 512], F32)
        junk = S.sb("junk", [128, 512], F32)
        ss_r = ring("ss", [128, 2], F32)
        yn_r = ring("yn", [128, 512], BF16)
        yo_r = ring("yo", [128, 4, TT], BF16)
        eps = C["eps%g" % EPS]

        def b8(ap):
            return ap.unsqueeze(2).to_broadcast([128, 8, 64])

        def v8(ap):
            return ap.rearrange("p (h q) -> p h q", h=8)

        for t in range(ntiles):
            ts_ = slice(t * TT, (t + 1) * TT)
            h = h_ring.next()
            A["load_h"](S, h, t)
            for ch in range(6):
                p = big.next()
                for kc in range(8):
                    S.mm(p, p[:], ws, ws[:, kc, 512 + ch * 128:640 + ch * 128], h, h[:, kc, :], start=(kc == 0), stop=(kc == 7))
                S.act(raw, raw[:, ch, 3:TT + 3], p, p[:], AF.Copy)
            for ch in range(6):
                acc = cacc_r.next()
                S.ts("dve", acc, acc[:], raw, raw[:, ch, 0:TT], cw[:, ch, 0:1], cb[:, ch:ch + 1], ALU.mult, ALU.add, rd=[cw, cb])
                for i in range(1, 4):
                    S.stt(acc, acc[:], raw, raw[:, ch, i:i + TT], cw[:, ch, i:i + 1], acc, acc[:], ALU.mult, ALU.add, rd=[cw])
                if ch < 4:
                    S.act(xc, xc[:, ch, :], acc, acc[:], AF.Silu)
                elif ch == 4:
                    S.act(BT, BT[:], acc, acc[:], AF.Silu)
                else:
                    S.act(CT, CT[:], acc, acc[:], AF.Silu)
            S.copy("pool", raw, raw[:, :, 0:3], raw, raw[:, :, TT:TT + 3])
            yo = yo_r.next()
            def stageA(c):
                cs = slice(c * 128, (c + 1) * 128)
                pz = big.next()
                for kc in range(8):
                    S.mm(pz, pz[:], h, h[:, kc, cs], ws, ws[:, kc, 0:512], start=(kc == 0), stop=(kc == 7))
                sz = sz_r.next()
                S.act(sz, sz[:], pz, pz[:], AF.Silu)
                pdt = sm.next()
                for kc in range(8):
                    S.mm(pdt, pdt[:, 0:8], h, h[:, kc, cs], ws, ws[:, kc, 1280:1288], start=(kc == 0), stop=(kc == 7))
                t8 = t8_r.next()
                S.tt("dve", t8, t8[:], pdt, pdt[:, 0:8], dtb, dtb[:], ALU.add)
                S.act(t8, t8[:], t8, t8[:], AF.Exp)
                dt = dt_r.next()
                S.act(dt, dt[:], t8, t8[:], AF.Ln, bias=1.0)
                da = da_r.next()
                S.tt("dve", da, da[:], dt, dt[:], a_b, a_b[:], ALU.mult)
                px = big.next()
                for ch in range(4):
                    S.mm(px, px[:, ch * 128:(ch + 1) * 128], xc, xc[:, ch, cs], idf, idf[:])
                xdt = xdt_r.next()
                xD = xD_r.next()
                S.tt("dve", xdt, v8(xdt[:]), px, v8(px[:]), dt, b8(dt[:]), ALU.mult)
                S.tt("dve", xD, v8(xD[:]), px, v8(px[:]), dsk, b8(dsk[:]), ALU.mult)
                pB = sm.next()
                S.mm(pB, pB[:], BT, BT[:, cs], idb, idb[:])
                Btok = Btok_r.next()
                S.copy("act", Btok, Btok[:], pB, pB[:])
                rda = rda_r.next()
                S.tt("pool", rda, rda[:], U, U[:].unsqueeze(1).to_broadcast([128, 8, 128]), da, da[:].unsqueeze(2).to_broadcast([128, 8, 128]), ALU.mult)
                pD = [big.next(), big.next()]
                for hf in range(2):
                    S.mm(pD[hf], pD[hf][:], UC, UC[:], rda, rda[:, hf * 4:(hf + 1) * 4, :].rearrange("p a b -> p (a b)"))
                pc = sm.next()
                S.mm(pc, pc[:, 0:8], U, U[:], da, da[:])
                pc2 = sm.next()
                S.mm(pc2, pc2[:, 0:8], C["ones"], C["ones"][:], da, da[:])
                csb = cs_r.next()
                S.copy("dve", csb, csb[:, 0:8], pc, pc[:, 0:8])
                S.copy("dve", csb, csb[:, 8:16], pc2, pc2[:, 0:8])
                ec = ec_r.next()
                S.act(ec, ec[:, 0:16], csb, csb[:, 0:16], AF.Exp)
                S.tt("dve", csb, csb[:, 0:8], csb, csb[:, 8:16], csb, csb[:, 0:8], ALU.subtract)
                S.act(ec, ec[:, 16:24], csb, csb[:, 0:8], AF.Exp)
                eL = eL_r.next()
                for hf in range(2):
                    S.act(eL, eL[:, hf * 4:(hf + 1) * 4, :].rearrange("p a b -> p (a b)"), pD[hf], pD[hf][:], AF.Exp)
                pG = sm.next()
                S.mm(pG, pG[:], BT, BT[:, cs], CT, CT[:, cs])
                GU = GU_r.next()
                S.tt("dve", GU, GU[:], pG, pG[:], U, U[:], ALU.mult)
                M = M_r.next()
                S.tt("pool", M, M[:], eL, eL[:], GU, GU[:].unsqueeze(1).to_broadcast([128, 8, 128]), ALU.mult)
                xdtd = xdtd_r.next()
                S.tt("pool", xdtd, v8(xdtd[:]), xdt, v8(xdt[:]), ec, b8(ec[:, 16:24]), ALU.mult)
                return dict(cs=cs, sz=sz, xdt=xdt, xD=xD, Btok=Btok, M=M, ec=ec, xdtd=xdtd)

            def stageB(c, v):
                cs, sz, xdt, xD, Btok, M, ec, xdtd = (v[k] for k in ('cs', 'sz', 'xdt', 'xD', 'Btok', 'M', 'ec', 'xdtd'))
                pyd = big.next()
                for hh in range(8):
                    S.mm(pyd, pyd[:, hh * 64:(hh + 1) * 64], M, M[:, hh, :], xdt, xdt[:, hh * 64:(hh + 1) * 64])
                pyo = big.next()
                S.mm(pyo, pyo[:], CT, CT[:, cs], Sbf, Sbf[:])
                pst = big.next()
                S.mm(pst, pst[:], Btok, Btok[:], xdtd, xdtd[:])
                S.tt("dve", Sst, v8(Sst[:]), Sst, v8(Sst[:]), ec, b8(ec[:, 8:16]), ALU.mult)
                S.tt("dve", Sst, Sst[:], Sst, Sst[:], pst, pst[:], ALU.add)
                S.copy("act", Sbf, Sbf[:], Sst, Sst[:])
                y1 = y1_r.next()
                S.tt("dve", y1, v8(y1[:]), pyo, v8(pyo[:]), ec, b8(ec[:, 0:8]), ALU.mult)
                S.tt("dve", y1, y1[:], y1, y1[:], pyd, pyd[:], ALU.add)
                y2 = y2_r.next()
                S.tt("pool", y2, y2[:], y1, y1[:], xD, xD[:], ALU.add)
                S.tt("pool", y2, y2[:], y2, y2[:], sz, sz[:], ALU.mult)
                ss = ss_r.next()
                S.act(junk, junk[:], y2, y2[:], AF.Square, accum=ss[:, 0:1], wr=[ss])
                S.act(ss, ss[:, 1:2], ss, ss[:, 0:1], AF.Sqrt, bias=eps[:, 0:1], scale=1.0 / 512, rd=[eps])
                S.recip(ss, ss[:, 0:1], ss, ss[:, 1:2])
                yn = yn_r.next()
                S.stt(yn, yn[:], y2, y2[:], ss[:, 0:1], ngs, ngs[:], ALU.mult, ALU.mult, rd=[ss])
                pT = big.next()
                for ch in range(4):
                    S.mm(pT, pT[:, ch * 128:(ch + 1) * 128], yn, yn[:, ch * 128:(ch + 1) * 128], idb, idb[:])
                S.copy("act", yo, yo[:, :, cs], pT, pT[:].rearrange("p (c n) -> p c n", c=4))

            va = {0: stageA(0)}
            for c in range(4):
                if c + 1 < 4:
                    va[c + 1] = stageA(c + 1)
                stageB(c, va.pop(c))
            A["store_y"](S, yo, t)


def ssm_inputs(inp, l, g, hT_b, cst):
    w = inp["w_in"][l]
    xcols = 5136 + np.arange(g * 512, (g + 1) * 512)
    bcols = 7184 + np.arange(g * 128, (g + 1) * 128)
    ccols = 7696 + np.arange(g * 128, (g + 1) * 128)
    cols = np.concatenate([3088 + np.arange(g * 512, (g + 1) * 512), xcols, bcols, ccols, 8208 + np.arange(g * 8, (g + 1) * 8)])
    cc = np.concatenate([xcols, bcols, ccols]) - 5136
    cwv = inp["ssm_conv_w"][l][:, cc]
    cw = np.ascontiguousarray(cwv.reshape(4, 6, 128).transpose(2, 1, 0))
    cb = np.ascontiguousarray(inp["ssm_conv_b"][l][cc].reshape(6, 128).T)
    hs = slice(g * 8, (g + 1) * 8)
    return {"hT": hT_b, "w_ssm": np.ascontiguousarray(w[:, cols]), "cw": cw, "cb": cb,
            "dtb": np.ascontiguousarray(inp["ssm_dt_bias"][l][hs].reshape(1, 8)),
            "alog": np.ascontiguousarray(inp["ssm_a_log"][l][hs].reshape(1, 8)),
            "dsk": np.ascontiguousarray(inp["ssm_d"][l][hs].reshape(1, 8)),
            "ngs": np.ascontiguousarray(inp["ssm_norm_g"][l][g * 512:(g + 1) * 512].reshape(1, 512)),
            "cst": cst}


C1_2PI = 6.28125
C2_2PI = 2.0 * math.pi - 6.28125


def build_diff(l, ntiles=SEQ // TT):
    nc = bass.Bass("TRN2", target_bir_lowering=False)
    hT = dram(nc, "hT", [DM, SEQ], BF16, "ExternalInput")
    A = {}
    A["w_diff"] = dram(nc, "w_diff", [DM, 1280], F32, "ExternalInput")
    A["pos"] = dram(nc, "pos", [1, SEQ], I32, "ExternalInput")
    A["invf"] = dram(nc, "invf", [128, 2], F32, "ExternalInput")
    A["lqk"] = dram(nc, "lqk", [4, 64], F32, "ExternalInput")
    A["ngd"] = dram(nc, "ngd", [1, 128], F32, "ExternalInput")
    A["cst"] = dram(nc, "cst", [4, 128, 128], F32, "ExternalInput")
    yT = dram(nc, "yT", [256, SEQ], BF16, "ExternalOutput")
    A["load_h"] = lambda S, h, t: S.dma("sp", h[:], kcp(hT[:, t * TT:(t + 1) * TT]), writes=[h])
    A["store_y"] = lambda S, yo, t: S.dma("sp", yT[:, t * TT:(t + 1) * TT].rearrange("(c p) n -> p c n", p=128), yo[:], reads=[yo])
    with ExitStack() as st:
        S = Sched(nc, st)
        S.begin_phase()
        emit_diff(nc, S, A, l, ntiles)
        S.end_phase()
    return nc


def emit_diff(nc, S, A, l, ntiles=SEQ // TT):
    lambda_init = 0.8 - 0.6 * math.exp(-0.3 * l)
    w_diff, posd, invfp, lqk, ngp, cst = (A[k] for k in ("w_diff", "pos", "invf", "lqk", "ngd", "cst"))
    if True:
        C = load_consts(S, nc, cst, extra_eps=(1e-5,))
        idb = S.sb("idb", [128, 128], BF16)
        S.dma("pool", idb[:], cst[3], writes=[idb])
        wd = S.sb("wd", [128, 8, 1280], BF16)
        S.dma("pool", wd[:], kcp(w_diff), writes=[wd])
        invf = S.sb("invf", [128, 2], F32)
        S.dma("sp", invf[:], invfp, writes=[invf])
        ngd = S.sb("ngd", [128, 128], F32)
        S.dma("sp", ngd[:], ngp.partition_broadcast(128), writes=[ngd])
        S.ts("dve", ngd, ngd[:], ngd, ngd[:], 1.0 - lambda_init, None, ALU.mult)
        lq = S.sb("lq", [128, 4, 64], F32)
        for i in range(4):
            S.dma("sp", lq[:, i, :], lqk[i:i + 1, :].partition_broadcast(128), writes=[lq], group=(i > 0))
        lt = S.sb("lt", [128, 2, 64], F32)
        S.tt("dve", lt, lt[:, 0, :], lq, lq[:, 0, :], lq, lq[:, 1, :], ALU.mult)
        S.tt("dve", lt, lt[:, 1, :], lq, lq[:, 2, :], lq, lq[:, 3, :], ALU.mult)
        ls = S.sb("ls", [128, 4], F32)
        S.op("dve", lambda E: E.reduce_sum(out=ls[:, 0:2], in_=lt[:], axis=AX.X), reads=[lt], writes=[ls])
        S.act(ls, ls[:, 0:2], ls, ls[:, 0:2], AF.Exp)
        S.tt("dve", ls, ls[:, 2:3], ls, ls[:, 1:2], ls, ls[:, 0:1], ALU.subtract)
        S.ts("dve", ls, ls[:, 3:4], ls, ls[:, 2:3], -lambda_init, None, ALU.add)
        nlam = ls

        def ring(name, shape, dt, n=2):
            return Ring([S.sb("%s%d" % (name, i), shape, dt) for i in range(n)])
        h_ring = ring("h", [128, 8, TT], BF16)
        KT = S.sb("KT", [128, 2, SEQ], BF16)
        VA = S.sb("VA", [128, 2, SEQ // 128, 129], BF16)
        S.memset("pool", VA, VA[:, :, :, 128:129], 1.0)
        QT_r = ring("QT", [128, 2, TT], BF16)
        big = Ring([S.ps("pb%d" % i, [128, 512], F32) for i in range(3)])
        pob = [[S.ps("po%d_%d" % (s, hf), [128, 512], F32) for hf in range(2)] for s in range(2)]
        sm = Ring(small_views(S, "sm", 1, 128))
        posi = S.sb("posi", [128, TT], I32)
        ang = S.sb("ang", [128, TT], F32)
        ang2 = S.sb("ang2", [128, TT], F32)
        ki = S.sb("ki", [128, TT], I32)
        kf = S.sb("kf", [128, TT], F32)
        yr = S.sb("yr", [128, TT], F32)
        Cs = S.sb("Cs", [128, TT], F32)
        Sn = S.sb("Sn", [128, TT], F32)
        ta_r = ring("ta", [128, TT], F32)
        tb_r = ring("tb", [128, TT], F32)
        PT_r = ring("PT", [128, TT], BF16, 5)
        r_r = ring("r", [128, 4], F32, 4)
        oa_r = ring("oa", [128, 128], F32, 4)
        junk_r = ring("junk", [128, 128], F32, 4)
        yn_r = ring("yn", [128, 128], BF16, 4)
        yo_r = ring("yo", [128, 2, TT], BF16)
        eps5 = C["eps%g" % 1e-5]

        def reduce_sin(dst, src):
            S.ts("dve", ki, ki[:], src, src[:], 1.0 / (2.0 * math.pi), None, ALU.mult)
            S.copy("dve", kf, kf[:], ki, ki[:])
            S.stt(yr, yr[:], kf, kf[:], -C1_2PI, src, src[:], ALU.mult, ALU.add)
            S.stt(yr, yr[:], kf, kf[:], -C2_2PI, yr, yr[:], ALU.mult, ALU.add)
            S.ts("dve", yr, yr[:], yr, yr[:], -3.1415925, 3.1415925, ALU.max, ALU.min)
            S.act(dst, dst[:], yr, yr[:], AF.Sin)

        for t in range(ntiles):
            ts_ = slice(t * TT, (t + 1) * TT)
            h = h_ring.next()
            A["load_h"](S, h, t)
            S.dma("sp", posi[:], posd[0:1, ts_].partition_broadcast(128), writes=[posi])
            S.copy("dve", ang, ang[:], posi, posi[:])
            S.ts("dve", ang, ang[:], ang, ang[:], invf[:, 0:1], None, ALU.mult, rd=[invf])
            reduce_sin(Sn, ang)
            S.ts("dve", Sn, Sn[:], Sn, Sn[:], invf[:, 1:2], None, ALU.mult, rd=[invf])
            S.ts("dve", ang2, ang2[:], ang, ang[:], math.pi / 2.0, None, ALU.add)
            reduce_sin(Cs, ang2)
            QT = QT_r.next()
            for hd in range(2):
                for (c0, dstb, dst) in ((hd * 128, QT, QT[:, hd, :]), (512 + hd * 128, KT, KT[:, hd, ts_])):
                    p1 = big.next()
                    for kc in range(8):
                        S.mm(p1, p1[:], wd, wd[:, kc, c0:c0 + 128], h, h[:, kc, :], start=(kc == 0), stop=(kc == 7))
                    p2 = big.next()
                    for kc in range(8):
                        S.mm(p2, p2[:], wd, wd[:, kc, c0 + 256:c0 + 384], h, h[:, kc, :], start=(kc == 0), stop=(kc == 7))
                    ta = ta_r.next()
                    tb = tb_r.next()
                    S.tt("dve", ta, ta[:], p1, p1[:], Cs, Cs[:], ALU.mult)
                    S.tt("dve", tb, tb[:], p2, p2[:], Sn, Sn[:], ALU.mult)
                    S.tt("pool", dstb, dst, ta, ta[:], tb, tb[:], ALU.add)
            for c in range(4):
                cs = slice(c * 128, (c + 1) * 128)
                pv = big.next()
                for kc in range(8):
                    S.mm(pv, pv[:, 0:256], h, h[:, kc, cs], wd, wd[:, kc, 1024:1280], start=(kc == 0), stop=(kc == 7))
                S.copy("act", VA, VA[:, :, 4 * t + c, 0:128], pv, pv[:, 0:256].rearrange("p (a b) -> p a b", a=2))
            yo = yo_r.next()
            for hd in range(2):
                nkb = 4 * t + 4
                started = [[False, False], [False, False]]
                iters = [(kb, s_) for kb in range(nkb) for s_ in range(2)]
                pend = {}

                def emit_qk(i):
                    kb, s_ = iters[i]
                    r = kb - 4 * t
                    q0 = max(r, 0)
                    qlo = q0 * 128
                    n = TT - qlo
                    ps_ = slice(s_ * 64, (s_ + 1) * 64)
                    pS = big.next()
                    S.mm(pS, pS[:, :n], KT, KT[ps_, hd, kb * 128:(kb + 1) * 128], QT, QT[ps_, hd, qlo:TT])
                    PT = PT_r.next()
                    S.act(PT, PT[:, :n], pS, pS[:, :n], AF.Exp, scale=0.125)
                    if r >= 0:
                        S.memset("pool", PT, PT[64:128, 0:64], 0.0)
                    pend[i] = (PT, q0, qlo)

                def emit_pv(i):
                    kb, s_ = iters[i]
                    PT, q0, qlo = pend.pop(i)
                    for qb in range(q0, 4):
                        col = qb * 128 - qlo
                        bank = pob[s_][qb // 2]
                        o = bank[:, (qb % 2) * 129:(qb % 2) * 129 + 129]
                        first = not started[s_][qb // 2]
                        started[s_][qb // 2] = True
                        S.mm(bank, o, PT, PT[:, col:col + 128], VA, VA[:, hd, kb, :], start=first, stop=(kb == 4 * t + qb and qb % 2 == 1))

                LOOK = 3
                for i in range(len(iters) + LOOK):
                    if i < len(iters):
                        emit_qk(i)
                    if i >= LOOK:
                        emit_pv(i - LOOK)
                QB = range(4)
                o1 = [pob[0][qb // 2] for qb in QB]
                o2 = [pob[1][qb // 2] for qb in QB]
                b0 = [(qb % 2) * 129 for qb in QB]
                rr = [r_r.next() for qb in QB]
                oa = [oa_r.next() for qb in QB]
                yn = [yn_r.next() for qb in QB]
                jk = [junk_r.next() for qb in QB]
                for qb in QB:
                    S.recip(rr[qb], rr[qb][:, 0:1], o1[qb], o1[qb][:, b0[qb] + 128:b0[qb] + 129])
                for qb in QB:
                    S.recip(rr[qb], rr[qb][:, 1:2], o2[qb], o2[qb][:, b0[qb] + 128:b0[qb] + 129])
                for qb in QB:
                    S.tt("dve", rr[qb], rr[qb][:, 2:3], rr[qb], rr[qb][:, 1:2], nlam, nlam[:, 3:4], ALU.mult)
                for qb in QB:
                    S.ts("dve", oa[qb], oa[qb][:], o1[qb], o1[qb][:, b0[qb]:b0[qb] + 128], rr[qb][:, 0:1], None, ALU.mult, rd=[rr[qb]])
                for qb in QB:
                    S.stt(oa[qb], oa[qb][:], o2[qb], o2[qb][:, b0[qb]:b0[qb] + 128], rr[qb][:, 2:3], oa[qb], oa[qb][:], ALU.mult, ALU.add, rd=[rr[qb]])
                for qb in QB:
                    S.act(jk[qb], jk[qb][:], oa[qb], oa[qb][:], AF.Square, accum=rr[qb][:, 3:4], wr=[rr[qb]])
                for qb in QB:
                    S.act(rr[qb], rr[qb][:, 1:2], rr[qb], rr[qb][:, 3:4], AF.Sqrt, bias=eps5[:, 0:1], scale=1.0 / 128, rd=[eps5])
                for qb in QB:
                    S.recip(rr[qb], rr[qb][:, 0:1], rr[qb], rr[qb][:, 1:2])
                for qb in QB:
                    S.stt(yn[qb], yn[qb][:], oa[qb], oa[qb][:], rr[qb][:, 0:1], ngd, ngd[:], ALU.mult, ALU.mult, rd=[rr[qb]])
                for qb in QB:
                    pT = sm.next()
                    S.mm(pT, pT[:], yn[qb], yn[qb][:], idb, idb[:])
                    S.copy("act", yo, yo[:, hd, qb * 128:(qb + 1) * 128], pT, pT[:])
            A["store_y"](S, yo, t)


def diff_inputs(inp, l, g, b, hT_b, cst):
    w = inp["w_in"][l]
    qc = 8240 + np.arange(g * 256, (g + 1) * 256)
    kc = 9264 + np.arange(g * 256, (g + 1) * 256)
    vc = 10288 + np.arange(g * 256, (g + 1) * 256)
    d = np.arange(256) % 64
    partner = np.arange(256) + np.where(d < 8, 8, np.where(d < 16, -8, 0))
    cols = np.concatenate([qc, qc[partner], kc, kc[partner], vc])
    p = np.arange(128) % 64
    invf = np.zeros((128, 2), np.float32)
    fr = (500000.0 ** (-np.arange(0, 16, 2, dtype=np.float32) / np.float32(16))).astype(np.float32)
    invf[:, 0] = np.where(p < 16, fr[p % 8], 0.0)
    invf[:, 1] = np.where(p < 8, -1.0, np.where(p < 16, 1.0, 0.0))
    lqk = np.stack([inp["diff_lq1"][l], inp["diff_lk1"][l], inp["diff_lq2"][l], inp["diff_lk2"][l]]).astype(np.float32)
    return {"hT": hT_b, "w_diff": np.ascontiguousarray(w[:, cols]),
            "pos": np.ascontiguousarray(inp["positions"][b].reshape(1, SEQ)), "invf": invf, "lqk": lqk,
            "ngd": np.ascontiguousarray(inp["diff_norm_g"][l].reshape(1, 128)), "cst": cst}


_PROGS = {}


def _prog(key, fn):
    if key not in _PROGS:
        _PROGS[key] = fn()
    return _PROGS[key]


def _run(nc, maps):
    return run_bass_kernel_spmd(nc, maps, core_ids=list(range(NCORES))).results


def tok_inputs(inp, l, xT_c, hT_c, yT_c, g_next, cst):
    w = inp["w_in"][l]
    return {"xT": xT_c, "g_next": pvec(g_next), "cst": cst, "hT": hT_c, "yT": yT_c,
            "w_gate": np.ascontiguousarray(w[:, 11312:14384]), "b_gate": pvec(inp["b_gate"][l]),
            "w_br": np.ascontiguousarray(np.concatenate([inp["w_br_gla"][l], inp["w_br_ssm"][l], inp["w_br_diff"][l]], axis=0)),
            "w_out": np.ascontiguousarray(inp["w_out"][l]), "w_up": np.ascontiguousarray(inp["w_mlp_up"][l]),
            "w_dn": np.ascontiguousarray(inp["w_mlp_down"][l]), "g_mlp": pvec(inp["norm_mlp_g"][l])}


def kernel_unfused(**inp):
    inp = {k: np.asarray(v) for k, v in inp.items()}
    cst = make_consts()
    x = inp["x"]
    xT = [np.ascontiguousarray(x[c // 4, (c % 4) * NT:(c % 4 + 1) * NT, :].T) for c in range(NCORES)]
    res = _run(_prog("first", lambda: build_tok("first")),
               [{"xT": xT[c], "g_next": pvec(inp["norm_mix_g"][0]), "cst": cst} for c in range(NCORES)])
    hT = [r["hT_out"] for r in res]
    hTl = [r["hT_lo"] for r in res]
    out = None
    for l in range(DEPTH):
        hTb = [np.ascontiguousarray(np.concatenate(hT[b * 4:(b + 1) * 4], axis=1)) for b in range(B)]
        hTlb = [np.ascontiguousarray(np.concatenate(hTl[b * 4:(b + 1) * 4], axis=1)) for b in range(B)]
        yg = _run(_prog("gla", build_gla), [gla_inputs(inp, l, c % 4, hTb[c // 4], cst, hTlb[c // 4]) for c in range(NCORES)])
        ys = _run(_prog("ssm", build_ssm), [ssm_inputs(inp, l, c % 4, hTb[c // 4], cst) for c in range(NCORES)])
        yd = _run(_prog(("diff", l), lambda: build_diff(l)), [diff_inputs(inp, l, c % 4, c // 4, hTb[c // 4], cst) for c in range(NCORES)])
        yTb = []
        for b in range(B):
            parts = [yg[b * 4 + g]["yT"] for g in range(4)] + [ys[b * 4 + g]["yT"] for g in range(4)] + [yd[b * 4 + g]["yT"] for g in range(4)]
            yTb.append(np.concatenate(parts, axis=0))
        last = (l == DEPTH - 1)
        g_next = inp["norm_final_g"] if last else inp["norm_mix_g"][l + 1]
        maps = [tok_inputs(inp, l, xT[c], hT[c], np.ascontiguousarray(yTb[c // 4][:, (c % 4) * NT:(c % 4 + 1) * NT]), g_next, cst)
                for c in range(NCORES)]
        mode = "last" if last else "mid"
        res = _run(_prog(mode, lambda: build_tok(mode)), maps)
        if last:
            out = np.empty((B, SEQ, DM), np.float32)
            for c in range(NCORES):
                out[c // 4, (c % 4) * NT:(c % 4 + 1) * NT, :] = res[c]["outT"].T
        else:
            xT = [r["xT_out"] for r in res]
            hT = [r["hT_out"] for r in res]
            hTl = [r["hT_lo"] for r in res]
    return out


GROUPS = [[0, 1, 2, 3], [4, 5, 6, 7]]


def build_fused(skip=()):
    nc = bass.Bass("TRN2", target_bir_lowering=False)
    ext = lambda name, shape, dt=F32: dram(nc, name, shape, dt, "ExternalInput")
    xT = ext("xT", [DM, NT])
    pos = ext("pos", [1, SEQ], I32)
    cst = ext("cst", [4, 128, 128])
    invf = ext("invf", [128, 2])
    g_mix = ext("g_mix", [DEPTH, 128, 8])
    g_fin = ext("g_fin", [128, 8])
    g_mlp = ext("g_mlp", [DEPTH, 128, 8])
    w_gate = ext("w_gate", [DEPTH, DM, 3072])
    b_gate = ext("b_gate", [DEPTH, 128, 24])
    w_br = ext("w_br", [DEPTH, 4096, DM])
    w_out = ext("w_out", [DEPTH, DM, DM])
    w_up = ext("w_up", [DEPTH, DM, 4096])
    w_dn = ext("w_dn", [DEPTH, 4096, DM])
    w_gla = ext("w_gla", [DEPTH, DM, 784])
    wgk2 = ext("wgk2", [DEPTH, 16, 128])
    bgk = ext("bgk", [DEPTH, 1, 128])
    ng = ext("ng", [DEPTH, 128, 2])
    w_ssm = ext("w_ssm", [DEPTH, DM, 1288])
    cw = ext("cw", [DEPTH, 128, 6, 4])
    cb = ext("cb", [DEPTH, 128, 6])
    dtb = ext("dtb", [DEPTH, 1, 8])
    alog = ext("alog", [DEPTH, 1, 8])
    dsk = ext("dsk", [DEPTH, 1, 8])
    ngs = ext("ngs", [DEPTH, 1, 512])
    w_diff = ext("w_diff", [DEPTH, DM, 1280])
    lqk = ext("lqk", [DEPTH, 4, 64])
    ngd = ext("ngd", [DEPTH, 1, 128])
    outT = dram(nc, "outT", [DM, NT], F32, "ExternalOutput")
    xs = nc.dram_tensor("xs_i", [DM, NT], F32).ap()
    TPR = NT // TT
    hsrc = nc.dram_tensor("hsrc_i", [2 * TPR, DM, TT], BF16).ap()
    hgat = nc.dram_tensor("hgat_i", [2 * TPR, 4 * DM, TT], BF16).ap()
    ysrc = nc.dram_tensor("ysrc_i", [4, 4, 256, NT], BF16).ap()
    ygat = nc.dram_tensor("ygat_i", [4, 4, 1024, NT], BF16).ap()
    ygat3 = ygat.rearrange("q a r n -> q (a r) n")

    with ExitStack() as st:
        S = Sched(nc, st)
        hsrc_b = [Buf("hsrc_b%d" % i) for i in range(2 * TPR)]
        hgat_b = [Buf("hgat_b%d" % i) for i in range(2 * TPR)]
        ysrc_b = [[Buf("ysrc_b%d_%d" % (q, a)) for a in range(4)] for q in range(4)]
        ygat_b = Buf("ygat_b")
        xs_b = Buf("xs_b")
        S.global_bufs += [ygat_b, xs_b] + hsrc_b + hgat_b + [b for row in ysrc_b for b in row]

        def after_h(S_, t):
            for which in range(2):
                i = 2 * t + which
                S_.collective("AllGather", hsrc_b[i], hsrc[i], hgat_b[i], hgat[i], GROUPS)

        def load_h_from(which):
            def f(S_, h, t):
                r, i = t // TPR, 2 * (t % TPR) + which
                S_.dma("sp", h[:], kcp(hgat[i][r * DM:(r + 1) * DM, :]), reads=[hgat_b[i]], writes=[h])
            return f

        h_io = {"h_in": (lambda t: hsrc[2 * t]), "h_out": (lambda t, which: hsrc[2 * t + which]),
                "h_wr": (lambda t: [[hsrc_b[2 * t]], [hsrc_b[2 * t + 1]]]), "after_h": after_h}

        def store_y_parts(parts):
            def f(S_, yo, t):
                q, tl = t // (NT // TT), (t % (NT // TT)) * TT
                for j, a in enumerate(parts):
                    S_.dma("sp", ysrc[q, a][:, tl:tl + TT].rearrange("(c p) n -> p c n", p=128), yo[:, 2 * j:2 * j + 2, :],
                           reads=[yo], writes=[ysrc_b[q][a]], sembuf=yo, group=True)
                if t % (NT // TT) == NT // TT - 1:
                    for a in parts:
                        S_.collective("AllGather", ysrc_b[q][a], ysrc[q, a], ygat_b, ygat[q, a], GROUPS)
            return f

        qcache = {}

        def load_y(S_, yub, t):
            tl = t * TT
            ph = S_.phase_id

            def qval(E):
                if ph not in qcache:
                    qcache[ph] = E.snap(E.partition_id() % 4)
                return qcache[ph]
            src = (lambda E, tl=tl: ygat3[bass.ds(qval(E), 1), :, tl:tl + TT].rearrange("o (k p) n -> p (o k) n", p=128))
            S_.dma("sp", yub[:], src, reads=[ygat_b], writes=[yub])

        S.begin_phase()
        if "first" not in skip:
            emit_tok(nc, S, dict(h_io, **{"xT": xT, "g_next": g_mix[0], "cst": cst}), "first")
        S.end_phase()
        for l in range(DEPTH):
            S.begin_phase()
            if "gla" not in skip:
              emit_gla(nc, S, {"w_gla": w_gla[l], "wgk2": wgk2[l], "bgk": bgk[l], "ng": ng[l], "cst": cst,
                             "load_h": load_h_from(0), "load_hlo": load_h_from(1), "store_y": store_y_parts((0,))}, SEQ // TT)
            S.end_phase()
            S.begin_phase()
            if "ssm" not in skip:
              emit_ssm(nc, S, {"w_ssm": w_ssm[l], "cw": cw[l], "cb": cb[l], "dtb": dtb[l], "alog": alog[l], "dsk": dsk[l], "ngs": ngs[l],
                             "cst": cst, "load_h": load_h_from(0), "store_y": store_y_parts((1, 2))}, SEQ // TT)
            S.end_phase()
            S.begin_phase()
            if "diff" not in skip:
              emit_diff(nc, S, {"w_diff": w_diff[l], "pos": pos, "invf": invf, "lqk": lqk[l], "ngd": ngd[l], "cst": cst,
                              "load_h": load_h_from(0), "store_y": store_y_parts((3,))}, l, SEQ // TT)
            S.end_phase()
            last = (l == DEPTH - 1)
            A = {"xT": (xT if l == 0 else xs), "g_next": (g_fin if last else g_mix[l + 1]), "cst": cst, "h_in": h_io["h_in"], "load_y": load_y,
                 "w_gate": w_gate[l], "b_gate": b_gate[l], "w_br": w_br[l], "w_out": w_out[l], "w_up": w_up[l], "w_dn": w_dn[l], "g_mlp": g_mlp[l]}
            if last:
                A["outT"] = outT
            else:
                A.update(h_io)
                A["xT_out"] = xs
            S.begin_phase()
            emit_tok(nc, S, A, "last" if last else "mid")
            S.end_phase()
    return nc


def fused_y_row_order():
    rows = []
    for a in range(4):
        for r in range(4):
            j = np.arange(256)
            if a == 0:
                rows.append(r * 256 + j)
            elif a in (1, 2):
                rows.append(1024 + r * 512 + (a - 1) * 256 + j)
            else:
                rows.append(3072 + r * 256 + j)
    return np.concatenate(rows)


def fused_inputs(inp, c, cst):
    b, g = c // 4, c % 4
    x = inp["x"]
    m = {"xT": np.ascontiguousarray(x[b, g * NT:(g + 1) * NT, :].T), "cst": cst,
         "pos": np.ascontiguousarray(inp["positions"][b].reshape(1, SEQ)),
         "g_mix": np.stack([pvec(inp["norm_mix_g"][l]) for l in range(DEPTH)]), "g_fin": pvec(inp["norm_final_g"]),
         "g_mlp": np.stack([pvec(inp["norm_mlp_g"][l]) for l in range(DEPTH)]),
         "w_gate": np.ascontiguousarray(inp["w_in"][:, :, 11312:14384]),
         "b_gate": np.stack([pvec(inp["b_gate"][l]) for l in range(DEPTH)]),
         "w_br": np.ascontiguousarray(np.concatenate([inp["w_br_gla"], inp["w_br_ssm"], inp["w_br_diff"]], axis=1)[:, fused_y_row_order(), :]),
         "w_out": np.ascontiguousarray(inp["w_out"]), "w_up": np.ascontiguousarray(inp["w_mlp_up"]), "w_dn": np.ascontiguousarray(inp["w_mlp_down"])}
    per = {}
    for l in range(DEPTH):
        d = {}
        d.update(gla_inputs(inp, l, g, None, cst, 0))
        d.update(ssm_inputs(inp, l, g, None, cst))
        d.update(diff_inputs(inp, l, g, b, None, cst))
        for k, v in d.items():
            if k in ("hT", "hTlo", "cst", "pos", "invf"):
                continue
            per.setdefault(k, []).append(v)
        if l == 0:
            m["invf"] = d["invf"]
    for k, v in per.items():
        m[k] = np.ascontiguousarray(np.stack(v))
    return m


_FUSED = []


def kernel(**inp):
    inp = {k: np.asarray(v) for k, v in inp.items()}
    cst = make_consts()
    if not _FUSED:
        _FUSED.append(build_fused())
    maps = [fused_inputs(inp, c, cst) for c in range(NCORES)]
    res = run_bass_kernel_spmd(_FUSED[0], maps, core_ids=list(range(NCORES))).results
    out = np.empty((B, SEQ, DM), np.float32)
    for c in range(NCORES):
        out[c // 4, (c % 4) * NT:(c % 4 + 1) * NT, :] = res[c]["outT"].T
    return out
```

```python
import math
from contextlib import ExitStack
import numpy as np
import ml_dtypes
import concourse.bass as bass
import concourse.mybir as mybir
from concourse.bass_utils import run_bass_kernel_spmd

F32 = mybir.dt.float32
BF16 = mybir.dt.bfloat16
I32 = mybir.dt.int32
AF = mybir.ActivationFunctionType
ALU = mybir.AluOpType
AX = mybir.AxisListType

NCORES = 8
B, SEQ, DM, DEPTH = 2, 8192, 1024, 2
NT = 2048
TT = 512
EPS = 1e-6
ENGS = ("pe", "act", "dve", "pool", "sp")


class Buf:
    __slots__ = ("name", "t", "w", "r", "dsem", "dcnt")

    def __init__(self, name, t=None):
        self.name = name
        self.t = t
        self.w = None
        self.r = {}
        self.dsem = None
        self.dcnt = 0

    def __getitem__(self, idx):
        return self.t[idx]


class VBuf:
    def __init__(self, name, parent, ap):
        self.name = name
        self.parent = parent
        self.t = ap

    def __getitem__(self, idx):
        return self.t[idx]

    w = property(lambda s: s.parent.w, lambda s, v: setattr(s.parent, "w", v))
    r = property(lambda s: s.parent.r, lambda s, v: setattr(s.parent, "r", v))
    dsem = property(lambda s: s.parent.dsem, lambda s, v: setattr(s.parent, "dsem", v))
    dcnt = property(lambda s: s.parent.dcnt, lambda s, v: setattr(s.parent, "dcnt", v))


class Sched:
    def __init__(self, nc, stack):
        self.nc = nc
        self.stack = stack
        self.q = {e: [] for e in ENGS}
        self.sem = {}
        for e in ENGS:
            self.sem[e] = stack.enter_context(nc.semaphore("s_" + e))
        self.cnt = {e: 0 for e in ENGS}
        self.waited = {e: {} for e in ENGS}
        self.ndsem = 0
        self.final_waits = {}
        self.alloc_stack = stack
        self.nname = 0
        self.free_dsems = {}
        self.dsem_cls = {}
        self.phase_bufs = []
        self.global_bufs = []

    def sb(self, name, shape, dt):
        self.nname += 1
        t = self.alloc_stack.enter_context(self.nc.sbuf_tensor("sb%d_%s" % (self.nname, name), list(shape), dt))
        b = Buf(name, t)
        self.phase_bufs.append(b)
        return b

    def ps(self, name, shape, dt=F32):
        self.nname += 1
        t = self.alloc_stack.enter_context(self.nc.psum_tensor("ps%d_%s" % (self.nname, name), list(shape), dt))
        return Buf(name, t)

    def view(self, name, ap):
        return Buf(name, ap)

    def _dsem(self, b, eng="sp"):
        cls = {"pool": "sw", "cc": "cc"}.get(eng, "hw")
        if b.dsem is not None:
            assert self.dsem_cls[b.dsem] == cls, (b.name, cls)
        if b.dsem is None:
            pool = self.free_dsems.setdefault(cls, [])
            if pool:
                b.dsem, b.dcnt = pool.pop()
            else:
                b.dsem = "d%d" % self.ndsem
                self.ndsem += 1
                self.sem[b.dsem] = self.stack.enter_context(self.nc.semaphore(b.dsem))
                self.dsem_cls[b.dsem] = cls
        return b.dsem

    def _deps(self, eng, reads, writes, same_ok):
        deps = {}

        def add(k, v):
            if deps.get(k, 0) < v:
                deps[k] = v

        for b in reads:
            if b.w is not None:
                add(*b.w)
        for b in writes:
            if b.w is not None:
                add(*b.w)
            for k, v in b.r.items():
                add(k, v)
        waits = []
        wd = self.waited[eng]
        for k, v in deps.items():
            if k == eng and same_ok:
                continue
            if wd.get(k, 0) >= v:
                continue
            wd[k] = v
            waits.append((k, v))
        return waits

    def _commit(self, tk, reads, writes):
        k, v = tk
        for b in writes:
            b.w = tk
            b.r = {}
        for b in reads:
            if b.r.get(k, 0) < v:
                b.r[k] = v

    def op(self, eng, fn, reads=(), writes=()):
        waits = self._deps(eng, reads, writes, same_ok=(eng == "pe"))
        self.cnt[eng] += 1
        tk = (eng, self.cnt[eng])
        sem = self.sem
        me = sem[eng]

        def emit(E):
            for k, v in waits:
                E.wait_ge(sem[k], v)
            fn(E).then_inc(me, 1)

        self.q[eng].append(emit)
        self._commit(tk, reads, writes)
        return tk

    def dma(self, eng, out_ap, in_ap, reads=(), writes=(), sembuf=None, group=False):
        sb = sembuf if sembuf is not None else (writes[0] if writes else reads[0])
        dk = self._dsem(sb, eng)
        saved = None
        if group and writes and writes[0].w is not None and writes[0].w[0] == dk:
            saved = writes[0].w
            writes[0].w = None
        waits = self._deps(eng, reads, writes, same_ok=False)
        if saved is not None:
            writes[0].w = saved
        sb.dcnt += 16
        tk = (dk, sb.dcnt)
        sem = self.sem
        ds = sem[dk]

        def emit(E):
            for k, v in waits:
                E.wait_ge(sem[k], v)
            E.dma_start(out=out_ap, in_=(in_ap(E) if callable(in_ap) else in_ap)).then_inc(ds, 16)

        self.q[eng].append(emit)
        self._commit(tk, reads, writes)
        self.final_waits[dk] = sb.dcnt
        return tk

    def collective(self, kind, sb_, s_ap, db_, d_ap, groups):
        dk = self._dsem(db_, "cc")
        waits = self._deps("pool", [sb_], [db_], same_ok=False)
        db_.dcnt += 1
        tk = (dk, db_.dcnt)
        sem = self.sem
        ds = sem[dk]

        def emit(E):
            for k, v in waits:
                E.wait_ge(sem[k], v)
            E.collective_compute(kind, ALU.bypass, replica_groups=groups, ins=[s_ap], outs=[d_ap]).then_inc(ds, 1)

        self.q["pool"].append(emit)
        self._commit(tk, [sb_], [db_])
        self.final_waits[dk] = db_.dcnt
        return tk

    def mm(self, ob, o, lb, l, rb, r, start=True, stop=True):
        return self.op("pe", lambda E: E.matmul(o, lhsT=l, rhs=r, start=start, stop=stop), reads=[lb, rb], writes=[ob])

    def act(self, ob, o, ib, i, func, bias=None, scale=None, accum=None, rd=(), wr=()):
        kw = {}
        if bias is not None:
            kw["bias"] = bias
        if scale is not None:
            kw["scale"] = scale
        if accum is not None:
            kw["accum_out"] = accum
        return self.op("act", lambda E: E.activation(out=o, in_=i, func=func, **kw), reads=[ib] + list(rd), writes=[ob] + list(wr))

    def tt(self, eng, ob, o, ab, a, bb, b, op):
        return self.op(eng, lambda E: E.tensor_tensor(out=o, in0=a, in1=b, op=op), reads=[ab, bb], writes=[ob])

    def ts(self, eng, ob, o, ab, a, s1, s2, op0, op1=None, rd=()):
        if op1 is None:
            return self.op(eng, lambda E: E.tensor_scalar(out=o, in0=a, scalar1=s1, scalar2=None, op0=op0), reads=[ab] + list(rd), writes=[ob])
        return self.op(eng, lambda E: E.tensor_scalar(out=o, in0=a, scalar1=s1, scalar2=s2, op0=op0, op1=op1), reads=[ab] + list(rd), writes=[ob])

    def stt(self, ob, o, ab, a, sc, bb, b, op0, op1, rd=()):
        return self.op("dve", lambda E: E.scalar_tensor_tensor(out=o, in0=a, scalar=sc, in1=b, op0=op0, op1=op1), reads=[ab, bb] + list(rd), writes=[ob])

    def copy(self, eng, ob, o, ib, i):
        if eng == "act":
            return self.op("act", lambda E: E.copy(out=o, in_=i), reads=[ib], writes=[ob])
        return self.op(eng, lambda E: E.tensor_copy(out=o, in_=i), reads=[ib], writes=[ob])

    def memset(self, eng, ob, o, val):
        return self.op(eng, lambda E: E.memset(o, val), writes=[ob])

    def recip(self, ob, o, ib, i):
        return self.op("dve", lambda E: E.reciprocal(out=o, in_=i), reads=[ib], writes=[ob])

    phase_id = 0

    def begin_phase(self):
        self.phase_id += 1
        self.gstack = self.stack if not hasattr(self, "gstack") else self.gstack
        self.pstack = ExitStack()
        self.pstack.__enter__()
        self.alloc_stack = self.pstack

    def end_phase(self):
        sem = self.sem
        cnt = dict(self.cnt)
        fw = dict(self.final_waits)
        for e in ENGS:
            waits = [(o, cnt[o]) for o in ENGS if o != e and cnt[o] > self.waited[e].get(o, 0)]
            waits += [(k, v) for k, v in fw.items() if v > self.waited[e].get(k, 0)]
            for k, v in waits:
                self.waited[e][k] = v

            def emit(E, waits=waits):
                for k, v in waits:
                    E.wait_ge(sem[k], v)
            self.q[e].append(emit)
        self.finish()
        self.q = {e: [] for e in ENGS}
        for e in ENGS:
            self.sem[e] = self.stack.enter_context(self.nc.semaphore("s_%s_%d" % (e, self.phase_id)))
            self.cnt[e] = 0
            for x in ENGS:
                self.waited[x].pop(e, None)
        for b in self.global_bufs:
            if b.w is not None and b.w[0] in ENGS:
                b.w = None
            b.r = {k: v for k, v in b.r.items() if k not in ENGS}
        for b in self.phase_bufs:
            if b.dsem is not None:
                self.free_dsems[self.dsem_cls[b.dsem]].append((b.dsem, b.dcnt))
                b.dsem = None
        self.phase_bufs = []
        self.pstack.__exit__(None, None, None)
        self.alloc_stack = self.stack

    def finish(self):
        nc = self.nc
        sem = self.sem
        q = self.q
        fw = dict(self.final_waits)
        with nc.Block() as block:
            @block.tensor
            def _(E):
                for f in q["pe"]:
                    f(E)

            @block.scalar
            def _(E):
                for f in q["act"]:
                    f(E)

            @block.vector
            def _(E):
                for f in q["dve"]:
                    f(E)

            @block.gpsimd
            def _(E):
                for f in q["pool"]:
                    f(E)

            @block.sync
            def _(E):
                for f in q["sp"]:
                    f(E)
                for k, v in fw.items():
                    E.wait_ge(sem[k], v)


class Ring:
    def __init__(self, bufs):
        self.bufs = bufs
        self.i = 0

    def next(self):
        b = self.bufs[self.i % len(self.bufs)]
        self.i += 1
        return b


def dram(nc, name, shape, dt, kind):
    return nc.dram_tensor(name, list(shape), dt, kind=kind).ap()


def kcp(ap):
    return ap.rearrange("(kc p) n -> p kc n", p=128)


def rms_stats(S, C, xb, nch, n, ps_ring, sq_ring, rstd, tmp, nfeat, eps):
    pss = ps_ring.next()
    for kc in range(nch):
        sq = sq_ring.next()
        S.act(sq, sq[:, :n], xb, xb[:, kc, :n], AF.Square)
        S.mm(pss, pss[:, :n], C["ones"], C["ones"][:], sq, sq[:, :n], start=(kc == 0), stop=(kc == nch - 1))
    S.act(tmp, tmp[:, :n], pss, pss[:, :n], AF.Sqrt, bias=C["eps%g" % eps][:, 0:1], scale=1.0 / nfeat, rd=[C["eps%g" % eps]])
    S.recip(rstd, rstd[:, :n], tmp, tmp[:, :n])


def load_consts(S, nc, cst_ap, extra_eps=()):
    C = {}
    C["ones"] = S.sb("c_ones", [128, 128], F32)
    S.dma("sp", C["ones"][:], cst_ap[0], writes=[C["ones"]])
    for e in (EPS,) + tuple(extra_eps):
        b = S.sb("c_eps%g" % e, [128, 1], F32)
        S.memset("dve", b, b[:], float(e))
        C["eps%g" % e] = b
    return C


def build_tok(mode):
    nc = bass.Bass("TRN2", target_bir_lowering=False)
    A = {}
    A["xT"] = dram(nc, "xT", [DM, NT], F32, "ExternalInput")
    A["g_next"] = dram(nc, "g_next", [128, 8], F32, "ExternalInput")
    A["cst"] = dram(nc, "cst", [4, 128, 128], F32, "ExternalInput")
    if mode != "first":
        A["hT"] = dram(nc, "hT", [DM, NT], BF16, "ExternalInput")
        yT = dram(nc, "yT", [4096, NT], BF16, "ExternalInput")
        A["load_y"] = lambda S, yub, t: S.dma("act", yub[:], kcp(yT[:, t * TT:(t + 1) * TT]), writes=[yub])
        A["w_gate"] = dram(nc, "w_gate", [DM, 3072], F32, "ExternalInput")
        A["b_gate"] = dram(nc, "b_gate", [128, 24], F32, "ExternalInput")
        A["w_br"] = dram(nc, "w_br", [4096, DM], F32, "ExternalInput")
        A["w_out"] = dram(nc, "w_out", [DM, DM], F32, "ExternalInput")
        A["w_up"] = dram(nc, "w_up", [DM, 4096], F32, "ExternalInput")
        A["w_dn"] = dram(nc, "w_dn", [4096, DM], F32, "ExternalInput")
        A["g_mlp"] = dram(nc, "g_mlp", [128, 8], F32, "ExternalInput")
    if mode == "last":
        A["outT"] = dram(nc, "outT", [DM, NT], F32, "ExternalOutput")
    else:
        A["hT_out"] = dram(nc, "hT_out", [DM, NT], BF16, "ExternalOutput")
        A["hT_lo"] = dram(nc, "hT_lo", [DM, NT], BF16, "ExternalOutput")
        if mode == "mid":
            A["xT_out"] = dram(nc, "xT_out", [DM, NT], F32, "ExternalOutput")
    with ExitStack() as st:
        S = Sched(nc, st)
        S.begin_phase()
        emit_tok(nc, S, A, mode)
        S.end_phase()
    return nc


def emit_tok(nc, S, A, mode):
    xT, gn, cst = A["xT"], A["g_next"], A["cst"]
    if mode != "first":
        w_gate, b_gate, w_br, w_out, w_up, w_dn, gm = (A[k] for k in ("w_gate", "b_gate", "w_br", "w_out", "w_up", "w_dn", "g_mlp"))
        h_in = A["h_in"] if "h_in" in A else (lambda t: A["hT"][:, t * TT:(t + 1) * TT])
    if mode == "last":
        outT = A["outT"]
    else:
        h_out = A["h_out"] if "h_out" in A else (lambda t, which: (A["hT_out"], A["hT_lo"])[which][:, t * TT:(t + 1) * TT])
        after_h = A.get("after_h", None)
        if mode == "mid":
            xTo = A["xT_out"]
    if True:
        C = load_consts(S, nc, cst)
        gnb = S.sb("gnb", [128, 8], F32)
        S.dma("sp", gnb[:], gn, writes=[gnb])
        xb = S.sb("xb", [128, 8, TT], F32)
        ps_ring = Ring([S.ps("ps%d" % i, [128, TT], F32) for i in range(7)])
        sq_ring = Ring([S.sb("sq%d" % i, [128, TT], F32) for i in range(2)])
        rstd = S.sb("rstd", [128, TT], F32)
        rtmp = S.sb("rtmp", [128, TT], F32)
        hn_ring = Ring([S.sb("hn%d" % i, [128, 8, TT], BF16) for i in range(2)])
        hl_ring = Ring([S.sb("hl%d" % i, [128, 8, TT], BF16) for i in range(2)])
        h32_ring = Ring([S.sb("h32_%d" % i, [128, TT], F32) for i in range(2)])
        if mode == "last":
            oc_ring = Ring([S.sb("oc%d" % i, [128, TT], F32) for i in range(3)])
        if mode != "first":
            bgb = S.sb("bgb", [128, 24], F32)
            S.dma("sp", bgb[:], b_gate, writes=[bgb])
            gmb = S.sb("gmb", [128, 8], F32)
            S.dma("sp", gmb[:], gm, writes=[gmb])
            hb_ring = Ring([S.sb("hb%d" % i, [128, 8, TT], BF16) for i in range(2)])
            yub = S.sb("yub", [128, 32, TT], BF16)
            mixed = S.sb("mixed", [128, 8, TT], BF16)
            h2 = S.sb("h2", [128, 8, TT], BF16)
            g_ring = Ring([S.sb("g%d" % i, [128, TT], F32) for i in range(6)])
            acc_ring = Ring([S.sb("acc%d" % i, [128, TT], F32) for i in range(2)])
            tmp_ring = Ring([S.sb("tmp%d" % i, [128, TT], F32) for i in range(2)])
            r_ring = Ring([S.sb("r%d" % i, [128, TT], F32) for i in range(2)])
            w_ring = Ring([S.sb("w%d" % i, [128, 7168], BF16) for i in range(3)])

        for t in range(NT // TT):
            ts_ = slice(t * TT, (t + 1) * TT)
            S.dma("sp", xb[:], kcp(xT[:, ts_]), writes=[xb])
            if mode != "first":
                hb = hb_ring.next()
                S.dma("sp", hb[:], kcp(h_in(t)), writes=[hb])
                A["load_y"](S, yub, t)
                for oc in range(8):
                    w = w_ring.next()
                    wg = w[:, 0:3072].rearrange("p (kc br j) -> p kc br j", kc=8, br=3)
                    wb = w[:, 3072:7168].rearrange("p (kc j) -> p kc j", kc=32)
                    for br in range(3):
                        c0 = br * 1024 + oc * 128
                        S.dma("pool", wg[:, :, br, :], kcp(w_gate[:, c0:c0 + 128]), writes=[w], group=(br > 0))
                    S.dma("pool", wb, kcp(w_br[:, oc * 128:(oc + 1) * 128]), writes=[w], group=True)
                    gts = []
                    for br in range(3):
                        pg = ps_ring.next()
                        for kc in range(8):
                            S.mm(pg, pg[:], w, wg[:, kc, br, :], hb, hb[:, kc, :], start=(kc == 0), stop=(kc == 7))
                        g = g_ring.next()
                        ch = br * 8 + oc
                        S.act(g, g[:], pg, pg[:], AF.Sigmoid, bias=bgb[:, ch:ch + 1], rd=[bgb])
                        gts.append(g)
                    acc = acc_ring.next()
                    koff = (0, 8, 24)
                    nk = (8, 16, 8)
                    for br in range(3):
                        pb = ps_ring.next()
                        for kc in range(nk[br]):
                            S.mm(pb, pb[:], w, wb[:, koff[br] + kc, :], yub, yub[:, koff[br] + kc, :], start=(kc == 0), stop=(kc == nk[br] - 1))
                        if br == 0:
                            S.tt("dve", acc, acc[:], pb, pb[:], gts[0], gts[0][:], ALU.mult)
                        else:
                            tmp = tmp_ring.next()
                            S.tt("dve", tmp, tmp[:], pb, pb[:], gts[br], gts[br][:], ALU.mult)
                            if br == 1:
                                S.tt("dve", acc, acc[:], acc, acc[:], tmp, tmp[:], ALU.add)
                            else:
                                S.tt("dve", mixed, mixed[:, oc, :], acc, acc[:], tmp, tmp[:], ALU.add)
                if t > 0 and mode == "mid" and after_h is not None:
                    after_h(S, t - 1)
                for half in range(2):
                    w = w_ring.next()
                    wo = w[:, 0:4096].rearrange("p (kc j) -> p kc j", kc=8)
                    S.dma("pool", wo, kcp(w_out[:, half * 512:(half + 1) * 512]), writes=[w])
                    for o4 in range(4):
                        oc = half * 4 + o4
                        po = ps_ring.next()
                        for kc in range(8):
                            S.mm(po, po[:], w, wo[:, kc, o4 * 128:(o4 + 1) * 128], mixed, mixed[:, kc, :], start=(kc == 0), stop=(kc == 7))
                        S.tt("dve", xb, xb[:, oc, :], xb, xb[:, oc, :], po, po[:], ALU.add)
                rms_stats(S, C, xb, 8, TT, ps_ring, sq_ring, rstd, rtmp, DM, EPS)
                for kc in range(8):
                    S.stt(h2, h2[:, kc, :], xb, xb[:, kc, :], gmb[:, kc:kc + 1], rstd, rstd[:], ALU.mult, ALU.mult, rd=[gmb])
                for o8 in range(8):
                    w = w_ring.next()
                    wu = w[:, 0:4096].rearrange("p (kc j) -> p kc j", kc=8)
                    S.dma("pool", wu, kcp(w_up[:, o8 * 512:(o8 + 1) * 512]), writes=[w])
                    for o4 in range(4):
                        oc = o8 * 4 + o4
                        pu = ps_ring.next()
                        for kc in range(8):
                            S.mm(pu, pu[:], w, wu[:, kc, o4 * 128:(o4 + 1) * 128], h2, h2[:, kc, :], start=(kc == 0), stop=(kc == 7))
                        r = r_ring.next()
                        S.act(r, r[:], pu, pu[:], AF.Relu)
                        S.tt("dve", yub, yub[:, oc, :], r, r[:], r, r[:], ALU.mult)
                for oc in range(8):
                    w = w_ring.next()
                    wd = w[:, 0:4096].rearrange("p (kc j) -> p kc j", kc=32)
                    S.dma("pool", wd, kcp(w_dn[:, oc * 128:(oc + 1) * 128]), writes=[w])
                    pd = ps_ring.next()
                    for kc in range(32):
                        S.mm(pd, pd[:], w, wd[:, kc, :], yub, yub[:, kc, :], start=(kc == 0), stop=(kc == 31))
                    S.tt("dve", xb, xb[:, oc, :], xb, xb[:, oc, :], pd, pd[:], ALU.add)
                if mode == "mid":
                    S.dma("sp", kcp(xTo[:, ts_]), xb[:], reads=[xb])
            rms_stats(S, C, xb, 8, TT, ps_ring, sq_ring, rstd, rtmp, DM, EPS)
            if mode == "last":
                for kc in range(8):
                    o = oc_ring.next()
                    S.stt(o, o[:], xb, xb[:, kc, :], gnb[:, kc:kc + 1], rstd, rstd[:], ALU.mult, ALU.mult, rd=[gnb])
                    S.dma("sp", outT[kc * 128:(kc + 1) * 128, ts_], o[:], reads=[o])
            else:
                hn = hn_ring.next()
                hl = hl_ring.next()
                for kc in range(8):
                    h32 = h32_ring.next()
                    S.stt(h32, h32[:], xb, xb[:, kc, :], gnb[:, kc:kc + 1], rstd, rstd[:], ALU.mult, ALU.mult, rd=[gnb])
                    S.copy("act", hn, hn[:, kc, :], h32, h32[:])
                    S.tt("pool", hl, hl[:, kc, :], h32, h32[:], hn, hn[:, kc, :], ALU.subtract)
                wr = A["h_wr"](t) if "h_wr" in A else [[], []]
                S.dma("sp", kcp(h_out(t, 0)), hn[:], reads=[hn], writes=wr[0], sembuf=hn)
                S.dma("sp", kcp(h_out(t, 1)), hl[:], reads=[hl], writes=wr[1], sembuf=hl)
                if after_h is not None and (mode == "first" or t == NT // TT - 1):
                    after_h(S, t)


def make_consts():
    c = np.zeros((4, 128, 128), np.float32)
    c[0] = 1.0
    c[1] = np.triu(np.ones((128, 128), np.float32))
    c[2] = 1.0 - c[1]
    c[3] = np.eye(128, dtype=np.float32)
    return c


def pvec(v):
    v = np.asarray(v)
    return np.ascontiguousarray(v.reshape(-1, 128).T)


def small_views(S, name, nbanks, width):
    out = []
    per = 512 // width
    banks = [S.ps("%s%d" % (name, i), [128, 512], F32) for i in range(nbanks)]
    for j in range(per):
        for i in range(nbanks):
            out.append(VBuf("%s%d_%d" % (name, i, j), banks[i], banks[i].t[:, j * width:(j + 1) * width]))
    return out


def build_gla(ntiles=SEQ // TT):
    nc = bass.Bass("TRN2", target_bir_lowering=False)
    hT = dram(nc, "hT", [DM, SEQ], BF16, "ExternalInput")
    hTl = dram(nc, "hTlo", [DM, SEQ], BF16, "ExternalInput")
    A = {}
    A["w_gla"] = dram(nc, "w_gla", [DM, 784], F32, "ExternalInput")
    A["wgk2"] = dram(nc, "wgk2", [16, 128], F32, "ExternalInput")
    A["bgk"] = dram(nc, "bgk", [1, 128], F32, "ExternalInput")
    A["ng"] = dram(nc, "ng", [128, 2], F32, "ExternalInput")
    A["cst"] = dram(nc, "cst", [4, 128, 128], F32, "ExternalInput")
    yT = dram(nc, "yT", [256, SEQ], BF16, "ExternalOutput")
    A["load_h"] = lambda S, h, t: S.dma("sp", h[:], kcp(hT[:, t * TT:(t + 1) * TT]), writes=[h])
    A["load_hlo"] = lambda S, h, t: S.dma("act", h[:], kcp(hTl[:, t * TT:(t + 1) * TT]), writes=[h])
    A["store_y"] = lambda S, yo, t: S.dma("sp", yT[:, t * TT:(t + 1) * TT].rearrange("(ec p) n -> p ec n", p=128), yo[:], reads=[yo])
    with ExitStack() as st:
        S = Sched(nc, st)
        S.begin_phase()
        emit_gla(nc, S, A, ntiles)
        S.end_phase()
    return nc


def emit_gla(nc, S, A, ntiles=SEQ // TT):
    w_gla, wgk2, bgk, ngp, cst = A["w_gla"], A["wgk2"], A["bgk"], A["ng"], A["cst"]
    if True:
        C = load_consts(S, nc, cst)
        U = S.sb("U", [128, 128], F32)
        UC = S.sb("UC", [128, 128], F32)
        S.dma("sp", U[:], cst[1], writes=[U])
        S.dma("sp", UC[:], cst[2], writes=[UC])
        wa = S.sb("wa", [128, 8, 784], BF16)
        S.dma("pool", wa[:], kcp(w_gla), writes=[wa])
        wqk32 = S.sb("wqk32", [128, 8, 256], F32)
        S.dma("sp", wqk32[:, :, 0:128], kcp(w_gla[:, 0:128]), writes=[wqk32])
        S.dma("sp", wqk32[:, :, 128:256], kcp(w_gla[:, 384:512]), writes=[wqk32], group=True)
        wqk_hi = S.sb("wqk_hi", [128, 8, 256], BF16)
        wqk_lo = S.sb("wqk_lo", [128, 8, 256], BF16)
        S.copy("act", wqk_hi, wqk_hi[:], wqk32, wqk32[:])
        S.tt("dve", wqk_lo, wqk_lo[:], wqk32, wqk32[:], wqk_hi, wqk_hi[:], ALU.subtract)
        w2 = S.sb("w2", [16, 128], F32)
        S.dma("sp", w2[:], wgk2, writes=[w2])
        bgb = S.sb("bgb", [128, 128], F32)
        S.dma("sp", bgb[:], bgk.partition_broadcast(128), writes=[bgb])
        ng = S.sb("ng", [128, 2], F32)
        S.dma("sp", ng[:], ngp, writes=[ng])
        h_ring = Ring([S.sb("h%d" % i, [128, 8, TT], BF16) for i in range(2)])
        hlo_ring = Ring([S.sb("hlo%d" % i, [128, 8, TT], BF16) for i in range(2)])
        big = Ring([S.ps("pb%d" % i, [128, 512], F32) for i in range(2)])
        po = [S.ps("po%d" % i, [128, 512], F32) for i in range(2)]
        pss_ring = Ring([S.ps("pss", [128, 512], F32)])
        sm = Ring(small_views(S, "sm", 3, 256))
        qTs = S.sb("qTs", [128, TT], F32)
        kTs = S.sb("kTs", [128, TT], F32)
        sg = S.sb("sg", [128, 2, TT], F32)
        gkl = S.sb("gkl", [16, TT], F32)

        def ring(name, shape, dt, n=2):
            return Ring([S.sb("%s%d" % (name, i), shape, dt) for i in range(n)])
        ktok_r = ring("ktok", [128, 128], F32)
        vtok_r = ring("vtok", [128, 256], BF16)
        t1_r = ring("t1", [128, 128], F32)
        e_r = ring("e", [128, 128], F32)
        gk_r = ring("gk", [128, 128], F32)
        ebT_r = ring("ebT", [128, 128], F32)
        enbT_r = ring("enbT", [128, 128], F32)
        ed2_r = ring("ed2", [128, 128], F32)
        qp_r = ring("qp", [128, 128], BF16)
        qp32_r = ring("qp32", [128, 128], F32)
        kp32_r = ring("kp32", [128, 128], F32)
        kpp_r = ring("kpp", [128, 128], BF16)
        AT_r = ring("AT", [128, 128], BF16)
        Sst = S.sb("Sst", [128, 256], F32)
        Sbf = S.sb("Sbf", [128, 256], BF16)
        S.memset("dve", Sst, Sst[:], 0.0)
        S.memset("dve", Sbf, Sbf[:], 0.0)
        sq_ring = ring("sq", [128, TT], F32)
        rstd = S.sb("rstd", [128, TT], F32)
        rtmp = S.sb("rtmp", [128, TT], F32)
        tmp_r = ring("tmp", [128, TT], F32)
        yo_r = ring("yo", [128, 2, TT], BF16)

        for t in range(ntiles):
            ts_ = slice(t * TT, (t + 1) * TT)
            h = h_ring.next()
            A["load_h"](S, h, t)

            def proj(c0, m):
                p = big.next()
                for kc in range(8):
                    S.mm(p, p[:m, :], wa, wa[:, kc, c0:c0 + m], h, h[:, kc, :], start=(kc == 0), stop=(kc == 7))
                return p
            hlo = hlo_ring.next()
            A["load_hlo"](S, hlo, t)

            def proj3(c0):
                p = big.next()
                n = 0
                for (wb_, hb_) in ((wqk_hi, h), (wqk_hi, hlo), (wqk_lo, h)):
                    for kc in range(8):
                        S.mm(p, p[:], wb_, wb_[:, kc, c0:c0 + 128], hb_, hb_[:, kc, :], start=(n == 0), stop=(n == 23))
                        n += 1
                return p
            p = proj3(0)
            S.act(qTs, qTs[:], p, p[:], AF.Copy, scale=128.0 ** -0.5)
            p = proj3(128)
            S.act(kTs, kTs[:], p, p[:], AF.Copy)
            for ec in range(2):
                p = proj(128 + ec * 128, 128)
                S.act(sg, sg[:, ec, :], p, p[:], AF.Silu)
            p = proj(768, 16)
            S.copy("dve", gkl, gkl[:], p, p[:16, :])
            def stageA(c):
                cs = slice(c * 128, (c + 1) * 128)
                p = big.next()
                for kc in range(8):
                    S.mm(p, p[:, :384], h, h[:, kc, cs], wa, wa[:, kc, 384:768], start=(kc == 0), stop=(kc == 7))
                ktok = ktok_r.next()
                vtok = vtok_r.next()
                S.copy("act", ktok, ktok[:], p, p[:, 0:128])
                S.copy("act", vtok, vtok[:], p, p[:, 128:384])
                pg = sm.next()
                S.mm(pg, pg[:, :128], gkl, gkl[:, cs], w2, w2[:], start=True, stop=True)
                t1 = t1_r.next()
                S.tt("dve", t1, t1[:], pg, pg[:, :128], bgb, bgb[:], ALU.add)
                e = e_r.next()
                S.act(e, e[:], t1, t1[:], AF.Exp, scale=-1.0)
                S.act(e, e[:], e, e[:], AF.Ln, bias=1.0)
                gk = gk_r.next()
                S.ts("dve", gk, gk[:], e, e[:], -1.0 / 16.0, None, ALU.mult)
                pbT = sm.next()
                S.mm(pbT, pbT[:, :128], gk, gk[:], U, U[:])
                pd2 = sm.next()
                S.mm(pd2, pd2[:, :128], UC, UC[:], gk, gk[:])
                ebT = ebT_r.next()
                enbT = enbT_r.next()
                ed2 = ed2_r.next()
                S.act(ebT, ebT[:], pbT, pbT[:, :128], AF.Exp)
                S.act(enbT, enbT[:], pbT, pbT[:, :128], AF.Exp, scale=-1.0)
                S.act(ed2, ed2[:], pd2, pd2[:, :128], AF.Exp)
                qp = qp_r.next()
                qp32 = qp32_r.next()
                kp32 = kp32_r.next()
                kpp = kpp_r.next()
                S.tt("dve", qp32, qp32[:], qTs, qTs[:, cs], ebT, ebT[:], ALU.mult)
                S.tt("dve", kp32, kp32[:], kTs, kTs[:, cs], enbT, enbT[:], ALU.mult)
                S.copy("act", qp, qp[:], qp32, qp32[:])
                S.tt("pool", kpp, kpp[:], ktok, ktok[:], ed2, ed2[:], ALU.mult)
                pA = sm.next()
                S.mm(pA, pA[:, :128], kp32, kp32[:], qp32, qp32[:])
                AT = AT_r.next()
                S.tt("dve", AT, AT[:], pA, pA[:, :128], U, U[:], ALU.mult)
                return dict(cs=cs, vtok=vtok, AT=AT, qp=qp, kpp=kpp, ebT=ebT)

            def stageB(c, v):
                cs, vtok, AT, qp, kpp, ebT = (v[k] for k in ('cs', 'vtok', 'AT', 'qp', 'kpp', 'ebT'))
                for ec in range(2):
                    es = slice(ec * 128, (ec + 1) * 128)
                    S.mm(po[ec], po[ec][:, cs], vtok, vtok[:, es], AT, AT[:], start=True, stop=False)
                    S.mm(po[ec], po[ec][:, cs], Sbf, Sbf[:, es], qp, qp[:], start=False, stop=True)
                pkv = sm.next()
                S.mm(pkv, pkv[:, :256], kpp, kpp[:], vtok, vtok[:])
                S.stt(Sst, Sst[:], Sst, Sst[:], ebT[:, 127:128], pkv, pkv[:, :256], ALU.mult, ALU.add, rd=[ebT])
                S.copy("act", Sbf, Sbf[:], Sst, Sst[:])

            va = {0: stageA(0)}
            for c in range(4):
                if c + 1 < 4:
                    va[c + 1] = stageA(c + 1)
                stageB(c, va.pop(c))
            pss = pss_ring.next()
            for ec in range(2):
                sq = sq_ring.next()
                S.act(sq, sq[:], po[ec], po[ec][:], AF.Square)
                S.mm(pss, pss[:], C["ones"], C["ones"][:], sq, sq[:], start=(ec == 0), stop=(ec == 1))
            S.act(rtmp, rtmp[:], pss, pss[:], AF.Sqrt, bias=C["eps%g" % EPS][:, 0:1], scale=1.0 / 256, rd=[C["eps%g" % EPS]])
            S.recip(rstd, rstd[:], rtmp, rtmp[:])
            yo = yo_r.next()
            for ec in range(2):
                tmp = tmp_r.next()
                S.stt(tmp, tmp[:], po[ec], po[ec][:], ng[:, ec:ec + 1], rstd, rstd[:], ALU.mult, ALU.mult, rd=[ng])
                S.tt("pool", yo, yo[:, ec, :], tmp, tmp[:], sg, sg[:, ec, :], ALU.mult)
            A["store_y"](S, yo, t)


def gla_inputs(inp, l, g, hT_b, cst, hTlo_b=None):
    w = inp["w_in"][l]
    cols = np.concatenate([np.arange(g * 128, (g + 1) * 128),
                           2064 + np.arange(g * 256, (g + 1) * 256),
                           512 + np.arange(g * 128, (g + 1) * 128),
                           1024 + np.arange(g * 256, (g + 1) * 256),
                           2048 + np.arange(16)])
    if hTlo_b is None:
        hTlo_b = np.zeros_like(hT_b)
    elif isinstance(hTlo_b, int):
        hTlo_b = None
    return {"hT": hT_b, "hTlo": hTlo_b, "w_gla": np.ascontiguousarray(w[:, cols]),
            "wgk2": np.ascontiguousarray(inp["gla_w_gk2"][l][:, g * 128:(g + 1) * 128]),
            "bgk": np.ascontiguousarray(inp["gla_b_gk"][l][g * 128:(g + 1) * 128].reshape(1, 128)),
            "ng": pvec(inp["gla_norm_g"][l]), "cst": cst}


def build_ssm(ntiles=SEQ // TT):
    nc = bass.Bass("TRN2", target_bir_lowering=False)
    hT = dram(nc, "hT", [DM, SEQ], BF16, "ExternalInput")
    A = {}
    A["w_ssm"] = dram(nc, "w_ssm", [DM, 1288], F32, "ExternalInput")
    A["cw"] = dram(nc, "cw", [128, 6, 4], F32, "ExternalInput")
    A["cb"] = dram(nc, "cb", [128, 6], F32, "ExternalInput")
    A["dtb"] = dram(nc, "dtb", [1, 8], F32, "ExternalInput")
    A["alog"] = dram(nc, "alog", [1, 8], F32, "ExternalInput")
    A["dsk"] = dram(nc, "dsk", [1, 8], F32, "ExternalInput")
    A["ngs"] = dram(nc, "ngs", [1, 512], F32, "ExternalInput")
    A["cst"] = dram(nc, "cst", [4, 128, 128], F32, "ExternalInput")
    yT = dram(nc, "yT", [512, SEQ], BF16, "ExternalOutput")
    A["load_h"] = lambda S, h, t: S.dma("sp", h[:], kcp(hT[:, t * TT:(t + 1) * TT]), writes=[h])
    A["store_y"] = lambda S, yo, t: S.dma("sp", yT[:, t * TT:(t + 1) * TT].rearrange("(c p) n -> p c n", p=128), yo[:], reads=[yo])
    with ExitStack() as st:
        S = Sched(nc, st)
        S.begin_phase()
        emit_ssm(nc, S, A, ntiles)
        S.end_phase()
    return nc


def emit_ssm(nc, S, A, ntiles=SEQ // TT):
    w_ssm, cwp, cbp, dtbp, alogp, dskp, ngp, cst = (A[k] for k in ("w_ssm", "cw", "cb", "dtb", "alog", "dsk", "ngs", "cst"))
    if True:
        C = load_consts(S, nc, cst)
        U = S.sb("U", [128, 128], F32)
        UC = S.sb("UC", [128, 128], F32)
        idf = S.sb("idf", [128, 128], F32)
        idb = S.sb("idb", [128, 128], BF16)
        S.dma("sp", U[:], cst[1], writes=[U])
        S.dma("sp", UC[:], cst[2], writes=[UC])
        S.dma("sp", idf[:], cst[3], writes=[idf])
        S.dma("pool", idb[:], cst[3], writes=[idb])
        ws = S.sb("ws", [128, 8, 1288], BF16)
        S.dma("pool", ws[:], kcp(w_ssm), writes=[ws])
        cw = S.sb("cw", [128, 6, 4], F32)
        cb = S.sb("cb", [128, 6], F32)
        S.dma("sp", cw[:], cwp, writes=[cw])
        S.dma("sp", cb[:], cbp, writes=[cb])
        dtb = S.sb("dtb", [128, 8], F32)
        a_b = S.sb("a_b", [128, 8], F32)
        dsk = S.sb("dsk", [128, 8], F32)
        ngs = S.sb("ngs", [128, 512], F32)
        S.dma("sp", dtb[:], dtbp.partition_broadcast(128), writes=[dtb])
        S.dma("sp", a_b[:], alogp.partition_broadcast(128), writes=[a_b])
        S.dma("sp", dsk[:], dskp.partition_broadcast(128), writes=[dsk])
        S.dma("sp", ngs[:], ngp.partition_broadcast(128), writes=[ngs])
        S.act(a_b, a_b[:], a_b, a_b[:], AF.Exp)
        S.ts("dve", a_b, a_b[:], a_b, a_b[:], -1.0, None, ALU.mult)

        def ring(name, shape, dt, n=2):
            return Ring([S.sb("%s%d" % (name, i), shape, dt) for i in range(n)])
        h_ring = ring("h", [128, 8, TT], BF16)
        big = Ring([S.ps("pb%d" % i, [128, 512], F32) for i in range(6)])
        sm = Ring(small_views(S, "sm", 2, 128))
        raw = S.sb("raw", [128, 6, TT + 3], F32)
        S.memset("dve", raw, raw[:, :, 0:3], 0.0)
        cacc_r = ring("cacc", [128, TT], F32)
        xc = S.sb("xc", [128, 4, TT], F32)
        BT = S.sb("BT", [128, TT], BF16)
        CT = S.sb("CT", [128, TT], BF16)
        sz_r = ring("sz", [128, 512], F32)
        t8_r = ring("t8", [128, 8], F32)
        dt_r = ring("dt", [128, 8], F32)
        da_r = ring("da", [128, 8], F32)
        xdt_r = ring("xdt", [128, 512], BF16)
        xD_r = ring("xD", [128, 512], F32)
        Btok_r = ring("Btok", [128, 128], BF16)
        rda_r = ring("rda", [128, 8, 128], F32)
        eL_r = ring("eL", [128, 8, 128], F32)
        GU_r = ring("GU", [128, 128], F32)
        M_r = ring("M", [128, 8, 128], BF16)
        cs_r = ring("cs", [128, 16], F32)
        ec_r = ring("ec", [128, 24], F32)
        xdtd_r = ring("xdtd", [128, 512], BF16)
        Sst = S.sb("Sst", [128, 512], F32)
        Sbf = S.sb("Sbf", [128, 512], BF16)
        S.memset("dve", Sst, Sst[:], 0.0)
        S.memset("dve", Sbf, Sbf[:], 0.0)
        y1_r = ring("y1", [128, 512], F32)
        y2_r = ring("y2", [128, 512], F32)
        junk = S.sb("junk", [128, 512], F32)
        ss_r = ring("ss", [128, 2], F32)
        yn_r = ring("yn", [128, 512], BF16)
        yo_r = ring("yo", [128, 4, TT], BF16)
        eps = C["eps%g" % EPS]

        def b8(ap):
            return ap.unsqueeze(2).to_broadcast([128, 8, 64])

        def v8(ap):
            return ap.rearrange("p (h q) -> p h q", h=8)

        for t in range(ntiles):
            ts_ = slice(t * TT, (t + 1) * TT)
            h = h_ring.next()
            A["load_h"](S, h, t)
            for ch in range(6):
                p = big.next()
                for kc in range(8):
                    S.mm(p, p[:], ws, ws[:, kc, 512 + ch * 128:640 + ch * 128], h, h[:, kc, :], start=(kc == 0), stop=(kc == 7))
                S.act(raw, raw[:, ch, 3:TT + 3], p, p[:], AF.Copy)
            for ch in range(6):
                acc = cacc_r.next()
                S.ts("dve", acc, acc[:], raw, raw[:, ch, 0:TT], cw[:, ch, 0:1], cb[:, ch:ch + 1], ALU.mult, ALU.add, rd=[cw, cb])
                for i in range(1, 4):
                    S.stt(acc, acc[:], raw, raw[:, ch, i:i + TT], cw[:, ch, i:i + 1], acc, acc[:], ALU.mult, ALU.add, rd=[cw])
                if ch < 4:
                    S.act(xc, xc[:, ch, :], acc, acc[:], AF.Silu)
                elif ch == 4:
                    S.act(BT, BT[:], acc, acc[:], AF.Silu)
                else:
                    S.act(CT, CT[:], acc, acc[:], AF.Silu)
            S.copy("pool", raw, raw[:, :, 0:3], raw, raw[:, :, TT:TT + 3])
            yo = yo_r.next()
            def stageA(c):
                cs = slice(c * 128, (c + 1) * 128)
                pz = big.next()
                for kc in range(8):
                    S.mm(pz, pz[:], h, h[:, kc, cs], ws, ws[:, kc, 0:512], start=(kc == 0), stop=(kc == 7))
                sz = sz_r.next()
                S.act(sz, sz[:], pz, pz[:], AF.Silu)
                pdt = sm.next()
                for kc in range(8):
                    S.mm(pdt, pdt[:, 0:8], h, h[:, kc, cs], ws, ws[:, kc, 1280:1288], start=(kc == 0), stop=(kc == 7))
                t8 = t8_r.next()
                S.tt("dve", t8, t8[:], pdt, pdt[:, 0:8], dtb, dtb[:], ALU.add)
                S.act(t8, t8[:], t8, t8[:], AF.Exp)
                dt = dt_r.next()
                S.act(dt, dt[:], t8, t8[:], AF.Ln, bias=1.0)
                da = da_r.next()
                S.tt("dve", da, da[:], dt, dt[:], a_b, a_b[:], ALU.mult)
                px = big.next()
                for ch in range(4):
                    S.mm(px, px[:, ch * 128:(ch + 1) * 128], xc, xc[:, ch, cs], idf, idf[:])
                xdt = xdt_r.next()
                xD = xD_r.next()
                S.tt("dve", xdt, v8(xdt[:]), px, v8(px[:]), dt, b8(dt[:]), ALU.mult)
                S.tt("dve", xD, v8(xD[:]), px, v8(px[:]), dsk, b8(dsk[:]), ALU.mult)
                pB = sm.next()
                S.mm(pB, pB[:], BT, BT[:, cs], idb, idb[:])
                Btok = Btok_r.next()
                S.copy("act", Btok, Btok[:], pB, pB[:])
                rda = rda_r.next()
                S.tt("pool", rda, rda[:], U, U[:].unsqueeze(1).to_broadcast([128, 8, 128]), da, da[:].unsqueeze(2).to_broadcast([128, 8, 128]), ALU.mult)
                pD = [big.next(), big.next()]
                for hf in range(2):
                    S.mm(pD[hf], pD[hf][:], UC, UC[:], rda, rda[:, hf * 4:(hf + 1) * 4, :].rearrange("p a b -> p (a b)"))
                pc = sm.next()
                S.mm(pc, pc[:, 0:8], U, U[:], da, da[:])
                pc2 = sm.next()
                S.mm(pc2, pc2[:, 0:8], C["ones"], C["ones"][:], da, da[:])
                csb = cs_r.next()
                S.copy("dve", csb, csb[:, 0:8], pc, pc[:, 0:8])
                S.copy("dve", csb, csb[:, 8:16], pc2, pc2[:, 0:8])
                ec = ec_r.next()
                S.act(ec, ec[:, 0:16], csb, csb[:, 0:16], AF.Exp)
                S.tt("dve", csb, csb[:, 0:8], csb, csb[:, 8:16], csb, csb[:, 0:8], ALU.subtract)
                S.act(ec, ec[:, 16:24], csb, csb[:, 0:8], AF.Exp)
                eL = eL_r.next()
                for hf in range(2):
                    S.act(eL, eL[:, hf * 4:(hf + 1) * 4, :].rearrange("p a b -> p (a b)"), pD[hf], pD[hf][:], AF.Exp)
                pG = sm.next()
                S.mm(pG, pG[:], BT, BT[:, cs], CT, CT[:, cs])
                GU = GU_r.next()
                S.tt("dve", GU, GU[:], pG, pG[:], U, U[:], ALU.mult)
                M = M_r.next()
                S.tt("pool", M, M[:], eL, eL[:], GU, GU[:].unsqueeze(1).to_broadcast([128, 8, 128]), ALU.mult)
                xdtd = xdtd_r.next()
                S.tt("pool", xdtd, v8(xdtd[:]), xdt, v8(xdt[:]), ec, b8(ec[:, 16:24]), ALU.mult)
                return dict(cs=cs, sz=sz, xdt=xdt, xD=xD, Btok=Btok, M=M, ec=ec, xdtd=xdtd)

            def stageB(c, v):
                cs, sz, xdt, xD, Btok, M, ec, xdtd = (v[k] for k in ('cs', 'sz', 'xdt', 'xD', 'Btok', 'M', 'ec', 'xdtd'))
                pyd = big.next()
                for hh in range(8):
                    S.mm(pyd, pyd[:, hh * 64:(hh + 1) * 64], M, M[:, hh, :], xdt, xdt[:, hh * 64:(hh + 1) * 64])
                pyo = big.next()
                S.mm(pyo, pyo[:], CT, CT[:, cs], Sbf, Sbf[:])
                pst = big.next()
                S.mm(pst, pst[:], Btok, Btok[:], xdtd, xdtd[:])
                S.tt("dve", Sst, v8(Sst[:]), Sst, v8(Sst[:]), ec, b8(ec[:, 8:16]), ALU.mult)
                S.tt("dve", Sst, Sst[:], Sst, Sst[:], pst, pst[:], ALU.add)
                S.copy("act", Sbf, Sbf[:], Sst, Sst[:])
                y1 = y1_r.next()
                S.tt("dve", y1, v8(y1[:]), pyo, v8(pyo[:]), ec, b8(ec[:, 0:8]), ALU.mult)
                S.tt("dve", y1, y1[:], y1, y1[:], pyd, pyd[:], ALU.add)
                y2 = y2_r.next()
                S.tt("dve", y2, y2[:], y1, y1[:], xD, xD[:], ALU.add)
                S.tt("pool", y2, y2[:], y2, y2[:], sz, sz[:], ALU.mult)
                ss = ss_r.next()
                S.act(junk, junk[:], y2, y2[:], AF.Square, accum=ss[:, 0:1], wr=[ss])
                S.act(ss, ss[:, 1:2], ss, ss[:, 0:1], AF.Sqrt, bias=eps[:, 0:1], scale=1.0 / 512, rd=[eps])
                S.recip(ss, ss[:, 0:1], ss, ss[:, 1:2])
                yn = yn_r.next()
                S.stt(yn, yn[:], y2, y2[:], ss[:, 0:1], ngs, ngs[:], ALU.mult, ALU.mult, rd=[ss])
                pT = big.next()
                for ch in range(4):
                    S.mm(pT, pT[:, ch * 128:(ch + 1) * 128], yn, yn[:, ch * 128:(ch + 1) * 128], idb, idb[:])
                S.copy("act", yo, yo[:, :, cs], pT, pT[:].rearrange("p (c n) -> p c n", c=4))

            va = {0: stageA(0)}
            for c in range(4):
                if c + 1 < 4:
                    va[c + 1] = stageA(c + 1)
                stageB(c, va.pop(c))
            A["store_y"](S, yo, t)


def ssm_inputs(inp, l, g, hT_b, cst):
    w = inp["w_in"][l]
    xcols = 5136 + np.arange(g * 512, (g + 1) * 512)
    bcols = 7184 + np.arange(g * 128, (g + 1) * 128)
    ccols = 7696 + np.arange(g * 128, (g + 1) * 128)
    cols = np.concatenate([3088 + np.arange(g * 512, (g + 1) * 512), xcols, bcols, ccols, 8208 + np.arange(g * 8, (g + 1) * 8)])
    cc = np.concatenate([xcols, bcols, ccols]) - 5136
    cwv = inp["ssm_conv_w"][l][:, cc]
    cw = np.ascontiguousarray(cwv.reshape(4, 6, 128).transpose(2, 1, 0))
    cb = np.ascontiguousarray(inp["ssm_conv_b"][l][cc].reshape(6, 128).T)
    hs = slice(g * 8, (g + 1) * 8)
    return {"hT": hT_b, "w_ssm": np.ascontiguousarray(w[:, cols]), "cw": cw, "cb": cb,
            "dtb": np.ascontiguousarray(inp["ssm_dt_bias"][l][hs].reshape(1, 8)),
            "alog": np.ascontiguousarray(inp["ssm_a_log"][l][hs].reshape(1, 8)),
            "dsk": np.ascontiguousarray(inp["ssm_d"][l][hs].reshape(1, 8)),
            "ngs": np.ascontiguousarray(inp["ssm_norm_g"][l][g * 512:(g + 1) * 512].reshape(1, 512)),
            "cst": cst}


C1_2PI = 6.28125
C2_2PI = 2.0 * math.pi - 6.28125


def build_diff(l, ntiles=SEQ // TT):
    nc = bass.Bass("TRN2", target_bir_lowering=False)
    hT = dram(nc, "hT", [DM, SEQ], BF16, "ExternalInput")
    A = {}
    A["w_diff"] = dram(nc, "w_diff", [DM, 1280], F32, "ExternalInput")
    A["pos"] = dram(nc, "pos", [1, SEQ], I32, "ExternalInput")
    A["invf"] = dram(nc, "invf", [128, 2], F32, "ExternalInput")
    A["lqk"] = dram(nc, "lqk", [4, 64], F32, "ExternalInput")
    A["ngd"] = dram(nc, "ngd", [1, 128], F32, "ExternalInput")
    A["cst"] = dram(nc, "cst", [4, 128, 128], F32, "ExternalInput")
    yT = dram(nc, "yT", [256, SEQ], BF16, "ExternalOutput")
    A["load_h"] = lambda S, h, t: S.dma("sp", h[:], kcp(hT[:, t * TT:(t + 1) * TT]), writes=[h])
    A["store_y"] = lambda S, yo, t: S.dma("sp", yT[:, t * TT:(t + 1) * TT].rearrange("(c p) n -> p c n", p=128), yo[:], reads=[yo])
    with ExitStack() as st:
        S = Sched(nc, st)
        S.begin_phase()
        emit_diff(nc, S, A, l, ntiles)
        S.end_phase()
    return nc


def emit_diff(nc, S, A, l, ntiles=SEQ // TT):
    lambda_init = 0.8 - 0.6 * math.exp(-0.3 * l)
    w_diff, posd, invfp, lqk, ngp, cst = (A[k] for k in ("w_diff", "pos", "invf", "lqk", "ngd", "cst"))
    if True:
        C = load_consts(S, nc, cst, extra_eps=(1e-5,))
        idb = S.sb("idb", [128, 128], BF16)
        S.dma("pool", idb[:], cst[3], writes=[idb])
        wd = S.sb("wd", [128, 8, 1280], BF16)
        S.dma("pool", wd[:], kcp(w_diff), writes=[wd])
        invf = S.sb("invf", [128, 2], F32)
        S.dma("sp", invf[:], invfp, writes=[invf])
        ngd = S.sb("ngd", [128, 128], F32)
        S.dma("sp", ngd[:], ngp.partition_broadcast(128), writes=[ngd])
        S.ts("dve", ngd, ngd[:], ngd, ngd[:], 1.0 - lambda_init, None, ALU.mult)
        lq = S.sb("lq", [128, 4, 64], F32)
        for i in range(4):
            S.dma("sp", lq[:, i, :], lqk[i:i + 1, :].partition_broadcast(128), writes=[lq], group=(i > 0))
        lt = S.sb("lt", [128, 2, 64], F32)
        S.tt("dve", lt, lt[:, 0, :], lq, lq[:, 0, :], lq, lq[:, 1, :], ALU.mult)
        S.tt("dve", lt, lt[:, 1, :], lq, lq[:, 2, :], lq, lq[:, 3, :], ALU.mult)
        ls = S.sb("ls", [128, 4], F32)
        S.op("dve", lambda E: E.reduce_sum(out=ls[:, 0:2], in_=lt[:], axis=AX.X), reads=[lt], writes=[ls])
        S.act(ls, ls[:, 0:2], ls, ls[:, 0:2], AF.Exp)
        S.tt("dve", ls, ls[:, 2:3], ls, ls[:, 1:2], ls, ls[:, 0:1], ALU.subtract)
        S.ts("dve", ls, ls[:, 3:4], ls, ls[:, 2:3], -lambda_init, None, ALU.add)
        nlam = ls

        def ring(name, shape, dt, n=2):
            return Ring([S.sb("%s%d" % (name, i), shape, dt) for i in range(n)])
        h_ring = ring("h", [128, 8, TT], BF16)
        KT = S.sb("KT", [128, 2, SEQ], BF16)
        VA = S.sb("VA", [128, 2, SEQ // 128, 129], BF16)
        S.memset("pool", VA, VA[:, :, :, 128:129], 1.0)
        QT_r = ring("QT", [128, 2, TT], BF16)
        big = Ring([S.ps("pb%d" % i, [128, 512], F32) for i in range(3)])
        pob = [[S.ps("po%d_%d" % (s, hf), [128, 512], F32) for hf in range(2)] for s in range(2)]
        sm = Ring(small_views(S, "sm", 1, 128))
        posi = S.sb("posi", [128, TT], I32)
        ang = S.sb("ang", [128, TT], F32)
        ang2 = S.sb("ang2", [128, TT], F32)
        ki = S.sb("ki", [128, TT], I32)
        kf = S.sb("kf", [128, TT], F32)
        yr = S.sb("yr", [128, TT], F32)
        Cs = S.sb("Cs", [128, TT], F32)
        Sn = S.sb("Sn", [128, TT], F32)
        ta_r = ring("ta", [128, TT], F32)
        tb_r = ring("tb", [128, TT], F32)
        PT_r = ring("PT", [128, TT], BF16, 5)
        r_r = ring("r", [128, 4], F32, 4)
        oa_r = ring("oa", [128, 128], F32, 4)
        junk_r = ring("junk", [128, 128], F32, 4)
        yn_r = ring("yn", [128, 128], BF16, 4)
        yo_r = ring("yo", [128, 2, TT], BF16)
        eps5 = C["eps%g" % 1e-5]

        def reduce_sin(dst, src):
            S.ts("dve", ki, ki[:], src, src[:], 1.0 / (2.0 * math.pi), None, ALU.mult)
            S.copy("dve", kf, kf[:], ki, ki[:])
            S.stt(yr, yr[:], kf, kf[:], -C1_2PI, src, src[:], ALU.mult, ALU.add)
            S.stt(yr, yr[:], kf, kf[:], -C2_2PI, yr, yr[:], ALU.mult, ALU.add)
            S.ts("dve", yr, yr[:], yr, yr[:], -3.1415925, 3.1415925, ALU.max, ALU.min)
            S.act(dst, dst[:], yr, yr[:], AF.Sin)

        for t in range(ntiles):
            ts_ = slice(t * TT, (t + 1) * TT)
            h = h_ring.next()
            A["load_h"](S, h, t)
            S.dma("sp", posi[:], posd[0:1, ts_].partition_broadcast(128), writes=[posi])
            S.copy("dve", ang, ang[:], posi, posi[:])
            S.ts("dve", ang, ang[:], ang, ang[:], invf[:, 0:1], None, ALU.mult, rd=[invf])
            reduce_sin(Sn, ang)
            S.ts("dve", Sn, Sn[:], Sn, Sn[:], invf[:, 1:2], None, ALU.mult, rd=[invf])
            S.ts("dve", ang2, ang2[:], ang, ang[:], math.pi / 2.0, None, ALU.add)
            reduce_sin(Cs, ang2)
            QT = QT_r.next()
            for hd in range(2):
                for (c0, dstb, dst) in ((hd * 128, QT, QT[:, hd, :]), (512 + hd * 128, KT, KT[:, hd, ts_])):
                    p1 = big.next()
                    for kc in range(8):
                        S.mm(p1, p1[:], wd, wd[:, kc, c0:c0 + 128], h, h[:, kc, :], start=(kc == 0), stop=(kc == 7))
                    p2 = big.next()
                    for kc in range(8):
                        S.mm(p2, p2[:], wd, wd[:, kc, c0 + 256:c0 + 384], h, h[:, kc, :], start=(kc == 0), stop=(kc == 7))
                    ta = ta_r.next()
                    tb = tb_r.next()
                    S.tt("dve", ta, ta[:], p1, p1[:], Cs, Cs[:], ALU.mult)
                    S.tt("dve", tb, tb[:], p2, p2[:], Sn, Sn[:], ALU.mult)
                    S.tt("pool", dstb, dst, ta, ta[:], tb, tb[:], ALU.add)
            for c in range(4):
                cs = slice(c * 128, (c + 1) * 128)
                pv = big.next()
                for kc in range(8):
                    S.mm(pv, pv[:, 0:256], h, h[:, kc, cs], wd, wd[:, kc, 1024:1280], start=(kc == 0), stop=(kc == 7))
                S.copy("act", VA, VA[:, :, 4 * t + c, 0:128], pv, pv[:, 0:256].rearrange("p (a b) -> p a b", a=2))
            yo = yo_r.next()
            for hd in range(2):
                nkb = 4 * t + 4
                started = [[False, False], [False, False]]
                iters = [(kb, s_) for kb in range(nkb) for s_ in range(2)]
                pend = {}

                def emit_qk(i):
                    kb, s_ = iters[i]
                    r = kb - 4 * t
                    q0 = max(r, 0)
                    qlo = q0 * 128
                    n = TT - qlo
                    ps_ = slice(s_ * 64, (s_ + 1) * 64)
                    pS = big.next()
                    S.mm(pS, pS[:, :n], KT, KT[ps_, hd, kb * 128:(kb + 1) * 128], QT, QT[ps_, hd, qlo:TT])
                    PT = PT_r.next()
                    S.act(PT, PT[:, :n], pS, pS[:, :n], AF.Exp, scale=0.125)
                    if r >= 0:
                        S.memset("pool", PT, PT[64:128, 0:64], 0.0)
                    pend[i] = (PT, q0, qlo)

                def emit_pv(i):
                    kb, s_ = iters[i]
                    PT, q0, qlo = pend.pop(i)
                    for qb in range(q0, 4):
                        col = qb * 128 - qlo
                        bank = pob[s_][qb // 2]
                        o = bank[:, (qb % 2) * 129:(qb % 2) * 129 + 129]
                        first = not started[s_][qb // 2]
                        started[s_][qb // 2] = True
                        S.mm(bank, o, PT, PT[:, col:col + 128], VA, VA[:, hd, kb, :], start=first, stop=(kb == 4 * t + qb and qb % 2 == 1))

                LOOK = 3
                for i in range(len(iters) + LOOK):
                    if i < len(iters):
                        emit_qk(i)
                    if i >= LOOK:
                        emit_pv(i - LOOK)
                QB = range(4)
                o1 = [pob[0][qb // 2] for qb in QB]
                o2 = [pob[1][qb // 2] for qb in QB]
                b0 = [(qb % 2) * 129 for qb in QB]
                rr = [r_r.next() for qb in QB]
                oa = [oa_r.next() for qb in QB]
                yn = [yn_r.next() for qb in QB]
                jk = [junk_r.next() for qb in QB]
                for qb in QB:
                    S.recip(rr[qb], rr[qb][:, 0:1], o1[qb], o1[qb][:, b0[qb] + 128:b0[qb] + 129])
                for qb in QB:
                    S.recip(rr[qb], rr[qb][:, 1:2], o2[qb], o2[qb][:, b0[qb] + 128:b0[qb] + 129])
                for qb in QB:
                    S.tt("dve", rr[qb], rr[qb][:, 2:3], rr[qb], rr[qb][:, 1:2], nlam, nlam[:, 3:4], ALU.mult)
                for qb in QB:
                    S.ts("dve", oa[qb], oa[qb][:], o1[qb], o1[qb][:, b0[qb]:b0[qb] + 128], rr[qb][:, 0:1], None, ALU.mult, rd=[rr[qb]])
                for qb in QB:
                    S.stt(oa[qb], oa[qb][:], o2[qb], o2[qb][:, b0[qb]:b0[qb] + 128], rr[qb][:, 2:3], oa[qb], oa[qb][:], ALU.mult, ALU.add, rd=[rr[qb]])
                for qb in QB:
                    S.act(jk[qb], jk[qb][:], oa[qb], oa[qb][:], AF.Square, accum=rr[qb][:, 3:4], wr=[rr[qb]])
                for qb in QB:
                    S.act(rr[qb], rr[qb][:, 1:2], rr[qb], rr[qb][:, 3:4], AF.Sqrt, bias=eps5[:, 0:1], scale=1.0 / 128, rd=[eps5])
                for qb in QB:
                    S.recip(rr[qb], rr[qb][:, 0:1], rr[qb], rr[qb][:, 1:2])
                for qb in QB:
                    S.stt(yn[qb], yn[qb][:], oa[qb], oa[qb][:], rr[qb][:, 0:1], ngd, ngd[:], ALU.mult, ALU.mult, rd=[rr[qb]])
                for qb in QB:
                    pT = sm.next()
                    S.mm(pT, pT[:], yn[qb], yn[qb][:], idb, idb[:])
                    S.copy("act", yo, yo[:, hd, qb * 128:(qb + 1) * 128], pT, pT[:])
            A["store_y"](S, yo, t)


def diff_inputs(inp, l, g, b, hT_b, cst):
    w = inp["w_in"][l]
    qc = 8240 + np.arange(g * 256, (g + 1) * 256)
    kc = 9264 + np.arange(g * 256, (g + 1) * 256)
    vc = 10288 + np.arange(g * 256, (g + 1) * 256)
    d = np.arange(256) % 64
    partner = np.arange(256) + np.where(d < 8, 8, np.where(d < 16, -8, 0))
    cols = np.concatenate([qc, qc[partner], kc, kc[partner], vc])
    p = np.arange(128) % 64
    invf = np.zeros((128, 2), np.float32)
    fr = (500000.0 ** (-np.arange(0, 16, 2, dtype=np.float32) / np.float32(16))).astype(np.float32)
    invf[:, 0] = np.where(p < 16, fr[p % 8], 0.0)
    invf[:, 1] = np.where(p < 8, -1.0, np.where(p < 16, 1.0, 0.0))
    lqk = np.stack([inp["diff_lq1"][l], inp["diff_lk1"][l], inp["diff_lq2"][l], inp["diff_lk2"][l]]).astype(np.float32)
    return {"hT": hT_b, "w_diff": np.ascontiguousarray(w[:, cols]),
            "pos": np.ascontiguousarray(inp["positions"][b].reshape(1, SEQ)), "invf": invf, "lqk": lqk,
            "ngd": np.ascontiguousarray(inp["diff_norm_g"][l].reshape(1, 128)), "cst": cst}


_PROGS = {}


def _prog(key, fn):
    if key not in _PROGS:
        _PROGS[key] = fn()
    return _PROGS[key]


def _run(nc, maps):
    return run_bass_kernel_spmd(nc, maps, core_ids=list(range(NCORES))).results


def tok_inputs(inp, l, xT_c, hT_c, yT_c, g_next, cst):
    w = inp["w_in"][l]
    return {"xT": xT_c, "g_next": pvec(g_next), "cst": cst, "hT": hT_c, "yT": yT_c,
            "w_gate": np.ascontiguousarray(w[:, 11312:14384]), "b_gate": pvec(inp["b_gate"][l]),
            "w_br": np.ascontiguousarray(np.concatenate([inp["w_br_gla"][l], inp["w_br_ssm"][l], inp["w_br_diff"][l]], axis=0)),
            "w_out": np.ascontiguousarray(inp["w_out"][l]), "w_up": np.ascontiguousarray(inp["w_mlp_up"][l]),
            "w_dn": np.ascontiguousarray(inp["w_mlp_down"][l]), "g_mlp": pvec(inp["norm_mlp_g"][l])}


def kernel_unfused(**inp):
    inp = {k: np.asarray(v) for k, v in inp.items()}
    cst = make_consts()
    x = inp["x"]
    xT = [np.ascontiguousarray(x[c // 4, (c % 4) * NT:(c % 4 + 1) * NT, :].T) for c in range(NCORES)]
    res = _run(_prog("first", lambda: build_tok("first")),
               [{"xT": xT[c], "g_next": pvec(inp["norm_mix_g"][0]), "cst": cst} for c in range(NCORES)])
    hT = [r["hT_out"] for r in res]
    hTl = [r["hT_lo"] for r in res]
    out = None
    for l in range(DEPTH):
        hTb = [np.ascontiguousarray(np.concatenate(hT[b * 4:(b + 1) * 4], axis=1)) for b in range(B)]
        hTlb = [np.ascontiguousarray(np.concatenate(hTl[b * 4:(b + 1) * 4], axis=1)) for b in range(B)]
        yg = _run(_prog("gla", build_gla), [gla_inputs(inp, l, c % 4, hTb[c // 4], cst, hTlb[c // 4]) for c in range(NCORES)])
        ys = _run(_prog("ssm", build_ssm), [ssm_inputs(inp, l, c % 4, hTb[c // 4], cst) for c in range(NCORES)])
        yd = _run(_prog(("diff", l), lambda: build_diff(l)), [diff_inputs(inp, l, c % 4, c // 4, hTb[c // 4], cst) for c in range(NCORES)])
        yTb = []
        for b in range(B):
            parts = [yg[b * 4 + g]["yT"] for g in range(4)] + [ys[b * 4 + g]["yT"] for g in range(4)] + [yd[b * 4 + g]["yT"] for g in range(4)]
            yTb.append(np.concatenate(parts, axis=0))
        last = (l == DEPTH - 1)
        g_next = inp["norm_final_g"] if last else inp["norm_mix_g"][l + 1]
        maps = [tok_inputs(inp, l, xT[c], hT[c], np.ascontiguousarray(yTb[c // 4][:, (c % 4) * NT:(c % 4 + 1) * NT]), g_next, cst)
                for c in range(NCORES)]
        mode = "last" if last else "mid"
        res = _run(_prog(mode, lambda: build_tok(mode)), maps)
        if last:
            out = np.empty((B, SEQ, DM), np.float32)
            for c in range(NCORES):
                out[c // 4, (c % 4) * NT:(c % 4 + 1) * NT, :] = res[c]["outT"].T
        else:
            xT = [r["xT_out"] for r in res]
            hT = [r["hT_out"] for r in res]
            hTl = [r["hT_lo"] for r in res]
    return out


GROUPS = [[0, 1, 2, 3], [4, 5, 6, 7]]


def build_fused(skip=()):
    nc = bass.Bass("TRN2", target_bir_lowering=False)
    ext = lambda name, shape, dt=F32: dram(nc, name, shape, dt, "ExternalInput")
    xT = ext("xT", [DM, NT])
    pos = ext("pos", [1, SEQ], I32)
    cst = ext("cst", [4, 128, 128])
    invf = ext("invf", [128, 2])
    g_mix = ext("g_mix", [DEPTH, 128, 8])
    g_fin = ext("g_fin", [128, 8])
    g_mlp = ext("g_mlp", [DEPTH, 128, 8])
    w_gate = ext("w_gate", [DEPTH, DM, 3072])
    b_gate = ext("b_gate", [DEPTH, 128, 24])
    w_br = ext("w_br", [DEPTH, 4096, DM])
    w_out = ext("w_out", [DEPTH, DM, DM])
    w_up = ext("w_up", [DEPTH, DM, 4096])
    w_dn = ext("w_dn", [DEPTH, 4096, DM])
    w_gla = ext("w_gla", [DEPTH, DM, 784])
    wgk2 = ext("wgk2", [DEPTH, 16, 128])
    bgk = ext("bgk", [DEPTH, 1, 128])
    ng = ext("ng", [DEPTH, 128, 2])
    w_ssm = ext("w_ssm", [DEPTH, DM, 1288])
    cw = ext("cw", [DEPTH, 128, 6, 4])
    cb = ext("cb", [DEPTH, 128, 6])
    dtb = ext("dtb", [DEPTH, 1, 8])
    alog = ext("alog", [DEPTH, 1, 8])
    dsk = ext("dsk", [DEPTH, 1, 8])
    ngs = ext("ngs", [DEPTH, 1, 512])
    w_diff = ext("w_diff", [DEPTH, DM, 1280])
    lqk = ext("lqk", [DEPTH, 4, 64])
    ngd = ext("ngd", [DEPTH, 1, 128])
    outT = dram(nc, "outT", [DM, NT], F32, "ExternalOutput")
    xs = nc.dram_tensor("xs_i", [DM, NT], F32).ap()
    TPR = NT // TT
    hsrc = nc.dram_tensor("hsrc_i", [2 * TPR, DM, TT], BF16).ap()
    hgat = nc.dram_tensor("hgat_i", [2 * TPR, 4 * DM, TT], BF16).ap()
    ysrc = nc.dram_tensor("ysrc_i", [4, 4, 256, NT], BF16).ap()
    ygat = nc.dram_tensor("ygat_i", [4, 4, 1024, NT], BF16).ap()
    ygat3 = ygat.rearrange("q a r n -> q (a r) n")

    with ExitStack() as st:
        S = Sched(nc, st)
        hsrc_b = [Buf("hsrc_b%d" % i) for i in range(2 * TPR)]
        hgat_b = [Buf("hgat_b%d" % i) for i in range(2 * TPR)]
        ysrc_b = [[Buf("ysrc_b%d_%d" % (q, a)) for a in range(4)] for q in range(4)]
        ygat_b = Buf("ygat_b")
        xs_b = Buf("xs_b")
        S.global_bufs += [ygat_b, xs_b] + hsrc_b + hgat_b + [b for row in ysrc_b for b in row]

        def after_h(S_, t):
            for which in range(2):
                i = 2 * t + which
                S_.collective("AllGather", hsrc_b[i], hsrc[i], hgat_b[i], hgat[i], GROUPS)

        def load_h_from(which):
            def f(S_, h, t):
                r, i = t // TPR, 2 * (t % TPR) + which
                S_.dma("sp", h[:], kcp(hgat[i][r * DM:(r + 1) * DM, :]), reads=[hgat_b[i]], writes=[h])
            return f

        h_io = {"h_in": (lambda t: hsrc[2 * t]), "h_out": (lambda t, which: hsrc[2 * t + which]),
                "h_wr": (lambda t: [[hsrc_b[2 * t]], [hsrc_b[2 * t + 1]]]), "after_h": after_h}

        def store_y_parts(parts):
            def f(S_, yo, t):
                q, tl = t // (NT // TT), (t % (NT // TT)) * TT
                for j, a in enumerate(parts):
                    S_.dma("sp", ysrc[q, a][:, tl:tl + TT].rearrange("(c p) n -> p c n", p=128), yo[:, 2 * j:2 * j + 2, :],
                           reads=[yo], writes=[ysrc_b[q][a]], sembuf=yo, group=True)
                if t % (NT // TT) == NT // TT - 1:
                    for a in parts:
                        S_.collective("AllGather", ysrc_b[q][a], ysrc[q, a], ygat_b, ygat[q, a], GROUPS)
            return f

        qcache = {}

        def load_y(S_, yub, t):
            tl = t * TT
            ph = S_.phase_id

            def qval(E):
                if ph not in qcache:
                    qcache[ph] = E.snap(E.partition_id() % 4)
                return qcache[ph]
            src = (lambda E, tl=tl: ygat3[bass.ds(qval(E), 1), :, tl:tl + TT].rearrange("o (k p) n -> p (o k) n", p=128))
            S_.dma("sp", yub[:], src, reads=[ygat_b], writes=[yub])

        S.begin_phase()
        if "first" not in skip:
            emit_tok(nc, S, dict(h_io, **{"xT": xT, "g_next": g_mix[0], "cst": cst}), "first")
        S.end_phase()
        for l in range(DEPTH):
            S.begin_phase()
            if "gla" not in skip:
              emit_gla(nc, S, {"w_gla": w_gla[l], "wgk2": wgk2[l], "bgk": bgk[l], "ng": ng[l], "cst": cst,
                             "load_h": load_h_from(0), "load_hlo": load_h_from(1), "store_y": store_y_parts((0,))}, SEQ // TT)
            S.end_phase()
            S.begin_phase()
            if "ssm" not in skip:
              emit_ssm(nc, S, {"w_ssm": w_ssm[l], "cw": cw[l], "cb": cb[l], "dtb": dtb[l], "alog": alog[l], "dsk": dsk[l], "ngs": ngs[l],
                             "cst": cst, "load_h": load_h_from(0), "store_y": store_y_parts((1, 2))}, SEQ // TT)
            S.end_phase()
            S.begin_phase()
            if "diff" not in skip:
              emit_diff(nc, S, {"w_diff": w_diff[l], "pos": pos, "invf": invf, "lqk": lqk[l], "ngd": ngd[l], "cst": cst,
                              "load_h": load_h_from(0), "store_y": store_y_parts((3,))}, l, SEQ // TT)
            S.end_phase()
            last = (l == DEPTH - 1)
            A = {"xT": (xT if l == 0 else xs), "g_next": (g_fin if last else g_mix[l + 1]), "cst": cst, "h_in": h_io["h_in"], "load_y": load_y,
                 "w_gate": w_gate[l], "b_gate": b_gate[l], "w_br": w_br[l], "w_out": w_out[l], "w_up": w_up[l], "w_dn": w_dn[l], "g_mlp": g_mlp[l]}
            if last:
                A["outT"] = outT
            else:
                A.update(h_io)
                A["xT_out"] = xs
            S.begin_phase()
            emit_tok(nc, S, A, "last" if last else "mid")
            S.end_phase()
    return nc


def fused_y_row_order():
    rows = []
    for a in range(4):
        for r in range(4):
            j = np.arange(256)
            if a == 0:
                rows.append(r * 256 + j)
            elif a in (1, 2):
                rows.append(1024 + r * 512 + (a - 1) * 256 + j)
            else:
                rows.append(3072 + r * 256 + j)
    return np.concatenate(rows)


def fused_inputs(inp, c, cst):
    b, g = c // 4, c % 4
    x = inp["x"]
    m = {"xT": np.ascontiguousarray(x[b, g * NT:(g + 1) * NT, :].T), "cst": cst,
         "pos": np.ascontiguousarray(inp["positions"][b].reshape(1, SEQ)),
         "g_mix": np.stack([pvec(inp["norm_mix_g"][l]) for l in range(DEPTH)]), "g_fin": pvec(inp["norm_final_g"]),
         "g_mlp": np.stack([pvec(inp["norm_mlp_g"][l]) for l in range(DEPTH)]),
         "w_gate": np.ascontiguousarray(inp["w_in"][:, :, 11312:14384]),
         "b_gate": np.stack([pvec(inp["b_gate"][l]) for l in range(DEPTH)]),
         "w_br": np.ascontiguousarray(np.concatenate([inp["w_br_gla"], inp["w_br_ssm"], inp["w_br_diff"]], axis=1)[:, fused_y_row_order(), :]),
         "w_out": np.ascontiguousarray(inp["w_out"]), "w_up": np.ascontiguousarray(inp["w_mlp_up"]), "w_dn": np.ascontiguousarray(inp["w_mlp_down"])}
    per = {}
    for l in range(DEPTH):
        d = {}
        d.update(gla_inputs(inp, l, g, None, cst, 0))
        d.update(ssm_inputs(inp, l, g, None, cst))
        d.update(diff_inputs(inp, l, g, b, None, cst))
        for k, v in d.items():
            if k in ("hT", "hTlo", "cst", "pos", "invf"):
                continue
            per.setdefault(k, []).append(v)
        if l == 0:
            m["invf"] = d["invf"]
    for k, v in per.items():
        m[k] = np.ascontiguousarray(np.stack(v))
    return m


_FUSED = []


def kernel(**inp):
    inp = {k: np.asarray(v) for k, v in inp.items()}
    cst = make_consts()
    if not _FUSED:
        _FUSED.append(build_fused())
    maps = [fused_inputs(inp, c, cst) for c in range(NCORES)]
    res = run_bass_kernel_spmd(_FUSED[0], maps, core_ids=list(range(NCORES))).results
    out = np.empty((B, SEQ, DM), np.float32)
    for c in range(NCORES):
        out[c // 4, (c % 4) * NT:(c % 4 + 1) * NT, :] = res[c]["outT"].T
    return out
```

```python
import math
from contextlib import ExitStack
import numpy as np
import ml_dtypes
import concourse.bass as bass
import concourse.mybir as mybir
from concourse.bass_utils import run_bass_kernel_spmd

F32 = mybir.dt.float32
BF16 = mybir.dt.bfloat16
I32 = mybir.dt.int32
AF = mybir.ActivationFunctionType
ALU = mybir.AluOpType
AX = mybir.AxisListType

NCORES = 8
B, SEQ, DM, DEPTH = 2, 8192, 1024, 2
NT = 2048
TT = 512
EPS = 1e-6
ENGS = ("pe", "act", "dve", "pool", "sp")


class Buf:
    __slots__ = ("name", "t", "w", "r", "dsem", "dcnt")

    def __init__(self, name, t=None):
        self.name = name
        self.t = t
        self.w = None
        self.r = {}
        self.dsem = None
        self.dcnt = 0

    def __getitem__(self, idx):
        return self.t[idx]


class VBuf:
    def __init__(self, name, parent, ap):
        self.name = name
        self.parent = parent
        self.t = ap

    def __getitem__(self, idx):
        return self.t[idx]

    w = property(lambda s: s.parent.w, lambda s, v: setattr(s.parent, "w", v))
    r = property(lambda s: s.parent.r, lambda s, v: setattr(s.parent, "r", v))
    dsem = property(lambda s: s.parent.dsem, lambda s, v: setattr(s.parent, "dsem", v))
    dcnt = property(lambda s: s.parent.dcnt, lambda s, v: setattr(s.parent, "dcnt", v))


class Sched:
    def __init__(self, nc, stack):
        self.nc = nc
        self.stack = stack
        self.q = {e: [] for e in ENGS}
        self.sem = {}
        for e in ENGS:
            self.sem[e] = stack.enter_context(nc.semaphore("s_" + e))
        self.cnt = {e: 0 for e in ENGS}
        self.waited = {e: {} for e in ENGS}
        self.ndsem = 0
        self.final_waits = {}
        self.alloc_stack = stack
        self.nname = 0
        self.free_dsems = {}
        self.dsem_cls = {}
        self.phase_bufs = []
        self.global_bufs = []

    def sb(self, name, shape, dt):
        self.nname += 1
        t = self.alloc_stack.enter_context(self.nc.sbuf_tensor("sb%d_%s" % (self.nname, name), list(shape), dt))
        b = Buf(name, t)
        self.phase_bufs.append(b)
        return b

    def ps(self, name, shape, dt=F32):
        self.nname += 1
        t = self.alloc_stack.enter_context(self.nc.psum_tensor("ps%d_%s" % (self.nname, name), list(shape), dt))
        return Buf(name, t)

    def view(self, name, ap):
        return Buf(name, ap)

    def _dsem(self, b, eng="sp"):
        cls = {"pool": "sw", "cc": "cc"}.get(eng, "hw")
        if b.dsem is not None:
            assert self.dsem_cls[b.dsem] == cls, (b.name, cls)
        if b.dsem is None:
            pool = self.free_dsems.setdefault(cls, [])
            if pool:
                b.dsem, b.dcnt = pool.pop()
            else:
                b.dsem = "d%d" % self.ndsem
                self.ndsem += 1
                self.sem[b.dsem] = self.stack.enter_context(self.nc.semaphore(b.dsem))
                self.dsem_cls[b.dsem] = cls
        return b.dsem

    def _deps(self, eng, reads, writes, same_ok):
        deps = {}

        def add(k, v):
            if deps.get(k, 0) < v:
                deps[k] = v

        for b in reads:
            if b.w is not None:
                add(*b.w)
        for b in writes:
            if b.w is not None:
                add(*b.w)
            for k, v in b.r.items():
                add(k, v)
        waits = []
        wd = self.waited[eng]
        for k, v in deps.items():
            if k == eng and same_ok:
                continue
            if wd.get(k, 0) >= v:
                continue
            wd[k] = v
            waits.append((k, v))
        return waits

    def _commit(self, tk, reads, writes):
        k, v = tk
        for b in writes:
            b.w = tk
            b.r = {}
        for b in reads:
            if b.r.get(k, 0) < v:
                b.r[k] = v

    def op(self, eng, fn, reads=(), writes=()):
        waits = self._deps(eng, reads, writes, same_ok=(eng == "pe"))
        self.cnt[eng] += 1
        tk = (eng, self.cnt[eng])
        sem = self.sem
        me = sem[eng]

        def emit(E):
            for k, v in waits:
                E.wait_ge(sem[k], v)
            fn(E).then_inc(me, 1)

        self.q[eng].append(emit)
        self._commit(tk, reads, writes)
        return tk

    def dma(self, eng, out_ap, in_ap, reads=(), writes=(), sembuf=None, group=False):
        sb = sembuf if sembuf is not None else (writes[0] if writes else reads[0])
        dk = self._dsem(sb, eng)
        saved = None
        if group and writes and writes[0].w is not None and writes[0].w[0] == dk:
            saved = writes[0].w
            writes[0].w = None
        waits = self._deps(eng, reads, writes, same_ok=False)
        if saved is not None:
            writes[0].w = saved
        sb.dcnt += 16
        tk = (dk, sb.dcnt)
        sem = self.sem
        ds = sem[dk]

        def emit(E):
            for k, v in waits:
                E.wait_ge(sem[k], v)
            E.dma_start(out=out_ap, in_=(in_ap(E) if callable(in_ap) else in_ap)).then_inc(ds, 16)

        self.q[eng].append(emit)
        self._commit(tk, reads, writes)
        self.final_waits[dk] = sb.dcnt
        return tk

    def collective(self, kind, sb_, s_ap, db_, d_ap, groups):
        dk = self._dsem(db_, "cc")
        waits = self._deps("pool", [sb_], [db_], same_ok=False)
        db_.dcnt += 1
        tk = (dk, db_.dcnt)
        sem = self.sem
        ds = sem[dk]

        def emit(E):
            for k, v in waits:
                E.wait_ge(sem[k], v)
            E.collective_compute(kind, ALU.bypass, replica_groups=groups, ins=[s_ap], outs=[d_ap]).then_inc(ds, 1)

        self.q["pool"].append(emit)
        self._commit(tk, [sb_], [db_])
        self.final_waits[dk] = db_.dcnt
        return tk

    def mm(self, ob, o, lb, l, rb, r, start=True, stop=True):
        return self.op("pe", lambda E: E.matmul(o, lhsT=l, rhs=r, start=start, stop=stop), reads=[lb, rb], writes=[ob])

    def act(self, ob, o, ib, i, func, bias=None, scale=None, accum=None, rd=(), wr=()):
        kw = {}
        if bias is not None:
            kw["bias"] = bias
        if scale is not None:
            kw["scale"] = scale
        if accum is not None:
            kw["accum_out"] = accum
        return self.op("act", lambda E: E.activation(out=o, in_=i, func=func, **kw), reads=[ib] + list(rd), writes=[ob] + list(wr))

    def tt(self, eng, ob, o, ab, a, bb, b, op):
        return self.op(eng, lambda E: E.tensor_tensor(out=o, in0=a, in1=b, op=op), reads=[ab, bb], writes=[ob])

    def ts(self, eng, ob, o, ab, a, s1, s2, op0, op1=None, rd=()):
        if op1 is None:
            return self.op(eng, lambda E: E.tensor_scalar(out=o, in0=a, scalar1=s1, scalar2=None, op0=op0), reads=[ab] + list(rd), writes=[ob])
        return self.op(eng, lambda E: E.tensor_scalar(out=o, in0=a, scalar1=s1, scalar2=s2, op0=op0, op1=op1), reads=[ab] + list(rd), writes=[ob])

    def stt(self, ob, o, ab, a, sc, bb, b, op0, op1, rd=()):
        return self.op("dve", lambda E: E.scalar_tensor_tensor(out=o, in0=a, scalar=sc, in1=b, op0=op0, op1=op1), reads=[ab, bb] + list(rd), writes=[ob])

    def copy(self, eng, ob, o, ib, i):
        if eng == "act":
            return self.op("act", lambda E: E.copy(out=o, in_=i), reads=[ib], writes=[ob])
        return self.op(eng, lambda E: E.tensor_copy(out=o, in_=i), reads=[ib], writes=[ob])

    def memset(self, eng, ob, o, val):
        return self.op(eng, lambda E: E.memset(o, val), writes=[ob])

    def recip(self, ob, o, ib, i):
        return self.op("dve", lambda E: E.reciprocal(out=o, in_=i), reads=[ib], writes=[ob])

    phase_id = 0

    def begin_phase(self):
        self.phase_id += 1
        self.gstack = self.stack if not hasattr(self, "gstack") else self.gstack
        self.pstack = ExitStack()
        self.pstack.__enter__()
        self.alloc_stack = self.pstack

    def end_phase(self):
        sem = self.sem
        cnt = dict(self.cnt)
        fw = dict(self.final_waits)
        for e in ENGS:
            waits = [(o, cnt[o]) for o in ENGS if o != e and cnt[o] > self.waited[e].get(o, 0)]
            waits += [(k, v) for k, v in fw.items() if v > self.waited[e].get(k, 0)]
            for k, v in waits:
                self.waited[e][k] = v

            def emit(E, waits=waits):
                for k, v in waits:
                    E.wait_ge(sem[k], v)
            self.q[e].append(emit)
        self.finish()
        self.q = {e: [] for e in ENGS}
        for e in ENGS:
            self.sem[e] = self.stack.enter_context(self.nc.semaphore("s_%s_%d" % (e, self.phase_id)))
            self.cnt[e] = 0
            for x in ENGS:
                self.waited[x].pop(e, None)
        for b in self.global_bufs:
            if b.w is not None and b.w[0] in ENGS:
                b.w = None
            b.r = {k: v for k, v in b.r.items() if k not in ENGS}
        for b in self.phase_bufs:
            if b.dsem is not None:
                self.free_dsems[self.dsem_cls[b.dsem]].append((b.dsem, b.dcnt))
                b.dsem = None
        self.phase_bufs = []
        self.pstack.__exit__(None, None, None)
        self.alloc_stack = self.stack

    def finish(self):
        nc = self.nc
        sem = self.sem
        q = self.q
        fw = dict(self.final_waits)
        with nc.Block() as block:
            @block.tensor
            def _(E):
                for f in q["pe"]:
                    f(E)

            @block.scalar
            def _(E):
                for f in q["act"]:
                    f(E)

            @block.vector
            def _(E):
                for f in q["dve"]:
                    f(E)

            @block.gpsimd
            def _(E):
                for f in q["pool"]:
                    f(E)

            @block.sync
            def _(E):
                for f in q["sp"]:
                    f(E)
                for k, v in fw.items():
                    E.wait_ge(sem[k], v)


class Ring:
    def __init__(self, bufs):
        self.bufs = bufs
        self.i = 0

    def next(self):
        b = self.bufs[self.i % len(self.bufs)]
        self.i += 1
        return b


def dram(nc, name, shape, dt, kind):
    return nc.dram_tensor(name, list(shape), dt, kind=kind).ap()


def kcp(ap):
    return ap.rearrange("(kc p) n -> p kc n", p=128)


def rms_stats(S, C, xb, nch, n, ps_ring, sq_ring, rstd, tmp, nfeat, eps):
    pss = ps_ring.next()
    for kc in range(nch):
        sq = sq_ring.next()
        S.act(sq, sq[:, :n], xb, xb[:, kc, :n], AF.Square)
        S.mm(pss, pss[:, :n], C["ones"], C["ones"][:], sq, sq[:, :n], start=(kc == 0), stop=(kc == nch - 1))
    S.act(tmp, tmp[:, :n], pss, pss[:, :n], AF.Sqrt, bias=C["eps%g" % eps][:, 0:1], scale=1.0 / nfeat, rd=[C["eps%g" % eps]])
    S.recip(rstd, rstd[:, :n], tmp, tmp[:, :n])


def load_consts(S, nc, cst_ap, extra_eps=()):
    C = {}
    C["ones"] = S.sb("c_ones", [128, 128], F32)
    S.dma("sp", C["ones"][:], cst_ap[0], writes=[C["ones"]])
    for e in (EPS,) + tuple(extra_eps):
        b = S.sb("c_eps%g" % e, [128, 1], F32)
        S.memset("dve", b, b[:], float(e))
        C["eps%g" % e] = b
    return C


def build_tok(mode):
    nc = bass.Bass("TRN2", target_bir_lowering=False)
    A = {}
    A["xT"] = dram(nc, "xT", [DM, NT], F32, "ExternalInput")
    A["g_next"] = dram(nc, "g_next", [128, 8], F32, "ExternalInput")
    A["cst"] = dram(nc, "cst", [4, 128, 128], F32, "ExternalInput")
    if mode != "first":
        A["hT"] = dram(nc, "hT", [DM, NT], BF16, "ExternalInput")
        yT = dram(nc, "yT", [4096, NT], BF16, "ExternalInput")
        A["load_y"] = lambda S, yub, t: S.dma("act", yub[:], kcp(yT[:, t * TT:(t + 1) * TT]), writes=[yub])
        A["w_gate"] = dram(nc, "w_gate", [DM, 3072], F32, "ExternalInput")
        A["b_gate"] = dram(nc, "b_gate", [128, 24], F32, "ExternalInput")
        A["w_br"] = dram(nc, "w_br", [4096, DM], F32, "ExternalInput")
        A["w_out"] = dram(nc, "w_out", [DM, DM], F32, "ExternalInput")
        A["w_up"] = dram(nc, "w_up", [DM, 4096], F32, "ExternalInput")
        A["w_dn"] = dram(nc, "w_dn", [4096, DM], F32, "ExternalInput")
        A["g_mlp"] = dram(nc, "g_mlp", [128, 8], F32, "ExternalInput")
    if mode == "last":
        A["outT"] = dram(nc, "outT", [DM, NT], F32, "ExternalOutput")
    else:
        A["hT_out"] = dram(nc, "hT_out", [DM, NT], BF16, "ExternalOutput")
        A["hT_lo"] = dram(nc, "hT_lo", [DM, NT], BF16, "ExternalOutput")
        if mode == "mid":
            A["xT_out"] = dram(nc, "xT_out", [DM, NT], F32, "ExternalOutput")
    with ExitStack() as st:
        S = Sched(nc, st)
        S.begin_phase()
        emit_tok(nc, S, A, mode)
        S.end_phase()
    return nc


def emit_tok(nc, S, A, mode):
    xT, gn, cst = A["xT"], A["g_next"], A["cst"]
    if mode != "first":
        w_gate, b_gate, w_br, w_out, w_up, w_dn, gm = (A[k] for k in ("w_gate", "b_gate", "w_br", "w_out", "w_up", "w_dn", "g_mlp"))
        h_in = A["h_in"] if "h_in" in A else (lambda t: A["hT"][:, t * TT:(t + 1) * TT])
    if mode == "last":
        outT = A["outT"]
    else:
        h_out = A["h_out"] if "h_out" in A else (lambda t, which: (A["hT_out"], A["hT_lo"])[which][:, t * TT:(t + 1) * TT])
        after_h = A.get("after_h", None)
        if mode == "mid":
            xTo = A["xT_out"]
    if True:
        C = load_consts(S, nc, cst)
        gnb = S.sb("gnb", [128, 8], F32)
        S.dma("sp", gnb[:], gn, writes=[gnb])
        xb = S.sb("xb", [128, 8, TT], F32)
        ps_ring = Ring([S.ps("ps%d" % i, [128, TT], F32) for i in range(7)])
        sq_ring = Ring([S.sb("sq%d" % i, [128, TT], F32) for i in range(2)])
        rstd = S.sb("rstd", [128, TT], F32)
        rtmp = S.sb("rtmp", [128, TT], F32)
        hn_ring = Ring([S.sb("hn%d" % i, [128, 8, TT], BF16) for i in range(2)])
        hl_ring = Ring([S.sb("hl%d" % i, [128, 8, TT], BF16) for i in range(2)])
        h32_ring = Ring([S.sb("h32_%d" % i, [128, TT], F32) for i in range(2)])
        if mode == "last":
            oc_ring = Ring([S.sb("oc%d" % i, [128, TT], F32) for i in range(3)])
        if mode != "first":
            bgb = S.sb("bgb", [128, 24], F32)
            S.dma("sp", bgb[:], b_gate, writes=[bgb])
            gmb = S.sb("gmb", [128, 8], F32)
            S.dma("sp", gmb[:], gm, writes=[gmb])
            hb_ring = Ring([S.sb("hb%d" % i, [128, 8, TT], BF16) for i in range(2)])
            yub = S.sb("yub", [128, 32, TT], BF16)
            mixed = S.sb("mixed", [128, 8, TT], BF16)
            h2 = S.sb("h2", [128, 8, TT], BF16)
            g_ring = Ring([S.sb("g%d" % i, [128, TT], F32) for i in range(6)])
            acc_ring = Ring([S.sb("acc%d" % i, [128, TT], F32) for i in range(2)])
            tmp_ring = Ring([S.sb("tmp%d" % i, [128, TT], F32) for i in range(2)])
            r_ring = Ring([S.sb("r%d" % i, [128, TT], F32) for i in range(2)])
            w_ring = Ring([S.sb("w%d" % i, [128, 7168], BF16) for i in range(3)])

        for t in range(NT // TT):
            ts_ = slice(t * TT, (t + 1) * TT)
            S.dma("sp", xb[:], kcp(xT[:, ts_]), writes=[xb])
            if mode != "first":
                hb = hb_ring.next()
                S.dma("sp", hb[:], kcp(h_in(t)), writes=[hb])
                A["load_y"](S, yub, t)
                for oc in range(8):
                    w = w_ring.next()
                    wg = w[:, 0:3072].rearrange("p (kc br j) -> p kc br j", kc=8, br=3)
                    wb = w[:, 3072:7168].rearrange("p (kc j) -> p kc j", kc=32)
                    for br in range(3):
                        c0 = br * 1024 + oc * 128
                        S.dma("pool", wg[:, :, br, :], kcp(w_gate[:, c0:c0 + 128]), writes=[w], group=(br > 0))
                    S.dma("pool", wb, kcp(w_br[:, oc * 128:(oc + 1) * 128]), writes=[w], group=True)
                    gts = []
                    for br in range(3):
                        pg = ps_ring.next()
                        for kc in range(8):
                            S.mm(pg, pg[:], w, wg[:, kc, br, :], hb, hb[:, kc, :], start=(kc == 0), stop=(kc == 7))
                        g = g_ring.next()
                        ch = br * 8 + oc
                        S.act(g, g[:], pg, pg[:], AF.Sigmoid, bias=bgb[:, ch:ch + 1], rd=[bgb])
                        gts.append(g)
                    acc = acc_ring.next()
                    koff = (0, 8, 24)
                    nk = (8, 16, 8)
                    for br in range(3):
                        pb = ps_ring.next()
                        for kc in range(nk[br]):
                            S.mm(pb, pb[:], w, wb[:, koff[br] + kc, :], yub, yub[:, koff[br] + kc, :], start=(kc == 0), stop=(kc == nk[br] - 1))
                        if br == 0:
                            S.tt("dve", acc, acc[:], pb, pb[:], gts[0], gts[0][:], ALU.mult)
                        else:
                            tmp = tmp_ring.next()
                            S.tt("dve", tmp, tmp[:], pb, pb[:], gts[br], gts[br][:], ALU.mult)
                            if br == 1:
                                S.tt("dve", acc, acc[:], acc, acc[:], tmp, tmp[:], ALU.add)
                            else:
                                S.tt("dve", mixed, mixed[:, oc, :], acc, acc[:], tmp, tmp[:], ALU.add)
                if t > 0 and mode == "mid" and after_h is not None:
                    after_h(S, t - 1)
                for half in range(2):
                    w = w_ring.next()
                    wo = w[:, 0:4096].rearrange("p (kc j) -> p kc j", kc=8)
                    S.dma("pool", wo, kcp(w_out[:, half * 512:(half + 1) * 512]), writes=[w])
                    for o4 in range(4):
                        oc = half * 4 + o4
                        po = ps_ring.next()
                        for kc in range(8):
                            S.mm(po, po[:], w, wo[:, kc, o4 * 128:(o4 + 1) * 128], mixed, mixed[:, kc, :], start=(kc == 0), stop=(kc == 7))
                        S.tt("dve", xb, xb[:, oc, :], xb, xb[:, oc, :], po, po[:], ALU.add)
                rms_stats(S, C, xb, 8, TT, ps_ring, sq_ring, rstd, rtmp, DM, EPS)
                for kc in range(8):
                    S.stt(h2, h2[:, kc, :], xb, xb[:, kc, :], gmb[:, kc:kc + 1], rstd, rstd[:], ALU.mult, ALU.mult, rd=[gmb])
                for o8 in range(8):
                    w = w_ring.next()
                    wu = w[:, 0:4096].rearrange("p (kc j) -> p kc j", kc=8)
                    S.dma("pool", wu, kcp(w_up[:, o8 * 512:(o8 + 1) * 512]), writes=[w])
                    for o4 in range(4):
                        oc = o8 * 4 + o4
                        pu = ps_ring.next()
                        for kc in range(8):
                            S.mm(pu, pu[:], w, wu[:, kc, o4 * 128:(o4 + 1) * 128], h2, h2[:, kc, :], start=(kc == 0), stop=(kc == 7))
                        r = r_ring.next()
                        S.act(r, r[:], pu, pu[:], AF.Relu)
                        S.tt("dve", yub, yub[:, oc, :], r, r[:], r, r[:], ALU.mult)
                for oc in range(8):
                    w = w_ring.next()
                    wd = w[:, 0:4096].rearrange("p (kc j) -> p kc j", kc=32)
                    S.dma("pool", wd, kcp(w_dn[:, oc * 128:(oc + 1) * 128]), writes=[w])
                    pd = ps_ring.next()
                    for kc in range(32):
                        S.mm(pd, pd[:], w, wd[:, kc, :], yub, yub[:, kc, :], start=(kc == 0), stop=(kc == 31))
                    S.tt("dve", xb, xb[:, oc, :], xb, xb[:, oc, :], pd, pd[:], ALU.add)
                if mode == "mid":
                    S.dma("sp", kcp(xTo[:, ts_]), xb[:], reads=[xb])
            rms_stats(S, C, xb, 8, TT, ps_ring, sq_ring, rstd, rtmp, DM, EPS)
            if mode == "last":
                for kc in range(8):
                    o = oc_ring.next()
                    S.stt(o, o[:], xb, xb[:, kc, :], gnb[:, kc:kc + 1], rstd, rstd[:], ALU.mult, ALU.mult, rd=[gnb])
                    S.dma("sp", outT[kc * 128:(kc + 1) * 128, ts_], o[:], reads=[o])
            else:
                hn = hn_ring.next()
                hl = hl_ring.next()
                for kc in range(8):
                    h32 = h32_ring.next()
                    S.stt(h32, h32[:], xb, xb[:, kc, :], gnb[:, kc:kc + 1], rstd, rstd[:], ALU.mult, ALU.mult, rd=[gnb])
                    S.copy("act", hn, hn[:, kc, :], h32, h32[:])
                    S.tt("pool", hl, hl[:, kc, :], h32, h32[:], hn, hn[:, kc, :], ALU.subtract)
                wr = A["h_wr"](t) if "h_wr" in A else [[], []]
                S.dma("sp", kcp(h_out(t, 0)), hn[:], reads=[hn], writes=wr[0], sembuf=hn)
                S.dma("sp", kcp(h_out(t, 1)), hl[:], reads=[hl], writes=wr[1], sembuf=hl)
                if after_h is not None and (mode == "first" or t == NT // TT - 1):
                    after_h(S, t)


def make_consts():
    c = np.zeros((4, 128, 128), np.float32)
    c[0] = 1.0
    c[1] = np.triu(np.ones((128, 128), np.float32))
    c[2] = 1.0 - c[1]
    c[3] = np.eye(128, dtype=np.float32)
    return c


def pvec(v):
    v = np.asarray(v)
    return np.ascontiguousarray(v.reshape(-1, 128).T)


def small_views(S, name, nbanks, width):
    out = []
    per = 512 // width
    banks = [S.ps("%s%d" % (name, i), [128, 512], F32) for i in range(nbanks)]
    for j in range(per):
        for i in range(nbanks):
            out.append(VBuf("%s%d_%d" % (name, i, j), banks[i], banks[i].t[:, j * width:(j + 1) * width]))
    return out


def build_gla(ntiles=SEQ // TT):
    nc = bass.Bass("TRN2", target_bir_lowering=False)
    hT = dram(nc, "hT", [DM, SEQ], BF16, "ExternalInput")
    hTl = dram(nc, "hTlo", [DM, SEQ], BF16, "ExternalInput")
    A = {}
    A["w_gla"] = dram(nc, "w_gla", [DM, 784], F32, "ExternalInput")
    A["wgk2"] = dram(nc, "wgk2", [16, 128], F32, "ExternalInput")
    A["bgk"] = dram(nc, "bgk", [1, 128], F32, "ExternalInput")
    A["ng"] = dram(nc, "ng", [128, 2], F32, "ExternalInput")
    A["cst"] = dram(nc, "cst", [4, 128, 128], F32, "ExternalInput")
    yT = dram(nc, "yT", [256, SEQ], BF16, "ExternalOutput")
    A["load_h"] = lambda S, h, t: S.dma("sp", h[:], kcp(hT[:, t * TT:(t + 1) * TT]), writes=[h])
    A["load_hlo"] = lambda S, h, t: S.dma("act", h[:], kcp(hTl[:, t * TT:(t + 1) * TT]), writes=[h])
    A["store_y"] = lambda S, yo, t: S.dma("sp", yT[:, t * TT:(t + 1) * TT].rearrange("(ec p) n -> p ec n", p=128), yo[:], reads=[yo])
    with ExitStack() as st:
        S = Sched(nc, st)
        S.begin_phase()
        emit_gla(nc, S, A, ntiles)
        S.end_phase()
    return nc


def emit_gla(nc, S, A, ntiles=SEQ // TT):
    w_gla, wgk2, bgk, ngp, cst = A["w_gla"], A["wgk2"], A["bgk"], A["ng"], A["cst"]
    if True:
        C = load_consts(S, nc, cst)
        U = S.sb("U", [128, 128], F32)
        UC = S.sb("UC", [128, 128], F32)
        S.dma("sp", U[:], cst[1], writes=[U])
        S.dma("sp", UC[:], cst[2], writes=[UC])
        wa = S.sb("wa", [128, 8, 784], BF16)
        S.dma("pool", wa[:], kcp(w_gla), writes=[wa])
        wqk32 = S.sb("wqk32", [128, 8, 256], F32)
        S.dma("sp", wqk32[:, :, 0:128], kcp(w_gla[:, 0:128]), writes=[wqk32])
        S.dma("sp", wqk32[:, :, 128:256], kcp(w_gla[:, 384:512]), writes=[wqk32], group=True)
        wqk_hi = S.sb("wqk_hi", [128, 8, 256], BF16)
        wqk_lo = S.sb("wqk_lo", [128, 8, 256], BF16)
        S.copy("act", wqk_hi, wqk_hi[:], wqk32, wqk32[:])
        S.tt("dve", wqk_lo, wqk_lo[:], wqk32, wqk32[:], wqk_hi, wqk_hi[:], ALU.subtract)
        w2 = S.sb("w2", [16, 128], F32)
        S.dma("sp", w2[:], wgk2, writes=[w2])
        bgb = S.sb("bgb", [128, 128], F32)
        S.dma("sp", bgb[:], bgk.partition_broadcast(128), writes=[bgb])
        ng = S.sb("ng", [128, 2], F32)
        S.dma("sp", ng[:], ngp, writes=[ng])
        h_ring = Ring([S.sb("h%d" % i, [128, 8, TT], BF16) for i in range(2)])
        hlo_ring = Ring([S.sb("hlo%d" % i, [128, 8, TT], BF16) for i in range(2)])
        big = Ring([S.ps("pb%d" % i, [128, 512], F32) for i in range(2)])
        po = [S.ps("po%d" % i, [128, 512], F32) for i in range(2)]
        pss_ring = Ring([S.ps("pss", [128, 512], F32)])
        sm = Ring(small_views(S, "sm", 3, 256))
        qTs = S.sb("qTs", [128, TT], F32)
        kTs = S.sb("kTs", [128, TT], F32)
        sg = S.sb("sg", [128, 2, TT], F32)
        gkl = S.sb("gkl", [16, TT], F32)

        def ring(name, shape, dt, n=2):
            return Ring([S.sb("%s%d" % (name, i), shape, dt) for i in range(n)])
        ktok_r = ring("ktok", [128, 128], F32)
        vtok_r = ring("vtok", [128, 256], BF16)
        t1_r = ring("t1", [128, 128], F32)
        e_r = ring("e", [128, 128], F32)
        gk_r = ring("gk", [128, 128], F32)
        ebT_r = ring("ebT", [128, 128], F32)
        enbT_r = ring("enbT", [128, 128], F32)
        ed2_r = ring("ed2", [128, 128], F32)
        qp_r = ring("qp", [128, 128], BF16)
        qp32_r = ring("qp32", [128, 128], F32)
        kp32_r = ring("kp32", [128, 128], F32)
        kpp_r = ring("kpp", [128, 128], BF16)
        AT_r = ring("AT", [128, 128], BF16)
        Sst = S.sb("Sst", [128, 256], F32)
        Sbf = S.sb("Sbf", [128, 256], BF16)
        S.memset("dve", Sst, Sst[:], 0.0)
        S.memset("dve", Sbf, Sbf[:], 0.0)
        sq_ring = ring("sq", [128, TT], F32)
        rstd = S.sb("rstd", [128, TT], F32)
        rtmp = S.sb("rtmp", [128, TT], F32)
        tmp_r = ring("tmp", [128, TT], F32)
        yo_r = ring("yo", [128, 2, TT], BF16)

        for t in range(ntiles):
            ts_ = slice(t * TT, (t + 1) * TT)
            h = h_ring.next()
            A["load_h"](S, h, t)

            def proj(c0, m):
                p = big.next()
                for kc in range(8):
                    S.mm(p, p[:m, :], wa, wa[:, kc, c0:c0 + m], h, h[:, kc, :], start=(kc == 0), stop=(kc == 7))
                return p
            hlo = hlo_ring.next()
            A["load_hlo"](S, hlo, t)

            def proj3(c0):
                p = big.next()
                n = 0
                for (wb_, hb_) in ((wqk_hi, h), (wqk_hi, hlo), (wqk_lo, h)):
                    for kc in range(8):
                        S.mm(p, p[:], wb_, wb_[:, kc, c0:c0 + 128], hb_, hb_[:, kc, :], start=(n == 0), stop=(n == 23))
                        n += 1
                return p
            p = proj3(0)
            S.act(qTs, qTs[:], p, p[:], AF.Copy, scale=128.0 ** -0.5)
            p = proj3(128)
            S.act(kTs, kTs[:], p, p[:], AF.Copy)
            for ec in range(2):
                p = proj(128 + ec * 128, 128)
                S.act(sg, sg[:, ec, :], p, p[:], AF.Silu)
            p = proj(768, 16)
            S.copy("dve", gkl, gkl[:], p, p[:16, :])
            def stageA(c):
                cs = slice(c * 128, (c + 1) * 128)
                p = big.next()
                for kc in range(8):
                    S.mm(p, p[:, :384], h, h[:, kc, cs], wa, wa[:, kc, 384:768], start=(kc == 0), stop=(kc == 7))
                ktok = ktok_r.next()
                vtok = vtok_r.next()
                S.copy("act", ktok, ktok[:], p, p[:, 0:128])
                S.copy("act", vtok, vtok[:], p, p[:, 128:384])
                pg = sm.next()
                S.mm(pg, pg[:, :128], gkl, gkl[:, cs], w2, w2[:], start=True, stop=True)
                t1 = t1_r.next()
                S.tt("dve", t1, t1[:], pg, pg[:, :128], bgb, bgb[:], ALU.add)
                e = e_r.next()
                S.act(e, e[:], t1, t1[:], AF.Exp, scale=-1.0)
                S.act(e, e[:], e, e[:], AF.Ln, bias=1.0)
                gk = gk_r.next()
                S.ts("dve", gk, gk[:], e, e[:], -1.0 / 16.0, None, ALU.mult)
                pbT = sm.next()
                S.mm(pbT, pbT[:, :128], gk, gk[:], U, U[:])
                pd2 = sm.next()
                S.mm(pd2, pd2[:, :128], UC, UC[:], gk, gk[:])
                ebT = ebT_r.next()
                enbT = enbT_r.next()
                ed2 = ed2_r.next()
                S.act(ebT, ebT[:], pbT, pbT[:, :128], AF.Exp)
                S.act(enbT, enbT[:], pbT, pbT[:, :128], AF.Exp, scale=-1.0)
                S.act(ed2, ed2[:], pd2, pd2[:, :128], AF.Exp)
                qp = qp_r.next()
                qp32 = qp32_r.next()
                kp32 = kp32_r.next()
                kpp = kpp_r.next()
                S.tt("dve", qp32, qp32[:], qTs, qTs[:, cs], ebT, ebT[:], ALU.mult)
                S.tt("dve", kp32, kp32[:], kTs, kTs[:, cs], enbT, enbT[:], ALU.mult)
                S.copy("act", qp, qp[:], qp32, qp32[:])
                S.tt("pool", kpp, kpp[:], ktok, ktok[:], ed2, ed2[:], ALU.mult)
                pA = sm.next()
                S.mm(pA, pA[:, :128], kp32, kp32[:], qp32, qp32[:])
                AT = AT_r.next()
                S.tt("dve", AT, AT[:], pA, pA[:, :128], U, U[:], ALU.mult)
                return dict(cs=cs, vtok=vtok, AT=AT, qp=qp, kpp=kpp, ebT=ebT)

            def stageB(c, v):
                cs, vtok, AT, qp, kpp, ebT = (v[k] for k in ('cs', 'vtok', 'AT', 'qp', 'kpp', 'ebT'))
                for ec in range(2):
                    es = slice(ec * 128, (ec + 1) * 128)
                    S.mm(po[ec], po[ec][:, cs], vtok, vtok[:, es], AT, AT[:], start=True, stop=False)
                    S.mm(po[ec], po[ec][:, cs], Sbf, Sbf[:, es], qp, qp[:], start=False, stop=True)
                pkv = sm.next()
                S.mm(pkv, pkv[:, :256], kpp, kpp[:], vtok, vtok[:])
                S.stt(Sst, Sst[:], Sst, Sst[:], ebT[:, 127:128], pkv, pkv[:, :256], ALU.mult, ALU.add, rd=[ebT])
                S.copy("act", Sbf, Sbf[:], Sst, Sst[:])

            va = {0: stageA(0)}
            for c in range(4):
                if c + 1 < 4:
                    va[c + 1] = stageA(c + 1)
                stageB(c, va.pop(c))
            pss = pss_ring.next()
            for ec in range(2):
                sq = sq_ring.next()
                S.act(sq, sq[:], po[ec], po[ec][:], AF.Square)
                S.mm(pss, pss[:], C["ones"], C["ones"][:], sq, sq[:], start=(ec == 0), stop=(ec == 1))
            S.act(rtmp, rtmp[:], pss, pss[:], AF.Sqrt, bias=C["eps%g" % EPS][:, 0:1], scale=1.0 / 256, rd=[C["eps%g" % EPS]])
            S.recip(rstd, rstd[:], rtmp, rtmp[:])
            yo = yo_r.next()
            for ec in range(2):
                tmp = tmp_r.next()
                S.stt(tmp, tmp[:], po[ec], po[ec][:], ng[:, ec:ec + 1], rstd, rstd[:], ALU.mult, ALU.mult, rd=[ng])
                S.tt("pool", yo, yo[:, ec, :], tmp, tmp[:], sg, sg[:, ec, :], ALU.mult)
            A["store_y"](S, yo, t)


def gla_inputs(inp, l, g, hT_b, cst, hTlo_b=None):
    w = inp["w_in"][l]
    cols = np.concatenate([np.arange(g * 128, (g + 1) * 128),
                           2064 + np.arange(g * 256, (g + 1) * 256),
                           512 + np.arange(g * 128, (g + 1) * 128),
                           1024 + np.arange(g * 256, (g + 1) * 256),
                           2048 + np.arange(16)])
    if hTlo_b is None:
        hTlo_b = np.zeros_like(hT_b)
    elif isinstance(hTlo_b, int):
        hTlo_b = None
    return {"hT": hT_b, "hTlo": hTlo_b, "w_gla": np.ascontiguousarray(w[:, cols]),
            "wgk2": np.ascontiguousarray(inp["gla_w_gk2"][l][:, g * 128:(g + 1) * 128]),
            "bgk": np.ascontiguousarray(inp["gla_b_gk"][l][g * 128:(g + 1) * 128].reshape(1, 128)),
            "ng": pvec(inp["gla_norm_g"][l]), "cst": cst}


def build_ssm(ntiles=SEQ // TT):
    nc = bass.Bass("TRN2", target_bir_lowering=False)
    hT = dram(nc, "hT", [DM, SEQ], BF16, "ExternalInput")
    A = {}
    A["w_ssm"] = dram(nc, "w_ssm", [DM, 1288], F32, "ExternalInput")
    A["cw"] = dram(nc, "cw", [128, 6, 4], F32, "ExternalInput")
    A["cb"] = dram(nc, "cb", [128, 6], F32, "ExternalInput")
    A["dtb"] = dram(nc, "dtb", [1, 8], F32, "ExternalInput")
    A["alog"] = dram(nc, "alog", [1, 8], F32, "ExternalInput")
    A["dsk"] = dram(nc, "dsk", [1, 8], F32, "ExternalInput")
    A["ngs"] = dram(nc, "ngs", [1, 512], F32, "ExternalInput")
    A["cst"] = dram(nc, "cst", [4, 128, 128], F32, "ExternalInput")
    yT = dram(nc, "yT", [512, SEQ], BF16, "ExternalOutput")
    A["load_h"] = lambda S, h, t: S.dma("sp", h[:], kcp(hT[:, t * TT:(t + 1) * TT]), writes=[h])
    A["store_y"] = lambda S, yo, t: S.dma("sp", yT[:, t * TT:(t + 1) * TT].rearrange("(c p) n -> p c n", p=128), yo[:], reads=[yo])
    with ExitStack() as st:
        S = Sched(nc, st)
        S.begin_phase()
        emit_ssm(nc, S, A, ntiles)
        S.end_phase()
    return nc


def emit_ssm(nc, S, A, ntiles=SEQ // TT):
    w_ssm, cwp, cbp, dtbp, alogp, dskp, ngp, cst = (A[k] for k in ("w_ssm", "cw", "cb", "dtb", "alog", "dsk", "ngs", "cst"))
    if True:
        C = load_consts(S, nc, cst)
        U = S.sb("U", [128, 128], F32)
        UC = S.sb("UC", [128, 128], F32)
        idf = S.sb("idf", [128, 128], F32)
        idb = S.sb("idb", [128, 128], BF16)
        S.dma("sp", U[:], cst[1], writes=[U])
        S.dma("sp", UC[:], cst[2], writes=[UC])
        S.dma("sp", idf[:], cst[3], writes=[idf])
        S.dma("pool", idb[:], cst[3], writes=[idb])
        ws = S.sb("ws", [128, 8, 1288], BF16)
        S.dma("pool", ws[:], kcp(w_ssm), writes=[ws])
        cw = S.sb("cw", [128, 6, 4], F32)
        cb = S.sb("cb", [128, 6], F32)
        S.dma("sp", cw[:], cwp, writes=[cw])
        S.dma("sp", cb[:], cbp, writes=[cb])
        dtb = S.sb("dtb", [128, 8], F32)
        a_b = S.sb("a_b", [128, 8], F32)
        dsk = S.sb("dsk", [128, 8], F32)
        ngs = S.sb("ngs", [128, 512], F32)
        S.dma("sp", dtb[:], dtbp.partition_broadcast(128), writes=[dtb])
        S.dma("sp", a_b[:], alogp.partition_broadcast(128), writes=[a_b])
        S.dma("sp", dsk[:], dskp.partition_broadcast(128), writes=[dsk])
        S.dma("sp", ngs[:], ngp.partition_broadcast(128), writes=[ngs])
        S.act(a_b, a_b[:], a_b, a_b[:], AF.Exp)
        S.ts("dve", a_b, a_b[:], a_b, a_b[:], -1.0, None, ALU.mult)

        def ring(name, shape, dt, n=2):
            return Ring([S.sb("%s%d" % (name, i), shape, dt) for i in range(n)])
        h_ring = ring("h", [128, 8, TT], BF16)
        big = Ring([S.ps("pb%d" % i, [128, 512], F32) for i in range(6)])
        sm = Ring(small_views(S, "sm", 2, 128))
        raw = S.sb("raw", [128, 6, TT + 3], F32)
        S.memset("dve", raw, raw[:, :, 0:3], 0.0)
        cacc_r = ring("cacc", [128, TT], F32)
        xc = S.sb("xc", [128, 4, TT], F32)
        BT = S.sb("BT", [128, TT], BF16)
        CT = S.sb("CT", [128, TT], BF16)
        sz_r = ring("sz", [128, 512], F32)
        t8_r = ring("t8", [128, 8], F32)
        dt_r = ring("dt", [128, 8], F32)
        da_r = ring("da", [128, 8], F32)
        xdt_r = ring("xdt", [128, 512], BF16)
        xD_r = ring("xD", [128, 512], F32)
        Btok_r = ring("Btok", [128, 128], BF16)
        rda_r = ring("rda", [128, 8, 128], F32)
        eL_r = ring("eL", [128, 8, 128], F32)
        GU_r = ring("GU", [128, 128], F32)
        M_r = ring("M", [128, 8, 128], BF16)
        cs_r = ring("cs", [128, 16], F32)
        ec_r = ring("ec", [128, 24], F32)
        xdtd_r = ring("xdtd", [128, 512], BF16)
        Sst = S.sb("Sst", [128, 512], F32)
        Sbf = S.sb("Sbf", [128, 512], BF16)
        S.memset("dve", Sst, Sst[:], 0.0)
        S.memset("dve", Sbf, Sbf[:], 0.0)
        y1_r = ring("y1", [128, 512], F32)
        y2_r = ring("y2", [128, 512], F32)
        junk = S.sb("junk", [128, 512], F32)
        ss_r = ring("ss", [128, 2], F32)
        yn_r = ring("yn", [128, 512], BF16)
        yo_r = ring("yo", [128, 4, TT], BF16)
        eps = C["eps%g" % EPS]

        def b8(ap):
            return ap.unsqueeze(2).to_broadcast([128, 8, 64])

        def v8(ap):
            return ap.rearrange("p (h q) -> p h q", h=8)

        for t in range(ntiles):
            ts_ = slice(t * TT, (t + 1) * TT)
            h = h_ring.next()
            A["load_h"](S, h, t)
            for ch in range(6):
                p = big.next()
                for kc in range(8):
                    S.mm(p, p[:], ws, ws[:, kc, 512 + ch * 128:640 + ch * 128], h, h[:, kc, :], start=(kc == 0), stop=(kc == 7))
                S.act(raw, raw[:, ch, 3:TT + 3], p, p[:], AF.Copy)
            for ch in range(6):
                acc = cacc_r.next()
                S.ts("dve", acc, acc[:], raw, raw[:, ch, 0:TT], cw[:, ch, 0:1], cb[:, ch:ch + 1], ALU.mult, ALU.add, rd=[cw, cb])
                for i in range(1, 4):
                    S.stt(acc, acc[:], raw, raw[:, ch, i:i + TT], cw[:, ch, i:i + 1], acc, acc[:], ALU.mult, ALU.add, rd=[cw])
                if ch < 4:
                    S.act(xc, xc[:, ch, :], acc, acc[:], AF.Silu)
                elif ch == 4:
                    S.act(BT, BT[:], acc, acc[:], AF.Silu)
                else:
                    S.act(CT, CT[:], acc, acc[:], AF.Silu)
            S.copy("pool", raw, raw[:, :, 0:3], raw, raw[:, :, TT:TT + 3])
            yo = yo_r.next()
            def stageA(c):
                cs = slice(c * 128, (c + 1) * 128)
                pz = big.next()
                for kc in range(8):
                    S.mm(pz, pz[:], h, h[:, kc, cs], ws, ws[:, kc, 0:512], start=(kc == 0), stop=(kc == 7))
                sz = sz_r.next()
                S.act(sz, sz[:], pz, pz[:], AF.Silu)
                pdt = sm.next()
                for kc in range(8):
                    S.mm(pdt, pdt[:, 0:8], h, h[:, kc, cs], ws, ws[:, kc, 1280:1288], start=(kc == 0), stop=(kc == 7))
                t8 = t8_r.next()
                S.tt("dve", t8, t8[:], pdt, pdt[:, 0:8], dtb, dtb[:], ALU.add)
                S.act(t8, t8[:], t8, t8[:], AF.Exp)
                dt = dt_r.next()
                S.act(dt, dt[:], t8, t8[:], AF.Ln, bias=1.0)
                da = da_r.next()
                S.tt("dve", da, da[:], dt, dt[:], a_b, a_b[:], ALU.mult)
                px = big.next()
                for ch in range(4):
                    S.mm(px, px[:, ch * 128:(ch + 1) * 128], xc, xc[:, ch, cs], idf, idf[:])
                xdt = xdt_r.next()
                xD = xD_r.next()
                S.tt("dve", xdt, v8(xdt[:]), px, v8(px[:]), dt, b8(dt[:]), ALU.mult)
                S.tt("dve", xD, v8(xD[:]), px, v8(px[:]), dsk, b8(dsk[:]), ALU.mult)
                pB = sm.next()
                S.mm(pB, pB[:], BT, BT[:, cs], idb, idb[:])
                Btok = Btok_r.next()
                S.copy("act", Btok, Btok[:], pB, pB[:])
                rda = rda_r.next()
                S.tt("pool", rda, rda[:], U, U[:].unsqueeze(1).to_broadcast([128, 8, 128]), da, da[:].unsqueeze(2).to_broadcast([128, 8, 128]), ALU.mult)
                pD = [big.next(), big.next()]
                for hf in range(2):
                    S.mm(pD[hf], pD[hf][:], UC, UC[:], rda, rda[:, hf * 4:(hf + 1) * 4, :].rearrange("p a b -> p (a b)"))
                pc = sm.next()
                S.mm(pc, pc[:, 0:8], U, U[:], da, da[:])
                pc2 = sm.next()
                S.mm(pc2, pc2[:, 0:8], C["ones"], C["ones"][:], da, da[:])
                csb = cs_r.next()
                S.copy("dve", csb, csb[:, 0:8], pc, pc[:, 0:8])
                S.copy("dve", csb, csb[:, 8:16], pc2, pc2[:, 0:8])
                ec = ec_r.next()
                S.act(ec, ec[:, 0:16], csb, csb[:, 0:16], AF.Exp)
                S.tt("dve", csb, csb[:, 0:8], csb, csb[:, 8:16], csb, csb[:, 0:8], ALU.subtract)
                S.act(ec, ec[:, 16:24], csb, csb[:, 0:8], AF.Exp)
                eL = eL_r.next()
                for hf in range(2):
                    S.act(eL, eL[:, hf * 4:(hf + 1) * 4, :].rearrange("p a b -> p (a b)"), pD[hf], pD[hf][:], AF.Exp)
                pG = sm.next()
                S.mm(pG, pG[:], BT, BT[:, cs], CT, CT[:, cs])
                GU = GU_r.next()
                S.tt("dve", GU, GU[:], pG, pG[:], U, U[:], ALU.mult)
                M = M_r.next()
                S.tt("pool", M, M[:], eL, eL[:], GU, GU[:].unsqueeze(1).to_broadcast([128, 8, 128]), ALU.mult)
                xdtd = xdtd_r.next()
                S.tt("dve", xdtd, v8(xdtd[:]), xdt, v8(xdt[:]), ec, b8(ec[:, 16:24]), ALU.mult)
                return dict(cs=cs, sz=sz, xdt=xdt, xD=xD, Btok=Btok, M=M, ec=ec, xdtd=xdtd)

            def stageB(c, v):
                cs, sz, xdt, xD, Btok, M, ec, xdtd = (v[k] for k in ('cs', 'sz', 'xdt', 'xD', 'Btok', 'M', 'ec', 'xdtd'))
                pyd = big.next()
                for hh in range(8):
                    S.mm(pyd, pyd[:, hh * 64:(hh + 1) * 64], M, M[:, hh, :], xdt, xdt[:, hh * 64:(hh + 1) * 64])
                pyo = big.next()
                S.mm(pyo, pyo[:], CT, CT[:, cs], Sbf, Sbf[:])
                pst = big.next()
                S.mm(pst, pst[:], Btok, Btok[:], xdtd, xdtd[:])
                S.tt("dve", Sst, v8(Sst[:]), Sst, v8(Sst[:]), ec, b8(ec[:, 8:16]), ALU.mult)
                S.tt("dve", Sst, Sst[:], Sst, Sst[:], pst, pst[:], ALU.add)
                S.copy("act", Sbf, Sbf[:], Sst, Sst[:])
                y1 = y1_r.next()
                S.tt("dve", y1, v8(y1[:]), pyo, v8(pyo[:]), ec, b8(ec[:, 0:8]), ALU.mult)
                S.tt("dve", y1, y1[:], y1, y1[:], pyd, pyd[:], ALU.add)
                y2 = y2_r.next()
                S.tt("dve", y2, y2[:], y1, y1[:], xD, xD[:], ALU.add)
                S.tt("dve", y2, y2[:], y2, y2[:], sz, sz[:], ALU.mult)
                ss = ss_r.next()
                S.act(junk, junk[:], y2, y2[:], AF.Square, accum=ss[:, 0:1], wr=[ss])
                S.act(ss, ss[:, 1:2], ss, ss[:, 0:1], AF.Sqrt, bias=eps[:, 0:1], scale=1.0 / 512, rd=[eps])
                S.recip(ss, ss[:, 0:1], ss, ss[:, 1:2])
                yn = yn_r.next()
                S.stt(yn, yn[:], y2, y2[:], ss[:, 0:1], ngs, ngs[:], ALU.mult, ALU.mult, rd=[ss])
                pT = big.next()
                for ch in range(4):
                    S.mm(pT, pT[:, ch * 128:(ch + 1) * 128], yn, yn[:, ch * 128:(ch + 1) * 128], idb, idb[:])
                S.copy("act", yo, yo[:, :, cs], pT, pT[:].rearrange("p (c n) -> p c n", c=4))

            va = {0: stageA(0)}
            for c in range(4):
                if c + 1 < 4:
                    va[c + 1] = stageA(c + 1)
                stageB(c, va.pop(c))
            A["store_y"](S, yo, t)


def ssm_inputs(inp, l, g, hT_b, cst):
    w = inp["w_in"][l]
    xcols = 5136 + np.arange(g * 512, (g + 1) * 512)
    bcols = 7184 + np.arange(g * 128, (g + 1) * 128)
    ccols = 7696 + np.arange(g * 128, (g + 1) * 128)
    cols = np.concatenate([3088 + np.arange(g * 512, (g + 1) * 512), xcols, bcols, ccols, 8208 + np.arange(g * 8, (g + 1) * 8)])
    cc = np.concatenate([xcols, bcols, ccols]) - 5136
    cwv = inp["ssm_conv_w"][l][:, cc]
    cw = np.ascontiguousarray(cwv.reshape(4, 6, 128).transpose(2, 1, 0))
    cb = np.ascontiguousarray(inp["ssm_conv_b"][l][cc].reshape(6, 128).T)
    hs = slice(g * 8, (g + 1) * 8)
    return {"hT": hT_b, "w_ssm": np.ascontiguousarray(w[:, cols]), "cw": cw, "cb": cb,
            "dtb": np.ascontiguousarray(inp["ssm_dt_bias"][l][hs].reshape(1, 8)),
            "alog": np.ascontiguousarray(inp["ssm_a_log"][l][hs].reshape(1, 8)),
            "dsk": np.ascontiguousarray(inp["ssm_d"][l][hs].reshape(1, 8)),
            "ngs": np.ascontiguousarray(inp["ssm_norm_g"][l][g * 512:(g + 1) * 512].reshape(1, 512)),
            "cst": cst}


C1_2PI = 6.28125
C2_2PI = 2.0 * math.pi - 6.28125


def build_diff(l, ntiles=SEQ // TT):
    nc = bass.Bass("TRN2", target_bir_lowering=False)
    hT = dram(nc, "hT", [DM, SEQ], BF16, "ExternalInput")
    A = {}
    A["w_diff"] = dram(nc, "w_diff", [DM, 1280], F32, "ExternalInput")
    A["pos"] = dram(nc, "pos", [1, SEQ], I32, "ExternalInput")
    A["invf"] = dram(nc, "invf", [128, 2], F32, "ExternalInput")
    A["lqk"] = dram(nc, "lqk", [4, 64], F32, "ExternalInput")
    A["ngd"] = dram(nc, "ngd", [1, 128], F32, "ExternalInput")
    A["cst"] = dram(nc, "cst", [4, 128, 128], F32, "ExternalInput")
    yT = dram(nc, "yT", [256, SEQ], BF16, "ExternalOutput")
    A["load_h"] = lambda S, h, t: S.dma("sp", h[:], kcp(hT[:, t * TT:(t + 1) * TT]), writes=[h])
    A["store_y"] = lambda S, yo, t: S.dma("sp", yT[:, t * TT:(t + 1) * TT].rearrange("(c p) n -> p c n", p=128), yo[:], reads=[yo])
    with ExitStack() as st:
        S = Sched(nc, st)
        S.begin_phase()
        emit_diff(nc, S, A, l, ntiles)
        S.end_phase()
    return nc


def emit_diff(nc, S, A, l, ntiles=SEQ // TT):
    lambda_init = 0.8 - 0.6 * math.exp(-0.3 * l)
    w_diff, posd, invfp, lqk, ngp, cst = (A[k] for k in ("w_diff", "pos", "invf", "lqk", "ngd", "cst"))
    if True:
        C = load_consts(S, nc, cst, extra_eps=(1e-5,))
        idb = S.sb("idb", [128, 128], BF16)
        S.dma("pool", idb[:], cst[3], writes=[idb])
        wd = S.sb("wd", [128, 8, 1280], BF16)
        S.dma("pool", wd[:], kcp(w_diff), writes=[wd])
        invf = S.sb("invf", [128, 2], F32)
        S.dma("sp", invf[:], invfp, writes=[invf])
        ngd = S.sb("ngd", [128, 128], F32)
        S.dma("sp", ngd[:], ngp.partition_broadcast(128), writes=[ngd])
        S.ts("dve", ngd, ngd[:], ngd, ngd[:], 1.0 - lambda_init, None, ALU.mult)
        lq = S.sb("lq", [128, 4, 64], F32)
        for i in range(4):
            S.dma("sp", lq[:, i, :], lqk[i:i + 1, :].partition_broadcast(128), writes=[lq], group=(i > 0))
        lt = S.sb("lt", [128, 2, 64], F32)
        S.tt("dve", lt, lt[:, 0, :], lq, lq[:, 0, :], lq, lq[:, 1, :], ALU.mult)
        S.tt("dve", lt, lt[:, 1, :], lq, lq[:, 2, :], lq, lq[:, 3, :], ALU.mult)
        ls = S.sb("ls", [128, 4], F32)
        S.op("dve", lambda E: E.reduce_sum(out=ls[:, 0:2], in_=lt[:], axis=AX.X), reads=[lt], writes=[ls])
        S.act(ls, ls[:, 0:2], ls, ls[:, 0:2], AF.Exp)
        S.tt("dve", ls, ls[:, 2:3], ls, ls[:, 1:2], ls, ls[:, 0:1], ALU.subtract)
        S.ts("dve", ls, ls[:, 3:4], ls, ls[:, 2:3], -lambda_init, None, ALU.add)
        nlam = ls

        def ring(name, shape, dt, n=2):
            return Ring([S.sb("%s%d" % (name, i), shape, dt) for i in range(n)])
        h_ring = ring("h", [128, 8, TT], BF16)
        KT = S.sb("KT", [128, 2, SEQ], BF16)
        VA = S.sb("VA", [128, 2, SEQ // 128, 129], BF16)
        S.memset("pool", VA, VA[:, :, :, 128:129], 1.0)
        QT_r = ring("QT", [128, 2, TT], BF16)
        big = Ring([S.ps("pb%d" % i, [128, 512], F32) for i in range(3)])
        pob = [[S.ps("po%d_%d" % (s, hf), [128, 512], F32) for hf in range(2)] for s in range(2)]
        sm = Ring(small_views(S, "sm", 1, 128))
        posi = S.sb("posi", [128, TT], I32)
        ang = S.sb("ang", [128, TT], F32)
        ang2 = S.sb("ang2", [128, TT], F32)
        ki = S.sb("ki", [128, TT], I32)
        kf = S.sb("kf", [128, TT], F32)
        yr = S.sb("yr", [128, TT], F32)
        Cs = S.sb("Cs", [128, TT], F32)
        Sn = S.sb("Sn", [128, TT], F32)
        ta_r = ring("ta", [128, TT], F32)
        tb_r = ring("tb", [128, TT], F32)
        PT_r = ring("PT", [128, TT], BF16, 5)
        r_r = ring("r", [128, 4], F32, 4)
        oa_r = ring("oa", [128, 128], F32, 4)
        junk_r = ring("junk", [128, 128], F32, 4)
        yn_r = ring("yn", [128, 128], BF16, 4)
        yo_r = ring("yo", [128, 2, TT], BF16)
        eps5 = C["eps%g" % 1e-5]

        def reduce_sin(dst, src):
            S.ts("dve", ki, ki[:], src, src[:], 1.0 / (2.0 * math.pi), None, ALU.mult)
            S.copy("dve", kf, kf[:], ki, ki[:])
            S.stt(yr, yr[:], kf, kf[:], -C1_2PI, src, src[:], ALU.mult, ALU.add)
            S.stt(yr, yr[:], kf, kf[:], -C2_2PI, yr, yr[:], ALU.mult, ALU.add)
            S.ts("dve", yr, yr[:], yr, yr[:], -3.1415925, 3.1415925, ALU.max, ALU.min)
            S.act(dst, dst[:], yr, yr[:], AF.Sin)

        for t in range(ntiles):
            ts_ = slice(t * TT, (t + 1) * TT)
            h = h_ring.next()
            A["load_h"](S, h, t)
            S.dma("sp", posi[:], posd[0:1, ts_].partition_broadcast(128), writes=[posi])
            S.copy("dve", ang, ang[:], posi, posi[:])
            S.ts("dve", ang, ang[:], ang, ang[:], invf[:, 0:1], None, ALU.mult, rd=[invf])
            reduce_sin(Sn, ang)
            S.ts("dve", Sn, Sn[:], Sn, Sn[:], invf[:, 1:2], None, ALU.mult, rd=[invf])
            S.ts("dve", ang2, ang2[:], ang, ang[:], math.pi / 2.0, None, ALU.add)
            reduce_sin(Cs, ang2)
            QT = QT_r.next()
            for hd in range(2):
                for (c0, dstb, dst) in ((hd * 128, QT, QT[:, hd, :]), (512 + hd * 128, KT, KT[:, hd, ts_])):
                    p1 = big.next()
                    for kc in range(8):
                        S.mm(p1, p1[:], wd, wd[:, kc, c0:c0 + 128], h, h[:, kc, :], start=(kc == 0), stop=(kc == 7))
                    p2 = big.next()
                    for kc in range(8):
                        S.mm(p2, p2[:], wd, wd[:, kc, c0 + 256:c0 + 384], h, h[:, kc, :], start=(kc == 0), stop=(kc == 7))
                    ta = ta_r.next()
                    tb = tb_r.next()
                    S.tt("dve", ta, ta[:], p1, p1[:], Cs, Cs[:], ALU.mult)
                    S.tt("dve", tb, tb[:], p2, p2[:], Sn, Sn[:], ALU.mult)
                    S.tt("pool", dstb, dst, ta, ta[:], tb, tb[:], ALU.add)
            for c in range(4):
                cs = slice(c * 128, (c + 1) * 128)
                pv = big.next()
                for kc in range(8):
                    S.mm(pv, pv[:, 0:256], h, h[:, kc, cs], wd, wd[:, kc, 1024:1280], start=(kc == 0), stop=(kc == 7))
                S.copy("act", VA, VA[:, :, 4 * t + c, 0:128], pv, pv[:, 0:256].rearrange("p (a b) -> p a b", a=2))
            yo = yo_r.next()
            for hd in range(2):
                nkb = 4 * t + 4
                started = [[False, False], [False, False]]
                iters = [(kb, s_) for kb in range(nkb) for s_ in range(2)]
                pend = {}

                def emit_qk(i):
                    kb, s_ = iters[i]
                    r = kb - 4 * t
                    q0 = max(r, 0)
                    qlo = q0 * 128
                    n = TT - qlo
                    ps_ = slice(s_ * 64, (s_ + 1) * 64)
                    pS = big.next()
                    S.mm(pS, pS[:, :n], KT, KT[ps_, hd, kb * 128:(kb + 1) * 128], QT, QT[ps_, hd, qlo:TT])
                    PT = PT_r.next()
                    S.act(PT, PT[:, :n], pS, pS[:, :n], AF.Exp, scale=0.125)
                    if r >= 0:
                        S.memset("pool", PT, PT[64:128, 0:64], 0.0)
                    pend[i] = (PT, q0, qlo)

                def emit_pv(i):
                    kb, s_ = iters[i]
                    PT, q0, qlo = pend.pop(i)
                    for qb in range(q0, 4):
                        col = qb * 128 - qlo
                        bank = pob[s_][qb // 2]
                        o = bank[:, (qb % 2) * 129:(qb % 2) * 129 + 129]
                        first = not started[s_][qb // 2]
                        started[s_][qb // 2] = True
                        S.mm(bank, o, PT, PT[:, col:col + 128], VA, VA[:, hd, kb, :], start=first, stop=(kb == 4 * t + qb and qb % 2 == 1))

                LOOK = 3
                for i in range(len(iters) + LOOK):
                    if i < len(iters):
                        emit_qk(i)
                    if i >= LOOK:
                        emit_pv(i - LOOK)
                QB = range(4)
                o1 = [pob[0][qb // 2] for qb in QB]
                o2 = [pob[1][qb // 2] for qb in QB]
                b0 = [(qb % 2) * 129 for qb in QB]
                rr = [r_r.next() for qb in QB]
                oa = [oa_r.next() for qb in QB]
                yn = [yn_r.next() for qb in QB]
                jk = [junk_r.next() for qb in QB]
                for qb in QB:
                    S.recip(rr[qb], rr[qb][:, 0:1], o1[qb], o1[qb][:, b0[qb] + 128:b0[qb] + 129])
                for qb in QB:
                    S.recip(rr[qb], rr[qb][:, 1:2], o2[qb], o2[qb][:, b0[qb] + 128:b0[qb] + 129])
                for qb in QB:
                    S.tt("dve", rr[qb], rr[qb][:, 2:3], rr[qb], rr[qb][:, 1:2], nlam, nlam[:, 3:4], ALU.mult)
                for qb in QB:
                    S.ts("dve", oa[qb], oa[qb][:], o1[qb], o1[qb][:, b0[qb]:b0[qb] + 128], rr[qb][:, 0:1], None, ALU.mult, rd=[rr[qb]])
                for qb in QB:
                    S.stt(oa[qb], oa[qb][:], o2[qb], o2[qb][:, b0[qb]:b0[qb] + 128], rr[qb][:, 2:3], oa[qb], oa[qb][:], ALU.mult, ALU.add, rd=[rr[qb]])
                for qb in QB:
                    S.act(jk[qb], jk[qb][:], oa[qb], oa[qb][:], AF.Square, accum=rr[qb][:, 3:4], wr=[rr[qb]])
                for qb in QB:
                    S.act(rr[qb], rr[qb][:, 1:2], rr[qb], rr[qb][:, 3:4], AF.Sqrt, bias=eps5[:, 0:1], scale=1.0 / 128, rd=[eps5])
                for qb in QB:
                    S.recip(rr[qb], rr[qb][:, 0:1], rr[qb], rr[qb][:, 1:2])
                for qb in QB:
                    S.stt(yn[qb], yn[qb][:], oa[qb], oa[qb][:], rr[qb][:, 0:1], ngd, ngd[:], ALU.mult, ALU.mult, rd=[rr[qb]])
                for qb in QB:
                    pT = sm.next()
                    S.mm(pT, pT[:], yn[qb], yn[qb][:], idb, idb[:])
                    S.copy("act", yo, yo[:, hd, qb * 128:(qb + 1) * 128], pT, pT[:])
            A["store_y"](S, yo, t)


def diff_inputs(inp, l, g, b, hT_b, cst):
    w = inp["w_in"][l]
    qc = 8240 + np.arange(g * 256, (g + 1) * 256)
    kc = 9264 + np.arange(g * 256, (g + 1) * 256)
    vc = 10288 + np.arange(g * 256, (g + 1) * 256)
    d = np.arange(256) % 64
    partner = np.arange(256) + np.where(d < 8, 8, np.where(d < 16, -8, 0))
    cols = np.concatenate([qc, qc[partner], kc, kc[partner], vc])
    p = np.arange(128) % 64
    invf = np.zeros((128, 2), np.float32)
    fr = (500000.0 ** (-np.arange(0, 16, 2, dtype=np.float32) / np.float32(16))).astype(np.float32)
    invf[:, 0] = np.where(p < 16, fr[p % 8], 0.0)
    invf[:, 1] = np.where(p < 8, -1.0, np.where(p < 16, 1.0, 0.0))
    lqk = np.stack([inp["diff_lq1"][l], inp["diff_lk1"][l], inp["diff_lq2"][l], inp["diff_lk2"][l]]).astype(np.float32)
    return {"hT": hT_b, "w_diff": np.ascontiguousarray(w[:, cols]),
            "pos": np.ascontiguousarray(inp["positions"][b].reshape(1, SEQ)), "invf": invf, "lqk": lqk,
            "ngd": np.ascontiguousarray(inp["diff_norm_g"][l].reshape(1, 128)), "cst": cst}


_PROGS = {}


def _prog(key, fn):
    if key not in _PROGS:
        _PROGS[key] = fn()
    return _PROGS[key]


def _run(nc, maps):
    return run_bass_kernel_spmd(nc, maps, core_ids=list(range(NCORES))).results


def tok_inputs(inp, l, xT_c, hT_c, yT_c, g_next, cst):
    w = inp["w_in"][l]
    return {"xT": xT_c, "g_next": pvec(g_next), "cst": cst, "hT": hT_c, "yT": yT_c,
            "w_gate": np.ascontiguousarray(w[:, 11312:14384]), "b_gate": pvec(inp["b_gate"][l]),
            "w_br": np.ascontiguousarray(np.concatenate([inp["w_br_gla"][l], inp["w_br_ssm"][l], inp["w_br_diff"][l]], axis=0)),
            "w_out": np.ascontiguousarray(inp["w_out"][l]), "w_up": np.ascontiguousarray(inp["w_mlp_up"][l]),
            "w_dn": np.ascontiguousarray(inp["w_mlp_down"][l]), "g_mlp": pvec(inp["norm_mlp_g"][l])}


def kernel_unfused(**inp):
    inp = {k: np.asarray(v) for k, v in inp.items()}
    cst = make_consts()
    x = inp["x"]
    xT = [np.ascontiguousarray(x[c // 4, (c % 4) * NT:(c % 4 + 1) * NT, :].T) for c in range(NCORES)]
    res = _run(_prog("first", lambda: build_tok("first")),
               [{"xT": xT[c], "g_next": pvec(inp["norm_mix_g"][0]), "cst": cst} for c in range(NCORES)])
    hT = [r["hT_out"] for r in res]
    hTl = [r["hT_lo"] for r in res]
    out = None
    for l in range(DEPTH):
        hTb = [np.ascontiguousarray(np.concatenate(hT[b * 4:(b + 1) * 4], axis=1)) for b in range(B)]
        hTlb = [np.ascontiguousarray(np.concatenate(hTl[b * 4:(b + 1) * 4], axis=1)) for b in range(B)]
        yg = _run(_prog("gla", build_gla), [gla_inputs(inp, l, c % 4, hTb[c // 4], cst, hTlb[c // 4]) for c in range(NCORES)])
        ys = _run(_prog("ssm", build_ssm), [ssm_inputs(inp, l, c % 4, hTb[c // 4], cst) for c in range(NCORES)])
        yd = _run(_prog(("diff", l), lambda: build_diff(l)), [diff_inputs(inp, l, c % 4, c // 4, hTb[c // 4], cst) for c in range(NCORES)])
        yTb = []
        for b in range(B):
            parts = [yg[b * 4 + g]["yT"] for g in range(4)] + [ys[b * 4 + g]["yT"] for g in range(4)] + [yd[b * 4 + g]["yT"] for g in range(4)]
            yTb.append(np.concatenate(parts, axis=0))
        last = (l == DEPTH - 1)
        g_next = inp["norm_final_g"] if last else inp["norm_mix_g"][l + 1]
        maps = [tok_inputs(inp, l, xT[c], hT[c], np.ascontiguousarray(yTb[c // 4][:, (c % 4) * NT:(c % 4 + 1) * NT]), g_next, cst)
                for c in range(NCORES)]
        mode = "last" if last else "mid"
        res = _run(_prog(mode, lambda: build_tok(mode)), maps)
        if last:
            out = np.empty((B, SEQ, DM), np.float32)
            for c in range(NCORES):
                out[c // 4, (c % 4) * NT:(c % 4 + 1) * NT, :] = res[c]["outT"].T
        else:
            xT = [r["xT_out"] for r in res]
            hT = [r["hT_out"] for r in res]
            hTl = [r["hT_lo"] for r in res]
    return out


GROUPS = [[0, 1, 2, 3], [4, 5, 6, 7]]


def build_fused(skip=()):
    nc = bass.Bass("TRN2", target_bir_lowering=False)
    ext = lambda name, shape, dt=F32: dram(nc, name, shape, dt, "ExternalInput")
    xT = ext("xT", [DM, NT])
    pos = ext("pos", [1, SEQ], I32)
    cst = ext("cst", [4, 128, 128])
    invf = ext("invf", [128, 2])
    g_mix = ext("g_mix", [DEPTH, 128, 8])
    g_fin = ext("g_fin", [128, 8])
    g_mlp = ext("g_mlp", [DEPTH, 128, 8])
    w_gate = ext("w_gate", [DEPTH, DM, 3072])
    b_gate = ext("b_gate", [DEPTH, 128, 24])
    w_br = ext("w_br", [DEPTH, 4096, DM])
    w_out = ext("w_out", [DEPTH, DM, DM])
    w_up = ext("w_up", [DEPTH, DM, 4096])
    w_dn = ext("w_dn", [DEPTH, 4096, DM])
    w_gla = ext("w_gla", [DEPTH, DM, 784])
    wgk2 = ext("wgk2", [DEPTH, 16, 128])
    bgk = ext("bgk", [DEPTH, 1, 128])
    ng = ext("ng", [DEPTH, 128, 2])
    w_ssm = ext("w_ssm", [DEPTH, DM, 1288])
    cw = ext("cw", [DEPTH, 128, 6, 4])
    cb = ext("cb", [DEPTH, 128, 6])
    dtb = ext("dtb", [DEPTH, 1, 8])
    alog = ext("alog", [DEPTH, 1, 8])
    dsk = ext("dsk", [DEPTH, 1, 8])
    ngs = ext("ngs", [DEPTH, 1, 512])
    w_diff = ext("w_diff", [DEPTH, DM, 1280])
    lqk = ext("lqk", [DEPTH, 4, 64])
    ngd = ext("ngd", [DEPTH, 1, 128])
    outT = dram(nc, "outT", [DM, NT], F32, "ExternalOutput")
    xs = nc.dram_tensor("xs_i", [DM, NT], F32).ap()
    TPR = NT // TT
    hsrc = nc.dram_tensor("hsrc_i", [2 * TPR, DM, TT], BF16).ap()
    hgat = nc.dram_tensor("hgat_i", [2 * TPR, 4 * DM, TT], BF16).ap()
    ysrc = nc.dram_tensor("ysrc_i", [4, 4, 256, NT], BF16).ap()
    ygat = nc.dram_tensor("ygat_i", [4, 4, 1024, NT], BF16).ap()
    ygat3 = ygat.rearrange("q a r n -> q (a r) n")

    with ExitStack() as st:
        S = Sched(nc, st)
        hsrc_b = [Buf("hsrc_b%d" % i) for i in range(2 * TPR)]
        hgat_b = [Buf("hgat_b%d" % i) for i in range(2 * TPR)]
        ysrc_b = [[Buf("ysrc_b%d_%d" % (q, a)) for a in range(4)] for q in range(4)]
        ygat_b = Buf("ygat_b")
        xs_b = Buf("xs_b")
        S.global_bufs += [ygat_b, xs_b] + hsrc_b + hgat_b + [b for row in ysrc_b for b in row]

        def after_h(S_, t):
            for which in range(2):
                i = 2 * t + which
                S_.collective("AllGather", hsrc_b[i], hsrc[i], hgat_b[i], hgat[i], GROUPS)

        def load_h_from(which):
            def f(S_, h, t):
                r, i = t // TPR, 2 * (t % TPR) + which
                S_.dma("sp", h[:], kcp(hgat[i][r * DM:(r + 1) * DM, :]), reads=[hgat_b[i]], writes=[h])
            return f

        h_io = {"h_in": (lambda t: hsrc[2 * t]), "h_out": (lambda t, which: hsrc[2 * t + which]),
                "h_wr": (lambda t: [[hsrc_b[2 * t]], [hsrc_b[2 * t + 1]]]), "after_h": after_h}

        def store_y_parts(parts):
            def f(S_, yo, t):
                q, tl = t // (NT // TT), (t % (NT // TT)) * TT
                for j, a in enumerate(parts):
                    S_.dma("sp", ysrc[q, a][:, tl:tl + TT].rearrange("(c p) n -> p c n", p=128), yo[:, 2 * j:2 * j + 2, :],
                           reads=[yo], writes=[ysrc_b[q][a]], sembuf=yo, group=True)
                if t % (NT // TT) == NT // TT - 1:
                    for a in parts:
                        S_.collective("AllGather", ysrc_b[q][a], ysrc[q, a], ygat_b, ygat[q, a], GROUPS)
            return f

        qcache = {}

        def load_y(S_, yub, t):
            tl = t * TT
            ph = S_.phase_id

            def qval(E):
                if ph not in qcache:
                    qcache[ph] = E.snap(E.partition_id() % 4)
                return qcache[ph]
            src = (lambda E, tl=tl: ygat3[bass.ds(qval(E), 1), :, tl:tl + TT].rearrange("o (k p) n -> p (o k) n", p=128))
            S_.dma("sp", yub[:], src, reads=[ygat_b], writes=[yub])

        S.begin_phase()
        if "first" not in skip:
            emit_tok(nc, S, dict(h_io, **{"xT": xT, "g_next": g_mix[0], "cst": cst}), "first")
        S.end_phase()
        for l in range(DEPTH):
            S.begin_phase()
            if "gla" not in skip:
              emit_gla(nc, S, {"w_gla": w_gla[l], "wgk2": wgk2[l], "bgk": bgk[l], "ng": ng[l], "cst": cst,
                             "load_h": load_h_from(0), "load_hlo": load_h_from(1), "store_y": store_y_parts((0,))}, SEQ // TT)
            S.end_phase()
            S.begin_phase()
            if "ssm" not in skip:
              emit_ssm(nc, S, {"w_ssm": w_ssm[l], "cw": cw[l], "cb": cb[l], "dtb": dtb[l], "alog": alog[l], "dsk": dsk[l], "ngs": ngs[l],
                             "cst": cst, "load_h": load_h_from(0), "store_y": store_y_parts((1, 2))}, SEQ // TT)
            S.end_phase()
            S.begin_phase()
            if "diff" not in skip:
              emit_diff(nc, S, {"w_diff": w_diff[l], "pos": pos, "invf": invf, "lqk": lqk[l], "ngd": ngd[l], "cst": cst,
                              "load_h": load_h_from(0), "store_y": store_y_parts((3,))}, l, SEQ // TT)
            S.end_phase()
            last = (l == DEPTH - 1)
            A = {"xT": (xT if l == 0 else xs), "g_next": (g_fin if last else g_mix[l + 1]), "cst": cst, "h_in": h_io["h_in"], "load_y": load_y,
                 "w_gate": w_gate[l], "b_gate": b_gate[l], "w_br": w_br[l], "w_out": w_out[l], "w_up": w_up[l], "w_dn": w_dn[l], "g_mlp": g_mlp[l]}
            if last:
                A["outT"] = outT
            else:
                A.update(h_io)
                A["xT_out"] = xs
            S.begin_phase()
            emit_tok(nc, S, A, "last" if last else "mid")
            S.end_phase()
    return nc


def fused_y_row_order():
    rows = []
    for a in range(4):
        for r in range(4):
            j = np.arange(256)
            if a == 0:
                rows.append(r * 256 + j)
            elif a in (1, 2):
                rows.append(1024 + r * 512 + (a - 1) * 256 + j)
            else:
                rows.append(3072 + r * 256 + j)
    return np.concatenate(rows)


def fused_inputs(inp, c, cst):
    b, g = c // 4, c % 4
    x = inp["x"]
    m = {"xT": np.ascontiguousarray(x[b, g * NT:(g + 1) * NT, :].T), "cst": cst,
         "pos": np.ascontiguousarray(inp["positions"][b].reshape(1, SEQ)),
         "g_mix": np.stack([pvec(inp["norm_mix_g"][l]) for l in range(DEPTH)]), "g_fin": pvec(inp["norm_final_g"]),
         "g_mlp": np.stack([pvec(inp["norm_mlp_g"][l]) for l in range(DEPTH)]),
         "w_gate": np.ascontiguousarray(inp["w_in"][:, :, 11312:14384]),
         "b_gate": np.stack([pvec(inp["b_gate"][l]) for l in range(DEPTH)]),
         "w_br": np.ascontiguousarray(np.concatenate([inp["w_br_gla"], inp["w_br_ssm"], inp["w_br_diff"]], axis=1)[:, fused_y_row_order(), :]),
         "w_out": np.ascontiguousarray(inp["w_out"]), "w_up": np.ascontiguousarray(inp["w_mlp_up"]), "w_dn": np.ascontiguousarray(inp["w_mlp_down"])}
    per = {}
    for l in range(DEPTH):
        d = {}
        d.update(gla_inputs(inp, l, g, None, cst, 0))
        d.update(ssm_inputs(inp, l, g, None, cst))
        d.update(diff_inputs(inp, l, g, b, None, cst))
        for k, v in d.items():
            if k in ("hT", "hTlo", "cst", "pos", "invf"):
                continue
            per.setdefault(k, []).append(v)
        if l == 0:
            m["invf"] = d["invf"]
    for k, v in per.items():
        m[k] = np.ascontiguousarray(np.stack(v))
    return m


_FUSED = []


def kernel(**inp):
    inp = {k: np.asarray(v) for k, v in inp.items()}
    cst = make_consts()
    if not _FUSED:
        _FUSED.append(build_fused())
    maps = [fused_inputs(inp, c, cst) for c in range(NCORES)]
    res = run_bass_kernel_spmd(_FUSED[0], maps, core_ids=list(range(NCORES))).results
    out = np.empty((B, SEQ, DM), np.float32)
    for c in range(NCORES):
        out[c // 4, (c % 4) * NT:(c % 4 + 1) * NT, :] = res[c]["outT"].T
    return out
```

```python
import math
from contextlib import ExitStack
import numpy as np
import ml_dtypes
import concourse.bass as bass
import concourse.mybir as mybir
from concourse.bass_utils import run_bass_kernel_spmd

F32 = mybir.dt.float32
BF16 = mybir.dt.bfloat16
I32 = mybir.dt.int32
AF = mybir.ActivationFunctionType
ALU = mybir.AluOpType
AX = mybir.AxisListType

NCORES = 8
B, SEQ, DM, DEPTH = 2, 8192, 1024, 2
NT = 2048
TT = 512
EPS = 1e-6
ENGS = ("pe", "act", "dve", "pool", "sp")


class Buf:
    __slots__ = ("name", "t", "w", "r", "dsem", "dcnt")

    def __init__(self, name, t=None):
        self.name = name
        self.t = t
        self.w = None
        self.r = {}
        self.dsem = None
        self.dcnt = 0

    def __getitem__(self, idx):
        return self.t[idx]


class VBuf:
    def __init__(self, name, parent, ap):
        self.name = name
        self.parent = parent
        self.t = ap

    def __getitem__(self, idx):
        return self.t[idx]

    w = property(lambda s: s.parent.w, lambda s, v: setattr(s.parent, "w", v))
    r = property(lambda s: s.parent.r, lambda s, v: setattr(s.parent, "r", v))
    dsem = property(lambda s: s.parent.dsem, lambda s, v: setattr(s.parent, "dsem", v))
    dcnt = property(lambda s: s.parent.dcnt, lambda s, v: setattr(s.parent, "dcnt", v))


class Sched:
    def __init__(self, nc, stack):
        self.nc = nc
        self.stack = stack
        self.q = {e: [] for e in ENGS}
        self.sem = {}
        for e in ENGS:
            self.sem[e] = stack.enter_context(nc.semaphore("s_" + e))
        self.cnt = {e: 0 for e in ENGS}
        self.waited = {e: {} for e in ENGS}
        self.ndsem = 0
        self.final_waits = {}
        self.alloc_stack = stack
        self.nname = 0
        self.free_dsems = {}
        self.dsem_cls = {}
        self.phase_bufs = []
        self.global_bufs = []

    def sb(self, name, shape, dt):
        self.nname += 1
        t = self.alloc_stack.enter_context(self.nc.sbuf_tensor("sb%d_%s" % (self.nname, name), list(shape), dt))
        b = Buf(name, t)
        self.phase_bufs.append(b)
        return b

    def ps(self, name, shape, dt=F32):
        self.nname += 1
        t = self.alloc_stack.enter_context(self.nc.psum_tensor("ps%d_%s" % (self.nname, name), list(shape), dt))
        return Buf(name, t)

    def view(self, name, ap):
        return Buf(name, ap)

    def _dsem(self, b, eng="sp"):
        cls = {"pool": "sw", "cc": "cc"}.get(eng, "hw")
        if b.dsem is not None:
            assert self.dsem_cls[b.dsem] == cls, (b.name, cls)
        if b.dsem is None:
            pool = self.free_dsems.setdefault(cls, [])
            if pool:
                b.dsem, b.dcnt = pool.pop()
            else:
                b.dsem = "d%d" % self.ndsem
                self.ndsem += 1
                self.sem[b.dsem] = self.stack.enter_context(self.nc.semaphore(b.dsem))
                self.dsem_cls[b.dsem] = cls
        return b.dsem

    def _deps(self, eng, reads, writes, same_ok):
        deps = {}

        def add(k, v):
            if deps.get(k, 0) < v:
                deps[k] = v

        for b in reads:
            if b.w is not None:
                add(*b.w)
        for b in writes:
            if b.w is not None:
                add(*b.w)
            for k, v in b.r.items():
                add(k, v)
        waits = []
        wd = self.waited[eng]
        for k, v in deps.items():
            if k == eng and same_ok:
                continue
            if wd.get(k, 0) >= v:
                continue
            wd[k] = v
            waits.append((k, v))
        return waits

    def _commit(self, tk, reads, writes):
        k, v = tk
        for b in writes:
            b.w = tk
            b.r = {}
        for b in reads:
            if b.r.get(k, 0) < v:
                b.r[k] = v

    def op(self, eng, fn, reads=(), writes=()):
        waits = self._deps(eng, reads, writes, same_ok=(eng == "pe"))
        self.cnt[eng] += 1
        tk = (eng, self.cnt[eng])
        sem = self.sem
        me = sem[eng]

        def emit(E):
            for k, v in waits:
                E.wait_ge(sem[k], v)
            fn(E).then_inc(me, 1)

        self.q[eng].append(emit)
        self._commit(tk, reads, writes)
        return tk

    def dma(self, eng, out_ap, in_ap, reads=(), writes=(), sembuf=None, group=False):
        sb = sembuf if sembuf is not None else (writes[0] if writes else reads[0])
        dk = self._dsem(sb, eng)
        saved = None
        if group and writes and writes[0].w is not None and writes[0].w[0] == dk:
            saved = writes[0].w
            writes[0].w = None
        waits = self._deps(eng, reads, writes, same_ok=False)
        if saved is not None:
            writes[0].w = saved
        sb.dcnt += 16
        tk = (dk, sb.dcnt)
        sem = self.sem
        ds = sem[dk]

        def emit(E):
            for k, v in waits:
                E.wait_ge(sem[k], v)
            E.dma_start(out=out_ap, in_=(in_ap(E) if callable(in_ap) else in_ap)).then_inc(ds, 16)

        self.q[eng].append(emit)
        self._commit(tk, reads, writes)
        self.final_waits[dk] = sb.dcnt
        return tk

    def collective(self, kind, sb_, s_ap, db_, d_ap, groups):
        dk = self._dsem(db_, "cc")
        waits = self._deps("pool", [sb_], [db_], same_ok=False)
        db_.dcnt += 1
        tk = (dk, db_.dcnt)
        sem = self.sem
        ds = sem[dk]

        def emit(E):
            for k, v in waits:
                E.wait_ge(sem[k], v)
            E.collective_compute(kind, ALU.bypass, replica_groups=groups, ins=[s_ap], outs=[d_ap]).then_inc(ds, 1)

        self.q["pool"].append(emit)
        self._commit(tk, [sb_], [db_])
        self.final_waits[dk] = db_.dcnt
        return tk

    def mm(self, ob, o, lb, l, rb, r, start=True, stop=True):
        return self.op("pe", lambda E: E.matmul(o, lhsT=l, rhs=r, start=start, stop=stop), reads=[lb, rb], writes=[ob])

    def act(self, ob, o, ib, i, func, bias=None, scale=None, accum=None, rd=(), wr=()):
        kw = {}
        if bias is not None:
            kw["bias"] = bias
        if scale is not None:
            kw["scale"] = scale
        if accum is not None:
            kw["accum_out"] = accum
        return self.op("act", lambda E: E.activation(out=o, in_=i, func=func, **kw), reads=[ib] + list(rd), writes=[ob] + list(wr))

    def tt(self, eng, ob, o, ab, a, bb, b, op):
        return self.op(eng, lambda E: E.tensor_tensor(out=o, in0=a, in1=b, op=op), reads=[ab, bb], writes=[ob])

    def ts(self, eng, ob, o, ab, a, s1, s2, op0, op1=None, rd=()):
        if op1 is None:
            return self.op(eng, lambda E: E.tensor_scalar(out=o, in0=a, scalar1=s1, scalar2=None, op0=op0), reads=[ab] + list(rd), writes=[ob])
        return self.op(eng, lambda E: E.tensor_scalar(out=o, in0=a, scalar1=s1, scalar2=s2, op0=op0, op1=op1), reads=[ab] + list(rd), writes=[ob])

    def stt(self, ob, o, ab, a, sc, bb, b, op0, op1, rd=()):
        return self.op("dve", lambda E: E.scalar_tensor_tensor(out=o, in0=a, scalar=sc, in1=b, op0=op0, op1=op1), reads=[ab, bb] + list(rd), writes=[ob])

    def copy(self, eng, ob, o, ib, i):
        if eng == "act":
            return self.op("act", lambda E: E.copy(out=o, in_=i), reads=[ib], writes=[ob])
        return self.op(eng, lambda E: E.tensor_copy(out=o, in_=i), reads=[ib], writes=[ob])

    def memset(self, eng, ob, o, val):
        return self.op(eng, lambda E: E.memset(o, val), writes=[ob])

    def recip(self, ob, o, ib, i):
        return self.op("dve", lambda E: E.reciprocal(out=o, in_=i), reads=[ib], writes=[ob])

    phase_id = 0

    def begin_phase(self):
        self.phase_id += 1
        self.gstack = self.stack if not hasattr(self, "gstack") else self.gstack
        self.pstack = ExitStack()
        self.pstack.__enter__()
        self.alloc_stack = self.pstack

    def end_phase(self):
        sem = self.sem
        cnt = dict(self.cnt)
        fw = dict(self.final_waits)
        for e in ENGS:
            waits = [(o, cnt[o]) for o in ENGS if o != e and cnt[o] > self.waited[e].get(o, 0)]
            waits += [(k, v) for k, v in fw.items() if v > self.waited[e].get(k, 0)]
            for k, v in waits:
                self.waited[e][k] = v

            def emit(E, waits=waits):
                for k, v in waits:
                    E.wait_ge(sem[k], v)
            self.q[e].append(emit)
        self.finish()
        self.q = {e: [] for e in ENGS}
        for e in ENGS:
            self.sem[e] = self.stack.enter_context(self.nc.semaphore("s_%s_%d" % (e, self.phase_id)))
            self.cnt[e] = 0
            for x in ENGS:
                self.waited[x].pop(e, None)
        for b in self.global_bufs:
            if b.w is not None and b.w[0] in ENGS:
                b.w = None
            b.r = {k: v for k, v in b.r.items() if k not in ENGS}
        for b in self.phase_bufs:
            if b.dsem is not None:
                self.free_dsems[self.dsem_cls[b.dsem]].append((b.dsem, b.dcnt))
                b.dsem = None
        self.phase_bufs = []
        self.pstack.__exit__(None, None, None)
        self.alloc_stack = self.stack

    def finish(self):
        nc = self.nc
        sem = self.sem
        q = self.q
        fw = dict(self.final_waits)
        with nc.Block() as block:
            @block.tensor
            def _(E):
                for f in q["pe"]:
                    f(E)

            @block.scalar
            def _(E):
                for f in q["act"]:
                    f(E)

            @block.vector
            def _(E):
                for f in q["dve"]:
                    f(E)

            @block.gpsimd
            def _(E):
                for f in q["pool"]:
                    f(E)

            @block.sync
            def _(E):
                for f in q["sp"]:
                    f(E)
                for k, v in fw.items():
                    E.wait_ge(sem[k], v)


class Ring:
    def __init__(self, bufs):
        self.bufs = bufs
        self.i = 0

    def next(self):
        b = self.bufs[self.i % len(self.bufs)]
        self.i += 1
        return b


def dram(nc, name, shape, dt, kind):
    return nc.dram_tensor(name, list(shape), dt, kind=kind).ap()


def kcp(ap):
    return ap.rearrange("(kc p) n -> p kc n", p=128)


def rms_stats(S, C, xb, nch, n, ps_ring, sq_ring, rstd, tmp, nfeat, eps):
    pss = ps_ring.next()
    for kc in range(nch):
        sq = sq_ring.next()
        S.act(sq, sq[:, :n], xb, xb[:, kc, :n], AF.Square)
        S.mm(pss, pss[:, :n], C["ones"], C["ones"][:], sq, sq[:, :n], start=(kc == 0), stop=(kc == nch - 1))
    S.act(tmp, tmp[:, :n], pss, pss[:, :n], AF.Sqrt, bias=C["eps%g" % eps][:, 0:1], scale=1.0 / nfeat, rd=[C["eps%g" % eps]])
    S.recip(rstd, rstd[:, :n], tmp, tmp[:, :n])


def load_consts(S, nc, cst_ap, extra_eps=()):
    C = {}
    C["ones"] = S.sb("c_ones", [128, 128], F32)
    S.dma("sp", C["ones"][:], cst_ap[0], writes=[C["ones"]])
    for e in (EPS,) + tuple(extra_eps):
        b = S.sb("c_eps%g" % e, [128, 1], F32)
        S.memset("dve", b, b[:], float(e))
        C["eps%g" % e] = b
    return C


def build_tok(mode):
    nc = bass.Bass("TRN2", target_bir_lowering=False)
    A = {}
    A["xT"] = dram(nc, "xT", [DM, NT], F32, "ExternalInput")
    A["g_next"] = dram(nc, "g_next", [128, 8], F32, "ExternalInput")
    A["cst"] = dram(nc, "cst", [4, 128, 128], F32, "ExternalInput")
    if mode != "first":
        A["hT"] = dram(nc, "hT", [DM, NT], BF16, "ExternalInput")
        yT = dram(nc, "yT", [4096, NT], BF16, "ExternalInput")
        A["load_y"] = lambda S, yub, t: S.dma("act", yub[:], kcp(yT[:, t * TT:(t + 1) * TT]), writes=[yub])
        A["w_gate"] = dram(nc, "w_gate", [DM, 3072], F32, "ExternalInput")
        A["b_gate"] = dram(nc, "b_gate", [128, 24], F32, "ExternalInput")
        A["w_br"] = dram(nc, "w_br", [4096, DM], F32, "ExternalInput")
        A["w_out"] = dram(nc, "w_out", [DM, DM], F32, "ExternalInput")
        A["w_up"] = dram(nc, "w_up", [DM, 4096], F32, "ExternalInput")
        A["w_dn"] = dram(nc, "w_dn", [4096, DM], F32, "ExternalInput")
        A["g_mlp"] = dram(nc, "g_mlp", [128, 8], F32, "ExternalInput")
    if mode == "last":
        A["outT"] = dram(nc, "outT", [DM, NT], F32, "ExternalOutput")
    else:
        A["hT_out"] = dram(nc, "hT_out", [DM, NT], BF16, "ExternalOutput")
        A["hT_lo"] = dram(nc, "hT_lo", [DM, NT], BF16, "ExternalOutput")
        if mode == "mid":
            A["xT_out"] = dram(nc, "xT_out", [DM, NT], F32, "ExternalOutput")
    with ExitStack() as st:
        S = Sched(nc, st)
        S.begin_phase()
        emit_tok(nc, S, A, mode)
        S.end_phase()
    return nc


def emit_tok(nc, S, A, mode):
    xT, gn, cst = A["xT"], A["g_next"], A["cst"]
    if mode != "first":
        w_gate, b_gate, w_br, w_out, w_up, w_dn, gm = (A[k] for k in ("w_gate", "b_gate", "w_br", "w_out", "w_up", "w_dn", "g_mlp"))
        h_in = A["h_in"] if "h_in" in A else (lambda t: A["hT"][:, t * TT:(t + 1) * TT])
    if mode == "last":
        outT = A["outT"]
    else:
        h_out = A["h_out"] if "h_out" in A else (lambda t, which: (A["hT_out"], A["hT_lo"])[which][:, t * TT:(t + 1) * TT])
        after_h = A.get("after_h", None)
        if mode == "mid":
            xTo = A["xT_out"]
    if True:
        C = load_consts(S, nc, cst)
        gnb = S.sb("gnb", [128, 8], F32)
        S.dma("sp", gnb[:], gn, writes=[gnb])
        xb = S.sb("xb", [128, 8, TT], F32)
        ps_ring = Ring([S.ps("ps%d" % i, [128, TT], F32) for i in range(7)])
        sq_ring = Ring([S.sb("sq%d" % i, [128, TT], F32) for i in range(2)])
        rstd = S.sb("rstd", [128, TT], F32)
        rtmp = S.sb("rtmp", [128, TT], F32)
        hn_ring = Ring([S.sb("hn%d" % i, [128, 8, TT], BF16) for i in range(2)])
        hl_ring = Ring([S.sb("hl%d" % i, [128, 8, TT], BF16) for i in range(2)])
        h32_ring = Ring([S.sb("h32_%d" % i, [128, TT], F32) for i in range(2)])
        if mode == "last":
            oc_ring = Ring([S.sb("oc%d" % i, [128, TT], F32) for i in range(3)])
        if mode != "first":
            bgb = S.sb("bgb", [128, 24], F32)
            S.dma("sp", bgb[:], b_gate, writes=[bgb])
            gmb = S.sb("gmb", [128, 8], F32)
            S.dma("sp", gmb[:], gm, writes=[gmb])
            hb_ring = Ring([S.sb("hb%d" % i, [128, 8, TT], BF16) for i in range(2)])
            yub = S.sb("yub", [128, 32, TT], BF16)
            mixed = S.sb("mixed", [128, 8, TT], BF16)
            h2 = S.sb("h2", [128, 8, TT], BF16)
            g_ring = Ring([S.sb("g%d" % i, [128, TT], F32) for i in range(6)])
            acc_ring = Ring([S.sb("acc%d" % i, [128, TT], F32) for i in range(2)])
            tmp_ring = Ring([S.sb("tmp%d" % i, [128, TT], F32) for i in range(2)])
            r_ring = Ring([S.sb("r%d" % i, [128, TT], F32) for i in range(2)])
            w_ring = Ring([S.sb("w%d" % i, [128, 7168], BF16) for i in range(3)])

        for t in range(NT // TT):
            ts_ = slice(t * TT, (t + 1) * TT)
            S.dma("sp", xb[:], kcp(xT[:, ts_]), writes=[xb])
            if mode != "first":
                hb = hb_ring.next()
                S.dma("sp", hb[:], kcp(h_in(t)), writes=[hb])
                A["load_y"](S, yub, t)
                for oc in range(8):
                    w = w_ring.next()
                    wg = w[:, 0:3072].rearrange("p (kc br j) -> p kc br j", kc=8, br=3)
                    wb = w[:, 3072:7168].rearrange("p (kc j) -> p kc j", kc=32)
                    for br in range(3):
                        c0 = br * 1024 + oc * 128
                        S.dma("pool", wg[:, :, br, :], kcp(w_gate[:, c0:c0 + 128]), writes=[w], group=(br > 0))
                    S.dma("pool", wb, kcp(w_br[:, oc * 128:(oc + 1) * 128]), writes=[w], group=True)
                    gts = []
                    for br in range(3):
                        pg = ps_ring.next()
                        for kc in range(8):
                            S.mm(pg, pg[:], w, wg[:, kc, br, :], hb, hb[:, kc, :], start=(kc == 0), stop=(kc == 7))
                        g = g_ring.next()
                        ch = br * 8 + oc
                        S.act(g, g[:], pg, pg[:], AF.Sigmoid, bias=bgb[:, ch:ch + 1], rd=[bgb])
                        gts.append(g)
                    acc = acc_ring.next()
                    koff = (0, 8, 24)
                    nk = (8, 16, 8)
                    for br in range(3):
                        pb = ps_ring.next()
                        for kc in range(nk[br]):
                            S.mm(pb, pb[:], w, wb[:, koff[br] + kc, :], yub, yub[:, koff[br] + kc, :], start=(kc == 0), stop=(kc == nk[br] - 1))
                        if br == 0:
                            S.tt("dve", acc, acc[:], pb, pb[:], gts[0], gts[0][:], ALU.mult)
                        else:
                            tmp = tmp_ring.next()
                            S.tt("dve", tmp, tmp[:], pb, pb[:], gts[br], gts[br][:], ALU.mult)
                            if br == 1:
                                S.tt("dve", acc, acc[:], acc, acc[:], tmp, tmp[:], ALU.add)
                            else:
                                S.tt("dve", mixed, mixed[:, oc, :], acc, acc[:], tmp, tmp[:], ALU.add)
                if t > 0 and mode == "mid" and after_h is not None:
                    after_h(S, t - 1)
                for half in range(2):
                    w = w_ring.next()
                    wo = w[:, 0:4096].rearrange("p (kc j) -> p kc j", kc=8)
                    S.dma("pool", wo, kcp(w_out[:, half * 512:(half + 1) * 512]), writes=[w])
                    for o4 in range(4):
                        oc = half * 4 + o4
                        po = ps_ring.next()
                        for kc in range(8):
                            S.mm(po, po[:], w, wo[:, kc, o4 * 128:(o4 + 1) * 128], mixed, mixed[:, kc, :], start=(kc == 0), stop=(kc == 7))
                        S.tt("dve", xb, xb[:, oc, :], xb, xb[:, oc, :], po, po[:], ALU.add)
                rms_stats(S, C, xb, 8, TT, ps_ring, sq_ring, rstd, rtmp, DM, EPS)
                for kc in range(8):
                    S.stt(h2, h2[:, kc, :], xb, xb[:, kc, :], gmb[:, kc:kc + 1], rstd, rstd[:], ALU.mult, ALU.mult, rd=[gmb])
                for o8 in range(8):
                    w = w_ring.next()
                    wu = w[:, 0:4096].rearrange("p (kc j) -> p kc j", kc=8)
                    S.dma("pool", wu, kcp(w_up[:, o8 * 512:(o8 + 1) * 512]), writes=[w])
                    for o4 in range(4):
                        oc = o8 * 4 + o4
                        pu = ps_ring.next()
                        for kc in range(8):
                            S.mm(pu, pu[:], w, wu[:, kc, o4 * 128:(o4 + 1) * 128], h2, h2[:, kc, :], start=(kc == 0), stop=(kc == 7))
                        r = r_ring.next()
                        S.act(r, r[:], pu, pu[:], AF.Relu)
                        S.tt("dve", yub, yub[:, oc, :], r, r[:], r, r[:], ALU.mult)
                for oc in range(8):
                    w = w_ring.next()
                    wd = w[:, 0:4096].rearrange("p (kc j) -> p kc j", kc=32)
                    S.dma("pool", wd, kcp(w_dn[:, oc * 128:(oc + 1) * 128]), writes=[w])
                    pd = ps_ring.next()
                    for kc in range(32):
                        S.mm(pd, pd[:], w, wd[:, kc, :], yub, yub[:, kc, :], start=(kc == 0), stop=(kc == 31))
                    S.tt("dve", xb, xb[:, oc, :], xb, xb[:, oc, :], pd, pd[:], ALU.add)
                if mode == "mid":
                    S.dma("sp", kcp(xTo[:, ts_]), xb[:], reads=[xb])
            rms_stats(S, C, xb, 8, TT, ps_ring, sq_ring, rstd, rtmp, DM, EPS)
            if mode == "last":
                for kc in range(8):
                    o = oc_ring.next()
                    S.stt(o, o[:], xb, xb[:, kc, :], gnb[:, kc:kc + 1], rstd, rstd[:], ALU.mult, ALU.mult, rd=[gnb])
                    S.dma("sp", outT[kc * 128:(kc + 1) * 128, ts_], o[:], reads=[o])
            else:
                hn = hn_ring.next()
                hl = hl_ring.next()
                for kc in range(8):
                    h32 = h32_ring.next()
                    S.stt(h32, h32[:], xb, xb[:, kc, :], gnb[:, kc:kc + 1], rstd, rstd[:], ALU.mult, ALU.mult, rd=[gnb])
                    S.copy("act", hn, hn[:, kc, :], h32, h32[:])
                    S.tt("dve", hl, hl[:, kc, :], h32, h32[:], hn, hn[:, kc, :], ALU.subtract)
                wr = A["h_wr"](t) if "h_wr" in A else [[], []]
                S.dma("sp", kcp(h_out(t, 0)), hn[:], reads=[hn], writes=wr[0], sembuf=hn)
                S.dma("sp", kcp(h_out(t, 1)), hl[:], reads=[hl], writes=wr[1], sembuf=hl)
                if after_h is not None and (mode == "first" or t == NT // TT - 1):
                    after_h(S, t)


def make_consts():
    c = np.zeros((4, 128, 128), np.float32)
    c[0] = 1.0
    c[1] = np.triu(np.ones((128, 128), np.float32))
    c[2] = 1.0 - c[1]
    c[3] = np.eye(128, dtype=np.float32)
    return c


def pvec(v):
    v = np.asarray(v)
    return np.ascontiguousarray(v.reshape(-1, 128).T)


def small_views(S, name, nbanks, width):
    out = []
    per = 512 // width
    banks = [S.ps("%s%d" % (name, i), [128, 512], F32) for i in range(nbanks)]
    for j in range(per):
        for i in range(nbanks):
            out.append(VBuf("%s%d_%d" % (name, i, j), banks[i], banks[i].t[:, j * width:(j + 1) * width]))
    return out


def build_gla(ntiles=SEQ // TT):
    nc = bass.Bass("TRN2", target_bir_lowering=False)
    hT = dram(nc, "hT", [DM, SEQ], BF16, "ExternalInput")
    hTl = dram(nc, "hTlo", [DM, SEQ], BF16, "ExternalInput")
    A = {}
    A["w_gla"] = dram(nc, "w_gla", [DM, 784], F32, "ExternalInput")
    A["wgk2"] = dram(nc, "wgk2", [16, 128], F32, "ExternalInput")
    A["bgk"] = dram(nc, "bgk", [1, 128], F32, "ExternalInput")
    A["ng"] = dram(nc, "ng", [128, 2], F32, "ExternalInput")
    A["cst"] = dram(nc, "cst", [4, 128, 128], F32, "ExternalInput")
    yT = dram(nc, "yT", [256, SEQ], BF16, "ExternalOutput")
    A["load_h"] = lambda S, h, t: S.dma("sp", h[:], kcp(hT[:, t * TT:(t + 1) * TT]), writes=[h])
    A["load_hlo"] = lambda S, h, t: S.dma("act", h[:], kcp(hTl[:, t * TT:(t + 1) * TT]), writes=[h])
    A["store_y"] = lambda S, yo, t: S.dma("sp", yT[:, t * TT:(t + 1) * TT].rearrange("(ec p) n -> p ec n", p=128), yo[:], reads=[yo])
    with ExitStack() as st:
        S = Sched(nc, st)
        S.begin_phase()
        emit_gla(nc, S, A, ntiles)
        S.end_phase()
    return nc


def emit_gla(nc, S, A, ntiles=SEQ // TT):
    w_gla, wgk2, bgk, ngp, cst = A["w_gla"], A["wgk2"], A["bgk"], A["ng"], A["cst"]
    if True:
        C = load_consts(S, nc, cst)
        U = S.sb("U", [128, 128], F32)
        UC = S.sb("UC", [128, 128], F32)
        S.dma("sp", U[:], cst[1], writes=[U])
        S.dma("sp", UC[:], cst[2], writes=[UC])
        wa = S.sb("wa", [128, 8, 784], BF16)
        S.dma("pool", wa[:], kcp(w_gla), writes=[wa])
        wqk32 = S.sb("wqk32", [128, 8, 256], F32)
        S.dma("sp", wqk32[:, :, 0:128], kcp(w_gla[:, 0:128]), writes=[wqk32])
        S.dma("sp", wqk32[:, :, 128:256], kcp(w_gla[:, 384:512]), writes=[wqk32], group=True)
        wqk_hi = S.sb("wqk_hi", [128, 8, 256], BF16)
        wqk_lo = S.sb("wqk_lo", [128, 8, 256], BF16)
        S.copy("act", wqk_hi, wqk_hi[:], wqk32, wqk32[:])
        S.tt("dve", wqk_lo, wqk_lo[:], wqk32, wqk32[:], wqk_hi, wqk_hi[:], ALU.subtract)
        w2 = S.sb("w2", [16, 128], F32)
        S.dma("sp", w2[:], wgk2, writes=[w2])
        bgb = S.sb("bgb", [128, 128], F32)
        S.dma("sp", bgb[:], bgk.partition_broadcast(128), writes=[bgb])
        ng = S.sb("ng", [128, 2], F32)
        S.dma("sp", ng[:], ngp, writes=[ng])
        h_ring = Ring([S.sb("h%d" % i, [128, 8, TT], BF16) for i in range(2)])
        hlo_ring = Ring([S.sb("hlo%d" % i, [128, 8, TT], BF16) for i in range(2)])
        big = Ring([S.ps("pb%d" % i, [128, 512], F32) for i in range(2)])
        po = [S.ps("po%d" % i, [128, 512], F32) for i in range(2)]
        pss_ring = Ring([S.ps("pss", [128, 512], F32)])
        sm = Ring(small_views(S, "sm", 3, 256))
        qTs = S.sb("qTs", [128, TT], F32)
        kTs = S.sb("kTs", [128, TT], F32)
        sg = S.sb("sg", [128, 2, TT], F32)
        gkl = S.sb("gkl", [16, TT], F32)

        def ring(name, shape, dt, n=2):
            return Ring([S.sb("%s%d" % (name, i), shape, dt) for i in range(n)])
        ktok_r = ring("ktok", [128, 128], F32)
        vtok_r = ring("vtok", [128, 256], BF16)
        t1_r = ring("t1", [128, 128], F32)
        e_r = ring("e", [128, 128], F32)
        gk_r = ring("gk", [128, 128], F32)
        ebT_r = ring("ebT", [128, 128], F32)
        enbT_r = ring("enbT", [128, 128], F32)
        ed2_r = ring("ed2", [128, 128], F32)
        qp_r = ring("qp", [128, 128], BF16)
        qp32_r = ring("qp32", [128, 128], F32)
        kp32_r = ring("kp32", [128, 128], F32)
        kpp_r = ring("kpp", [128, 128], BF16)
        AT_r = ring("AT", [128, 128], BF16)
        Sst = S.sb("Sst", [128, 256], F32)
        Sbf = S.sb("Sbf", [128, 256], BF16)
        S.memset("dve", Sst, Sst[:], 0.0)
        S.memset("dve", Sbf, Sbf[:], 0.0)
        sq_ring = ring("sq", [128, TT], F32)
        rstd = S.sb("rstd", [128, TT], F32)
        rtmp = S.sb("rtmp", [128, TT], F32)
        tmp_r = ring("tmp", [128, TT], F32)
        yo_r = ring("yo", [128, 2, TT], BF16)

        for t in range(ntiles):
            ts_ = slice(t * TT, (t + 1) * TT)
            h = h_ring.next()
            A["load_h"](S, h, t)

            def proj(c0, m):
                p = big.next()
                for kc in range(8):
                    S.mm(p, p[:m, :], wa, wa[:, kc, c0:c0 + m], h, h[:, kc, :], start=(kc == 0), stop=(kc == 7))
                return p
            hlo = hlo_ring.next()
            A["load_hlo"](S, hlo, t)

            def proj3(c0):
                p = big.next()
                n = 0
                for (wb_, hb_) in ((wqk_hi, h), (wqk_hi, hlo), (wqk_lo, h)):
                    for kc in range(8):
                        S.mm(p, p[:], wb_, wb_[:, kc, c0:c0 + 128], hb_, hb_[:, kc, :], start=(n == 0), stop=(n == 23))
                        n += 1
                return p
            p = proj3(0)
            S.act(qTs, qTs[:], p, p[:], AF.Copy, scale=128.0 ** -0.5)
            p = proj3(128)
            S.act(kTs, kTs[:], p, p[:], AF.Copy)
            for ec in range(2):
                p = proj(128 + ec * 128, 128)
                S.act(sg, sg[:, ec, :], p, p[:], AF.Silu)
            p = proj(768, 16)
            S.copy("dve", gkl, gkl[:], p, p[:16, :])
            def stageA(c):
                cs = slice(c * 128, (c + 1) * 128)
                p = big.next()
                for kc in range(8):
                    S.mm(p, p[:, :384], h, h[:, kc, cs], wa, wa[:, kc, 384:768], start=(kc == 0), stop=(kc == 7))
                ktok = ktok_r.next()
                vtok = vtok_r.next()
                S.copy("act", ktok, ktok[:], p, p[:, 0:128])
                S.copy("act", vtok, vtok[:], p, p[:, 128:384])
                pg = sm.next()
                S.mm(pg, pg[:, :128], gkl, gkl[:, cs], w2, w2[:], start=True, stop=True)
                t1 = t1_r.next()
                S.tt("dve", t1, t1[:], pg, pg[:, :128], bgb, bgb[:], ALU.add)
                e = e_r.next()
                S.act(e, e[:], t1, t1[:], AF.Exp, scale=-1.0)
                S.act(e, e[:], e, e[:], AF.Ln, bias=1.0)
                gk = gk_r.next()
                S.ts("dve", gk, gk[:], e, e[:], -1.0 / 16.0, None, ALU.mult)
                pbT = sm.next()
                S.mm(pbT, pbT[:, :128], gk, gk[:], U, U[:])
                pd2 = sm.next()
                S.mm(pd2, pd2[:, :128], UC, UC[:], gk, gk[:])
                ebT = ebT_r.next()
                enbT = enbT_r.next()
                ed2 = ed2_r.next()
                S.act(ebT, ebT[:], pbT, pbT[:, :128], AF.Exp)
                S.act(enbT, enbT[:], pbT, pbT[:, :128], AF.Exp, scale=-1.0)
                S.act(ed2, ed2[:], pd2, pd2[:, :128], AF.Exp)
                qp = qp_r.next()
                qp32 = qp32_r.next()
                kp32 = kp32_r.next()
                kpp = kpp_r.next()
                S.tt("dve", qp32, qp32[:], qTs, qTs[:, cs], ebT, ebT[:], ALU.mult)
                S.tt("dve", kp32, kp32[:], kTs, kTs[:, cs], enbT, enbT[:], ALU.mult)
                S.copy("act", qp, qp[:], qp32, qp32[:])
                S.tt("pool", kpp, kpp[:], ktok, ktok[:], ed2, ed2[:], ALU.mult)
                pA = sm.next()
                S.mm(pA, pA[:, :128], kp32, kp32[:], qp32, qp32[:])
                AT = AT_r.next()
                S.tt("dve", AT, AT[:], pA, pA[:, :128], U, U[:], ALU.mult)
                return dict(cs=cs, vtok=vtok, AT=AT, qp=qp, kpp=kpp, ebT=ebT)

            def stageB(c, v):
                cs, vtok, AT, qp, kpp, ebT = (v[k] for k in ('cs', 'vtok', 'AT', 'qp', 'kpp', 'ebT'))
                for ec in range(2):
                    es = slice(ec * 128, (ec + 1) * 128)
                    S.mm(po[ec], po[ec][:, cs], vtok, vtok[:, es], AT, AT[:], start=True, stop=False)
                    S.mm(po[ec], po[ec][:, cs], Sbf, Sbf[:, es], qp, qp[:], start=False, stop=True)
                pkv = sm.next()
                S.mm(pkv, pkv[:, :256], kpp, kpp[:], vtok, vtok[:])
                S.stt(Sst, Sst[:], Sst, Sst[:], ebT[:, 127:128], pkv, pkv[:, :256], ALU.mult, ALU.add, rd=[ebT])
                S.copy("act", Sbf, Sbf[:], Sst, Sst[:])

            va = {0: stageA(0)}
            for c in range(4):
                if c + 1 < 4:
                    va[c + 1] = stageA(c + 1)
                stageB(c, va.pop(c))
            pss = pss_ring.next()
            for ec in range(2):
                sq = sq_ring.next()
                S.act(sq, sq[:], po[ec], po[ec][:], AF.Square)
                S.mm(pss, pss[:], C["ones"], C["ones"][:], sq, sq[:], start=(ec == 0), stop=(ec == 1))
            S.act(rtmp, rtmp[:], pss, pss[:], AF.Sqrt, bias=C["eps%g" % EPS][:, 0:1], scale=1.0 / 256, rd=[C["eps%g" % EPS]])
            S.recip(rstd, rstd[:], rtmp, rtmp[:])
            yo = yo_r.next()
            for ec in range(2):
                tmp = tmp_r.next()
                S.stt(tmp, tmp[:], po[ec], po[ec][:], ng[:, ec:ec + 1], rstd, rstd[:], ALU.mult, ALU.mult, rd=[ng])
                S.tt("pool", yo, yo[:, ec, :], tmp, tmp[:], sg, sg[:, ec, :], ALU.mult)
            A["store_y"](S, yo, t)


def gla_inputs(inp, l, g, hT_b, cst, hTlo_b=None):
    w = inp["w_in"][l]
    cols = np.concatenate([np.arange(g * 128, (g + 1) * 128),
                           2064 + np.arange(g * 256, (g + 1) * 256),
                           512 + np.arange(g * 128, (g + 1) * 128),
                           1024 + np.arange(g * 256, (g + 1) * 256),
                           2048 + np.arange(16)])
    if hTlo_b is None:
        hTlo_b = np.zeros_like(hT_b)
    elif isinstance(hTlo_b, int):
        hTlo_b = None
    return {"hT": hT_b, "hTlo": hTlo_b, "w_gla": np.ascontiguousarray(w[:, cols]),
            "wgk2": np.ascontiguousarray(inp["gla_w_gk2"][l][:, g * 128:(g + 1) * 128]),
            "bgk": np.ascontiguousarray(inp["gla_b_gk"][l][g * 128:(g + 1) * 128].reshape(1, 128)),
            "ng": pvec(inp["gla_norm_g"][l]), "cst": cst}


def build_ssm(ntiles=SEQ // TT):
    nc = bass.Bass("TRN2", target_bir_lowering=False)
    hT = dram(nc, "hT", [DM, SEQ], BF16, "ExternalInput")
    A = {}
    A["w_ssm"] = dram(nc, "w_ssm", [DM, 1288], F32, "ExternalInput")
    A["cw"] = dram(nc, "cw", [128, 6, 4], F32, "ExternalInput")
    A["cb"] = dram(nc, "cb", [128, 6], F32, "ExternalInput")
    A["dtb"] = dram(nc, "dtb", [1, 8], F32, "ExternalInput")
    A["alog"] = dram(nc, "alog", [1, 8], F32, "ExternalInput")
    A["dsk"] = dram(nc, "dsk", [1, 8], F32, "ExternalInput")
    A["ngs"] = dram(nc, "ngs", [1, 512], F32, "ExternalInput")
    A["cst"] = dram(nc, "cst", [4, 128, 128], F32, "ExternalInput")
    yT = dram(nc, "yT", [512, SEQ], BF16, "ExternalOutput")
    A["load_h"] = lambda S, h, t: S.dma("sp", h[:], kcp(hT[:, t * TT:(t + 1) * TT]), writes=[h])
    A["store_y"] = lambda S, yo, t: S.dma("sp", yT[:, t * TT:(t + 1) * TT].rearrange("(c p) n -> p c n", p=128), yo[:], reads=[yo])
    with ExitStack() as st:
        S = Sched(nc, st)
        S.begin_phase()
        emit_ssm(nc, S, A, ntiles)
        S.end_phase()
    return nc


def emit_ssm(nc, S, A, ntiles=SEQ // TT):
    w_ssm, cwp, cbp, dtbp, alogp, dskp, ngp, cst = (A[k] for k in ("w_ssm", "cw", "cb", "dtb", "alog", "dsk", "ngs", "cst"))
    if True:
        C = load_consts(S, nc, cst)
        U = S.sb("U", [128, 128], F32)
        UC = S.sb("UC", [128, 128], F32)
        idf = S.sb("idf", [128, 128], F32)
        idb = S.sb("idb", [128, 128], BF16)
        S.dma("sp", U[:], cst[1], writes=[U])
        S.dma("sp", UC[:], cst[2], writes=[UC])
        S.dma("sp", idf[:], cst[3], writes=[idf])
        S.dma("pool", idb[:], cst[3], writes=[idb])
        ws = S.sb("ws", [128, 8, 1288], BF16)
        S.dma("pool", ws[:], kcp(w_ssm), writes=[ws])
        cw = S.sb("cw", [128, 6, 4], F32)
        cb = S.sb("cb", [128, 6], F32)
        S.dma("sp", cw[:], cwp, writes=[cw])
        S.dma("sp", cb[:], cbp, writes=[cb])
        dtb = S.sb("dtb", [128, 8], F32)
        a_b = S.sb("a_b", [128, 8], F32)
        dsk = S.sb("dsk", [128, 8], F32)
        ngs = S.sb("ngs", [128, 512], F32)
        S.dma("sp", dtb[:], dtbp.partition_broadcast(128), writes=[dtb])
        S.dma("sp", a_b[:], alogp.partition_broadcast(128), writes=[a_b])
        S.dma("sp", dsk[:], dskp.partition_broadcast(128), writes=[dsk])
        S.dma("sp", ngs[:], ngp.partition_broadcast(128), writes=[ngs])
        S.act(a_b, a_b[:], a_b, a_b[:], AF.Exp)
        S.ts("dve", a_b, a_b[:], a_b, a_b[:], -1.0, None, ALU.mult)

        def ring(name, shape, dt, n=2):
            return Ring([S.sb("%s%d" % (name, i), shape, dt) for i in range(n)])
        h_ring = ring("h", [128, 8, TT], BF16)
        big = Ring([S.ps("pb%d" % i, [128, 512], F32) for i in range(6)])
        sm = Ring(small_views(S, "sm", 2, 128))
        raw = S.sb("raw", [128, 6, TT + 3], F32)
        S.memset("dve", raw, raw[:, :, 0:3], 0.0)
        cacc_r = ring("cacc", [128, TT], F32)
        xc = S.sb("xc", [128, 4, TT], F32)
        BT = S.sb("BT", [128, TT], BF16)
        CT = S.sb("CT", [128, TT], BF16)
        sz_r = ring("sz", [128, 512], F32)
        t8_r = ring("t8", [128, 8], F32)
        dt_r = ring("dt", [128, 8], F32)
        da_r = ring("da", [128, 8], F32)
        xdt_r = ring("xdt", [128, 512], BF16)
        xD_r = ring("xD", [128, 512], F32)
        Btok_r = ring("Btok", [128, 128], BF16)
        rda_r = ring("rda", [128, 8, 128], F32)
        eL_r = ring("eL", [128, 8, 128], F32)
        GU_r = ring("GU", [128, 128], F32)
        M_r = ring("M", [128, 8, 128], BF16)
        cs_r = ring("cs", [128, 16], F32)
        ec_r = ring("ec", [128, 24], F32)
        xdtd_r = ring("xdtd", [128, 512], BF16)
        Sst = S.sb("Sst", [128, 512], F32)
        Sbf = S.sb("Sbf", [128, 512], BF16)
        S.memset("dve", Sst, Sst[:], 0.0)
        S.memset("dve", Sbf, Sbf[:], 0.0)
        y1_r = ring("y1", [128, 512], F32)
        y2_r = ring("y2", [128, 512], F32)
        junk = S.sb("junk", [128, 512], F32)
        ss_r = ring("ss", [128, 2], F32)
        yn_r = ring("yn", [128, 512], BF16)
        yo_r = ring("yo", [128, 4, TT], BF16)
        eps = C["eps%g" % EPS]

        def b8(ap):
            return ap.unsqueeze(2).to_broadcast([128, 8, 64])

        def v8(ap):
            return ap.rearrange("p (h q) -> p h q", h=8)

        for t in range(ntiles):
            ts_ = slice(t * TT, (t + 1) * TT)
            h = h_ring.next()
            A["load_h"](S, h, t)
            for ch in range(6):
                p = big.next()
                for kc in range(8):
                    S.mm(p, p[:], ws, ws[:, kc, 512 + ch * 128:640 + ch * 128], h, h[:, kc, :], start=(kc == 0), stop=(kc == 7))
                S.act(raw, raw[:, ch, 3:TT + 3], p, p[:], AF.Copy)
            for ch in range(6):
                acc = cacc_r.next()
                S.ts("dve", acc, acc[:], raw, raw[:, ch, 0:TT], cw[:, ch, 0:1], cb[:, ch:ch + 1], ALU.mult, ALU.add, rd=[cw, cb])
                for i in range(1, 4):
                    S.stt(acc, acc[:], raw, raw[:, ch, i:i + TT], cw[:, ch, i:i + 1], acc, acc[:], ALU.mult, ALU.add, rd=[cw])
                if ch < 4:
                    S.act(xc, xc[:, ch, :], acc, acc[:], AF.Silu)
                elif ch == 4:
                    S.act(BT, BT[:], acc, acc[:], AF.Silu)
                else:
                    S.act(CT, CT[:], acc, acc[:], AF.Silu)
            S.copy("pool", raw, raw[:, :, 0:3], raw, raw[:, :, TT:TT + 3])
            yo = yo_r.next()
            def stageA(c):
                cs = slice(c * 128, (c + 1) * 128)
                pz = big.next()
                for kc in range(8):
                    S.mm(pz, pz[:], h, h[:, kc, cs], ws, ws[:, kc, 0:512], start=(kc == 0), stop=(kc == 7))
                sz = sz_r.next()
                S.act(sz, sz[:], pz, pz[:], AF.Silu)
                pdt = sm.next()
                for kc in range(8):
                    S.mm(pdt, pdt[:, 0:8], h, h[:, kc, cs], ws, ws[:, kc, 1280:1288], start=(kc == 0), stop=(kc == 7))
                t8 = t8_r.next()
                S.tt("dve", t8, t8[:], pdt, pdt[:, 0:8], dtb, dtb[:], ALU.add)
                S.act(t8, t8[:], t8, t8[:], AF.Exp)
                dt = dt_r.next()
                S.act(dt, dt[:], t8, t8[:], AF.Ln, bias=1.0)
                da = da_r.next()
                S.tt("dve", da, da[:], dt, dt[:], a_b, a_b[:], ALU.mult)
                px = big.next()
                for ch in range(4):
                    S.mm(px, px[:, ch * 128:(ch + 1) * 128], xc, xc[:, ch, cs], idf, idf[:])
                xdt = xdt_r.next()
                xD = xD_r.next()
                S.tt("dve", xdt, v8(xdt[:]), px, v8(px[:]), dt, b8(dt[:]), ALU.mult)
                S.tt("dve", xD, v8(xD[:]), px, v8(px[:]), dsk, b8(dsk[:]), ALU.mult)
                pB = sm.next()
                S.mm(pB, pB[:], BT, BT[:, cs], idb, idb[:])
                Btok = Btok_r.next()
                S.copy("act", Btok, Btok[:], pB, pB[:])
                rda = rda_r.next()
                S.tt("pool", rda, rda[:], U, U[:].unsqueeze(1).to_broadcast([128, 8, 128]), da, da[:].unsqueeze(2).to_broadcast([128, 8, 128]), ALU.mult)
                pD = [big.next(), big.next()]
                for hf in range(2):
                    S.mm(pD[hf], pD[hf][:], UC, UC[:], rda, rda[:, hf * 4:(hf + 1) * 4, :].rearrange("p a b -> p (a b)"))
                pc = sm.next()
                S.mm(pc, pc[:, 0:8], U, U[:], da, da[:])
                pc2 = sm.next()
                S.mm(pc2, pc2[:, 0:8], C["ones"], C["ones"][:], da, da[:])
                csb = cs_r.next()
                S.copy("dve", csb, csb[:, 0:8], pc, pc[:, 0:8])
                S.copy("dve", csb, csb[:, 8:16], pc2, pc2[:, 0:8])
                ec = ec_r.next()
                S.act(ec, ec[:, 0:16], csb, csb[:, 0:16], AF.Exp)
                S.tt("dve", csb, csb[:, 0:8], csb, csb[:, 8:16], csb, csb[:, 0:8], ALU.subtract)
                S.act(ec, ec[:, 16:24], csb, csb[:, 0:8], AF.Exp)
                eL = eL_r.next()
                for hf in range(2):
                    S.act(eL, eL[:, hf * 4:(hf + 1) * 4, :].rearrange("p a b -> p (a b)"), pD[hf], pD[hf][:], AF.Exp)
                pG = sm.next()
                S.mm(pG, pG[:], BT, BT[:, cs], CT, CT[:, cs])
                GU = GU_r.next()
                S.tt("dve", GU, GU[:], pG, pG[:], U, U[:], ALU.mult)
                M = M_r.next()
                S.tt("pool", M, M[:], eL, eL[:], GU, GU[:].unsqueeze(1).to_broadcast([128, 8, 128]), ALU.mult)
                xdtd = xdtd_r.next()
                S.tt("dve", xdtd, v8(xdtd[:]), xdt, v8(xdt[:]), ec, b8(ec[:, 16:24]), ALU.mult)
                return dict(cs=cs, sz=sz, xdt=xdt, xD=xD, Btok=Btok, M=M, ec=ec, xdtd=xdtd)

            def stageB(c, v):
                cs, sz, xdt, xD, Btok, M, ec, xdtd = (v[k] for k in ('cs', 'sz', 'xdt', 'xD', 'Btok', 'M', 'ec', 'xdtd'))
                pyd = big.next()
                for hh in range(8):
                    S.mm(pyd, pyd[:, hh * 64:(hh + 1) * 64], M, M[:, hh, :], xdt, xdt[:, hh * 64:(hh + 1) * 64])
                pyo = big.next()
                S.mm(pyo, pyo[:], CT, CT[:, cs], Sbf, Sbf[:])
                pst = big.next()
                S.mm(pst, pst[:], Btok, Btok[:], xdtd, xdtd[:])
                S.tt("dve", Sst, v8(Sst[:]), Sst, v8(Sst[:]), ec, b8(ec[:, 8:16]), ALU.mult)
                S.tt("dve", Sst, Sst[:], Sst, Sst[:], pst, pst[:], ALU.add)
                S.copy("act", Sbf, Sbf[:], Sst, Sst[:])
                y1 = y1_r.next()
                S.tt("dve", y1, v8(y1[:]), pyo, v8(pyo[:]), ec, b8(ec[:, 0:8]), ALU.mult)
                S.tt("dve", y1, y1[:], y1, y1[:], pyd, pyd[:], ALU.add)
                y2 = y2_r.next()
                S.tt("dve", y2, y2[:], y1, y1[:], xD, xD[:], ALU.add)
                S.tt("dve", y2, y2[:], y2, y2[:], sz, sz[:], ALU.mult)
                ss = ss_r.next()
                S.act(junk, junk[:], y2, y2[:], AF.Square, accum=ss[:, 0:1], wr=[ss])
                S.act(ss, ss[:, 1:2], ss, ss[:, 0:1], AF.Sqrt, bias=eps[:, 0:1], scale=1.0 / 512, rd=[eps])
                S.recip(ss, ss[:, 0:1], ss, ss[:, 1:2])
                yn = yn_r.next()
                S.stt(yn, yn[:], y2, y2[:], ss[:, 0:1], ngs, ngs[:], ALU.mult, ALU.mult, rd=[ss])
                pT = big.next()
                for ch in range(4):
                    S.mm(pT, pT[:, ch * 128:(ch + 1) * 128], yn, yn[:, ch * 128:(ch + 1) * 128], idb, idb[:])
                S.copy("act", yo, yo[:, :, cs], pT, pT[:].rearrange("p (c n) -> p c n", c=4))

            va = {0: stageA(0)}
            for c in range(4):
                if c + 1 < 4:
                    va[c + 1] = stageA(c + 1)
                stageB(c, va.pop(c))
            A["store_y"](S, yo, t)


def ssm_inputs(inp, l, g, hT_b, cst):
    w = inp["w_in"][l]
    xcols = 5136 + np.arange(g * 512, (g + 1) * 512)
    bcols = 7184 + np.arange(g * 128, (g + 1) * 128)
    ccols = 7696 + np.arange(g * 128, (g + 1) * 128)
    cols = np.concatenate([3088 + np.arange(g * 512, (g + 1) * 512), xcols, bcols, ccols, 8208 + np.arange(g * 8, (g + 1) * 8)])
    cc = np.concatenate([xcols, bcols, ccols]) - 5136
    cwv = inp["ssm_conv_w"][l][:, cc]
    cw = np.ascontiguousarray(cwv.reshape(4, 6, 128).transpose(2, 1, 0))
    cb = np.ascontiguousarray(inp["ssm_conv_b"][l][cc].reshape(6, 128).T)
    hs = slice(g * 8, (g + 1) * 8)
    return {"hT": hT_b, "w_ssm": np.ascontiguousarray(w[:, cols]), "cw": cw, "cb": cb,
            "dtb": np.ascontiguousarray(inp["ssm_dt_bias"][l][hs].reshape(1, 8)),
            "alog": np.ascontiguousarray(inp["ssm_a_log"][l][hs].reshape(1, 8)),
            "dsk": np.ascontiguousarray(inp["ssm_d"][l][hs].reshape(1, 8)),
            "ngs": np.ascontiguousarray(inp["ssm_norm_g"][l][g * 512:(g + 1) * 512].reshape(1, 512)),
            "cst": cst}


C1_2PI = 6.28125
C2_2PI = 2.0 * math.pi - 6.28125


def build_diff(l, ntiles=SEQ // TT):
    nc = bass.Bass("TRN2", target_bir_lowering=False)
    hT = dram(nc, "hT", [DM, SEQ], BF16, "ExternalInput")
    A = {}
    A["w_diff"] = dram(nc, "w_diff", [DM, 1280], F32, "ExternalInput")
    A["pos"] = dram(nc, "pos", [1, SEQ], I32, "ExternalInput")
    A["invf"] = dram(nc, "invf", [128, 2], F32, "ExternalInput")
    A["lqk"] = dram(nc, "lqk", [4, 64], F32, "ExternalInput")
    A["ngd"] = dram(nc, "ngd", [1, 128], F32, "ExternalInput")
    A["cst"] = dram(nc, "cst", [4, 128, 128], F32, "ExternalInput")
    yT = dram(nc, "yT", [256, SEQ], BF16, "ExternalOutput")
    A["load_h"] = lambda S, h, t: S.dma("sp", h[:], kcp(hT[:, t * TT:(t + 1) * TT]), writes=[h])
    A["store_y"] = lambda S, yo, t: S.dma("sp", yT[:, t * TT:(t + 1) * TT].rearrange("(c p) n -> p c n", p=128), yo[:], reads=[yo])
    with ExitStack() as st:
        S = Sched(nc, st)
        S.begin_phase()
        emit_diff(nc, S, A, l, ntiles)
        S.end_phase()
    return nc


def emit_diff(nc, S, A, l, ntiles=SEQ // TT):
    lambda_init = 0.8 - 0.6 * math.exp(-0.3 * l)
    w_diff, posd, invfp, lqk, ngp, cst = (A[k] for k in ("w_diff", "pos", "invf", "lqk", "ngd", "cst"))
    if True:
        C = load_consts(S, nc, cst, extra_eps=(1e-5,))
        idb = S.sb("idb", [128, 128], BF16)
        S.dma("pool", idb[:], cst[3], writes=[idb])
        wd = S.sb("wd", [128, 8, 1280], BF16)
        S.dma("pool", wd[:], kcp(w_diff), writes=[wd])
        invf = S.sb("invf", [128, 2], F32)
        S.dma("sp", invf[:], invfp, writes=[invf])
        ngd = S.sb("ngd", [128, 128], F32)
        S.dma("sp", ngd[:], ngp.partition_broadcast(128), writes=[ngd])
        S.ts("dve", ngd, ngd[:], ngd, ngd[:], 1.0 - lambda_init, None, ALU.mult)
        lq = S.sb("lq", [128, 4, 64], F32)
        for i in range(4):
            S.dma("sp", lq[:, i, :], lqk[i:i + 1, :].partition_broadcast(128), writes=[lq], group=(i > 0))
        lt = S.sb("lt", [128, 2, 64], F32)
        S.tt("dve", lt, lt[:, 0, :], lq, lq[:, 0, :], lq, lq[:, 1, :], ALU.mult)
        S.tt("dve", lt, lt[:, 1, :], lq, lq[:, 2, :], lq, lq[:, 3, :], ALU.mult)
        ls = S.sb("ls", [128, 4], F32)
        S.op("dve", lambda E: E.reduce_sum(out=ls[:, 0:2], in_=lt[:], axis=AX.X), reads=[lt], writes=[ls])
        S.act(ls, ls[:, 0:2], ls, ls[:, 0:2], AF.Exp)
        S.tt("dve", ls, ls[:, 2:3], ls, ls[:, 1:2], ls, ls[:, 0:1], ALU.subtract)
        S.ts("dve", ls, ls[:, 3:4], ls, ls[:, 2:3], -lambda_init, None, ALU.add)
        nlam = ls

        def ring(name, shape, dt, n=2):
            return Ring([S.sb("%s%d" % (name, i), shape, dt) for i in range(n)])
        h_ring = ring("h", [128, 8, TT], BF16)
        KT = S.sb("KT", [128, 2, SEQ], BF16)
        VA = S.sb("VA", [128, 2, SEQ // 128, 129], BF16)
        S.memset("pool", VA, VA[:, :, :, 128:129], 1.0)
        QT_r = ring("QT", [128, 2, TT], BF16)
        big = Ring([S.ps("pb%d" % i, [128, 512], F32) for i in range(3)])
        pob = [[S.ps("po%d_%d" % (s, hf), [128, 512], F32) for hf in range(2)] for s in range(2)]
        sm = Ring(small_views(S, "sm", 1, 128))
        posi = S.sb("posi", [128, TT], I32)
        ang = S.sb("ang", [128, TT], F32)
        ang2 = S.sb("ang2", [128, TT], F32)
        ki = S.sb("ki", [128, TT], I32)
        kf = S.sb("kf", [128, TT], F32)
        yr = S.sb("yr", [128, TT], F32)
        Cs = S.sb("Cs", [128, TT], F32)
        Sn = S.sb("Sn", [128, TT], F32)
        ta_r = ring("ta", [128, TT], F32)
        tb_r = ring("tb", [128, TT], F32)
        PT_r = ring("PT", [128, TT], BF16, 5)
        r_r = ring("r", [128, 4], F32, 4)
        oa_r = ring("oa", [128, 128], F32, 4)
        junk_r = ring("junk", [128, 128], F32, 4)
        yn_r = ring("yn", [128, 128], BF16, 4)
        yo_r = ring("yo", [128, 2, TT], BF16)
        eps5 = C["eps%g" % 1e-5]

        def reduce_sin(dst, src):
            S.ts("dve", ki, ki[:], src, src[:], 1.0 / (2.0 * math.pi), None, ALU.mult)
            S.copy("dve", kf, kf[:], ki, ki[:])
            S.stt(yr, yr[:], kf, kf[:], -C1_2PI, src, src[:], ALU.mult, ALU.add)
            S.stt(yr, yr[:], kf, kf[:], -C2_2PI, yr, yr[:], ALU.mult, ALU.add)
            S.ts("dve", yr, yr[:], yr, yr[:], -3.1415925, 3.1415925, ALU.max, ALU.min)
            S.act(dst, dst[:], yr, yr[:], AF.Sin)

        for t in range(ntiles):
            ts_ = slice(t * TT, (t + 1) * TT)
            h = h_ring.next()
            A["load_h"](S, h, t)
            S.dma("sp", posi[:], posd[0:1, ts_].partition_broadcast(128), writes=[posi])
            S.copy("dve", ang, ang[:], posi, posi[:])
            S.ts("dve", ang, ang[:], ang, ang[:], invf[:, 0:1], None, ALU.mult, rd=[invf])
            reduce_sin(Sn, ang)
            S.ts("dve", Sn, Sn[:], Sn, Sn[:], invf[:, 1:2], None, ALU.mult, rd=[invf])
            S.ts("dve", ang2, ang2[:], ang, ang[:], math.pi / 2.0, None, ALU.add)
            reduce_sin(Cs, ang2)
            QT = QT_r.next()
            for hd in range(2):
                for (c0, dstb, dst) in ((hd * 128, QT, QT[:, hd, :]), (512 + hd * 128, KT, KT[:, hd, ts_])):
                    p1 = big.next()
                    for kc in range(8):
                        S.mm(p1, p1[:], wd, wd[:, kc, c0:c0 + 128], h, h[:, kc, :], start=(kc == 0), stop=(kc == 7))
                    p2 = big.next()
                    for kc in range(8):
                        S.mm(p2, p2[:], wd, wd[:, kc, c0 + 256:c0 + 384], h, h[:, kc, :], start=(kc == 0), stop=(kc == 7))
                    ta = ta_r.next()
                    tb = tb_r.next()
                    S.tt("dve", ta, ta[:], p1, p1[:], Cs, Cs[:], ALU.mult)
                    S.tt("dve", tb, tb[:], p2, p2[:], Sn, Sn[:], ALU.mult)
                    S.tt("pool", dstb, dst, ta, ta[:], tb, tb[:], ALU.add)
            for c in range(4):
                cs = slice(c * 128, (c + 1) * 128)
                pv = big.next()
                for kc in range(8):
                    S.mm(pv, pv[:, 0:256], h, h[:, kc, cs], wd, wd[:, kc, 1024:1280], start=(kc == 0), stop=(kc == 7))
                S.copy("act", VA, VA[:, :, 4 * t + c, 0:128], pv, pv[:, 0:256].rearrange("p (a b) -> p a b", a=2))
            yo = yo_r.next()
            for hd in range(2):
                nkb = 4 * t + 4
                started = [[False, False], [False, False]]
                iters = [(kb, s_) for kb in range(nkb) for s_ in range(2)]
                pend = {}

                def emit_qk(i):
                    kb, s_ = iters[i]
                    r = kb - 4 * t
                    q0 = max(r, 0)
                    qlo = q0 * 128
                    n = TT - qlo
                    ps_ = slice(s_ * 64, (s_ + 1) * 64)
                    pS = big.next()
                    S.mm(pS, pS[:, :n], KT, KT[ps_, hd, kb * 128:(kb + 1) * 128], QT, QT[ps_, hd, qlo:TT])
                    PT = PT_r.next()
                    S.act(PT, PT[:, :n], pS, pS[:, :n], AF.Exp, scale=0.125)
                    if r >= 0:
                        S.memset("pool", PT, PT[64:128, 0:64], 0.0)
                    pend[i] = (PT, q0, qlo)

                def emit_pv(i):
                    kb, s_ = iters[i]
                    PT, q0, qlo = pend.pop(i)
                    for qb in range(q0, 4):
                        col = qb * 128 - qlo
                        bank = pob[s_][qb // 2]
                        o = bank[:, (qb % 2) * 129:(qb % 2) * 129 + 129]
                        first = not started[s_][qb // 2]
                        started[s_][qb // 2] = True
                        S.mm(bank, o, PT, PT[:, col:col + 128], VA, VA[:, hd, kb, :], start=first, stop=(kb == 4 * t + qb and qb % 2 == 1))

                LOOK = 3
                for i in range(len(iters) + LOOK):
                    if i < len(iters):
                        emit_qk(i)
                    if i >= LOOK:
                        emit_pv(i - LOOK)
                QB = range(4)
                o1 = [pob[0][qb // 2] for qb in QB]
                o2 = [pob[1][qb // 2] for qb in QB]
                b0 = [(qb % 2) * 129 for qb in QB]
                rr = [r_r.next() for qb in QB]
                oa = [oa_r.next() for qb in QB]
                yn = [yn_r.next() for qb in QB]
                jk = [junk_r.next() for qb in QB]
                for qb in QB:
                    S.recip(rr[qb], rr[qb][:, 0:1], o1[qb], o1[qb][:, b0[qb] + 128:b0[qb] + 129])
                for qb in QB:
                    S.recip(rr[qb], rr[qb][:, 1:2], o2[qb], o2[qb][:, b0[qb] + 128:b0[qb] + 129])
                for qb in QB:
                    S.tt("dve", rr[qb], rr[qb][:, 2:3], rr[qb], rr[qb][:, 1:2], nlam, nlam[:, 3:4], ALU.mult)
                for qb in QB:
                    S.ts("dve", oa[qb], oa[qb][:], o1[qb], o1[qb][:, b0[qb]:b0[qb] + 128], rr[qb][:, 0:1], None, ALU.mult, rd=[rr[qb]])
                for qb in QB:
                    S.stt(oa[qb], oa[qb][:], o2[qb], o2[qb][:, b0[qb]:b0[qb] + 128], rr[qb][:, 2:3], oa[qb], oa[qb][:], ALU.mult, ALU.add, rd=[rr[qb]])
                for qb in QB:
                    S.act(jk[qb], jk[qb][:], oa[qb], oa[qb][:], AF.Square, accum=rr[qb][:, 3:4], wr=[rr[qb]])
                for qb in QB:
                    S.act(rr[qb], rr[qb][:, 1:2], rr[qb], rr[qb][:, 3:4], AF.Sqrt, bias=eps5[:, 0:1], scale=1.0 / 128, rd=[eps5])
                for qb in QB:
                    S.recip(rr[qb], rr[qb][:, 0:1], rr[qb], rr[qb][:, 1:2])
                for qb in QB:
                    S.stt(yn[qb], yn[qb][:], oa[qb], oa[qb][:], rr[qb][:, 0:1], ngd, ngd[:], ALU.mult, ALU.mult, rd=[rr[qb]])
                for qb in QB:
                    pT = sm.next()
                    S.mm(pT, pT[:], yn[qb], yn[qb][:], idb, idb[:])
                    S.copy("act", yo, yo[:, hd, qb * 128:(qb + 1) * 128], pT, pT[:])
            A["store_y"](S, yo, t)


def diff_inputs(inp, l, g, b, hT_b, cst):
    w = inp["w_in"][l]
    qc = 8240 + np.arange(g * 256, (g + 1) * 256)
    kc = 9264 + np.arange(g * 256, (g + 1) * 256)
    vc = 10288 + np.arange(g * 256, (g + 1) * 256)
    d = np.arange(256) % 64
    partner = np.arange(256) + np.where(d < 8, 8, np.where(d < 16, -8, 0))
    cols = np.concatenate([qc, qc[partner], kc, kc[partner], vc])
    p = np.arange(128) % 64
    invf = np.zeros((128, 2), np.float32)
    fr = (500000.0 ** (-np.arange(0, 16, 2, dtype=np.float32) / np.float32(16))).astype(np.float32)
    invf[:, 0] = np.where(p < 16, fr[p % 8], 0.0)
    invf[:, 1] = np.where(p < 8, -1.0, np.where(p < 16, 1.0, 0.0))
    lqk = np.stack([inp["diff_lq1"][l], inp["diff_lk1"][l], inp["diff_lq2"][l], inp["diff_lk2"][l]]).astype(np.float32)
    return {"hT": hT_b, "w_diff": np.ascontiguousarray(w[:, cols]),
            "pos": np.ascontiguousarray(inp["positions"][b].reshape(1, SEQ)), "invf": invf, "lqk": lqk,
            "ngd": np.ascontiguousarray(inp["diff_norm_g"][l].reshape(1, 128)), "cst": cst}


_PROGS = {}


def _prog(key, fn):
    if key not in _PROGS:
        _PROGS[key] = fn()
    return _PROGS[key]


def _run(nc, maps):
    return run_bass_kernel_spmd(nc, maps, core_ids=list(range(NCORES))).results


def tok_inputs(inp, l, xT_c, hT_c, yT_c, g_next, cst):
    w = inp["w_in"][l]
    return {"xT": xT_c, "g_next": pvec(g_next), "cst": cst, "hT": hT_c, "yT": yT_c,
            "w_gate": np.ascontiguousarray(w[:, 11312:14384]), "b_gate": pvec(inp["b_gate"][l]),
            "w_br": np.ascontiguousarray(np.concatenate([inp["w_br_gla"][l], inp["w_br_ssm"][l], inp["w_br_diff"][l]], axis=0)),
            "w_out": np.ascontiguousarray(inp["w_out"][l]), "w_up": np.ascontiguousarray(inp["w_mlp_up"][l]),
            "w_dn": np.ascontiguousarray(inp["w_mlp_down"][l]), "g_mlp": pvec(inp["norm_mlp_g"][l])}


def kernel_unfused(**inp):
    inp = {k: np.asarray(v) for k, v in inp.items()}
    cst = make_consts()
    x = inp["x"]
    xT = [np.ascontiguousarray(x[c // 4, (c % 4) * NT:(c % 4 + 1) * NT, :].T) for c in range(NCORES)]
    res = _run(_prog("first", lambda: build_tok("first")),
               [{"xT": xT[c], "g_next": pvec(inp["norm_mix_g"][0]), "cst": cst} for c in range(NCORES)])
    hT = [r["hT_out"] for r in res]
    hTl = [r["hT_lo"] for r in res]
    out = None
    for l in range(DEPTH):
        hTb = [np.ascontiguousarray(np.concatenate(hT[b * 4:(b + 1) * 4], axis=1)) for b in range(B)]
        hTlb = [np.ascontiguousarray(np.concatenate(hTl[b * 4:(b + 1) * 4], axis=1)) for b in range(B)]
        yg = _run(_prog("gla", build_gla), [gla_inputs(inp, l, c % 4, hTb[c // 4], cst, hTlb[c // 4]) for c in range(NCORES)])
        ys = _run(_prog("ssm", build_ssm), [ssm_inputs(inp, l, c % 4, hTb[c // 4], cst) for c in range(NCORES)])
        yd = _run(_prog(("diff", l), lambda: build_diff(l)), [diff_inputs(inp, l, c % 4, c // 4, hTb[c // 4], cst) for c in range(NCORES)])
        yTb = []
        for b in range(B):
            parts = [yg[b * 4 + g]["yT"] for g in range(4)] + [ys[b * 4 + g]["yT"] for g in range(4)] + [yd[b * 4 + g]["yT"] for g in range(4)]
            yTb.append(np.concatenate(parts, axis=0))
        last = (l == DEPTH - 1)
        g_next = inp["norm_final_g"] if last else inp["norm_mix_g"][l + 1]
        maps = [tok_inputs(inp, l, xT[c], hT[c], np.ascontiguousarray(yTb[c // 4][:, (c % 4) * NT:(c % 4 + 1) * NT]), g_next, cst)
                for c in range(NCORES)]
        mode = "last" if last else "mid"
        res = _run(_prog(mode, lambda: build_tok(mode)), maps)
        if last:
            out = np.empty((B, SEQ, DM), np.float32)
            for c in range(NCORES):
                out[c // 4, (c % 4) * NT:(c % 4 + 1) * NT, :] = res[c]["outT"].T
        else:
            xT = [r["xT_out"] for r in res]
            hT = [r["hT_out"] for r in res]
            hTl = [r["hT_lo"] for r in res]
    return out


GROUPS = [[0, 1, 2, 3], [4, 5, 6, 7]]


def build_fused(skip=()):
    nc = bass.Bass("TRN2", target_bir_lowering=False)
    ext = lambda name, shape, dt=F32: dram(nc, name, shape, dt, "ExternalInput")
    xT = ext("xT", [DM, NT])
    pos = ext("pos", [1, SEQ], I32)
    cst = ext("cst", [4, 128, 128])
    invf = ext("invf", [128, 2])
    g_mix = ext("g_mix", [DEPTH, 128, 8])
    g_fin = ext("g_fin", [128, 8])
    g_mlp = ext("g_mlp", [DEPTH, 128, 8])
    w_gate = ext("w_gate", [DEPTH, DM, 3072])
    b_gate = ext("b_gate", [DEPTH, 128, 24])
    w_br = ext("w_br", [DEPTH, 4096, DM])
    w_out = ext("w_out", [DEPTH, DM, DM])
    w_up = ext("w_up", [DEPTH, DM, 4096])
    w_dn = ext("w_dn", [DEPTH, 4096, DM])
    w_gla = ext("w_gla", [DEPTH, DM, 784])
    wgk2 = ext("wgk2", [DEPTH, 16, 128])
    bgk = ext("bgk", [DEPTH, 1, 128])
    ng = ext("ng", [DEPTH, 128, 2])
    w_ssm = ext("w_ssm", [DEPTH, DM, 1288])
    cw = ext("cw", [DEPTH, 128, 6, 4])
    cb = ext("cb", [DEPTH, 128, 6])
    dtb = ext("dtb", [DEPTH, 1, 8])
    alog = ext("alog", [DEPTH, 1, 8])
    dsk = ext("dsk", [DEPTH, 1, 8])
    ngs = ext("ngs", [DEPTH, 1, 512])
    w_diff = ext("w_diff", [DEPTH, DM, 1280])
    lqk = ext("lqk", [DEPTH, 4, 64])
    ngd = ext("ngd", [DEPTH, 1, 128])
    outT = dram(nc, "outT", [DM, NT], F32, "ExternalOutput")
    xs = nc.dram_tensor("xs_i", [DM, NT], F32).ap()
    TPR = NT // TT
    hsrc = nc.dram_tensor("hsrc_i", [2 * TPR, DM, TT], BF16).ap()
    hgat = nc.dram_tensor("hgat_i", [2 * TPR, 4 * DM, TT], BF16).ap()
    ysrc = nc.dram_tensor("ysrc_i", [4, 4, 256, NT], BF16).ap()
    ygat = nc.dram_tensor("ygat_i", [4, 4, 1024, NT], BF16).ap()
    ygat3 = ygat.rearrange("q a r n -> q (a r) n")

    with ExitStack() as st:
        S = Sched(nc, st)
        hsrc_b = [Buf("hsrc_b%d" % i) for i in range(2 * TPR)]
        hgat_b = [Buf("hgat_b%d" % i) for i in range(2 * TPR)]
        ysrc_b = [[Buf("ysrc_b%d_%d" % (q, a)) for a in range(4)] for q in range(4)]
        ygat_b = Buf("ygat_b")
        xs_b = Buf("xs_b")
        S.global_bufs += [ygat_b, xs_b] + hsrc_b + hgat_b + [b for row in ysrc_b for b in row]

        def after_h(S_, t):
            for which in range(2):
                i = 2 * t + which
                S_.collective("AllGather", hsrc_b[i], hsrc[i], hgat_b[i], hgat[i], GROUPS)

        def load_h_from(which):
            def f(S_, h, t):
                r, i = t // TPR, 2 * (t % TPR) + which
                S_.dma("sp", h[:], kcp(hgat[i][r * DM:(r + 1) * DM, :]), reads=[hgat_b[i]], writes=[h])
            return f

        h_io = {"h_in": (lambda t: hsrc[2 * t]), "h_out": (lambda t, which: hsrc[2 * t + which]),
                "h_wr": (lambda t: [[hsrc_b[2 * t]], [hsrc_b[2 * t + 1]]]), "after_h": after_h}

        def store_y_parts(parts):
            def f(S_, yo, t):
                q, tl = t // (NT // TT), (t % (NT // TT)) * TT
                for j, a in enumerate(parts):
                    S_.dma("sp", ysrc[q, a][:, tl:tl + TT].rearrange("(c p) n -> p c n", p=128), yo[:, 2 * j:2 * j + 2, :],
                           reads=[yo], writes=[ysrc_b[q][a]], sembuf=yo, group=True)
                if t % (NT // TT) == NT // TT - 1:
                    for a in parts:
                        S_.collective("AllGather", ysrc_b[q][a], ysrc[q, a], ygat_b, ygat[q, a], GROUPS)
            return f

        qcache = {}

        def load_y(S_, yub, t):
            tl = t * TT
            ph = S_.phase_id

            def qval(E):
                if ph not in qcache:
                    qcache[ph] = E.snap(E.partition_id() % 4)
                return qcache[ph]
            src = (lambda E, tl=tl: ygat3[bass.ds(qval(E), 1), :, tl:tl + TT].rearrange("o (k p) n -> p (o k) n", p=128))
            S_.dma("sp", yub[:], src, reads=[ygat_b], writes=[yub])

        S.begin_phase()
        if "first" not in skip:
            emit_tok(nc, S, dict(h_io, **{"xT": xT, "g_next": g_mix[0], "cst": cst}), "first")
        S.end_phase()
        for l in range(DEPTH):
            S.begin_phase()
            if "gla" not in skip:
              emit_gla(nc, S, {"w_gla": w_gla[l], "wgk2": wgk2[l], "bgk": bgk[l], "ng": ng[l], "cst": cst,
                             "load_h": load_h_from(0), "load_hlo": load_h_from(1), "store_y": store_y_parts((0,))}, SEQ // TT)
            S.end_phase()
            S.begin_phase()
            if "ssm" not in skip:
              emit_ssm(nc, S, {"w_ssm": w_ssm[l], "cw": cw[l], "cb": cb[l], "dtb": dtb[l], "alog": alog[l], "dsk": dsk[l], "ngs": ngs[l],
                             "cst": cst, "load_h": load_h_from(0), "store_y": store_y_parts((1, 2))}, SEQ // TT)
            S.end_phase()
            S.begin_phase()
            if "diff" not in skip:
              emit_diff(nc, S, {"w_diff": w_diff[l], "pos": pos, "invf": invf, "lqk": lqk[l], "ngd": ngd[l], "cst": cst,
                              "load_h": load_h_from(0), "store_y": store_y_parts((3,))}, l, SEQ // TT)
            S.end_phase()
            last = (l == DEPTH - 1)
            A = {"xT": (xT if l == 0 else xs), "g_next": (g_fin if last else g_mix[l + 1]), "cst": cst, "h_in": h_io["h_in"], "load_y": load_y,
                 "w_gate": w_gate[l], "b_gate": b_gate[l], "w_br": w_br[l], "w_out": w_out[l], "w_up": w_up[l], "w_dn": w_dn[l], "g_mlp": g_mlp[l]}
            if last:
                A["outT"] = outT
            else:
                A.update(h_io)
                A["xT_out"] = xs
            S.begin_phase()
            emit_tok(nc, S, A, "last" if last else "mid")
            S.end_phase()
    return nc


def fused_y_row_order():
    rows = []
    for a in range(4):
        for r in range(4):
            j = np.arange(256)
            if a == 0:
                rows.append(r * 256 + j)
            elif a in (1, 2):
                rows.append(1024 + r * 512 + (a - 1) * 256 + j)
            else:
                rows.append(3072 + r * 256 + j)
    return np.concatenate(rows)


def fused_inputs(inp, c, cst):
    b, g = c // 4, c % 4
    x = inp["x"]
    m = {"xT": np.ascontiguousarray(x[b, g * NT:(g + 1) * NT, :].T), "cst": cst,
         "pos": np.ascontiguousarray(inp["positions"][b].reshape(1, SEQ)),
         "g_mix": np.stack([pvec(inp["norm_mix_g"][l]) for l in range(DEPTH)]), "g_fin": pvec(inp["norm_final_g"]),
         "g_mlp": np.stack([pvec(inp["norm_mlp_g"][l]) for l in range(DEPTH)]),
         "w_gate": np.ascontiguousarray(inp["w_in"][:, :, 11312:14384]),
         "b_gate": np.stack([pvec(inp["b_gate"][l]) for l in range(DEPTH)]),
         "w_br": np.ascontiguousarray(np.concatenate([inp["w_br_gla"], inp["w_br_ssm"], inp["w_br_diff"]], axis=1)[:, fused_y_row_order(), :]),
         "w_out": np.ascontiguousarray(inp["w_out"]), "w_up": np.ascontiguousarray(inp["w_mlp_up"]), "w_dn": np.ascontiguousarray(inp["w_mlp_down"])}
    per = {}
    for l in range(DEPTH):
        d = {}
        d.update(gla_inputs(inp, l, g, None, cst, 0))
        d.update(ssm_inputs(inp, l, g, None, cst))
        d.update(diff_inputs(inp, l, g, b, None, cst))
        for k, v in d.items():
            if k in ("hT", "hTlo", "cst", "pos", "invf"):
                continue
            per.setdefault(k, []).append(v)
        if l == 0:
            m["invf"] = d["invf"]
    for k, v in per.items():
        m[k] = np.ascontiguousarray(np.stack(v))
    return m


_FUSED = []


def kernel(**inp):
    inp = {k: np.asarray(v) for k, v in inp.items()}
    cst = make_consts()
    if not _FUSED:
        _FUSED.append(build_fused())
    maps = [fused_inputs(inp, c, cst) for c in range(NCORES)]
    res = run_bass_kernel_spmd(_FUSED[0], maps, core_ids=list(range(NCORES))).results
    out = np.empty((B, SEQ, DM), np.float32)
    for c in range(NCORES):
        out[c // 4, (c % 4) * NT:(c % 4 + 1) * NT, :] = res[c]["outT"].T
    return out
```
